# Optimizing a Trainium2 kernel written in Bass

```python
import math
import jax
import jax.numpy as jnp
from jax import lax
import numpy as np

D_MODEL = 1024
BATCH = 4
SEQ = 4096
DEPTH = 2

HEAD = 64
D_MIX = D_MODEL
NORM_EPS = 1e-6

RW_WIDTH = D_MIX // 4
RW_HEADS = RW_WIDTH // HEAD
RW_W_RANK = 64
RW_A_RANK = 64
RW_G_RANK = 128
RW_DECAY_SCALE = math.exp(-0.5)
RW_LN_EPS = 64e-5
RW_SIZES = (RW_WIDTH, RW_WIDTH, RW_WIDTH, RW_W_RANK, RW_A_RANK, RW_G_RANK)
RW_PROJ = sum(RW_SIZES)

LRU_WIDTH = D_MIX // 2
LRU_BLOCKS = LRU_WIDTH // HEAD
LRU_CONV = 4
LRU_C = 8.0
LRU_SIZES = (LRU_WIDTH, LRU_WIDTH)
LRU_PROJ = sum(LRU_SIZES)

GLA_WIDTH = D_MIX - RW_WIDTH - LRU_WIDTH
GLA_HEADS = 4
GLA_DV = GLA_WIDTH // GLA_HEADS
GLA_DK = GLA_DV // 2
GLA_KEY_WIDTH = GLA_HEADS * GLA_DK
GLA_GATE_RANK = 16
GLA_GATE_NORM = 16.0
GLA_CHUNK = 64
GLA_SIZES = (GLA_KEY_WIDTH, GLA_KEY_WIDTH, GLA_WIDTH, GLA_GATE_RANK, GLA_WIDTH)
GLA_PROJ = sum(GLA_SIZES)

P_IN = RW_PROJ + LRU_PROJ + GLA_PROJ
D_FF = ((8 * D_MODEL + 3 * 256 - 1) // (3 * 256)) * 256

kernel_name = "hybrid_rwkv7_rglru_gla_block"


def _split(z, sizes):
    return jnp.split(z, np.cumsum(sizes)[:-1].tolist(), axis=-1)


def rms_norm(x, g, eps=NORM_EPS):
    xf = x.astype(jnp.float32)
    y = xf * lax.rsqrt(jnp.mean(xf * xf, axis=-1, keepdims=True) + eps)
    return (y * g.astype(jnp.float32)).astype(x.dtype)


def token_shift(z):
    return jnp.pad(z, ((0, 0), (1, 0), (0, 0)))[:, :-1]


def wkv7_scan(r, w, k, v, a, b):
    bsz, _, nh, n = r.shape

    def step(S, inp):
        r_t, w_t, k_t, v_t, a_t, b_t = inp
        sa = jnp.einsum('bhvk,bhk->bhv', S, a_t)
        S = (S * w_t[:, :, None, :] + sa[..., None] * b_t[:, :, None, :]
             + v_t[..., None] * k_t[:, :, None, :])
        return S, jnp.einsum('bhvk,bhk->bhv', S, r_t)

    xs = tuple(jnp.swapaxes(z, 0, 1) for z in (r, w, k, v, a, b))
    s0 = jnp.zeros((bsz, nh, n, n), jnp.float32)
    _, ys = lax.scan(step, s0, xs)
    return jnp.swapaxes(ys, 0, 1)


def rwkv7_mixer(p, mu, w0, w_up, a0, a_up, g_up, k_k, k_a, r_k, ln_g, ln_b):
    bsz, t, _ = p.shape
    f32 = jnp.float32
    p = p + (token_shift(p) - p) * mu
    r, k, v, w_lo, a_lo, g_lo = _split(p, RW_SIZES)
    w_raw = w0 + jnp.tanh(w_lo) @ w_up
    decay = jnp.exp(-RW_DECAY_SCALE * jax.nn.sigmoid(w_raw.astype(f32)))
    a = jax.nn.sigmoid(a0 + a_lo @ a_up)
    g = jax.nn.sigmoid(g_lo) @ g_up

    def hs(z):
        return z.reshape(bsz, t, RW_HEADS, HEAD).astype(f32)

    kk = hs(k * k_k)
    kk = kk / jnp.maximum(jnp.sqrt(jnp.sum(kk * kk, axis=-1, keepdims=True)), 1e-12)
    k = k * (1 + (a - 1) * k_a)
    r_h, k_h, v_h, a_h = hs(r), hs(k), hs(v), hs(a)
    y = wkv7_scan(r_h, hs(decay), k_h, v_h, -kk, kk * a_h)
    mean = jnp.mean(y, axis=-1, keepdims=True)
    var = jnp.mean(jnp.square(y - mean), axis=-1, keepdims=True)
    y = (y - mean) * lax.rsqrt(var + RW_LN_EPS)
    y = y * ln_g.reshape(RW_HEADS, HEAD).astype(f32) + ln_b.reshape(RW_HEADS, HEAD).astype(f32)
    y = y + jnp.sum(r_h * k_h * r_k.astype(f32), axis=-1, keepdims=True) * v_h
    return (y.reshape(bsz, t, RW_WIDTH) * g.astype(f32)).astype(p.dtype)


def causal_depthwise_conv(z, w, b):
    t = z.shape[1]
    zp = jnp.pad(z, ((0, 0), (LRU_CONV - 1, 0), (0, 0)))
    return b + sum(zp[:, j:j + t] * w[j] for j in range(LRU_CONV))


def rglru_mixer(p, conv_w, conv_b, wa, ba, wx, bx, lam, norm_g):
    bsz, t, _ = p.shape
    f32 = jnp.float32
    xb, gate = _split(p, LRU_SIZES)
    xc = causal_depthwise_conv(xb, conv_w, conv_b)
    xblk = xc.reshape(bsz, t, LRU_BLOCKS, HEAD)
    gate_r = jax.nn.sigmoid(jnp.einsum('btnc,ncd->btnd', xblk, wa).reshape(bsz, t, LRU_WIDTH) + ba)
    gate_i = jax.nn.sigmoid(jnp.einsum('btnc,ncd->btnd', xblk, wx).reshape(bsz, t, LRU_WIDTH) + bx)
    log_a = (-LRU_C * gate_r.astype(f32)) * jax.nn.softplus(-lam.astype(f32))
    a = jnp.exp(log_a)
    u = jnp.sqrt(-jnp.expm1(2.0 * log_a)) * (gate_i * xc).astype(f32)

    def combine(left, right):
        a_l, u_l = left
        a_r, u_r = right
        return a_l * a_r, a_r * u_l + u_r

    _, h = lax.associative_scan(combine, (a, u), axis=1)
    y = h * jax.nn.gelu(gate.astype(f32))
    y = rms_norm(y.reshape(bsz, t, LRU_BLOCKS, HEAD), norm_g.reshape(LRU_BLOCKS, HEAD))
    return y.reshape(bsz, t, LRU_WIDTH).astype(p.dtype)


def gla_chunked(q, k, v, log_a):
    bsz, nh, t, dk = q.shape
    dv = v.shape[-1]
    c = GLA_CHUNK
    n = t // c
    q, k, v, log_a = (z.reshape(bsz, nh, n, c, z.shape[-1]) for z in (q, k, v, log_a))
    bcum = jnp.cumsum(log_a, axis=3)
    q_in = q * jnp.exp(bcum)
    k_in = k * jnp.exp(-bcum)
    causal = jnp.tril(jnp.ones((c, c), dtype=bool))
    scores = jnp.where(causal, jnp.einsum('bhnid,bhnjd->bhnij', q_in, k_in), 0.0)
    o_intra = jnp.einsum('bhnij,bhnjv->bhniv', scores, v)
    b_last = bcum[:, :, :, -1:, :]
    chunk_kv = jnp.einsum('bhncd,bhncv->bhndv', k * jnp.exp(b_last - bcum), v)
    chunk_decay = jnp.exp(b_last[:, :, :, 0, :])

    def step(S, inp):
        dec, kv = inp
        return S * dec[..., None] + kv, S

    s0 = jnp.zeros((bsz, nh, dk, dv), jnp.float32)
    _, s_prev = lax.scan(step, s0, (jnp.moveaxis(chunk_decay, 2, 0), jnp.moveaxis(chunk_kv, 2, 0)))
    s_prev = jnp.moveaxis(s_prev, 0, 2)
    o_inter = jnp.einsum('bhncd,bhndv->bhncv', q_in, s_prev)
    return (o_intra + o_inter).reshape(bsz, nh, t, dv)


def gla_mixer(p, gk_up, gk_b, norm_g):
    bsz, t, _ = p.shape
    f32 = jnp.float32
    q, k, v, gk_lo, g = _split(p, GLA_SIZES)
    log_a = jax.nn.log_sigmoid((gk_lo @ gk_up + gk_b).astype(f32)) / GLA_GATE_NORM

    def heads(z, d):
        return jnp.swapaxes(z.reshape(bsz, t, GLA_HEADS, d), 1, 2).astype(f32)

    o = gla_chunked(heads(q, GLA_DK) * GLA_DK ** -0.5, heads(k, GLA_DK),
                    heads(v, GLA_DV), heads(log_a, GLA_DK))
    o = jnp.swapaxes(o, 1, 2)
    o = rms_norm(o, norm_g) * jax.nn.silu(g.reshape(bsz, t, GLA_HEADS, GLA_DV).astype(f32))
    return o.reshape(bsz, t, GLA_WIDTH).astype(p.dtype)


def setup_inputs(seed: int = 0) -> dict:
    key = jax.random.key(seed)
    ks = iter(jax.random.split(key, 40))
    L = DEPTH

    def nrm(shape, scale):
        return scale * jax.random.normal(next(ks), shape, jnp.float32)

    def uni(shape, lo, hi):
        return jax.random.uniform(next(ks), shape, jnp.float32, lo, hi)

    a_init = uni((L, LRU_WIDTH), 0.9, 0.999)
    s = a_init ** (1.0 / LRU_C)
    lam = jnp.log(s) - jnp.log1p(-s)
    return {
        "x": nrm((BATCH, SEQ, D_MODEL), 1.0),
        "norm1_g": 1.0 + nrm((L, D_MODEL), 0.02),
        "w_in": nrm((L, D_MODEL, P_IN), D_MODEL ** -0.5),
        "rw_mu": uni((L, RW_PROJ), 0.0, 1.0),
        "rw_w0": uni((L, RW_WIDTH), -4.0, 1.0),
        "rw_w_up": nrm((L, RW_W_RANK, RW_WIDTH), 0.5 * RW_W_RANK ** -0.5),
        "rw_a0": nrm((L, RW_WIDTH), 0.1),
        "rw_a_up": nrm((L, RW_A_RANK, RW_WIDTH), 0.5 * RW_A_RANK ** -0.5),
        "rw_g_up": nrm((L, RW_G_RANK, RW_WIDTH), RW_G_RANK ** -0.5),
        "rw_k_k": 0.85 + nrm((L, RW_WIDTH), 0.02),
        "rw_k_a": 1.0 + nrm((L, RW_WIDTH), 0.02),
        "rw_r_k": nrm((L, RW_HEADS, HEAD), 0.1),
        "rw_ln_g": 1.0 + nrm((L, RW_WIDTH), 0.02),
        "rw_ln_b": nrm((L, RW_WIDTH), 0.01),
        "lru_conv_w": nrm((L, LRU_CONV, LRU_WIDTH), LRU_CONV ** -0.5),
        "lru_conv_b": nrm((L, LRU_WIDTH), 0.01),
        "lru_wa": nrm((L, LRU_BLOCKS, HEAD, HEAD), HEAD ** -0.5),
        "lru_ba": nrm((L, LRU_WIDTH), 0.01),
        "lru_wx": nrm((L, LRU_BLOCKS, HEAD, HEAD), HEAD ** -0.5),
        "lru_bx": nrm((L, LRU_WIDTH), 0.01),
        "lru_lam": lam,
        "lru_norm_g": 1.0 + nrm((L, LRU_WIDTH), 0.02),
        "gla_gk_up": nrm((L, GLA_GATE_RANK, GLA_KEY_WIDTH), GLA_GATE_RANK ** -0.5),
        "gla_gk_b": nrm((L, GLA_KEY_WIDTH), 0.1),
        "gla_norm_g": 1.0 + nrm((L, GLA_DV), 0.02),
        "w_out": nrm((L, D_MIX, D_MODEL), D_MIX ** -0.5),
        "norm2_g": 1.0 + nrm((L, D_MODEL), 0.02),
        "ffn_w_gate": nrm((L, D_MODEL, D_FF), D_MODEL ** -0.5),
        "ffn_w_up": nrm((L, D_MODEL, D_FF), D_MODEL ** -0.5),
        "ffn_w_down": nrm((L, D_FF, D_MODEL), D_FF ** -0.5),
        "final_norm_g": 1.0 + nrm((D_MODEL,), 0.02),
    }


def reference(x, norm1_g, w_in, rw_mu, rw_w0, rw_w_up, rw_a0, rw_a_up, rw_g_up, rw_k_k,
              rw_k_a, rw_r_k, rw_ln_g, rw_ln_b, lru_conv_w, lru_conv_b, lru_wa, lru_ba,
              lru_wx, lru_bx, lru_lam, lru_norm_g, gla_gk_up, gla_gk_b, gla_norm_g, w_out,
              norm2_g, ffn_w_gate, ffn_w_up, ffn_w_down, final_norm_g):
    for l in range(DEPTH):
        hn = rms_norm(x, norm1_g[l])
        p_rw, p_lru, p_gla = _split(hn @ w_in[l], (RW_PROJ, LRU_PROJ, GLA_PROJ))
        y_rw = rwkv7_mixer(p_rw, rw_mu[l], rw_w0[l], rw_w_up[l], rw_a0[l], rw_a_up[l],
                           rw_g_up[l], rw_k_k[l], rw_k_a[l], rw_r_k[l], rw_ln_g[l], rw_ln_b[l])
        y_lru = rglru_mixer(p_lru, lru_conv_w[l], lru_conv_b[l], lru_wa[l], lru_ba[l],
                            lru_wx[l], lru_bx[l], lru_lam[l], lru_norm_g[l])
        y_gla = gla_mixer(p_gla, gla_gk_up[l], gla_gk_b[l], gla_norm_g[l])
        x = x + jnp.concatenate([y_rw, y_lru, y_gla], axis=-1) @ w_out[l]
        hn = rms_norm(x, norm2_g[l])
        x = x + (jax.nn.silu(hn @ ffn_w_gate[l]) * (hn @ ffn_w_up[l])) @ ffn_w_down[l]
    return rms_norm(x, final_norm_g)
```

```python
import contextlib
import math
import os
import numpy as np
import concourse.bass as bass
import concourse.mybir as mybir
from concourse.bass_utils import run_bass_kernel_spmd

F32 = mybir.dt.float32
BF16 = mybir.dt.bfloat16
AF = mybir.ActivationFunctionType
ALU = mybir.AluOpType

CHUNK = 8000
SAMEQ = os.environ.get('K_SAMEQ', '1') == '1'
NDMASEM = 12

D = 1024
PIN = 2832
DFF = 2816
NFC = DFF // 128
EPS = 1e-6
RW_EPS = 64e-5
DEC = math.exp(-0.5)
T = 256
NCH = T // 128


class Buf:
    __slots__ = ("name", "writers", "readers")

    def __init__(self, name=""):
        self.name = name
        self.writers = {}
        self.readers = {}


def _dep_kv(d):
    if d[0] == "e":
        return ("e", d[1], d[2] // CHUNK), d[2] % CHUNK + 1
    return ("d", d[1], d[2]), d[3]


def _merge(dst, src):
    for k, v in src.items():
        if dst.get(k, 0) < v:
            dst[k] = v


class Tile:
    __slots__ = ("ap", "buf")

    def __init__(self, ap, buf=None):
        self.ap = ap
        self.buf = buf if buf is not None else Buf()

    def __getitem__(self, k):
        return Tile(self.ap[k], self.buf)

    def bitcast(self, dt):
        return Tile(self.ap.bitcast(dt), self.buf)

    def re(self, s, **kw):
        return Tile(self.ap.rearrange(s, **kw), self.buf)


class Prog:
    ENG = ("pe", "act", "dve", "pool", "sp")

    def __init__(self, nc):
        self.nc = nc
        self.stack = contextlib.ExitStack()
        self.streams = {e: [] for e in self.ENG}
        self.count = {e: 0 for e in self.ENG}
        self.esems = {e: [] for e in self.ENG}
        self.dsems = {}
        self.dma_n = {e: 0 for e in self.ENG}
        self.dma_hist = {e: {} for e in self.ENG}
        self.waited = {e: {} for e in self.ENG}
        self.n_t = 0
        self.final_deps = []
        self.arena = None
        self.arena_off = 0
        self.arena_size = 0
        self.psb = []
        self.ps_i = 0
        self.live = []

    def init_mem(self, arena_f32_cols):
        self.arena_size = arena_f32_cols
        self.arena = self.stack.enter_context(
            self.nc.sbuf_tensor("arena", [128, arena_f32_cols], F32))
        for i in range(8):
            t = self.stack.enter_context(self.nc.psum_tensor(f"psb{i}", [128, 512], F32))
            self.psb.append(Tile(t[:, :], Buf(f"ps{i}")))

    def alloc(self, free_elems, dt=F32, name=""):
        ncol = free_elems if dt == F32 else (free_elems + 1) // 2
        if self.arena_off + ncol > self.arena_size:
            raise RuntimeError(f"arena overflow at {name}: {self.arena_off}+{ncol}>{self.arena_size}")
        s0, s1 = self.arena_off, self.arena_off + ncol
        self.arena_off += ncol
        self.hi = max(getattr(self, "hi", 0), s1)
        keep, over = [], []
        for ent in self.live:
            (over if (ent[0] < s1 and s0 < ent[1]) else keep).append(ent)
        if len(over) == 1 and over[0][0] == s0 and over[0][1] == s1 and over[0][2] == (dt, free_elems):
            return over[0][3]
        ap = self.arena[:, s0:s1]
        if dt != F32:
            ap = ap.bitcast(dt)[:, 0:free_elems]
        buf = Buf(name)
        for ent in over:
            _merge(buf.writers, ent[3].buf.writers)
            _merge(buf.readers, ent[3].buf.readers)
        t = Tile(ap, buf)
        keep.append((s0, s1, (dt, free_elems), t))
        self.live = keep
        return t

    def psum(self):
        t = self.psb[self.ps_i]
        self.ps_i = (self.ps_i + 1) % 8
        return t

    def _deps(self, e, reads, writes, is_dma=False):
        need = {}
        for r in reads:
            _merge(need, r.writers)
        for w in writes:
            _merge(need, w.writers)
            _merge(need, w.readers)
        out = []
        for key, val in need.items():
            if key[0] == "e" and key[1] == e and (e == "pe" or not SAMEQ):
                continue
            if self.waited[e].get(key, 0) >= val:
                continue
            self.waited[e][key] = val
            out.append((key, val))
        return out

    def _record(self, d, reads, writes, is_dma):
        k, v = _dep_kv(d)
        for w in writes:
            if is_dma:
                w.writers = {kk: vv for kk, vv in w.writers.items() if kk[0] == "d"}
            else:
                w.writers = {}
            w.writers[k] = v
            w.readers = {}
        for r in reads:
            if r.readers.get(k, 0) < v:
                r.readers[k] = v

    def _sem(self, key):
        if key[0] == "e":
            return self.esems[key[1]][key[2]]
        return self.dsems[(key[1], key[2])]

    def op(self, e, fn, reads=(), writes=()):
        reads = [r.buf if isinstance(r, Tile) else r for r in reads]
        writes = [w.buf if isinstance(w, Tile) else w for w in writes]
        waits = self._deps(e, reads, writes)
        idx = self.count[e]
        self.count[e] += 1
        mykey = ("e", e, idx // CHUNK)

        def emit(eng, waits=waits, fn=fn, mykey=mykey):
            for k, v in waits:
                eng.wait_ge(self._sem(k), v)
            fn(eng).then_inc(self._sem(mykey), 1)

        self.streams[e].append(emit)
        d = ("e", e, idx)
        self._record(d, reads, writes, False)
        return d

    def dma(self, q, out, in_, reads=(), writes=(), **kw):
        reads = [r.buf if isinstance(r, Tile) else r for r in reads]
        writes = [w.buf if isinstance(w, Tile) else w for w in writes]
        waits = self._deps(q, reads, writes, True)
        n = self.dma_n[q]
        self.dma_n[q] += 1
        slot = n % NDMASEM
        prev = self.dma_hist[q].get(slot, 0)
        val = prev + 16
        self.dma_hist[q][slot] = val
        key = ("d", q, slot)
        if prev > 0 and self.waited[q].get(key, 0) < prev:
            waits = waits + [(key, prev)]
            self.waited[q][key] = prev

        def emit(eng, waits=waits, key=key):
            for k, v in waits:
                eng.wait_ge(self._sem(k), v)
            eng.dma_start(out=out, in_=in_, **kw).then_inc(self._sem(key), 16)

        self.streams[q].append(emit)
        d = ("d", q, slot, val)
        self._record(d, reads, writes, True)
        return d

    def barrier(self):
        keys = []
        for e in self.ENG:
            if self.count[e] > 0:
                idx = self.count[e] - 1
                keys.append((("e", e, idx // CHUNK), idx % CHUNK + 1))
            for slot, val in self.dma_hist[e].items():
                keys.append((("d", e, slot), val))
        for f in self.ENG:
            mine = []
            for k, v in keys:
                if k[0] == "e" and k[1] == f:
                    continue
                if self.waited[f].get(k, 0) >= v:
                    continue
                self.waited[f][k] = v
                mine.append((k, v))

            def emit(eng, mine=mine):
                for k, v in mine:
                    eng.wait_ge(self._sem(k), v)

            self.streams[f].append(emit)

    def finish(self, deps):
        self.final_deps = list(deps)

    def build(self):
        nc = self.nc
        st = self.stack
        for e in self.ENG:
            nsem = (self.count[e] + CHUNK - 1) // CHUNK
            self.esems[e] = [st.enter_context(nc.semaphore(f"s_{e}_{i}")) for i in range(nsem)]
            nd = min(self.dma_n[e], NDMASEM)
            for s in range(nd):
                self.dsems[(e, s)] = st.enter_context(nc.semaphore(f"d_{e}_{s}"))
        fin = []
        for d in self.final_deps:
            if d[0] == "e":
                fin.append((("e", d[1], d[2] // CHUNK), d[2] % CHUNK + 1))
            else:
                fin.append((("d", d[1], d[2]), d[3]))
        block = st.enter_context(nc.Block())
        streams = self.streams

        @block.tensor
        def _(eng):
            for f in streams["pe"]:
                f(eng)

        @block.scalar
        def _(eng):
            for f in streams["act"]:
                f(eng)

        @block.vector
        def _(eng):
            for f in streams["dve"]:
                f(eng)

        @block.gpsimd
        def _(eng):
            for f in streams["pool"]:
                f(eng)

        @block.sync
        def _(eng):
            for f in streams["sp"]:
                f(eng)
            for k, v in fin:
                eng.wait_ge(self._sem(k), v)

        st.close()


def _ap(x):
    return x.ap if isinstance(x, Tile) else x


def _tl(*xs):
    return [x for x in xs if isinstance(x, Tile)]


def ACT(P, out, in_, func, bias=None, scale=None, accum=None):
    kw = {}
    if bias is not None:
        kw["bias"] = _ap(bias)
    if scale is not None:
        kw["scale"] = _ap(scale)
    if accum is not None:
        kw["accum_out"] = _ap(accum)
    P.op("act", lambda e: e.activation(out=out.ap, in_=in_.ap, func=func, **kw),
         reads=_tl(in_, bias, scale), writes=_tl(out, accum))


def TT(P, eng, out, a, b, op):
    P.op(eng, lambda e: e.tensor_tensor(out=out.ap, in0=a.ap, in1=b.ap, op=op),
         reads=_tl(a, b), writes=[out])


def TS(P, eng, out, a, s1, op0, s2=None, op1=None):
    if op1 is None:
        P.op(eng, lambda e: e.tensor_scalar(out=out.ap, in0=a.ap, scalar1=_ap(s1), scalar2=None, op0=op0),
             reads=_tl(a, s1), writes=[out])
    else:
        P.op(eng, lambda e: e.tensor_scalar(out=out.ap, in0=a.ap, scalar1=_ap(s1), scalar2=_ap(s2),
                                            op0=op0, op1=op1),
             reads=_tl(a, s1, s2), writes=[out])


def STT(P, out, in0, scalar, in1, op0, op1):
    P.op("dve", lambda e: e.scalar_tensor_tensor(out=out.ap, in0=in0.ap, scalar=_ap(scalar), in1=in1.ap,
                                                 op0=op0, op1=op1),
         reads=_tl(in0, scalar, in1), writes=[out])


def CP(P, eng, out, in_):
    if eng == "act":
        P.op("act", lambda e: e.activation(out=out.ap, in_=in_.ap, func=AF.Copy), reads=[in_], writes=[out])
    else:
        P.op(eng, lambda e: e.tensor_copy(out=out.ap, in_=in_.ap), reads=[in_], writes=[out])


def MM(P, out, lhsT, rhs, start=True, stop=True):
    P.op("pe", lambda e: e.matmul(out.ap, lhsT=lhsT.ap, rhs=rhs.ap, start=start, stop=stop),
         reads=[lhsT, rhs], writes=[out])


def TR(P, out, in_, ident):
    P.op("pe", lambda e: e.transpose(out=out.ap, in_=in_.ap, identity=ident.ap),
         reads=[in_, ident], writes=[out])


def SCAN(P, out, d0, d1, init):
    P.op("dve", lambda e: e.tensor_tensor_scan(out=out.ap, data0=d0.ap, data1=d1.ap, initial=_ap(init),
                                               op0=ALU.mult, op1=ALU.add),
         reads=_tl(d0, d1, init), writes=[out])


def MEMSET(P, eng, out, val):
    P.op(eng, lambda e: e.memset(out.ap, val), writes=[out])


COLS = {}


def _col_layout():
    names = [("mu", 8), ("w0", 2), ("a0", 2), ("k_k", 2), ("k_a", 2), ("r_k", 2), ("ln_g", 2), ("ln_b", 2),
             ("cw0", 4), ("cw1", 4), ("cw2", 4), ("cw3", 4), ("cb", 4), ("ba", 4), ("bx", 4), ("lam", 4),
             ("lng", 4), ("gkb", 1), ("gng", 1)]
    off = 0
    for n, c in names:
        COLS[n] = (off, c)
        off += c
    return off


NCOL = _col_layout()


def pack_cols(inp, l):
    out = np.zeros((128, NCOL), np.float32)

    def put(name, vec):
        o, c = COLS[name]
        out[:, o:o + c] = np.asarray(vec, np.float32).reshape(c, 128).T

    put("mu", inp["rw_mu"][l])
    put("w0", inp["rw_w0"][l])
    put("a0", inp["rw_a0"][l])
    put("k_k", inp["rw_k_k"][l])
    put("k_a", inp["rw_k_a"][l])
    put("r_k", inp["rw_r_k"][l].reshape(-1))
    put("ln_g", inp["rw_ln_g"][l])
    put("ln_b", inp["rw_ln_b"][l])
    for j in range(4):
        put(f"cw{j}", inp["lru_conv_w"][l, j])
    put("cb", inp["lru_conv_b"][l])
    put("ba", inp["lru_ba"][l])
    put("bx", inp["lru_bx"][l])
    put("lam", inp["lru_lam"][l])
    put("lng", inp["lru_norm_g"][l])
    put("gkb", inp["gla_gk_b"][l])
    put("gng", np.concatenate([inp["gla_norm_g"][l], inp["gla_norm_g"][l]]))
    return out


def make_consts():
    c = {}
    idx = np.arange(128)
    su = (idx[:, None] < idx[None, :]).astype(np.float32)
    ui = (idx[:, None] <= idx[None, :]).astype(np.float32)
    sl = (idx[:, None] > idx[None, :]).astype(np.float32)
    c["ident"] = np.eye(128, dtype=np.float32)
    c["cmask"] = np.concatenate([su, su, ui, ui], axis=1)
    c["sl4"] = np.concatenate([sl] * 4, axis=1)
    c["ui4"] = np.concatenate([ui] * NCH, axis=1)
    ob = np.zeros((128, 128), np.float32)
    ob[:64, :64] = 1
    ob[64:, 64:] = 1
    c["ones64"] = ob
    rm = np.ones((128, T), np.float32)
    rm[:, ::128] = 0
    c["rmask"] = rm
    bm = np.zeros((128, 256), np.float32)
    for h in range(4):
        bm[32 * h:32 * h + 32, 64 * h:64 * h + 64] = 1
    c["bmask"] = bm
    par = np.zeros((128, 2), np.float32)
    for h in range(4):
        par[32 * h:32 * h + 32, h % 2] = 1
    c["par"] = par
    return c


CONST_SHAPES = {"ident": 128, "cmask": 512, "sl4": 512, "ui4": T, "ones64": 128,
                "rmask": T, "bmask": 256, "par": 2}


def build_program(ntok, nlayers, debug=()):
    nc = bass.Bass("TRN2", target_bir_lowering=False)
    NT = ntok // T
    L = nlayers

    def din(name, shape):
        return nc.dram_tensor(name, list(shape), F32, kind="ExternalInput").ap()

    x_in = din("x", [ntok, D])
    w_in = din("w_in", [L, D, PIN])
    w_out = din("w_out", [L, D, D])
    w_gate = din("ffn_w_gate", [L, D, DFF])
    w_up = din("ffn_w_up", [L, D, DFF])
    w_down = din("ffn_w_down", [L, DFF, D])
    rw_w_up = din("rw_w_up", [L, 64, 256])
    rw_a_up = din("rw_a_up", [L, 64, 256])
    rw_g_up = din("rw_g_up", [L, 128, 256])
    lru_wa = din("lru_wa", [L, 8, 64, 64])
    lru_wx = din("lru_wx", [L, 8, 64, 64])
    gk_up = din("gla_gk_up", [L, 16, 128])
    cols_d = din("cols", [L, 128, NCOL])
    norm1_g = din("norm1_g", [L, D])
    norm2_g = din("norm2_g", [L, D])
    final_g = din("final_norm_g", [1, D])
    cdram = {k: din("c_" + k, [128, n]) for k, n in CONST_SHAPES.items()}
    out_d = nc.dram_tensor("out", [ntok, D], F32, kind="ExternalOutput").ap()
    xa_d = nc.dram_tensor("xa_scr", [ntok, D], F32).ap()
    xb_d = nc.dram_tensor("xb_scr", [ntok, D], F32).ap()
    xa_buf, xb_buf = Buf("xa"), Buf("xb")
    dbg_out = {}

    P = Prog(nc)
    P.init_mem(51000)
    fin = []

    def dbg(name, tile, rows, cols, tok0=None, total_cols=None):
        if name not in debug:
            return
        if name not in dbg_out:
            tc = total_cols if total_cols is not None else cols
            dbg_out[name] = nc.dram_tensor("dbg_" + name, [rows, tc], F32, kind="ExternalOutput").ap()
        dst = dbg_out[name]
        c0 = tok0 if tok0 is not None else 0
        fin.append(P.dma("pool", dst[0:rows, c0:c0 + cols], tile.ap, reads=[tile]))

    cst = {}
    for k, n in CONST_SHAPES.items():
        cst[k] = P.alloc(n, F32, "c_" + k)
        P.dma("sp", cst[k].ap, cdram[k][:, :], writes=[cst[k]])
    identf = cst["ident"]
    identb = P.alloc(128, BF16, "identb")
    CP(P, "pool", identb, identf)
    ident4b = P.alloc(512, BF16, "ident4b")
    for i in range(4):
        CP(P, "pool", ident4b[:, i * 128:(i + 1) * 128], identf)
    ones64b = P.alloc(128, BF16, "ones64b")
    CP(P, "pool", ones64b, cst["ones64"])
    cmask, sl4, ui4, rmask, bmask, par = (cst[k] for k in ("cmask", "sl4", "ui4", "rmask", "bmask", "par"))
    gBf = P.alloc(D, F32, "gBf")
    P.dma("sp", gBf.ap, final_g.partition_broadcast(128), writes=[gBf])
    epsc = P.alloc(2, F32, "epsc")
    MEMSET(P, "pool", epsc[:, 0:1], EPS)
    MEMSET(P, "pool", epsc[:, 1:2], RW_EPS)

    rw_carry = P.alloc(8, F32, "rw_carry")
    lru_xc = [P.alloc(3, F32, f"lru_xc{j}") for j in range(4)]
    lru_h = P.alloc(4, F32, "lru_h")
    rw_H = [P.alloc(64, F32, f"rwH{hp}") for hp in range(2)]
    gla_S = P.alloc(256, F32, "glaS")
    persist_mark = P.arena_off

    def rms_block(xblk, gB, hn_out):
        ssq = P.alloc(1, F32, "ssq")
        rstd = P.alloc(1, F32, "rstd")
        ACT(P, hn_out, xblk, AF.Square, accum=ssq)
        ACT(P, rstd, ssq, AF.Ln, bias=epsc[:, 0:1], scale=1.0 / D)
        ACT(P, rstd, rstd, AF.Exp, scale=-0.5)
        STT(P, hn_out, xblk, rstd[:, 0:1], gB, ALU.mult, ALU.mult)

    for l in range(L):
        src_d, src_buf = (x_in, None) if l == 0 else (xb_d, xb_buf)
        last = l == L - 1
        P.barrier()
        P.arena_off = persist_mark
        colsT = P.alloc(NCOL, F32, "cols")
        P.dma("sp", colsT.ap, cols_d[l], writes=[colsT])

        def col(name, j=0, rows=slice(0, 128)):
            o, c = COLS[name]
            return colsT[rows, o + j:o + j + 1]

        dcol = P.alloc(24, F32, "dcol")
        o_mu = COLS["mu"][0]
        omm = dcol[:, 0:8]
        TS(P, "pool", omm, colsT[:, o_mu:o_mu + 8], -1.0, ALU.mult, 1.0, ALU.add)
        o_ka = COLS["k_a"][0]
        omka = dcol[:, 8:10]
        TS(P, "pool", omka, colsT[:, o_ka:o_ka + 2], -1.0, ALU.mult, 1.0, ALU.add)
        o_lam = COLS["lam"][0]
        c1 = dcol[:, 10:14]
        c2 = dcol[:, 14:18]
        ACT(P, c1, colsT[:, o_lam:o_lam + 4], AF.Exp, scale=-1.0)
        ACT(P, c1, c1, AF.Ln, bias=1.0)
        TS(P, "pool", c2, c1, -16.0, ALU.mult)
        TS(P, "pool", c1, c1, -8.0, ALU.mult)
        MEMSET(P, "pool", rw_carry, 0.0)
        for j in range(4):
            MEMSET(P, "pool", lru_xc[j], 0.0)
        MEMSET(P, "pool", lru_h, 0.0)
        for hp in range(2):
            MEMSET(P, "pool", rw_H[hp], 0.0)
        MEMSET(P, "pool", gla_S, 0.0)

        gB1 = P.alloc(D, F32, "gB1")
        P.dma("sp", gB1.ap, norm1_g[l:l + 1, :].partition_broadcast(128), writes=[gB1])
        w_in_sb = P.alloc(8 * PIN, BF16, "w_in")
        for kc in range(8):
            for hf in range(2):
                P.dma("pool", w_in_sb.ap[:, kc * PIN + hf * 1416: kc * PIN + (hf + 1) * 1416],
                      w_in[l, kc * 128:(kc + 1) * 128, hf * 1416:(hf + 1) * 1416], writes=[w_in_sb])
        w_out_sb = P.alloc(8 * D, BF16, "w_out")
        for kc in range(8):
            P.dma("pool", w_out_sb.ap[:, kc * D:(kc + 1) * D], w_out[l, kc * 128:(kc + 1) * 128, :],
                  writes=[w_out_sb])
        wa_up = P.alloc(256, BF16, "wa_up")
        P.dma("pool", wa_up.ap[0:64, :], rw_w_up[l], writes=[wa_up])
        P.dma("pool", wa_up.ap[64:128, :], rw_a_up[l], writes=[wa_up])
        g_up = P.alloc(256, BF16, "g_up")
        P.dma("pool", g_up.ap, rw_g_up[l], writes=[g_up])
        gkup = P.alloc(128, BF16, "gkup")
        P.dma("pool", gkup.ap[0:16, :], gk_up[l], writes=[gkup])
        wabd = P.alloc(4 * 128, BF16, "wabd")
        wxbd = P.alloc(4 * 128, BF16, "wxbd")
        MEMSET(P, "pool", wabd, 0.0)
        MEMSET(P, "pool", wxbd, 0.0)
        for j in range(4):
            for s in range(2):
                P.dma("pool", wabd.ap[64 * s:64 * s + 64, j * 128 + 64 * s: j * 128 + 64 * s + 64],
                      lru_wa[l, 2 * j + s], writes=[wabd])
                P.dma("pool", wxbd.ap[64 * s:64 * s + 64, j * 128 + 64 * s: j * 128 + 64 * s + 64],
                      lru_wx[l, 2 * j + s], writes=[wxbd])
        mbbd = [[P.alloc(128, F32, f"mbbd{hp}{c}") for c in range(NCH)] for hp in range(2)]
        for hp in range(2):
            for c in range(NCH):
                MEMSET(P, "pool", mbbd[hp][c], 0.0)
        tile_mark = P.arena_off

        for ti in range(NT):
            P.arena_off = tile_mark
            tok0 = ti * T
            xT = [P.alloc(D, F32, f"xT{b}") for b in range(NCH)]
            hnF = P.alloc(8 * T, BF16, "hnF")
            hnF3 = hnF.re("p (k t) -> p k t", k=8)
            ycat = P.alloc(8 * T, BF16, "ycat")
            ycat3 = ycat.re("p (k t) -> p k t", k=8)
            for b in range(NCH):
                rd = [src_buf] if src_buf is not None else []
                P.dma("sp", xT[b].ap, src_d[tok0 + b * 128: tok0 + (b + 1) * 128, :], reads=rd, writes=[xT[b]])
            m0 = P.arena_off
            for b in range(NCH):
                P.arena_off = m0
                hn = P.alloc(D, BF16, "hn")
                rms_block(xT[b], gB1, hn)
                pst = P.psum()
                pstb = pst.bitcast(BF16)
                for kc in range(8):
                    TR(P, pstb[:, kc * 128:(kc + 1) * 128], hn[:, kc * 128:(kc + 1) * 128], identb)
                CP(P, "act", hnF3[:, :, b * 128:(b + 1) * 128], pstb.re("p (k t) -> p k t", k=8))
            P.arena_off = m0

            def proj(c0, ncols, evac):
                ps = P.psum()
                for kc in range(8):
                    MM(P, ps[0:ncols, 0:T], w_in_sb[:, kc * PIN + c0: kc * PIN + c0 + ncols], hnF3[:, kc, :],
                       start=(kc == 0), stop=(kc == 7))
                evac(ps[0:ncols, 0:T])


            base_thr = P.arena_off

            class Region:
                def __init__(self, start, size):
                    self.off = start
                    self.end = start + size

                def alloc(self, n, dt=F32, name=""):
                    save = P.arena_off
                    P.arena_off = self.off
                    t = P.alloc(n, dt, name)
                    self.off = P.arena_off
                    P.arena_off = save
                    if self.off > self.end:
                        raise RuntimeError(f"region overflow {name} {self.off}>{self.end}")
                    return t

            SZ_RW, SZ_LRU, SZ_GLA = 13400, 5200, 5900

            def rsqrt_act(out, in_, scale, bias):
                ACT(P, out, in_, AF.Ln, bias=bias, scale=scale)
                ACT(P, out, out, AF.Exp, scale=-0.5)

            def rw_thread():
                R = Region(base_thr, SZ_RW)
                A = R.alloc
                pm = {}
                ptmp = A(1 + T, F32, "ptmp")
                ltmp = A(T, F32, "ltmp")
                for gi, gname in enumerate(["r0", "r1", "k0", "k1", "v0", "v1", "wa", "glo"]):
                    pm[gname] = A(T, F32, "pm_" + gname)

                    def ev(ps, gi=gi, gname=gname):
                        CP(P, "act", ptmp[:, 1:1 + T], ps)
                        CP(P, "pool", ptmp[:, 0:1], rw_carry[:, gi:gi + 1])
                        TS(P, "pool", ltmp, ptmp[:, 0:T], col("mu", gi), ALU.mult)
                        STT(P, pm[gname], ptmp[:, 1:1 + T], omm[:, gi:gi + 1], ltmp, ALU.mult, ALU.add)
                        CP(P, "pool", rw_carry[:, gi:gi + 1], ptmp[:, T:T + 1])
                    proj(gi * 128, 128, ev)
                    yield
                wab = A(T, BF16, "wab")
                ACT(P, wab[0:64, :], pm["wa"][0:64, :], AF.Tanh)
                CP(P, "pool", wab[64:128, :], pm["wa"][64:128, :])
                sgl = A(T, BF16, "sgl")
                ACT(P, sgl, pm["glo"], AF.Sigmoid)
                yield
                sgs, avs, gSs = [], [], []
                for hp in range(2):
                    ps_w, ps_a, ps_g = P.psum(), P.psum(), P.psum()
                    MM(P, ps_w[:, 0:T], wa_up[0:64, hp * 128:(hp + 1) * 128], wab[0:64, :])
                    MM(P, ps_a[:, 0:T], wa_up[64:128, hp * 128:(hp + 1) * 128], wab[64:128, :])
                    MM(P, ps_g[:, 0:T], g_up[:, hp * 128:(hp + 1) * 128], sgl)
                    sg = A(T, F32, f"sg{hp}")
                    a = A(T, F32, f"a{hp}")
                    gS = A(T, F32, f"gS{hp}")
                    ACT(P, sg, ps_w[:, 0:T], AF.Sigmoid, bias=col("w0", hp))
                    ACT(P, a, ps_a[:, 0:T], AF.Sigmoid, bias=col("a0", hp))
                    CP(P, "dve", gS, ps_g[:, 0:T])
                    sgs.append(sg)
                    avs.append(a)
                    gSs.append(gS)
                    yield
                yield "B"
                m_hp = R.off
                for hp in range(2):
                    R.off = m_hp
                    r, k, v = pm[f"r{hp}"], pm[f"k{hp}"], pm[f"v{hp}"]
                    sg, a, gS = sgs[hp], avs[hp], gSs[hp]
                    kk = A(T, F32, "kk")
                    sqk = A(T, BF16, "sqk")
                    TS(P, "pool", kk, k, col("k_k", hp), ALU.mult)
                    TT(P, "pool", sqk, kk, kk, ALU.mult)
                    ps_n = P.psum()
                    MM(P, ps_n[:, 0:T], ones64b, sqk)
                    rn = A(T, F32, "rn")
                    rsqrt_act(rn, ps_n[:, 0:T], 1.0, 1e-24)
                    yield
                    TT(P, "dve", kk, kk, rn, ALU.mult)
                    kmod = A(T, F32, "kmod")
                    TS(P, "pool", kmod, a, col("k_a", hp), ALU.mult, omka[:, hp:hp + 1], ALU.add)
                    TT(P, "pool", kmod, kmod, k, ALU.mult)
                    yield
                    bvec = A(T, F32, "bvec")
                    TT(P, "pool", bvec, kk, a, ALU.mult)
                    rk = rn
                    TT(P, "dve", rk, r, kmod, ALU.mult)
                    rkb = sqk
                    TS(P, "pool", rkb, rk, col("r_k", hp), ALU.mult)
                    ps_b = P.psum()
                    MM(P, ps_b[:, 0:T], ones64b, rkb)
                    bonus = A(T, F32, "bonus")
                    TT(P, "dve", bonus, ps_b[:, 0:T], v, ALU.mult)
                    yield
                    css = A(T, F32, "css")
                    SCAN(P, css, rmask, sg, 0.0)
                    cse = A(T, F32, "cse")
                    TT(P, "pool", cse, css, sg, ALU.subtract)
                    E1 = A(T, F32, "E1")
                    E0 = cse
                    Einv = A(T, F32, "Einv")
                    Eend = sg
                    nb = A(NCH, F32, "nb")
                    TS(P, "pool", nb, css.re("p (c t) -> p c t", c=NCH)[:, :, 127], -DEC, ALU.mult)
                    ACT(P, E1, css, AF.Exp, scale=-DEC)
                    ACT(P, Einv, css, AF.Exp, scale=DEC)
                    yield
                    for c in range(NCH):
                        ACT(P, Eend[:, c * 128:(c + 1) * 128], css[:, c * 128:(c + 1) * 128], AF.Exp,
                            bias=nb[:, c:c + 1], scale=DEC)
                    ACT(P, E0, cse, AF.Exp, scale=-DEC)
                    gC = A(NCH, F32, "gC")
                    CP(P, "pool", gC, E1.re("p (c t) -> p c t", c=NCH)[:, :, 127])
                    yield
                    Rt = A(T, BF16, "Rt")
                    At = A(T, BF16, "At")
                    Bt = A(T, BF16, "Bt")
                    Kt = A(T, BF16, "Kt")
                    Kh = A(T, BF16, "Kh")
                    Bh = A(T, BF16, "Bh")
                    vb = A(T, BF16, "vb")
                    TT(P, "dve", Rt, r, E1, ALU.mult)
                    STT(P, At, kk, -1.0, E0, ALU.mult, ALU.mult)
                    TT(P, "pool", Bt, bvec, Einv, ALU.mult)
                    TT(P, "dve", Kt, kmod, Einv, ALU.mult)
                    yield
                    TT(P, "dve", Kh, kmod, Eend, ALU.mult)
                    TT(P, "pool", Bh, bvec, Eend, ALU.mult)
                    CP(P, "pool", vb, v)
                    yield
                    TM = []
                    for c in range(NCH):
                        pst = P.psum()
                        pstb = pst.bitcast(BF16)
                        for i, src_ in enumerate([vb, Kh, Bh, At]):
                            TR(P, pstb[:, i * 128:(i + 1) * 128], src_[:, c * 128:(c + 1) * 128], identb)
                        tm = A(512, BF16, f"TM{c}")
                        CP(P, "act", tm, pstb[:, 0:512])
                        TM.append(tm)
                        yield
                    NJ = 2 * NCH
                    Aall = []
                    P0 = A(NJ * 128, BF16, "P0")
                    P0T = A(NJ * 128, BF16, "P0T")
                    for s in range(2):
                        rows = slice(64 * s, 64 * s + 64)
                        ps_p0 = P.psum()
                        for c in range(NCH):
                            j = s * NCH + c
                            cs = slice(c * 128, (c + 1) * 128)
                            ps = P.psum()
                            MM(P, ps[:, 0:128], Bt[rows, cs], At[rows, cs])
                            MM(P, ps[:, 128:256], Kt[rows, cs], At[rows, cs])
                            MM(P, ps[:, 256:384], Bt[rows, cs], Rt[rows, cs])
                            MM(P, ps[:, 384:512], Kt[rows, cs], Rt[rows, cs])
                            aa = A(512, BF16, f"Aall{j}")
                            TT(P, "dve", aa, ps, cmask, ALU.mult)
                            Aall.append(aa)
                            CP(P, "pool", P0T[:, j * 128:(j + 1) * 128], aa[:, 0:128])
                            MM(P, ps_p0[:, c * 128:(c + 1) * 128], At[rows, cs], Bt[rows, cs])
                        TT(P, "dve", P0[:, s * T:(s + 1) * T], ps_p0[:, 0:T], sl4[:, 0:T], ALU.mult)
                        yield
                    G = A(NJ * 128, BF16, "G")
                    TT(P, "pool", G, P0T, ident4b[:, 0:NJ * 128], ALU.add)
                    Pk, PkT = P0, P0T
                    Pn = [A(NJ * 128, BF16, f"Pn{i}") for i in range(2)]
                    PnT = [A(NJ * 128, BF16, f"PnT{i}") for i in range(2)]
                    NLEV = 6
                    for lev in range(NLEV):
                        nP, nPT = Pn[lev % 2], PnT[lev % 2]
                        ps1 = P.psum()
                        for j in range(NJ):
                            js = slice(j * 128, (j + 1) * 128)
                            MM(P, ps1[:, js], PkT[:, js], Pk[:, js])
                        CP(P, "act", nP, ps1[:, 0:NJ * 128])
                        if lev < NLEV - 1:
                            ps2 = P.psum()
                            for j in range(NJ):
                                js = slice(j * 128, (j + 1) * 128)
                                MM(P, ps2[:, js], Pk[:, js], PkT[:, js])
                            CP(P, "dve", nPT, ps2[:, 0:NJ * 128])
                        yield
                        ps3 = P.psum()
                        for j in range(NJ):
                            js = slice(j * 128, (j + 1) * 128)
                            MM(P, ps3[:, js], nP[:, js], G[:, js])
                        TT(P, "dve", G, G, ps3[:, 0:NJ * 128], ALU.add)
                        Pk, PkT = nP, nPT
                        yield
                    XW = A(NJ * 128, BF16, "XW")
                    ps = P.psum()
                    for s in range(2):
                        for c in range(NCH):
                            j = s * NCH + c
                            MM(P, ps[:, j * 128:j * 128 + 64], Aall[j][:, 128:256], TM[c][:, 64 * s:64 * s + 64])
                            MM(P, ps[:, j * 128 + 64:j * 128 + 128], G[:, j * 128:(j + 1) * 128],
                               TM[c][:, 384 + 64 * s:384 + 64 * s + 64])
                    CP(P, "act", XW, ps[:, 0:NJ * 128])
                    yield
                    U0 = A(NJ * 64, BF16, "U0")
                    ps = P.psum()
                    for j in range(NJ):
                        MM(P, ps[:, j * 64:(j + 1) * 64], G[:, j * 128:(j + 1) * 128], XW[:, j * 128:j * 128 + 64])
                    CP(P, "act", U0, ps[:, 0:NJ * 64])
                    yield
                    ps = P.psum()
                    for s in range(2):
                        orow = slice(64 * s, 64 * s + 64)
                        for c in range(NCH):
                            j = s * NCH + c
                            MM(P, ps[orow, c * 128:c * 128 + 64], XW[:, j * 128 + 64:j * 128 + 128],
                               TM[c][:, 256 + 64 * s:256 + 64 * s + 64])
                            MM(P, ps[orow, c * 128 + 64:c * 128 + 128], TM[c][:, 256 + 64 * s:256 + 64 * s + 64],
                               U0[:, j * 64:(j + 1) * 64], start=True, stop=False)
                            MM(P, ps[orow, c * 128 + 64:c * 128 + 128], TM[c][:, 128 + 64 * s:128 + 64 * s + 64],
                               TM[c][:, 64 * s:64 * s + 64], start=False, stop=True)
                    Nn = A(NCH * 64, F32, "Nn")
                    for c in range(NCH):
                        CP(P, "act", mbbd[hp][c][0:64, 0:64], ps[0:64, c * 128:c * 128 + 64])
                        CP(P, "act", mbbd[hp][c][64:128, 64:128], ps[64:128, c * 128:c * 128 + 64])
                        CP(P, "dve", Nn[:, c * 64:(c + 1) * 64], ps[:, c * 128 + 64:c * 128 + 128])
                    yield
                    RhT = A(T, BF16, "RhT")
                    ps = P.psum()
                    for s in range(2):
                        orow = slice(64 * s, 64 * s + 64)
                        for c in range(NCH):
                            j = s * NCH + c
                            MM(P, ps[orow, c * 128:(c + 1) * 128], XW[:, j * 128 + 64:j * 128 + 128],
                               Aall[j][:, 256:384])
                    TT(P, "dve", RhT, ps[:, 0:T], Rt, ALU.add)
                    yield
                    Y0 = A(T, F32, "Y0")
                    ps = P.psum()
                    for s in range(2):
                        orow = slice(64 * s, 64 * s + 64)
                        for c in range(NCH):
                            j = s * NCH + c
                            MM(P, ps[orow, c * 128:(c + 1) * 128], U0[:, j * 64:(j + 1) * 64], Aall[j][:, 256:384],
                               start=True, stop=False)
                            MM(P, ps[orow, c * 128:(c + 1) * 128], TM[c][:, 64 * s:64 * s + 64],
                               Aall[j][:, 384:512], start=False, stop=True)
                    CP(P, "act", Y0, ps[:, 0:T])
                    yield
                    Hs = A((NCH + 1) * 64, F32, "Hs")
                    Hb = A(NCH * 64, BF16, "Hb")
                    CP(P, "pool", Hs[:, 0:64], rw_H[hp])
                    for c in range(NCH):
                        CP(P, "pool", Hb[:, c * 64:(c + 1) * 64], Hs[:, c * 64:(c + 1) * 64])
                        ps = P.psum()
                        MM(P, ps[:, 0:64], mbbd[hp][c], Hs[:, c * 64:(c + 1) * 64], start=True, stop=False)
                        MM(P, ps[:, 0:64], identf, Nn[:, c * 64:(c + 1) * 64], start=False, stop=True)
                        STT(P, Hs[:, (c + 1) * 64:(c + 2) * 64], Hs[:, c * 64:(c + 1) * 64], gC[:, c:c + 1],
                            ps[:, 0:64], ALU.mult, ALU.add)
                        yield
                    CP(P, "pool", rw_H[hp], Hs[:, NCH * 64:(NCH + 1) * 64])
                    y = A(T, F32, "y")
                    pse, pso = P.psum(), P.psum()
                    for c in range(NCH):
                        cs = slice(c * 128, (c + 1) * 128)
                        MM(P, pse[0:64, cs], Hb[0:64, c * 64:(c + 1) * 64], RhT[0:64, cs])
                        MM(P, pso[64:128, cs], Hb[64:128, c * 64:(c + 1) * 64], RhT[64:128, cs])
                    TT(P, "dve", y[0:64, :], pse[0:64, 0:T], Y0[0:64, :], ALU.add)
                    TT(P, "dve", y[64:128, :], pso[64:128, 0:T], Y0[64:128, :], ALU.add)
                    dbg(f"rw_y{hp}", y, 128, T, tok0, ntok)
                    yield
                    yb = A(T, BF16, "yb")
                    CP(P, "pool", yb, y)
                    ps_m = P.psum()
                    MM(P, ps_m[:, 0:T], ones64b, yb)
                    yc = A(T, F32, "yc")
                    STT(P, yc, ps_m[:, 0:T], -1.0 / 64, y, ALU.mult, ALU.add)
                    yield
                    TT(P, "pool", yb, yc, yc, ALU.mult)
                    ps_v = P.psum()
                    MM(P, ps_v[:, 0:T], ones64b, yb)
                    rs = y
                    rsqrt_act(rs, ps_v[:, 0:T], 1.0 / 64, epsc[:, 1:2])
                    yield
                    TT(P, "dve", yc, yc, rs, ALU.mult)
                    TS(P, "pool", yc, yc, col("ln_g", hp), ALU.mult, col("ln_b", hp), ALU.add)
                    TT(P, "pool", yc, yc, bonus, ALU.add)
                    TT(P, "dve", ycat3[:, hp, :], yc, gS, ALU.mult)
                    yield

            def lru_thread():
                R = Region(base_thr + SZ_RW, SZ_LRU)
                A = R.alloc
                xbuf = A(3 + T, F32, "xbuf")
                gt = A(T, F32, "gt")
                xc = A(T, F32, "xc")
                xcb = A(T, BF16, "xcb")
                st = []
                for j in range(4):
                    CP(P, "pool", xbuf[:, 0:3], lru_xc[j])
                    proj(1024 + j * 128, 128, lambda ps: CP(P, "act", xbuf[:, 3:3 + T], ps))
                    yield
                    proj(1536 + j * 128, 128, lambda ps: CP(P, "act", gt, ps))
                    CP(P, "pool", lru_xc[j], xbuf[:, T:T + 3])
                    yield
                    TS(P, "pool", xc, xbuf[:, 0:T], col("cw0", j), ALU.mult, col("cb", j), ALU.add)
                    for tap in range(1, 4):
                        STT(P, xc, xbuf[:, tap:tap + T], col(f"cw{tap}", j), xc, ALU.mult, ALU.add)
                    CP(P, "pool", xcb, xc)
                    yield
                    ps_r, ps_i = P.psum(), P.psum()
                    MM(P, ps_r[:, 0:T], wabd[:, j * 128:(j + 1) * 128], xcb)
                    MM(P, ps_i[:, 0:T], wxbd[:, j * 128:(j + 1) * 128], xcb)
                    gr = A(T, F32, f"gr{j}")
                    uu = A(T, F32, f"uu{j}")
                    ge = A(T, F32, f"ge{j}")
                    ACT(P, gr, ps_r[:, 0:T], AF.Sigmoid, bias=col("ba", j))
                    ACT(P, uu, ps_i[:, 0:T], AF.Sigmoid, bias=col("bx", j))
                    yield
                    TT(P, "dve", uu, uu, xc, ALU.mult)
                    TT(P, "pool", ge, gt, gt, ALU.mult)
                    TS(P, "pool", ge, ge, 0.044715, ALU.mult, 1.0, ALU.add)
                    TT(P, "pool", ge, ge, gt, ALU.mult)
                    ACT(P, ge, ge, AF.Sigmoid, scale=1.5957691216057308)
                    TT(P, "dve", ge, ge, gt, ALU.mult)
                    st.append((gr, uu, ge))
                    yield
                yield "B"
                av = A(T, F32, "av")
                a2 = A(T, F32, "a2")
                hh = A(T, F32, "hh")
                sqb = A(T, BF16, "sqb")
                for j in range(4):
                    gr, uu, ge = st[j]
                    ACT(P, av, gr, AF.Exp, scale=c1[:, j:j + 1])
                    ACT(P, a2, gr, AF.Exp, scale=c2[:, j:j + 1])
                    ACT(P, a2, a2, AF.Ln, bias=1.0, scale=-1.0)
                    ACT(P, a2, a2, AF.Exp, scale=0.5)
                    yield
                    TT(P, "dve", uu, uu, a2, ALU.mult)
                    SCAN(P, hh, av, uu, lru_h[:, j:j + 1])
                    CP(P, "pool", lru_h[:, j:j + 1], hh[:, T - 1:T])
                    yl = av
                    TT(P, "dve", yl, hh, ge, ALU.mult)
                    dbg(f"lru_y{j}", yl, 128, T, tok0, ntok)
                    yield
                    TT(P, "pool", sqb, yl, yl, ALU.mult)
                    ps_m = P.psum()
                    MM(P, ps_m[:, 0:T], ones64b, sqb)
                    rs = a2
                    rsqrt_act(rs, ps_m[:, 0:T], 1.0 / 64, epsc[:, 0:1])
                    STT(P, ycat3[:, 2 + j, :], yl, col("lng", j), rs, ALU.mult, ALU.mult)
                    yield

            def gla_thread():
                R = Region(base_thr + SZ_RW + SZ_LRU, SZ_GLA)
                A = R.alloc
                q = A(T, F32, "q")
                kg = A(T, F32, "kg")
                vg = [A(T, BF16, f"vg{i}") for i in range(2)]
                gg = [A(T, F32, f"gg{i}") for i in range(2)]
                gklo = A(T, BF16, "gklo")
                sgm = A(T, F32, "sgm")
                proj(2048, 128, lambda ps: CP(P, "act", q, ps))
                yield
                proj(2176, 128, lambda ps: CP(P, "act", kg, ps))
                yield
                proj(2304, 128, lambda ps: CP(P, "act", vg[0], ps))
                yield
                proj(2432, 128, lambda ps: CP(P, "act", vg[1], ps))
                yield
                proj(2560, 16, lambda ps: CP(P, "act", gklo[0:16, :], ps))
                yield
                for i in range(2):
                    def evg(ps, i=i):
                        ACT(P, sgm, ps, AF.Sigmoid)
                        TT(P, "dve", gg[i], ps, sgm, ALU.mult)
                    proj(2576 + 128 * i, 128, evg)
                    yield
                ps_gk = P.psum()
                MM(P, ps_gk[:, 0:T], gkup[0:16, :], gklo[0:16, :])
                la = A(T, F32, "la")
                ACT(P, la, ps_gk[:, 0:T], AF.Sigmoid, bias=col("gkb"))
                yield "B"
                ACT(P, la, la, AF.Ln)
                bc = A(T, F32, "bc")
                SCAN(P, bc, rmask, la, 0.0)
                Eq = A(T, F32, "Eq")
                Ek = A(T, F32, "Ek")
                Ee = la
                ACT(P, Eq, bc, AF.Exp, scale=1.0 / 16)
                ACT(P, Ek, bc, AF.Exp, scale=-1.0 / 16)
                yield
                nb = A(NCH, F32, "nbg")
                TS(P, "pool", nb, bc.re("p (c t) -> p c t", c=NCH)[:, :, 127], 1.0 / 16, ALU.mult)
                for c in range(NCH):
                    ACT(P, Ee[:, c * 128:(c + 1) * 128], bc[:, c * 128:(c + 1) * 128], AF.Exp,
                        bias=nb[:, c:c + 1], scale=-1.0 / 16)
                gCg = A(NCH, F32, "gCg")
                CP(P, "pool", gCg, Eq.re("p (c t) -> p c t", c=NCH)[:, :, 127])
                yield
                qin = A(T, BF16, "qin")
                STT(P, qin, q, 32.0 ** -0.5, Eq, ALU.mult, ALU.mult)
                kin = [A(T, BF16, f"kin{i}") for i in range(2)]
                for i in range(2):
                    STT(P, kin[i], kg, par[:, i:i + 1], Ek, ALU.mult, ALU.mult)
                kend = A(T, BF16, "kend")
                TT(P, "pool", kend, kg, Ee, ALU.mult)
                yield
                GT = []
                for c in range(NCH):
                    pst = P.psum()
                    pstb = pst.bitcast(BF16)
                    cs = slice(c * 128, (c + 1) * 128)
                    TR(P, pstb[:, 0:128], vg[0][:, cs], identb)
                    TR(P, pstb[:, 128:256], vg[1][:, cs], identb)
                    TR(P, pstb[:, 256:384], kend[:, cs], identb)
                    gt_ = A(384, BF16, f"GT{c}")
                    CP(P, "act", gt_, pstb[:, 0:384])
                    GT.append(gt_)
                    yield
                ST = []
                for h in range(4):
                    rows = slice(64 * (h // 2), 64 * (h // 2) + 64)
                    ps = P.psum()
                    for c in range(NCH):
                        cs = slice(c * 128, (c + 1) * 128)
                        MM(P, ps[:, cs], kin[h % 2][rows, cs], qin[rows, cs])
                    st_ = A(T, BF16, f"ST{h}")
                    TT(P, "dve", st_, ps[:, 0:T], ui4[:, 0:T], ALU.mult)
                    ST.append(st_)
                    yield
                Sb = []
                Scur = A((NCH + 1) * 256, F32, "Scur")
                CP(P, "pool", Scur[:, 0:256], gla_S)
                for c in range(NCH):
                    sb_ = A(256, BF16, f"Sb{c}")
                    TT(P, "pool", sb_, Scur[:, c * 256:(c + 1) * 256], bmask, ALU.mult)
                    Sb.append(sb_)
                    ps = P.psum()
                    MM(P, ps[:, 0:256], GT[c][:, 256:384], GT[c][:, 0:256])
                    STT(P, Scur[:, (c + 1) * 256:(c + 2) * 256], Scur[:, c * 256:(c + 1) * 256], gCg[:, c:c + 1],
                        ps[:, 0:256], ALU.mult, ALU.add)
                    yield
                CP(P, "pool", gla_S, Scur[:, NCH * 256:(NCH + 1) * 256])
                o = A(T, F32, "o")
                ob = A(T, BF16, "ob")
                rs = A(T, F32, "rsg")
                for vp in range(2):
                    ps_in, ps_it = P.psum(), P.psum()
                    for s in range(2):
                        h = 2 * vp + s
                        orow = slice(64 * s, 64 * s + 64)
                        krow = slice(64 * vp, 64 * vp + 64)
                        for c in range(NCH):
                            cs = slice(c * 128, (c + 1) * 128)
                            MM(P, ps_in[orow, cs], GT[c][:, 64 * h:64 * h + 64], ST[h][:, cs])
                            MM(P, ps_it[orow, cs], Sb[c][krow, 64 * h:64 * h + 64], qin[krow, cs])
                    CP(P, "act", o, ps_it[:, 0:T])
                    TT(P, "dve", o, o, ps_in[:, 0:T], ALU.add)
                    dbg(f"gla_o{vp}", o, 128, T, tok0, ntok)
                    yield
                    TT(P, "pool", ob, o, o, ALU.mult)
                    ps_m = P.psum()
                    MM(P, ps_m[:, 0:T], ones64b, ob)
                    rsqrt_act(rs, ps_m[:, 0:T], 1.0 / 64, epsc[:, 0:1])
                    STT(P, o, o, col("gng"), rs, ALU.mult, ALU.mult)
                    TT(P, "dve", ycat3[:, 6 + vp, :], o, gg[vp], ALU.mult)
                    yield

            threads = [rw_thread(), lru_thread(), gla_thread()]
            atB = []
            while threads:
                for th in list(threads):
                    try:
                        v_ = next(th)
                    except StopIteration:
                        threads.remove(th)
                        continue
                    if v_ == "B":
                        threads.remove(th)
                        atB.append(th)
            threads = atB
            while threads:
                for th in list(threads):
                    try:
                        next(th)
                    except StopIteration:
                        threads.remove(th)
            P.arena_off = base_thr

            for kc in range(8):
                dbg(f"ycat{kc}", ycat3[:, kc, :], 128, T, tok0, ntok)

            for b in range(NCH):
                for hf in range(2):
                    ps = P.psum()
                    for kc in range(8):
                        MM(P, ps, ycat3[:, kc, b * 128:(b + 1) * 128],
                           w_out_sb[:, kc * D + hf * 512: kc * D + (hf + 1) * 512], start=(kc == 0), stop=(kc == 7))
                    TT(P, "dve", xT[b][:, hf * 512:(hf + 1) * 512], xT[b][:, hf * 512:(hf + 1) * 512], ps, ALU.add)
                P.dma("sp", xa_d[tok0 + b * 128: tok0 + (b + 1) * 128, :], xT[b].ap, reads=[xT[b]],
                      writes=[xa_buf])

        P.barrier()
        P.arena_off = persist_mark
        gB2 = P.alloc(D, F32, "gB2")
        P.dma("sp", gB2.ap, norm2_g[l:l + 1, :].partition_broadcast(128), writes=[gB2])
        wg_sb = P.alloc(8 * DFF, BF16, "wg")
        wu_sb = P.alloc(8 * DFF, BF16, "wu")
        wd_sb = P.alloc(NFC * D, BF16, "wd")
        for kc in range(8):
            for hf in range(2):
                sl_ = slice(hf * 1408, (hf + 1) * 1408)
                P.dma("pool", wg_sb.ap[:, kc * DFF + hf * 1408: kc * DFF + (hf + 1) * 1408],
                      w_gate[l, kc * 128:(kc + 1) * 128, sl_], writes=[wg_sb])
                P.dma("pool", wu_sb.ap[:, kc * DFF + hf * 1408: kc * DFF + (hf + 1) * 1408],
                      w_up[l, kc * 128:(kc + 1) * 128, sl_], writes=[wu_sb])
        for fc in range(NFC):
            P.dma("pool", wd_sb.ap[:, fc * D:(fc + 1) * D], w_down[l, fc * 128:(fc + 1) * 128, :], writes=[wd_sb])
        tile_mark = P.arena_off
        for ti in range(NT):
            P.arena_off = tile_mark
            tok0 = ti * T
            xT = [P.alloc(D, F32, f"fxT{b}") for b in range(NCH)]
            hnF = P.alloc(8 * T, BF16, "fhnF")
            hnF3 = hnF.re("p (k t) -> p k t", k=8)
            hF = P.alloc(NFC * T, BF16, "hF")
            hF3 = hF.re("p (k t) -> p k t", k=NFC)
            for b in range(NCH):
                P.dma("sp", xT[b].ap, xa_d[tok0 + b * 128: tok0 + (b + 1) * 128, :], reads=[xa_buf], writes=[xT[b]])
            m0 = P.arena_off
            for b in range(NCH):
                P.arena_off = m0
                hn = P.alloc(D, BF16, "fhn")
                rms_block(xT[b], gB2, hn)
                pst = P.psum()
                pstb = pst.bitcast(BF16)
                for kc in range(8):
                    TR(P, pstb[:, kc * 128:(kc + 1) * 128], hn[:, kc * 128:(kc + 1) * 128], identb)
                CP(P, "act", hnF3[:, :, b * 128:(b + 1) * 128], pstb.re("p (k t) -> p k t", k=8))
            P.arena_off = m0
            sgt = [P.alloc(T, F32, f"sgt{i}") for i in range(2)]
            for fc in range(NFC):
                ps = P.psum()
                for kc in range(8):
                    MM(P, ps[:, 0:T], wg_sb[:, kc * DFF + fc * 128: kc * DFF + (fc + 1) * 128], hnF3[:, kc, :],
                       start=(kc == 0), stop=(kc == 7))
                for kc in range(8):
                    MM(P, ps[:, T:2 * T], wu_sb[:, kc * DFF + fc * 128: kc * DFF + (fc + 1) * 128], hnF3[:, kc, :],
                       start=(kc == 0), stop=(kc == 7))
                s_ = sgt[fc % 2]
                ACT(P, s_, ps[:, 0:T], AF.Silu)
                TT(P, "dve", hF3[:, fc, :], s_, ps[:, T:2 * T], ALU.mult)
            for b in range(NCH):
                for hf in range(2):
                    ps = P.psum()
                    for fc in range(NFC):
                        MM(P, ps, hF3[:, fc, b * 128:(b + 1) * 128],
                           wd_sb[:, fc * D + hf * 512: fc * D + (hf + 1) * 512], start=(fc == 0), stop=(fc == NFC - 1))
                    TT(P, "dve", xT[b][:, hf * 512:(hf + 1) * 512], xT[b][:, hf * 512:(hf + 1) * 512], ps, ALU.add)
                if last:
                    yo = P.alloc(D, F32, "yo")
                    rms_block(xT[b], gBf, yo)
                    fin.append(P.dma("sp", out_d[tok0 + b * 128: tok0 + (b + 1) * 128, :], yo.ap, reads=[yo]))
                else:
                    P.dma("sp", xb_d[tok0 + b * 128: tok0 + (b + 1) * 128, :], xT[b].ap, reads=[xT[b]],
                          writes=[xb_buf])
    P.finish(fin)
    P.build()
    return nc, list(dbg_out.keys())


def make_in_map(inputs, xs, nlayers):
    m = {"x": np.ascontiguousarray(xs, dtype=np.float32)}
    for k in ["w_in", "w_out", "ffn_w_gate", "ffn_w_up", "ffn_w_down", "rw_w_up", "rw_a_up", "rw_g_up",
              "lru_wa", "lru_wx", "gla_gk_up", "norm1_g", "norm2_g"]:
        m[k] = np.ascontiguousarray(np.asarray(inputs[k], np.float32)[:nlayers])
    m["cols"] = np.stack([pack_cols(inputs, l) for l in range(nlayers)])
    m["final_norm_g"] = np.asarray(inputs["final_norm_g"], np.float32).reshape(1, D)
    for k, v in make_consts().items():
        m["c_" + k] = v
    return m


_CACHE = {}


def kernel(**inputs):
    x = np.asarray(inputs["x"], np.float32)
    B, S, _ = x.shape
    L = np.asarray(inputs["w_in"]).shape[0]
    key = (S, L)
    if key not in _CACHE:
        _CACHE[key] = build_program(S, L)[0]
    nc = _CACHE[key]
    in_maps = [make_in_map(inputs, x[c % B], L) for c in range(8)]
    res = run_bass_kernel_spmd(nc, in_maps, core_ids=list(range(8)))
    return np.stack([res.results[b]["out"] for b in range(B)], axis=0)
```

```python
import contextlib
import math
import os
import numpy as np
import concourse.bass as bass
import concourse.mybir as mybir
from concourse.bass_utils import run_bass_kernel_spmd

F32 = mybir.dt.float32
BF16 = mybir.dt.bfloat16
AF = mybir.ActivationFunctionType
ALU = mybir.AluOpType

CHUNK = 8000
SAMEQ = os.environ.get('K_SAMEQ', '1') == '1'
NDMASEM = 12

D = 1024
PIN = 2832
DFF = 2816
NFC = DFF // 128
EPS = 1e-6
RW_EPS = 64e-5
DEC = math.exp(-0.5)
T = 256
NCH = T // 128


class Buf:
    __slots__ = ("name", "writers", "readers")

    def __init__(self, name=""):
        self.name = name
        self.writers = {}
        self.readers = {}


def _dep_kv(d):
    if d[0] == "e":
        return ("e", d[1], d[2] // CHUNK), d[2] % CHUNK + 1
    return ("d", d[1], d[2]), d[3]


def _merge(dst, src):
    for k, v in src.items():
        if dst.get(k, 0) < v:
            dst[k] = v


class Tile:
    __slots__ = ("ap", "buf")

    def __init__(self, ap, buf=None):
        self.ap = ap
        self.buf = buf if buf is not None else Buf()

    def __getitem__(self, k):
        return Tile(self.ap[k], self.buf)

    def bitcast(self, dt):
        return Tile(self.ap.bitcast(dt), self.buf)

    def re(self, s, **kw):
        return Tile(self.ap.rearrange(s, **kw), self.buf)


class Prog:
    ENG = ("pe", "act", "dve", "pool", "sp")

    def __init__(self, nc):
        self.nc = nc
        self.stack = contextlib.ExitStack()
        self.streams = {e: [] for e in self.ENG}
        self.count = {e: 0 for e in self.ENG}
        self.esems = {e: [] for e in self.ENG}
        self.dsems = {}
        self.dma_n = {e: 0 for e in self.ENG}
        self.dma_hist = {e: {} for e in self.ENG}
        self.waited = {e: {} for e in self.ENG}
        self.n_t = 0
        self.final_deps = []
        self.arena = None
        self.arena_off = 0
        self.arena_size = 0
        self.psb = []
        self.ps_i = 0
        self.live = []

    def init_mem(self, arena_f32_cols):
        self.arena_size = arena_f32_cols
        self.arena = self.stack.enter_context(
            self.nc.sbuf_tensor("arena", [128, arena_f32_cols], F32))
        for i in range(8):
            t = self.stack.enter_context(self.nc.psum_tensor(f"psb{i}", [128, 512], F32))
            self.psb.append(Tile(t[:, :], Buf(f"ps{i}")))

    def alloc(self, free_elems, dt=F32, name=""):
        ncol = free_elems if dt == F32 else (free_elems + 1) // 2
        if self.arena_off + ncol > self.arena_size:
            raise RuntimeError(f"arena overflow at {name}: {self.arena_off}+{ncol}>{self.arena_size}")
        s0, s1 = self.arena_off, self.arena_off + ncol
        self.arena_off += ncol
        self.hi = max(getattr(self, "hi", 0), s1)
        keep, over = [], []
        for ent in self.live:
            (over if (ent[0] < s1 and s0 < ent[1]) else keep).append(ent)
        if len(over) == 1 and over[0][0] == s0 and over[0][1] == s1 and over[0][2] == (dt, free_elems):
            return over[0][3]
        ap = self.arena[:, s0:s1]
        if dt != F32:
            ap = ap.bitcast(dt)[:, 0:free_elems]
        buf = Buf(name)
        for ent in over:
            _merge(buf.writers, ent[3].buf.writers)
            _merge(buf.readers, ent[3].buf.readers)
        t = Tile(ap, buf)
        keep.append((s0, s1, (dt, free_elems), t))
        self.live = keep
        return t

    def psum(self):
        t = self.psb[self.ps_i]
        self.ps_i = (self.ps_i + 1) % 8
        return t

    def _deps(self, e, reads, writes, is_dma=False):
        need = {}
        for r in reads:
            _merge(need, r.writers)
        for w in writes:
            _merge(need, w.writers)
            _merge(need, w.readers)
        out = []
        for key, val in need.items():
            if key[0] == "e" and key[1] == e and (e == "pe" or not SAMEQ):
                continue
            if self.waited[e].get(key, 0) >= val:
                continue
            self.waited[e][key] = val
            out.append((key, val))
        return out

    def _record(self, d, reads, writes, is_dma):
        k, v = _dep_kv(d)
        for w in writes:
            if is_dma:
                w.writers = {kk: vv for kk, vv in w.writers.items() if kk[0] == "d"}
            else:
                w.writers = {}
            w.writers[k] = v
            w.readers = {}
        for r in reads:
            if r.readers.get(k, 0) < v:
                r.readers[k] = v

    def _sem(self, key):
        if key[0] == "e":
            return self.esems[key[1]][key[2]]
        return self.dsems[(key[1], key[2])]

    def op(self, e, fn, reads=(), writes=()):
        reads = [r.buf if isinstance(r, Tile) else r for r in reads]
        writes = [w.buf if isinstance(w, Tile) else w for w in writes]
        waits = self._deps(e, reads, writes)
        idx = self.count[e]
        self.count[e] += 1
        mykey = ("e", e, idx // CHUNK)

        def emit(eng, waits=waits, fn=fn, mykey=mykey):
            for k, v in waits:
                eng.wait_ge(self._sem(k), v)
            fn(eng).then_inc(self._sem(mykey), 1)

        self.streams[e].append(emit)
        d = ("e", e, idx)
        self._record(d, reads, writes, False)
        return d

    def dma(self, q, out, in_, reads=(), writes=(), **kw):
        reads = [r.buf if isinstance(r, Tile) else r for r in reads]
        writes = [w.buf if isinstance(w, Tile) else w for w in writes]
        waits = self._deps(q, reads, writes, True)
        n = self.dma_n[q]
        self.dma_n[q] += 1
        slot = n % NDMASEM
        prev = self.dma_hist[q].get(slot, 0)
        val = prev + 16
        self.dma_hist[q][slot] = val
        key = ("d", q, slot)
        if prev > 0 and self.waited[q].get(key, 0) < prev:
            waits = waits + [(key, prev)]
            self.waited[q][key] = prev

        def emit(eng, waits=waits, key=key):
            for k, v in waits:
                eng.wait_ge(self._sem(k), v)
            eng.dma_start(out=out, in_=in_, **kw).then_inc(self._sem(key), 16)

        self.streams[q].append(emit)
        d = ("d", q, slot, val)
        self._record(d, reads, writes, True)
        return d

    def barrier(self):
        keys = []
        for e in self.ENG:
            if self.count[e] > 0:
                idx = self.count[e] - 1
                keys.append((("e", e, idx // CHUNK), idx % CHUNK + 1))
            for slot, val in self.dma_hist[e].items():
                keys.append((("d", e, slot), val))
        for f in self.ENG:
            mine = []
            for k, v in keys:
                if k[0] == "e" and k[1] == f:
                    continue
                if self.waited[f].get(k, 0) >= v:
                    continue
                self.waited[f][k] = v
                mine.append((k, v))

            def emit(eng, mine=mine):
                for k, v in mine:
                    eng.wait_ge(self._sem(k), v)

            self.streams[f].append(emit)

    def finish(self, deps):
        self.final_deps = list(deps)

    def build(self):
        nc = self.nc
        st = self.stack
        for e in self.ENG:
            nsem = (self.count[e] + CHUNK - 1) // CHUNK
            self.esems[e] = [st.enter_context(nc.semaphore(f"s_{e}_{i}")) for i in range(nsem)]
            nd = min(self.dma_n[e], NDMASEM)
            for s in range(nd):
                self.dsems[(e, s)] = st.enter_context(nc.semaphore(f"d_{e}_{s}"))
        fin = []
        for d in self.final_deps:
            if d[0] == "e":
                fin.append((("e", d[1], d[2] // CHUNK), d[2] % CHUNK + 1))
            else:
                fin.append((("d", d[1], d[2]), d[3]))
        block = st.enter_context(nc.Block())
        streams = self.streams

        @block.tensor
        def _(eng):
            for f in streams["pe"]:
                f(eng)

        @block.scalar
        def _(eng):
            for f in streams["act"]:
                f(eng)

        @block.vector
        def _(eng):
            for f in streams["dve"]:
                f(eng)

        @block.gpsimd
        def _(eng):
            for f in streams["pool"]:
                f(eng)

        @block.sync
        def _(eng):
            for f in streams["sp"]:
                f(eng)
            for k, v in fin:
                eng.wait_ge(self._sem(k), v)

        st.close()


def _ap(x):
    return x.ap if isinstance(x, Tile) else x


def _tl(*xs):
    return [x for x in xs if isinstance(x, Tile)]


def ACT(P, out, in_, func, bias=None, scale=None, accum=None):
    kw = {}
    if bias is not None:
        kw["bias"] = _ap(bias)
    if scale is not None:
        kw["scale"] = _ap(scale)
    if accum is not None:
        kw["accum_out"] = _ap(accum)
    P.op("act", lambda e: e.activation(out=out.ap, in_=in_.ap, func=func, **kw),
         reads=_tl(in_, bias, scale), writes=_tl(out, accum))


def TT(P, eng, out, a, b, op):
    P.op(eng, lambda e: e.tensor_tensor(out=out.ap, in0=a.ap, in1=b.ap, op=op),
         reads=_tl(a, b), writes=[out])


def TS(P, eng, out, a, s1, op0, s2=None, op1=None):
    if op1 is None:
        P.op(eng, lambda e: e.tensor_scalar(out=out.ap, in0=a.ap, scalar1=_ap(s1), scalar2=None, op0=op0),
             reads=_tl(a, s1), writes=[out])
    else:
        P.op(eng, lambda e: e.tensor_scalar(out=out.ap, in0=a.ap, scalar1=_ap(s1), scalar2=_ap(s2),
                                            op0=op0, op1=op1),
             reads=_tl(a, s1, s2), writes=[out])


def STT(P, out, in0, scalar, in1, op0, op1):
    P.op("dve", lambda e: e.scalar_tensor_tensor(out=out.ap, in0=in0.ap, scalar=_ap(scalar), in1=in1.ap,
                                                 op0=op0, op1=op1),
         reads=_tl(in0, scalar, in1), writes=[out])


def CP(P, eng, out, in_):
    if eng == "act":
        P.op("act", lambda e: e.activation(out=out.ap, in_=in_.ap, func=AF.Copy), reads=[in_], writes=[out])
    else:
        P.op(eng, lambda e: e.tensor_copy(out=out.ap, in_=in_.ap), reads=[in_], writes=[out])


def MM(P, out, lhsT, rhs, start=True, stop=True):
    P.op("pe", lambda e: e.matmul(out.ap, lhsT=lhsT.ap, rhs=rhs.ap, start=start, stop=stop),
         reads=[lhsT, rhs], writes=[out])


def TR(P, out, in_, ident):
    P.op("pe", lambda e: e.transpose(out=out.ap, in_=in_.ap, identity=ident.ap),
         reads=[in_, ident], writes=[out])


def SCAN(P, out, d0, d1, init):
    P.op("dve", lambda e: e.tensor_tensor_scan(out=out.ap, data0=d0.ap, data1=d1.ap, initial=_ap(init),
                                               op0=ALU.mult, op1=ALU.add),
         reads=_tl(d0, d1, init), writes=[out])


def MEMSET(P, eng, out, val):
    P.op(eng, lambda e: e.memset(out.ap, val), writes=[out])


COLS = {}


def _col_layout():
    names = [("mu", 8), ("w0", 2), ("a0", 2), ("k_k", 2), ("k_a", 2), ("r_k", 2), ("ln_g", 2), ("ln_b", 2),
             ("cw0", 4), ("cw1", 4), ("cw2", 4), ("cw3", 4), ("cb", 4), ("ba", 4), ("bx", 4), ("lam", 4),
             ("lng", 4), ("gkb", 1), ("gng", 1)]
    off = 0
    for n, c in names:
        COLS[n] = (off, c)
        off += c
    return off


NCOL = _col_layout()


def pack_cols(inp, l):
    out = np.zeros((128, NCOL), np.float32)

    def put(name, vec):
        o, c = COLS[name]
        out[:, o:o + c] = np.asarray(vec, np.float32).reshape(c, 128).T

    put("mu", inp["rw_mu"][l])
    put("w0", inp["rw_w0"][l])
    put("a0", inp["rw_a0"][l])
    put("k_k", inp["rw_k_k"][l])
    put("k_a", inp["rw_k_a"][l])
    put("r_k", inp["rw_r_k"][l].reshape(-1))
    put("ln_g", inp["rw_ln_g"][l])
    put("ln_b", inp["rw_ln_b"][l])
    for j in range(4):
        put(f"cw{j}", inp["lru_conv_w"][l, j])
    put("cb", inp["lru_conv_b"][l])
    put("ba", inp["lru_ba"][l])
    put("bx", inp["lru_bx"][l])
    put("lam", inp["lru_lam"][l])
    put("lng", inp["lru_norm_g"][l])
    put("gkb", inp["gla_gk_b"][l])
    put("gng", np.concatenate([inp["gla_norm_g"][l], inp["gla_norm_g"][l]]))
    return out


def make_consts():
    c = {}
    idx = np.arange(128)
    su = (idx[:, None] < idx[None, :]).astype(np.float32)
    ui = (idx[:, None] <= idx[None, :]).astype(np.float32)
    sl = (idx[:, None] > idx[None, :]).astype(np.float32)
    c["ident"] = np.eye(128, dtype=np.float32)
    c["cmask"] = np.concatenate([su, su, ui, ui], axis=1)
    c["sl4"] = np.concatenate([sl] * 4, axis=1)
    c["ui4"] = np.concatenate([ui] * NCH, axis=1)
    ob = np.zeros((128, 128), np.float32)
    ob[:64, :64] = 1
    ob[64:, 64:] = 1
    c["ones64"] = ob
    rm = np.ones((128, T), np.float32)
    rm[:, ::128] = 0
    c["rmask"] = rm
    bm = np.zeros((128, 256), np.float32)
    for h in range(4):
        bm[32 * h:32 * h + 32, 64 * h:64 * h + 64] = 1
    c["bmask"] = bm
    par = np.zeros((128, 2), np.float32)
    for h in range(4):
        par[32 * h:32 * h + 32, h % 2] = 1
    c["par"] = par
    return c


CONST_SHAPES = {"ident": 128, "cmask": 512, "sl4": 512, "ui4": T, "ones64": 128,
                "rmask": T, "bmask": 256, "par": 2}


def build_program(ntok, nlayers, debug=()):
    nc = bass.Bass("TRN2", target_bir_lowering=False)
    NT = ntok // T
    L = nlayers

    def din(name, shape):
        return nc.dram_tensor(name, list(shape), F32, kind="ExternalInput").ap()

    x_in = din("x", [ntok, D])
    w_in = din("w_in", [L, D, PIN])
    w_out = din("w_out", [L, D, D])
    w_gate = din("ffn_w_gate", [L, D, DFF])
    w_up = din("ffn_w_up", [L, D, DFF])
    w_down = din("ffn_w_down", [L, DFF, D])
    rw_w_up = din("rw_w_up", [L, 64, 256])
    rw_a_up = din("rw_a_up", [L, 64, 256])
    rw_g_up = din("rw_g_up", [L, 128, 256])
    lru_wa = din("lru_wa", [L, 8, 64, 64])
    lru_wx = din("lru_wx", [L, 8, 64, 64])
    gk_up = din("gla_gk_up", [L, 16, 128])
    cols_d = din("cols", [L, 128, NCOL])
    norm1_g = din("norm1_g", [L, D])
    norm2_g = din("norm2_g", [L, D])
    final_g = din("final_norm_g", [1, D])
    cdram = {k: din("c_" + k, [128, n]) for k, n in CONST_SHAPES.items()}
    out_d = nc.dram_tensor("out", [ntok, D], F32, kind="ExternalOutput").ap()
    xa_d = nc.dram_tensor("xa_scr", [ntok, D], F32).ap()
    xb_d = nc.dram_tensor("xb_scr", [ntok, D], F32).ap()
    xa_buf, xb_buf = Buf("xa"), Buf("xb")
    dbg_out = {}

    P = Prog(nc)
    P.init_mem(51000)
    fin = []

    def dbg(name, tile, rows, cols, tok0=None, total_cols=None):
        if name not in debug:
            return
        if name not in dbg_out:
            tc = total_cols if total_cols is not None else cols
            dbg_out[name] = nc.dram_tensor("dbg_" + name, [rows, tc], F32, kind="ExternalOutput").ap()
        dst = dbg_out[name]
        c0 = tok0 if tok0 is not None else 0
        fin.append(P.dma("pool", dst[0:rows, c0:c0 + cols], tile.ap, reads=[tile]))

    cst = {}
    for k, n in CONST_SHAPES.items():
        if k in ():
            cst[k] = P.alloc(n, BF16, "c_" + k)
            P.dma("pool", cst[k].ap, cdram[k][:, :], writes=[cst[k]])
        else:
            cst[k] = P.alloc(n, F32, "c_" + k)
            P.dma("sp", cst[k].ap, cdram[k][:, :], writes=[cst[k]])
    identf = cst["ident"]
    identb = P.alloc(128, BF16, "identb")
    CP(P, "pool", identb, identf)
    ident4b = P.alloc(512, BF16, "ident4b")
    for i in range(4):
        CP(P, "pool", ident4b[:, i * 128:(i + 1) * 128], identf)
    ones64b = P.alloc(128, BF16, "ones64b")
    CP(P, "pool", ones64b, cst["ones64"])
    cmask, sl4, ui4, rmask, bmask, par = (cst[k] for k in ("cmask", "sl4", "ui4", "rmask", "bmask", "par"))
    epsc = P.alloc(2, F32, "epsc")
    MEMSET(P, "pool", epsc[:, 0:1], EPS)
    MEMSET(P, "pool", epsc[:, 1:2], RW_EPS)

    rw_carry = P.alloc(8, F32, "rw_carry")
    lru_xc = [P.alloc(3, F32, f"lru_xc{j}") for j in range(4)]
    lru_h = P.alloc(4, F32, "lru_h")
    rw_H = [P.alloc(64, F32, f"rwH{hp}") for hp in range(2)]
    gla_S = P.alloc(256, F32, "glaS")
    persist_mark = P.arena_off

    def rms_block(xblk, gB, hn_out):
        ssq = P.alloc(1, F32, "ssq")
        rstd = P.alloc(1, F32, "rstd")
        ACT(P, hn_out, xblk, AF.Square, accum=ssq)
        ACT(P, rstd, ssq, AF.Ln, bias=epsc[:, 0:1], scale=1.0 / D)
        ACT(P, rstd, rstd, AF.Exp, scale=-0.5)
        STT(P, hn_out, xblk, rstd[:, 0:1], gB, ALU.mult, ALU.mult)

    for l in range(L):
        src_d, src_buf = (x_in, None) if l == 0 else (xb_d, xb_buf)
        last = l == L - 1
        P.barrier()
        P.arena_off = persist_mark
        colsT = P.alloc(NCOL, F32, "cols")
        P.dma("sp", colsT.ap, cols_d[l], writes=[colsT])

        def col(name, j=0, rows=slice(0, 128)):
            o, c = COLS[name]
            return colsT[rows, o + j:o + j + 1]

        dcol = P.alloc(24, F32, "dcol")
        o_mu = COLS["mu"][0]
        omm = dcol[:, 0:8]
        TS(P, "pool", omm, colsT[:, o_mu:o_mu + 8], -1.0, ALU.mult, 1.0, ALU.add)
        o_ka = COLS["k_a"][0]
        omka = dcol[:, 8:10]
        TS(P, "pool", omka, colsT[:, o_ka:o_ka + 2], -1.0, ALU.mult, 1.0, ALU.add)
        o_lam = COLS["lam"][0]
        c1 = dcol[:, 10:14]
        c2 = dcol[:, 14:18]
        ACT(P, c1, colsT[:, o_lam:o_lam + 4], AF.Exp, scale=-1.0)
        ACT(P, c1, c1, AF.Ln, bias=1.0)
        TS(P, "pool", c2, c1, -16.0, ALU.mult)
        TS(P, "pool", c1, c1, -8.0, ALU.mult)
        MEMSET(P, "pool", rw_carry, 0.0)
        for j in range(4):
            MEMSET(P, "pool", lru_xc[j], 0.0)
        MEMSET(P, "pool", lru_h, 0.0)
        for hp in range(2):
            MEMSET(P, "pool", rw_H[hp], 0.0)
        MEMSET(P, "pool", gla_S, 0.0)

        gB1 = P.alloc(D, F32, "gB1")
        P.dma("sp", gB1.ap, norm1_g[l:l + 1, :].partition_broadcast(128), writes=[gB1])
        w_in_sb = P.alloc(8 * PIN, BF16, "w_in")
        for kc in range(8):
            for hf in range(2):
                P.dma("pool", w_in_sb.ap[:, kc * PIN + hf * 1416: kc * PIN + (hf + 1) * 1416],
                      w_in[l, kc * 128:(kc + 1) * 128, hf * 1416:(hf + 1) * 1416], writes=[w_in_sb])
        w_out_sb = P.alloc(8 * D, BF16, "w_out")
        for kc in range(8):
            P.dma("pool", w_out_sb.ap[:, kc * D:(kc + 1) * D], w_out[l, kc * 128:(kc + 1) * 128, :],
                  writes=[w_out_sb])
        wa_up = P.alloc(256, BF16, "wa_up")
        P.dma("pool", wa_up.ap[0:64, :], rw_w_up[l], writes=[wa_up])
        P.dma("pool", wa_up.ap[64:128, :], rw_a_up[l], writes=[wa_up])
        g_up = P.alloc(256, BF16, "g_up")
        P.dma("pool", g_up.ap, rw_g_up[l], writes=[g_up])
        gkup = P.alloc(128, BF16, "gkup")
        P.dma("pool", gkup.ap[0:16, :], gk_up[l], writes=[gkup])
        wabd = P.alloc(4 * 128, BF16, "wabd")
        wxbd = P.alloc(4 * 128, BF16, "wxbd")
        MEMSET(P, "pool", wabd, 0.0)
        MEMSET(P, "pool", wxbd, 0.0)
        for j in range(4):
            for s in range(2):
                P.dma("pool", wabd.ap[64 * s:64 * s + 64, j * 128 + 64 * s: j * 128 + 64 * s + 64],
                      lru_wa[l, 2 * j + s], writes=[wabd])
                P.dma("pool", wxbd.ap[64 * s:64 * s + 64, j * 128 + 64 * s: j * 128 + 64 * s + 64],
                      lru_wx[l, 2 * j + s], writes=[wxbd])
        mbbd = [[P.alloc(128, F32, f"mbbd{hp}{c}") for c in range(NCH)] for hp in range(2)]
        for hp in range(2):
            for c in range(NCH):
                MEMSET(P, "pool", mbbd[hp][c], 0.0)
        tile_mark = P.arena_off

        for ti in range(NT):
            P.arena_off = tile_mark
            tok0 = ti * T
            xT = [P.alloc(D, F32, f"xT{b}") for b in range(NCH)]
            hnF = P.alloc(8 * T, BF16, "hnF")
            hnF3 = hnF.re("p (k t) -> p k t", k=8)
            ycat = P.alloc(8 * T, BF16, "ycat")
            ycat3 = ycat.re("p (k t) -> p k t", k=8)
            for b in range(NCH):
                rd = [src_buf] if src_buf is not None else []
                P.dma("sp", xT[b].ap, src_d[tok0 + b * 128: tok0 + (b + 1) * 128, :], reads=rd, writes=[xT[b]])
            m0 = P.arena_off
            for b in range(NCH):
                P.arena_off = m0
                hn = P.alloc(D, BF16, "hn")
                rms_block(xT[b], gB1, hn)
                pst = P.psum()
                pstb = pst.bitcast(BF16)
                for kc in range(8):
                    TR(P, pstb[:, kc * 128:(kc + 1) * 128], hn[:, kc * 128:(kc + 1) * 128], identb)
                CP(P, "act", hnF3[:, :, b * 128:(b + 1) * 128], pstb.re("p (k t) -> p k t", k=8))
            P.arena_off = m0

            def proj(c0, ncols, evac):
                ps = P.psum()
                for kc in range(8):
                    MM(P, ps[0:ncols, 0:T], w_in_sb[:, kc * PIN + c0: kc * PIN + c0 + ncols], hnF3[:, kc, :],
                       start=(kc == 0), stop=(kc == 7))
                evac(ps[0:ncols, 0:T])


            base_thr = P.arena_off

            class Region:
                def __init__(self, start, size):
                    self.off = start
                    self.end = start + size

                def alloc(self, n, dt=F32, name=""):
                    save = P.arena_off
                    P.arena_off = self.off
                    t = P.alloc(n, dt, name)
                    self.off = P.arena_off
                    P.arena_off = save
                    if self.off > self.end:
                        raise RuntimeError(f"region overflow {name} {self.off}>{self.end}")
                    return t

            SZ_RW, SZ_LRU, SZ_GLA = 15000, 5000, 5800

            def rsqrt_act(out, in_, scale, bias):
                ACT(P, out, in_, AF.Ln, bias=bias, scale=scale)
                ACT(P, out, out, AF.Exp, scale=-0.5)

            def rw_thread():
                R = Region(base_thr, SZ_RW)
                A = R.alloc
                pm = {}
                ptmp = A(1 + T, F32, "ptmp")
                ltmp = A(T, F32, "ltmp")
                for gi, gname in enumerate(["r0", "r1", "k0", "k1", "v0", "v1", "wa", "glo"]):
                    pm[gname] = A(T, F32, "pm_" + gname)

                    def ev(ps, gi=gi, gname=gname):
                        CP(P, "act", ptmp[:, 1:1 + T], ps)
                        CP(P, "pool", ptmp[:, 0:1], rw_carry[:, gi:gi + 1])
                        TS(P, "dve", ltmp, ptmp[:, 0:T], col("mu", gi), ALU.mult)
                        STT(P, pm[gname], ptmp[:, 1:1 + T], omm[:, gi:gi + 1], ltmp, ALU.mult, ALU.add)
                        CP(P, "pool", rw_carry[:, gi:gi + 1], ptmp[:, T:T + 1])
                    proj(gi * 128, 128, ev)
                    yield
                wab = A(T, BF16, "wab")
                ACT(P, wab[0:64, :], pm["wa"][0:64, :], AF.Tanh)
                CP(P, "pool", wab[64:128, :], pm["wa"][64:128, :])
                sgl = A(T, BF16, "sgl")
                ACT(P, sgl, pm["glo"], AF.Sigmoid)
                yield
                sgs, avs, gSs = [], [], []
                for hp in range(2):
                    ps_w, ps_a, ps_g = P.psum(), P.psum(), P.psum()
                    MM(P, ps_w[:, 0:T], wa_up[0:64, hp * 128:(hp + 1) * 128], wab[0:64, :])
                    MM(P, ps_a[:, 0:T], wa_up[64:128, hp * 128:(hp + 1) * 128], wab[64:128, :])
                    MM(P, ps_g[:, 0:T], g_up[:, hp * 128:(hp + 1) * 128], sgl)
                    sg = A(T, F32, f"sg{hp}")
                    a = A(T, F32, f"a{hp}")
                    gS = A(T, F32, f"gS{hp}")
                    ACT(P, sg, ps_w[:, 0:T], AF.Sigmoid, bias=col("w0", hp))
                    ACT(P, a, ps_a[:, 0:T], AF.Sigmoid, bias=col("a0", hp))
                    CP(P, "dve", gS, ps_g[:, 0:T])
                    sgs.append(sg)
                    avs.append(a)
                    gSs.append(gS)
                    yield
                yield "B"
                subs = [rw_hp(hp, R, pm, sgs[hp], avs[hp], gSs[hp]) for hp in range(2)]
                SEQ_HP = os.environ.get("K_SEQHP", "1") == "1"
                while subs:
                    for sub in (list(subs[:1]) if SEQ_HP else list(subs)):
                        try:
                            next(sub)
                        except StopIteration:
                            subs.remove(sub)
                    yield

            def rw_hp(hp, R, pm, sg, a, gS):
                A = R.alloc
                if True:
                    r, k, v = pm[f"r{hp}"], pm[f"k{hp}"], pm[f"v{hp}"]
                    kk = A(T, F32, "kk")
                    sqk = A(T, BF16, "sqk")
                    TS(P, "dve", kk, k, col("k_k", hp), ALU.mult)
                    TT(P, "pool", sqk, kk, kk, ALU.mult)
                    ps_n = P.psum()
                    MM(P, ps_n[:, 0:T], ones64b, sqk)
                    rn = A(T, F32, "rn")
                    rsqrt_act(rn, ps_n[:, 0:T], 1.0, 1e-24)
                    yield
                    TT(P, "dve", kk, kk, rn, ALU.mult)
                    kmod = A(T, F32, "kmod")
                    TS(P, "pool", kmod, a, col("k_a", hp), ALU.mult, omka[:, hp:hp + 1], ALU.add)
                    TT(P, "dve", kmod, kmod, k, ALU.mult)
                    yield
                    bvec = A(T, F32, "bvec")
                    TT(P, "dve", bvec, kk, a, ALU.mult)
                    rk = rn
                    TT(P, "dve", rk, r, kmod, ALU.mult)
                    rkb = sqk
                    TS(P, "pool", rkb, rk, col("r_k", hp), ALU.mult)
                    ps_b = P.psum()
                    MM(P, ps_b[:, 0:T], ones64b, rkb)
                    bonus = A(T, F32, "bonus")
                    TT(P, "dve", bonus, ps_b[:, 0:T], v, ALU.mult)
                    yield
                    css = A(T, F32, "css")
                    SCAN(P, css, rmask, sg, 0.0)
                    cse = A(T, F32, "cse")
                    TT(P, "pool", cse, css, sg, ALU.subtract)
                    E1 = A(T, F32, "E1")
                    E0 = cse
                    Einv = A(T, F32, "Einv")
                    Eend = sg
                    nb = A(NCH, F32, "nb")
                    TS(P, "pool", nb, css.re("p (c t) -> p c t", c=NCH)[:, :, 127], -DEC, ALU.mult)
                    ACT(P, E1, css, AF.Exp, scale=-DEC)
                    ACT(P, Einv, css, AF.Exp, scale=DEC)
                    yield
                    for c in range(NCH):
                        ACT(P, Eend[:, c * 128:(c + 1) * 128], css[:, c * 128:(c + 1) * 128], AF.Exp,
                            bias=nb[:, c:c + 1], scale=DEC)
                    ACT(P, E0, cse, AF.Exp, scale=-DEC)
                    gC = A(NCH, F32, "gC")
                    CP(P, "pool", gC, E1.re("p (c t) -> p c t", c=NCH)[:, :, 127])
                    yield
                    Rt = A(T, BF16, "Rt")
                    At = A(T, BF16, "At")
                    Bt = A(T, BF16, "Bt")
                    Kt = A(T, BF16, "Kt")
                    Kh = A(T, BF16, "Kh")
                    Bh = A(T, BF16, "Bh")
                    vb = A(T, BF16, "vb")
                    TT(P, "dve", Rt, r, E1, ALU.mult)
                    STT(P, At, kk, -1.0, E0, ALU.mult, ALU.mult)
                    TT(P, "dve", Bt, bvec, Einv, ALU.mult)
                    TT(P, "dve", Kt, kmod, Einv, ALU.mult)
                    yield
                    TT(P, "dve", Kh, kmod, Eend, ALU.mult)
                    TT(P, "pool", Bh, bvec, Eend, ALU.mult)
                    CP(P, "pool", vb, v)
                    yield
                    TM = []
                    for c in range(NCH):
                        pst = P.psum()
                        pstb = pst.bitcast(BF16)
                        for i, src_ in enumerate([vb, Kh, Bh, At]):
                            TR(P, pstb[:, i * 128:(i + 1) * 128], src_[:, c * 128:(c + 1) * 128], identb)
                        tm = A(512, BF16, f"TM{c}")
                        CP(P, "act", tm, pstb[:, 0:512])
                        TM.append(tm)
                        yield
                    NJ = 2 * NCH
                    Aall = []
                    al = [t_.bitcast(BF16) for t_ in (kk, kmod, bvec, rn, css, cse, E1)]
                    P0, P0T = al[0], al[1]
                    for s in range(2):
                        rows = slice(64 * s, 64 * s + 64)
                        ps_p0 = P.psum()
                        for c in range(NCH):
                            j = s * NCH + c
                            cs = slice(c * 128, (c + 1) * 128)
                            ps = P.psum()
                            MM(P, ps[:, 0:128], Bt[rows, cs], At[rows, cs])
                            MM(P, ps[:, 128:256], Kt[rows, cs], At[rows, cs])
                            MM(P, ps[:, 256:384], Bt[rows, cs], Rt[rows, cs])
                            MM(P, ps[:, 384:512], Kt[rows, cs], Rt[rows, cs])
                            aa = A(512, BF16, f"Aall{j}")
                            TT(P, "dve", aa, ps, cmask, ALU.mult)
                            Aall.append(aa)
                            CP(P, "act", P0T[:, j * 128:(j + 1) * 128], aa[:, 0:128])
                            MM(P, ps_p0[:, c * 128:(c + 1) * 128], At[rows, cs], Bt[rows, cs])
                        TT(P, "dve", P0[:, s * T:(s + 1) * T], ps_p0[:, 0:T], sl4[:, 0:T], ALU.mult)
                        yield
                    G = al[2]
                    TT(P, "pool", G, P0T, ident4b[:, 0:NJ * 128], ALU.add)
                    Pk, PkT = P0, P0T
                    Pn = [al[3], al[4]]
                    PnT = [al[5], al[6]]
                    NLEV = 6
                    for lev in range(NLEV):
                        nP, nPT = Pn[lev % 2], PnT[lev % 2]
                        ps1 = P.psum()
                        for j in range(NJ):
                            js = slice(j * 128, (j + 1) * 128)
                            MM(P, ps1[:, js], PkT[:, js], Pk[:, js])
                        CP(P, "act", nP, ps1[:, 0:NJ * 128])
                        if lev < NLEV - 1:
                            ps2 = P.psum()
                            for j in range(NJ):
                                js = slice(j * 128, (j + 1) * 128)
                                MM(P, ps2[:, js], Pk[:, js], PkT[:, js])
                            CP(P, "dve", nPT, ps2[:, 0:NJ * 128])
                        yield
                        ps3 = P.psum()
                        for j in range(NJ):
                            js = slice(j * 128, (j + 1) * 128)
                            MM(P, ps3[:, js], nP[:, js], G[:, js])
                        TT(P, "dve", G, G, ps3[:, 0:NJ * 128], ALU.add)
                        Pk, PkT = nP, nPT
                        yield
                    XW = Einv.bitcast(BF16)
                    ps = P.psum()
                    for s in range(2):
                        for c in range(NCH):
                            j = s * NCH + c
                            MM(P, ps[:, j * 128:j * 128 + 64], Aall[j][:, 128:256], TM[c][:, 64 * s:64 * s + 64])
                            MM(P, ps[:, j * 128 + 64:j * 128 + 128], G[:, j * 128:(j + 1) * 128],
                               TM[c][:, 384 + 64 * s:384 + 64 * s + 64])
                    CP(P, "act", XW, ps[:, 0:NJ * 128])
                    yield
                    U0 = r.bitcast(BF16)[:, 0:NJ * 64]
                    ps = P.psum()
                    for j in range(NJ):
                        MM(P, ps[:, j * 64:(j + 1) * 64], G[:, j * 128:(j + 1) * 128], XW[:, j * 128:j * 128 + 64])
                    CP(P, "act", U0, ps[:, 0:NJ * 64])
                    yield
                    ps = P.psum()
                    for s in range(2):
                        orow = slice(64 * s, 64 * s + 64)
                        for c in range(NCH):
                            j = s * NCH + c
                            MM(P, ps[orow, c * 128:c * 128 + 64], XW[:, j * 128 + 64:j * 128 + 128],
                               TM[c][:, 256 + 64 * s:256 + 64 * s + 64])
                            MM(P, ps[orow, c * 128 + 64:c * 128 + 128], TM[c][:, 256 + 64 * s:256 + 64 * s + 64],
                               U0[:, j * 64:(j + 1) * 64], start=True, stop=False)
                            MM(P, ps[orow, c * 128 + 64:c * 128 + 128], TM[c][:, 128 + 64 * s:128 + 64 * s + 64],
                               TM[c][:, 64 * s:64 * s + 64], start=False, stop=True)
                    Nn = k[:, 0:NCH * 64]
                    for c in range(NCH):
                        CP(P, "act", mbbd[hp][c][0:64, 0:64], ps[0:64, c * 128:c * 128 + 64])
                        CP(P, "act", mbbd[hp][c][64:128, 64:128], ps[64:128, c * 128:c * 128 + 64])
                        CP(P, "dve", Nn[:, c * 64:(c + 1) * 64], ps[:, c * 128 + 64:c * 128 + 128])
                    yield
                    RhT = k[:, NCH * 64:NCH * 64 + T // 2].bitcast(BF16)
                    ps = P.psum()
                    for s in range(2):
                        orow = slice(64 * s, 64 * s + 64)
                        for c in range(NCH):
                            j = s * NCH + c
                            MM(P, ps[orow, c * 128:(c + 1) * 128], XW[:, j * 128 + 64:j * 128 + 128],
                               Aall[j][:, 256:384])
                    TT(P, "dve", RhT, ps[:, 0:T], Rt, ALU.add)
                    yield
                    Y0 = v
                    ps = P.psum()
                    for s in range(2):
                        orow = slice(64 * s, 64 * s + 64)
                        for c in range(NCH):
                            j = s * NCH + c
                            MM(P, ps[orow, c * 128:(c + 1) * 128], U0[:, j * 64:(j + 1) * 64], Aall[j][:, 256:384],
                               start=True, stop=False)
                            MM(P, ps[orow, c * 128:(c + 1) * 128], TM[c][:, 64 * s:64 * s + 64],
                               Aall[j][:, 384:512], start=False, stop=True)
                    CP(P, "act", Y0, ps[:, 0:T])
                    yield
                    Hs = sg[:, 0:(NCH + 1) * 64]
                    Hb = r[:, 128:128 + NCH * 32].bitcast(BF16)
                    CP(P, "pool", Hs[:, 0:64], rw_H[hp])
                    for c in range(NCH):
                        CP(P, "pool", Hb[:, c * 64:(c + 1) * 64], Hs[:, c * 64:(c + 1) * 64])
                        ps = P.psum()
                        MM(P, ps[:, 0:64], mbbd[hp][c], Hs[:, c * 64:(c + 1) * 64], start=True, stop=False)
                        MM(P, ps[:, 0:64], identf, Nn[:, c * 64:(c + 1) * 64], start=False, stop=True)
                        STT(P, Hs[:, (c + 1) * 64:(c + 2) * 64], Hs[:, c * 64:(c + 1) * 64], gC[:, c:c + 1],
                            ps[:, 0:64], ALU.mult, ALU.add)
                        yield
                    CP(P, "pool", rw_H[hp], Hs[:, NCH * 64:(NCH + 1) * 64])
                    y = a
                    pse, pso = P.psum(), P.psum()
                    for c in range(NCH):
                        cs = slice(c * 128, (c + 1) * 128)
                        MM(P, pse[0:64, cs], Hb[0:64, c * 64:(c + 1) * 64], RhT[0:64, cs])
                        MM(P, pso[64:128, cs], Hb[64:128, c * 64:(c + 1) * 64], RhT[64:128, cs])
                    TT(P, "dve", y[0:64, :], pse[0:64, 0:T], Y0[0:64, :], ALU.add)
                    TT(P, "dve", y[64:128, :], pso[64:128, 0:T], Y0[64:128, :], ALU.add)
                    dbg(f"rw_y{hp}", y, 128, T, tok0, ntok)
                    yield
                    yb = A(T, BF16, "yb")
                    CP(P, "pool", yb, y)
                    ps_m = P.psum()
                    MM(P, ps_m[:, 0:T], ones64b, yb)
                    yc = A(T, F32, "yc")
                    STT(P, yc, ps_m[:, 0:T], -1.0 / 64, y, ALU.mult, ALU.add)
                    yield
                    TT(P, "pool", yb, yc, yc, ALU.mult)
                    ps_v = P.psum()
                    MM(P, ps_v[:, 0:T], ones64b, yb)
                    rs = y
                    rsqrt_act(rs, ps_v[:, 0:T], 1.0 / 64, epsc[:, 1:2])
                    yield
                    TT(P, "dve", yc, yc, rs, ALU.mult)
                    TS(P, "pool", yc, yc, col("ln_g", hp), ALU.mult, col("ln_b", hp), ALU.add)
                    TT(P, "dve", yc, yc, bonus, ALU.add)
                    TT(P, "dve", ycat3[:, hp, :], yc, gS, ALU.mult)
                    yield

            def lru_thread():
                R = Region(base_thr + SZ_RW, SZ_LRU)
                A = R.alloc
                xbuf = A(3 + T, F32, "xbuf")
                gt = A(T, F32, "gt")
                xc = A(T, F32, "xc")
                xcb = A(T, BF16, "xcb")
                st = []
                for j in range(4):
                    CP(P, "pool", xbuf[:, 0:3], lru_xc[j])
                    proj(1024 + j * 128, 128, lambda ps: CP(P, "act", xbuf[:, 3:3 + T], ps))
                    yield
                    proj(1536 + j * 128, 128, lambda ps: CP(P, "act", gt, ps))
                    CP(P, "pool", lru_xc[j], xbuf[:, T:T + 3])
                    yield
                    TS(P, "pool", xc, xbuf[:, 0:T], col("cw0", j), ALU.mult, col("cb", j), ALU.add)
                    for tap in range(1, 4):
                        STT(P, xc, xbuf[:, tap:tap + T], col(f"cw{tap}", j), xc, ALU.mult, ALU.add)
                    CP(P, "pool", xcb, xc)
                    yield
                    ps_r, ps_i = P.psum(), P.psum()
                    MM(P, ps_r[:, 0:T], wabd[:, j * 128:(j + 1) * 128], xcb)
                    MM(P, ps_i[:, 0:T], wxbd[:, j * 128:(j + 1) * 128], xcb)
                    gr = A(T, F32, f"gr{j}")
                    uu = A(T, F32, f"uu{j}")
                    ge = A(T, F32, f"ge{j}")
                    ACT(P, gr, ps_r[:, 0:T], AF.Sigmoid, bias=col("ba", j))
                    ACT(P, uu, ps_i[:, 0:T], AF.Sigmoid, bias=col("bx", j))
                    yield
                    TT(P, "dve", uu, uu, xc, ALU.mult)
                    TT(P, "pool", ge, gt, gt, ALU.mult)
                    TS(P, "pool", ge, ge, 0.044715, ALU.mult, 1.0, ALU.add)
                    TT(P, "dve", ge, ge, gt, ALU.mult)
                    ACT(P, ge, ge, AF.Sigmoid, scale=1.5957691216057308)
                    TT(P, "dve", ge, ge, gt, ALU.mult)
                    st.append((gr, uu, ge))
                    yield
                yield "B"
                av = A(T, F32, "av")
                a2 = A(T, F32, "a2")
                hh = A(T, F32, "hh")
                sqb = A(T, BF16, "sqb")
                for j in range(4):
                    gr, uu, ge = st[j]
                    ACT(P, av, gr, AF.Exp, scale=c1[:, j:j + 1])
                    ACT(P, a2, gr, AF.Exp, scale=c2[:, j:j + 1])
                    ACT(P, a2, a2, AF.Ln, bias=1.0, scale=-1.0)
                    ACT(P, a2, a2, AF.Exp, scale=0.5)
                    yield
                    TT(P, "dve", uu, uu, a2, ALU.mult)
                    SCAN(P, hh, av, uu, lru_h[:, j:j + 1])
                    CP(P, "pool", lru_h[:, j:j + 1], hh[:, T - 1:T])
                    yl = av
                    TT(P, "dve", yl, hh, ge, ALU.mult)
                    dbg(f"lru_y{j}", yl, 128, T, tok0, ntok)
                    yield
                    TT(P, "pool", sqb, yl, yl, ALU.mult)
                    ps_m = P.psum()
                    MM(P, ps_m[:, 0:T], ones64b, sqb)
                    rs = a2
                    rsqrt_act(rs, ps_m[:, 0:T], 1.0 / 64, epsc[:, 0:1])
                    STT(P, ycat3[:, 2 + j, :], yl, col("lng", j), rs, ALU.mult, ALU.mult)
                    yield

            def gla_thread():
                R = Region(base_thr + SZ_RW + SZ_LRU, SZ_GLA)
                A = R.alloc
                q = A(T, F32, "q")
                kg = A(T, F32, "kg")
                vg = [A(T, BF16, f"vg{i}") for i in range(2)]
                gg = [A(T, F32, f"gg{i}") for i in range(2)]
                gklo = A(T, BF16, "gklo")
                sgm = A(T, F32, "sgm")
                proj(2048, 128, lambda ps: CP(P, "act", q, ps))
                yield
                proj(2176, 128, lambda ps: CP(P, "act", kg, ps))
                yield
                proj(2304, 128, lambda ps: CP(P, "act", vg[0], ps))
                yield
                proj(2432, 128, lambda ps: CP(P, "act", vg[1], ps))
                yield
                proj(2560, 16, lambda ps: CP(P, "act", gklo[0:16, :], ps))
                yield
                for i in range(2):
                    def evg(ps, i=i):
                        ACT(P, sgm, ps, AF.Sigmoid)
                        TT(P, "dve", gg[i], ps, sgm, ALU.mult)
                    proj(2576 + 128 * i, 128, evg)
                    yield
                ps_gk = P.psum()
                MM(P, ps_gk[:, 0:T], gkup[0:16, :], gklo[0:16, :])
                la = A(T, F32, "la")
                ACT(P, la, ps_gk[:, 0:T], AF.Sigmoid, bias=col("gkb"))
                yield "B"
                ACT(P, la, la, AF.Ln)
                bc = A(T, F32, "bc")
                SCAN(P, bc, rmask, la, 0.0)
                Eq = A(T, F32, "Eq")
                Ek = A(T, F32, "Ek")
                Ee = la
                ACT(P, Eq, bc, AF.Exp, scale=1.0 / 16)
                ACT(P, Ek, bc, AF.Exp, scale=-1.0 / 16)
                yield
                nb = A(NCH, F32, "nbg")
                TS(P, "pool", nb, bc.re("p (c t) -> p c t", c=NCH)[:, :, 127], 1.0 / 16, ALU.mult)
                for c in range(NCH):
                    ACT(P, Ee[:, c * 128:(c + 1) * 128], bc[:, c * 128:(c + 1) * 128], AF.Exp,
                        bias=nb[:, c:c + 1], scale=-1.0 / 16)
                gCg = A(NCH, F32, "gCg")
                CP(P, "pool", gCg, Eq.re("p (c t) -> p c t", c=NCH)[:, :, 127])
                yield
                qin = A(T, BF16, "qin")
                STT(P, qin, q, 32.0 ** -0.5, Eq, ALU.mult, ALU.mult)
                kin = [A(T, BF16, f"kin{i}") for i in range(2)]
                for i in range(2):
                    STT(P, kin[i], kg, par[:, i:i + 1], Ek, ALU.mult, ALU.mult)
                kend = A(T, BF16, "kend")
                TT(P, "pool", kend, kg, Ee, ALU.mult)
                yield
                GT = []
                for c in range(NCH):
                    pst = P.psum()
                    pstb = pst.bitcast(BF16)
                    cs = slice(c * 128, (c + 1) * 128)
                    TR(P, pstb[:, 0:128], vg[0][:, cs], identb)
                    TR(P, pstb[:, 128:256], vg[1][:, cs], identb)
                    TR(P, pstb[:, 256:384], kend[:, cs], identb)
                    gt_ = A(384, BF16, f"GT{c}")
                    CP(P, "act", gt_, pstb[:, 0:384])
                    GT.append(gt_)
                    yield
                ST = []
                for h in range(4):
                    rows = slice(64 * (h // 2), 64 * (h // 2) + 64)
                    ps = P.psum()
                    for c in range(NCH):
                        cs = slice(c * 128, (c + 1) * 128)
                        MM(P, ps[:, cs], kin[h % 2][rows, cs], qin[rows, cs])
                    st_ = A(T, BF16, f"ST{h}")
                    TT(P, "dve", st_, ps[:, 0:T], ui4[:, 0:T], ALU.mult)
                    ST.append(st_)
                    yield
                Sb = []
                Scur = A((NCH + 1) * 256, F32, "Scur")
                CP(P, "pool", Scur[:, 0:256], gla_S)
                for c in range(NCH):
                    sb_ = A(256, BF16, f"Sb{c}")
                    TT(P, "pool", sb_, Scur[:, c * 256:(c + 1) * 256], bmask, ALU.mult)
                    Sb.append(sb_)
                    ps = P.psum()
                    MM(P, ps[:, 0:256], GT[c][:, 256:384], GT[c][:, 0:256])
                    STT(P, Scur[:, (c + 1) * 256:(c + 2) * 256], Scur[:, c * 256:(c + 1) * 256], gCg[:, c:c + 1],
                        ps[:, 0:256], ALU.mult, ALU.add)
                    yield
                CP(P, "pool", gla_S, Scur[:, NCH * 256:(NCH + 1) * 256])
                o = A(T, F32, "o")
                ob = A(T, BF16, "ob")
                rs = A(T, F32, "rsg")
                for vp in range(2):
                    ps_in, ps_it = P.psum(), P.psum()
                    for s in range(2):
                        h = 2 * vp + s
                        orow = slice(64 * s, 64 * s + 64)
                        krow = slice(64 * vp, 64 * vp + 64)
                        for c in range(NCH):
                            cs = slice(c * 128, (c + 1) * 128)
                            MM(P, ps_in[orow, cs], GT[c][:, 64 * h:64 * h + 64], ST[h][:, cs])
                            MM(P, ps_it[orow, cs], Sb[c][krow, 64 * h:64 * h + 64], qin[krow, cs])
                    CP(P, "act", o, ps_it[:, 0:T])
                    TT(P, "dve", o, o, ps_in[:, 0:T], ALU.add)
                    dbg(f"gla_o{vp}", o, 128, T, tok0, ntok)
                    yield
                    TT(P, "pool", ob, o, o, ALU.mult)
                    ps_m = P.psum()
                    MM(P, ps_m[:, 0:T], ones64b, ob)
                    rsqrt_act(rs, ps_m[:, 0:T], 1.0 / 64, epsc[:, 0:1])
                    STT(P, o, o, col("gng"), rs, ALU.mult, ALU.mult)
                    TT(P, "dve", ycat3[:, 6 + vp, :], o, gg[vp], ALU.mult)
                    yield

            threads = [rw_thread(), lru_thread(), gla_thread()]
            atB = []
            while threads:
                for th in list(threads):
                    try:
                        v_ = next(th)
                    except StopIteration:
                        threads.remove(th)
                        continue
                    if v_ == "B":
                        threads.remove(th)
                        atB.append(th)
            threads = atB
            while threads:
                for th in list(threads):
                    try:
                        next(th)
                    except StopIteration:
                        threads.remove(th)
            P.arena_off = base_thr

            for kc in range(8):
                dbg(f"ycat{kc}", ycat3[:, kc, :], 128, T, tok0, ntok)

            for b in range(NCH):
                for hf in range(2):
                    ps = P.psum()
                    for kc in range(8):
                        MM(P, ps, ycat3[:, kc, b * 128:(b + 1) * 128],
                           w_out_sb[:, kc * D + hf * 512: kc * D + (hf + 1) * 512], start=(kc == 0), stop=(kc == 7))
                    TT(P, "dve", xT[b][:, hf * 512:(hf + 1) * 512], xT[b][:, hf * 512:(hf + 1) * 512], ps, ALU.add)
                P.dma("sp", xa_d[tok0 + b * 128: tok0 + (b + 1) * 128, :], xT[b].ap, reads=[xT[b]],
                      writes=[xa_buf])

        P.barrier()
        P.arena_off = persist_mark
        if last:
            gBf = P.alloc(D, F32, "gBf")
            P.dma("sp", gBf.ap, final_g.partition_broadcast(128), writes=[gBf])
        gB2 = P.alloc(D, F32, "gB2")
        P.dma("sp", gB2.ap, norm2_g[l:l + 1, :].partition_broadcast(128), writes=[gB2])
        wg_sb = P.alloc(8 * DFF, BF16, "wg")
        wu_sb = P.alloc(8 * DFF, BF16, "wu")
        wd_sb = P.alloc(NFC * D, BF16, "wd")
        for kc in range(8):
            for hf in range(2):
                sl_ = slice(hf * 1408, (hf + 1) * 1408)
                P.dma("pool", wg_sb.ap[:, kc * DFF + hf * 1408: kc * DFF + (hf + 1) * 1408],
                      w_gate[l, kc * 128:(kc + 1) * 128, sl_], writes=[wg_sb])
                P.dma("pool", wu_sb.ap[:, kc * DFF + hf * 1408: kc * DFF + (hf + 1) * 1408],
                      w_up[l, kc * 128:(kc + 1) * 128, sl_], writes=[wu_sb])
        for fc in range(NFC):
            P.dma("pool", wd_sb.ap[:, fc * D:(fc + 1) * D], w_down[l, fc * 128:(fc + 1) * 128, :], writes=[wd_sb])
        tile_mark = P.arena_off
        for ti in range(NT):
            P.arena_off = tile_mark
            tok0 = ti * T
            xT = [P.alloc(D, F32, f"fxT{b}") for b in range(NCH)]
            hnF = P.alloc(8 * T, BF16, "fhnF")
            hnF3 = hnF.re("p (k t) -> p k t", k=8)
            hF = P.alloc(NFC * T, BF16, "hF")
            hF3 = hF.re("p (k t) -> p k t", k=NFC)
            for b in range(NCH):
                P.dma("sp", xT[b].ap, xa_d[tok0 + b * 128: tok0 + (b + 1) * 128, :], reads=[xa_buf], writes=[xT[b]])
            m0 = P.arena_off
            for b in range(NCH):
                P.arena_off = m0
                hn = P.alloc(D, BF16, "fhn")
                rms_block(xT[b], gB2, hn)
                pst = P.psum()
                pstb = pst.bitcast(BF16)
                for kc in range(8):
                    TR(P, pstb[:, kc * 128:(kc + 1) * 128], hn[:, kc * 128:(kc + 1) * 128], identb)
                CP(P, "act", hnF3[:, :, b * 128:(b + 1) * 128], pstb.re("p (k t) -> p k t", k=8))
            P.arena_off = m0
            sgt = [P.alloc(T, F32, f"sgt{i}") for i in range(2)]
            for fc in range(NFC):
                ps = P.psum()
                for kc in range(8):
                    MM(P, ps[:, 0:T], wg_sb[:, kc * DFF + fc * 128: kc * DFF + (fc + 1) * 128], hnF3[:, kc, :],
                       start=(kc == 0), stop=(kc == 7))
                for kc in range(8):
                    MM(P, ps[:, T:2 * T], wu_sb[:, kc * DFF + fc * 128: kc * DFF + (fc + 1) * 128], hnF3[:, kc, :],
                       start=(kc == 0), stop=(kc == 7))
                s_ = sgt[fc % 2]
                ACT(P, s_, ps[:, 0:T], AF.Silu)
                TT(P, "dve", hF3[:, fc, :], s_, ps[:, T:2 * T], ALU.mult)
            for b in range(NCH):
                for hf in range(2):
                    ps = P.psum()
                    for fc in range(NFC):
                        MM(P, ps, hF3[:, fc, b * 128:(b + 1) * 128],
                           wd_sb[:, fc * D + hf * 512: fc * D + (hf + 1) * 512], start=(fc == 0), stop=(fc == NFC - 1))
                    TT(P, "dve", xT[b][:, hf * 512:(hf + 1) * 512], xT[b][:, hf * 512:(hf + 1) * 512], ps, ALU.add)
                if last:
                    yo = P.alloc(D, F32, "yo")
                    rms_block(xT[b], gBf, yo)
                    fin.append(P.dma("sp", out_d[tok0 + b * 128: tok0 + (b + 1) * 128, :], yo.ap, reads=[yo]))
                else:
                    P.dma("sp", xb_d[tok0 + b * 128: tok0 + (b + 1) * 128, :], xT[b].ap, reads=[xT[b]],
                          writes=[xb_buf])
    P.finish(fin)
    P.build()
    return nc, list(dbg_out.keys())


def make_in_map(inputs, xs, nlayers):
    m = {"x": np.ascontiguousarray(xs, dtype=np.float32)}
    for k in ["w_in", "w_out", "ffn_w_gate", "ffn_w_up", "ffn_w_down", "rw_w_up", "rw_a_up", "rw_g_up",
              "lru_wa", "lru_wx", "gla_gk_up", "norm1_g", "norm2_g"]:
        m[k] = np.ascontiguousarray(np.asarray(inputs[k], np.float32)[:nlayers])
    m["cols"] = np.stack([pack_cols(inputs, l) for l in range(nlayers)])
    m["final_norm_g"] = np.asarray(inputs["final_norm_g"], np.float32).reshape(1, D)
    for k, v in make_consts().items():
        m["c_" + k] = v
    return m


_CACHE = {}


def kernel(**inputs):
    x = np.asarray(inputs["x"], np.float32)
    B, S, _ = x.shape
    L = np.asarray(inputs["w_in"]).shape[0]
    key = (S, L)
    if key not in _CACHE:
        _CACHE[key] = build_program(S, L)[0]
    nc = _CACHE[key]
    in_maps = [make_in_map(inputs, x[c % B], L) for c in range(8)]
    res = run_bass_kernel_spmd(nc, in_maps, core_ids=list(range(8)))
    return np.stack([res.results[b]["out"] for b in range(B)], axis=0)
```

```python
import contextlib
import math
import os
import numpy as np
import concourse.bass as bass
import concourse.mybir as mybir
from concourse.bass_utils import run_bass_kernel_spmd

F32 = mybir.dt.float32
BF16 = mybir.dt.bfloat16
AF = mybir.ActivationFunctionType
ALU = mybir.AluOpType

CHUNK = 8000
SAMEQ = os.environ.get('K_SAMEQ', '1') == '1'
NDMASEM = 12

D = 1024
PIN = 2832
DFF = 2816
NFC = DFF // 128
EPS = 1e-6
RW_EPS = 64e-5
DEC = math.exp(-0.5)
T = 256
NCH = T // 128


class Buf:
    __slots__ = ("name", "writers", "readers")

    def __init__(self, name=""):
        self.name = name
        self.writers = {}
        self.readers = {}


def _dep_kv(d):
    if d[0] == "e":
        return ("e", d[1], d[2] // CHUNK), d[2] % CHUNK + 1
    return ("d", d[1], d[2]), d[3]


def _merge(dst, src):
    for k, v in src.items():
        if dst.get(k, 0) < v:
            dst[k] = v


class Tile:
    __slots__ = ("ap", "buf")

    def __init__(self, ap, buf=None):
        self.ap = ap
        self.buf = buf if buf is not None else Buf()

    def __getitem__(self, k):
        return Tile(self.ap[k], self.buf)

    def bitcast(self, dt):
        return Tile(self.ap.bitcast(dt), self.buf)

    def re(self, s, **kw):
        return Tile(self.ap.rearrange(s, **kw), self.buf)


class Prog:
    ENG = ("pe", "act", "dve", "pool", "sp")

    def __init__(self, nc):
        self.nc = nc
        self.stack = contextlib.ExitStack()
        self.streams = {e: [] for e in self.ENG}
        self.count = {e: 0 for e in self.ENG}
        self.esems = {e: [] for e in self.ENG}
        self.dsems = {}
        self.dma_n = {e: 0 for e in self.ENG}
        self.dma_hist = {e: {} for e in self.ENG}
        self.waited = {e: {} for e in self.ENG}
        self.n_t = 0
        self.final_deps = []
        self.arena = None
        self.arena_off = 0
        self.arena_size = 0
        self.psb = []
        self.ps_i = 0
        self.live = []

    def init_mem(self, arena_f32_cols):
        self.arena_size = arena_f32_cols
        self.arena = self.stack.enter_context(
            self.nc.sbuf_tensor("arena", [128, arena_f32_cols], F32))
        for i in range(8):
            t = self.stack.enter_context(self.nc.psum_tensor(f"psb{i}", [128, 512], F32))
            self.psb.append(Tile(t[:, :], Buf(f"ps{i}")))

    def alloc(self, free_elems, dt=F32, name=""):
        ncol = free_elems if dt == F32 else (free_elems + 1) // 2
        if self.arena_off + ncol > self.arena_size:
            raise RuntimeError(f"arena overflow at {name}: {self.arena_off}+{ncol}>{self.arena_size}")
        s0, s1 = self.arena_off, self.arena_off + ncol
        self.arena_off += ncol
        self.hi = max(getattr(self, "hi", 0), s1)
        keep, over = [], []
        for ent in self.live:
            (over if (ent[0] < s1 and s0 < ent[1]) else keep).append(ent)
        if len(over) == 1 and over[0][0] == s0 and over[0][1] == s1 and over[0][2] == (dt, free_elems):
            return over[0][3]
        ap = self.arena[:, s0:s1]
        if dt != F32:
            ap = ap.bitcast(dt)[:, 0:free_elems]
        buf = Buf(name)
        for ent in over:
            _merge(buf.writers, ent[3].buf.writers)
            _merge(buf.readers, ent[3].buf.readers)
        t = Tile(ap, buf)
        keep.append((s0, s1, (dt, free_elems), t))
        self.live = keep
        return t

    def psum(self):
        t = self.psb[self.ps_i]
        self.ps_i = (self.ps_i + 1) % 8
        return t

    def _deps(self, e, reads, writes, is_dma=False):
        need = {}
        for r in reads:
            _merge(need, r.writers)
        for w in writes:
            _merge(need, w.writers)
            _merge(need, w.readers)
        out = []
        for key, val in need.items():
            if key[0] == "e" and key[1] == e and (e == "pe" or not SAMEQ):
                continue
            if self.waited[e].get(key, 0) >= val:
                continue
            self.waited[e][key] = val
            out.append((key, val))
        return out

    def _record(self, d, reads, writes, is_dma):
        k, v = _dep_kv(d)
        for w in writes:
            if is_dma:
                w.writers = {kk: vv for kk, vv in w.writers.items() if kk[0] == "d"}
            else:
                w.writers = {}
            w.writers[k] = v
            w.readers = {}
        for r in reads:
            if r.readers.get(k, 0) < v:
                r.readers[k] = v

    def _sem(self, key):
        if key[0] == "e":
            return self.esems[key[1]][key[2]]
        return self.dsems[(key[1], key[2])]

    def op(self, e, fn, reads=(), writes=()):
        reads = [r.buf if isinstance(r, Tile) else r for r in reads]
        writes = [w.buf if isinstance(w, Tile) else w for w in writes]
        waits = self._deps(e, reads, writes)
        idx = self.count[e]
        self.count[e] += 1
        mykey = ("e", e, idx // CHUNK)

        def emit(eng, waits=waits, fn=fn, mykey=mykey):
            for k, v in waits:
                eng.wait_ge(self._sem(k), v)
            fn(eng).then_inc(self._sem(mykey), 1)

        self.streams[e].append(emit)
        d = ("e", e, idx)
        self._record(d, reads, writes, False)
        return d

    def dma(self, q, out, in_, reads=(), writes=(), **kw):
        reads = [r.buf if isinstance(r, Tile) else r for r in reads]
        writes = [w.buf if isinstance(w, Tile) else w for w in writes]
        waits = self._deps(q, reads, writes, True)
        n = self.dma_n[q]
        self.dma_n[q] += 1
        slot = n % NDMASEM
        prev = self.dma_hist[q].get(slot, 0)
        val = prev + 16
        self.dma_hist[q][slot] = val
        key = ("d", q, slot)
        if prev > 0 and self.waited[q].get(key, 0) < prev:
            waits = waits + [(key, prev)]
            self.waited[q][key] = prev

        def emit(eng, waits=waits, key=key):
            for k, v in waits:
                eng.wait_ge(self._sem(k), v)
            eng.dma_start(out=out, in_=in_, **kw).then_inc(self._sem(key), 16)

        self.streams[q].append(emit)
        d = ("d", q, slot, val)
        self._record(d, reads, writes, True)
        return d

    def barrier(self):
        keys = []
        for e in self.ENG:
            if self.count[e] > 0:
                idx = self.count[e] - 1
                keys.append((("e", e, idx // CHUNK), idx % CHUNK + 1))
            for slot, val in self.dma_hist[e].items():
                keys.append((("d", e, slot), val))
        for f in self.ENG:
            mine = []
            for k, v in keys:
                if k[0] == "e" and k[1] == f:
                    continue
                if self.waited[f].get(k, 0) >= v:
                    continue
                self.waited[f][k] = v
                mine.append((k, v))

            def emit(eng, mine=mine):
                for k, v in mine:
                    eng.wait_ge(self._sem(k), v)

            self.streams[f].append(emit)

    def finish(self, deps):
        self.final_deps = list(deps)

    def build(self):
        nc = self.nc
        st = self.stack
        for e in self.ENG:
            nsem = (self.count[e] + CHUNK - 1) // CHUNK
            self.esems[e] = [st.enter_context(nc.semaphore(f"s_{e}_{i}")) for i in range(nsem)]
            nd = min(self.dma_n[e], NDMASEM)
            for s in range(nd):
                self.dsems[(e, s)] = st.enter_context(nc.semaphore(f"d_{e}_{s}"))
        fin = []
        for d in self.final_deps:
            if d[0] == "e":
                fin.append((("e", d[1], d[2] // CHUNK), d[2] % CHUNK + 1))
            else:
                fin.append((("d", d[1], d[2]), d[3]))
        block = st.enter_context(nc.Block())
        streams = self.streams

        @block.tensor
        def _(eng):
            for f in streams["pe"]:
                f(eng)

        @block.scalar
        def _(eng):
            for f in streams["act"]:
                f(eng)

        @block.vector
        def _(eng):
            for f in streams["dve"]:
                f(eng)

        @block.gpsimd
        def _(eng):
            for f in streams["pool"]:
                f(eng)

        @block.sync
        def _(eng):
            for f in streams["sp"]:
                f(eng)
            for k, v in fin:
                eng.wait_ge(self._sem(k), v)

        st.close()


def _ap(x):
    return x.ap if isinstance(x, Tile) else x


def _tl(*xs):
    return [x for x in xs if isinstance(x, Tile)]


def ACT(P, out, in_, func, bias=None, scale=None, accum=None):
    kw = {}
    if bias is not None:
        kw["bias"] = _ap(bias)
    if scale is not None:
        kw["scale"] = _ap(scale)
    if accum is not None:
        kw["accum_out"] = _ap(accum)
    P.op("act", lambda e: e.activation(out=out.ap, in_=in_.ap, func=func, **kw),
         reads=_tl(in_, bias, scale), writes=_tl(out, accum))


def TT(P, eng, out, a, b, op):
    P.op(eng, lambda e: e.tensor_tensor(out=out.ap, in0=a.ap, in1=b.ap, op=op),
         reads=_tl(a, b), writes=[out])


def TS(P, eng, out, a, s1, op0, s2=None, op1=None):
    if op1 is None:
        P.op(eng, lambda e: e.tensor_scalar(out=out.ap, in0=a.ap, scalar1=_ap(s1), scalar2=None, op0=op0),
             reads=_tl(a, s1), writes=[out])
    else:
        P.op(eng, lambda e: e.tensor_scalar(out=out.ap, in0=a.ap, scalar1=_ap(s1), scalar2=_ap(s2),
                                            op0=op0, op1=op1),
             reads=_tl(a, s1, s2), writes=[out])


def STT(P, out, in0, scalar, in1, op0, op1):
    P.op("dve", lambda e: e.scalar_tensor_tensor(out=out.ap, in0=in0.ap, scalar=_ap(scalar), in1=in1.ap,
                                                 op0=op0, op1=op1),
         reads=_tl(in0, scalar, in1), writes=[out])


def CP(P, eng, out, in_):
    if eng == "act":
        P.op("act", lambda e: e.activation(out=out.ap, in_=in_.ap, func=AF.Copy), reads=[in_], writes=[out])
    else:
        P.op(eng, lambda e: e.tensor_copy(out=out.ap, in_=in_.ap), reads=[in_], writes=[out])


def MM(P, out, lhsT, rhs, start=True, stop=True):
    P.op("pe", lambda e: e.matmul(out.ap, lhsT=lhsT.ap, rhs=rhs.ap, start=start, stop=stop),
         reads=[lhsT, rhs], writes=[out])


def TR(P, out, in_, ident):
    P.op("pe", lambda e: e.transpose(out=out.ap, in_=in_.ap, identity=ident.ap),
         reads=[in_, ident], writes=[out])


def SCAN(P, out, d0, d1, init):
    P.op("dve", lambda e: e.tensor_tensor_scan(out=out.ap, data0=d0.ap, data1=d1.ap, initial=_ap(init),
                                               op0=ALU.mult, op1=ALU.add),
         reads=_tl(d0, d1, init), writes=[out])


def MEMSET(P, eng, out, val):
    P.op(eng, lambda e: e.memset(out.ap, val), writes=[out])


COLS = {}


def _col_layout():
    names = [("mu", 8), ("w0", 2), ("a0", 2), ("k_k", 2), ("k_a", 2), ("r_k", 2), ("ln_g", 2), ("ln_b", 2),
             ("cw0", 4), ("cw1", 4), ("cw2", 4), ("cw3", 4), ("cb", 4), ("ba", 4), ("bx", 4), ("lam", 4),
             ("lng", 4), ("gkb", 1), ("gng", 1)]
    off = 0
    for n, c in names:
        COLS[n] = (off, c)
        off += c
    return off


NCOL = _col_layout()


def pack_cols(inp, l):
    out = np.zeros((128, NCOL), np.float32)

    def put(name, vec):
        o, c = COLS[name]
        out[:, o:o + c] = np.asarray(vec, np.float32).reshape(c, 128).T

    put("mu", inp["rw_mu"][l])
    put("w0", inp["rw_w0"][l])
    put("a0", inp["rw_a0"][l])
    put("k_k", inp["rw_k_k"][l])
    put("k_a", inp["rw_k_a"][l])
    put("r_k", inp["rw_r_k"][l].reshape(-1))
    put("ln_g", inp["rw_ln_g"][l])
    put("ln_b", inp["rw_ln_b"][l])
    for j in range(4):
        put(f"cw{j}", inp["lru_conv_w"][l, j])
    put("cb", inp["lru_conv_b"][l])
    put("ba", inp["lru_ba"][l])
    put("bx", inp["lru_bx"][l])
    put("lam", inp["lru_lam"][l])
    put("lng", inp["lru_norm_g"][l])
    put("gkb", inp["gla_gk_b"][l])
    put("gng", np.concatenate([inp["gla_norm_g"][l], inp["gla_norm_g"][l]]))
    return out


def make_consts():
    c = {}
    idx = np.arange(128)
    su = (idx[:, None] < idx[None, :]).astype(np.float32)
    ui = (idx[:, None] <= idx[None, :]).astype(np.float32)
    sl = (idx[:, None] > idx[None, :]).astype(np.float32)
    c["ident"] = np.eye(128, dtype=np.float32)
    c["cmask"] = np.concatenate([su, su, ui, ui], axis=1)
    c["sl4"] = np.concatenate([sl] * 4, axis=1)
    c["ui4"] = np.concatenate([ui] * NCH, axis=1)
    ob = np.zeros((128, 128), np.float32)
    ob[:64, :64] = 1
    ob[64:, 64:] = 1
    c["ones64"] = ob
    rm = np.ones((128, T), np.float32)
    rm[:, ::128] = 0
    c["rmask"] = rm
    bm = np.zeros((128, 256), np.float32)
    for h in range(4):
        bm[32 * h:32 * h + 32, 64 * h:64 * h + 64] = 1
    c["bmask"] = bm
    par = np.zeros((128, 2), np.float32)
    for h in range(4):
        par[32 * h:32 * h + 32, h % 2] = 1
    c["par"] = par
    return c


CONST_SHAPES = {"ident": 128, "cmask": 512, "sl4": 512, "ui4": T, "ones64": 128,
                "rmask": T, "bmask": 256, "par": 2}


def build_program(ntok, nlayers, debug=()):
    nc = bass.Bass("TRN2", target_bir_lowering=False)
    NT = ntok // T
    L = nlayers

    def din(name, shape):
        return nc.dram_tensor(name, list(shape), F32, kind="ExternalInput").ap()

    x_in = din("x", [ntok, D])
    w_in = din("w_in", [L, D, PIN])
    w_out = din("w_out", [L, D, D])
    w_gate = din("ffn_w_gate", [L, D, DFF])
    w_up = din("ffn_w_up", [L, D, DFF])
    w_down = din("ffn_w_down", [L, DFF, D])
    rw_w_up = din("rw_w_up", [L, 64, 256])
    rw_a_up = din("rw_a_up", [L, 64, 256])
    rw_g_up = din("rw_g_up", [L, 128, 256])
    lru_wa = din("lru_wa", [L, 8, 64, 64])
    lru_wx = din("lru_wx", [L, 8, 64, 64])
    gk_up = din("gla_gk_up", [L, 16, 128])
    cols_d = din("cols", [L, 128, NCOL])
    norm1_g = din("norm1_g", [L, D])
    norm2_g = din("norm2_g", [L, D])
    final_g = din("final_norm_g", [1, D])
    cdram = {k: din("c_" + k, [128, n]) for k, n in CONST_SHAPES.items()}
    out_d = nc.dram_tensor("out", [ntok, D], F32, kind="ExternalOutput").ap()
    xa_d = nc.dram_tensor("xa_scr", [ntok, D], F32).ap()
    xb_d = nc.dram_tensor("xb_scr", [ntok, D], F32).ap()
    xa_buf, xb_buf = Buf("xa"), Buf("xb")
    dbg_out = {}

    P = Prog(nc)
    P.init_mem(52500)
    fin = []

    def dbg(name, tile, rows, cols, tok0=None, total_cols=None):
        if name not in debug:
            return
        if name not in dbg_out:
            tc = total_cols if total_cols is not None else cols
            dbg_out[name] = nc.dram_tensor("dbg_" + name, [rows, tc], F32, kind="ExternalOutput").ap()
        dst = dbg_out[name]
        c0 = tok0 if tok0 is not None else 0
        fin.append(P.dma("pool", dst[0:rows, c0:c0 + cols], tile.ap, reads=[tile]))

    cst = {}
    for k, n in CONST_SHAPES.items():
        if k in ():
            cst[k] = P.alloc(n, BF16, "c_" + k)
            P.dma("pool", cst[k].ap, cdram[k][:, :], writes=[cst[k]])
        else:
            cst[k] = P.alloc(n, F32, "c_" + k)
            P.dma("sp", cst[k].ap, cdram[k][:, :], writes=[cst[k]])
    identf = cst["ident"]
    identb = P.alloc(128, BF16, "identb")
    CP(P, "pool", identb, identf)
    ident4b = P.alloc(512, BF16, "ident4b")
    for i in range(4):
        CP(P, "pool", ident4b[:, i * 128:(i + 1) * 128], identf)
    ones64b = P.alloc(128, BF16, "ones64b")
    CP(P, "pool", ones64b, cst["ones64"])
    cmask, sl4, ui4, rmask, bmask, par = (cst[k] for k in ("cmask", "sl4", "ui4", "rmask", "bmask", "par"))
    epsc = P.alloc(2, F32, "epsc")
    MEMSET(P, "pool", epsc[:, 0:1], EPS)
    MEMSET(P, "pool", epsc[:, 1:2], RW_EPS)

    rw_carry = P.alloc(8, F32, "rw_carry")
    lru_xc = [P.alloc(3, F32, f"lru_xc{j}") for j in range(4)]
    lru_h = P.alloc(4, F32, "lru_h")
    rw_H = [P.alloc(64, F32, f"rwH{hp}") for hp in range(2)]
    gla_S = P.alloc(256, F32, "glaS")
    persist_mark = P.arena_off

    def rms_block(xblk, gB, hn_out, sc=None):
        if sc is None:
            ssq = P.alloc(1, F32, "ssq")
            rstd = P.alloc(1, F32, "rstd")
        else:
            ssq, rstd = sc[:, 0:1], sc[:, 1:2]
        ACT(P, hn_out, xblk, AF.Square, accum=ssq)
        ACT(P, rstd, ssq, AF.Ln, bias=epsc[:, 0:1], scale=1.0 / D)
        ACT(P, rstd, rstd, AF.Exp, scale=-0.5)
        STT(P, hn_out, xblk, rstd[:, 0:1], gB, ALU.mult, ALU.mult)

    for l in range(L):
        src_d, src_buf = (x_in, None) if l == 0 else (xb_d, xb_buf)
        last = l == L - 1
        P.barrier()
        P.arena_off = persist_mark
        colsT = P.alloc(NCOL, F32, "cols")
        P.dma("sp", colsT.ap, cols_d[l], writes=[colsT])

        def col(name, j=0, rows=slice(0, 128)):
            o, c = COLS[name]
            return colsT[rows, o + j:o + j + 1]

        dcol = P.alloc(24, F32, "dcol")
        o_mu = COLS["mu"][0]
        omm = dcol[:, 0:8]
        TS(P, "pool", omm, colsT[:, o_mu:o_mu + 8], -1.0, ALU.mult, 1.0, ALU.add)
        o_ka = COLS["k_a"][0]
        omka = dcol[:, 8:10]
        TS(P, "pool", omka, colsT[:, o_ka:o_ka + 2], -1.0, ALU.mult, 1.0, ALU.add)
        o_lam = COLS["lam"][0]
        c1 = dcol[:, 10:14]
        c2 = dcol[:, 14:18]
        ACT(P, c1, colsT[:, o_lam:o_lam + 4], AF.Exp, scale=-1.0)
        ACT(P, c1, c1, AF.Ln, bias=1.0)
        TS(P, "pool", c2, c1, -16.0, ALU.mult)
        TS(P, "pool", c1, c1, -8.0, ALU.mult)
        MEMSET(P, "pool", rw_carry, 0.0)
        for j in range(4):
            MEMSET(P, "pool", lru_xc[j], 0.0)
        MEMSET(P, "pool", lru_h, 0.0)
        for hp in range(2):
            MEMSET(P, "pool", rw_H[hp], 0.0)
        MEMSET(P, "pool", gla_S, 0.0)

        gB1 = P.alloc(D, F32, "gB1")
        P.dma("sp", gB1.ap, norm1_g[l:l + 1, :].partition_broadcast(128), writes=[gB1])
        w_in_sb = P.alloc(8 * PIN, BF16, "w_in")
        for kc in range(8):
            for hf in range(2):
                P.dma("pool", w_in_sb.ap[:, kc * PIN + hf * 1416: kc * PIN + (hf + 1) * 1416],
                      w_in[l, kc * 128:(kc + 1) * 128, hf * 1416:(hf + 1) * 1416], writes=[w_in_sb])
        w_out_sb = P.alloc(8 * D, BF16, "w_out")
        for kc in range(8):
            P.dma("pool", w_out_sb.ap[:, kc * D:(kc + 1) * D], w_out[l, kc * 128:(kc + 1) * 128, :],
                  writes=[w_out_sb])
        wa_up = P.alloc(256, BF16, "wa_up")
        P.dma("pool", wa_up.ap[0:64, :], rw_w_up[l], writes=[wa_up])
        P.dma("pool", wa_up.ap[64:128, :], rw_a_up[l], writes=[wa_up])
        g_up = P.alloc(256, BF16, "g_up")
        P.dma("pool", g_up.ap, rw_g_up[l], writes=[g_up])
        gkup = P.alloc(128, BF16, "gkup")
        P.dma("pool", gkup.ap[0:16, :], gk_up[l], writes=[gkup])
        wabd = P.alloc(4 * 128, BF16, "wabd")
        wxbd = P.alloc(4 * 128, BF16, "wxbd")
        MEMSET(P, "pool", wabd, 0.0)
        MEMSET(P, "pool", wxbd, 0.0)
        for j in range(4):
            for s in range(2):
                P.dma("pool", wabd.ap[64 * s:64 * s + 64, j * 128 + 64 * s: j * 128 + 64 * s + 64],
                      lru_wa[l, 2 * j + s], writes=[wabd])
                P.dma("pool", wxbd.ap[64 * s:64 * s + 64, j * 128 + 64 * s: j * 128 + 64 * s + 64],
                      lru_wx[l, 2 * j + s], writes=[wxbd])
        mbbd = [[P.alloc(128, F32, f"mbbd{hp}{c}") for c in range(NCH)] for hp in range(2)]
        for hp in range(2):
            for c in range(NCH):
                MEMSET(P, "pool", mbbd[hp][c], 0.0)
        tile_mark = P.arena_off

        for ti in range(NT):
            P.arena_off = tile_mark
            tok0 = ti * T
            xT = [P.alloc(D, F32, f"xT{b}") for b in range(NCH)]
            hnF = P.alloc(8 * T, BF16, "hnF")
            hnF3 = hnF.re("p (k t) -> p k t", k=8)
            ycat = P.alloc(8 * T, BF16, "ycat")
            ycat3 = ycat.re("p (k t) -> p k t", k=8)
            for b in range(NCH):
                rd = [src_buf] if src_buf is not None else []
                P.dma("sp", xT[b].ap, src_d[tok0 + b * 128: tok0 + (b + 1) * 128, :], reads=rd, writes=[xT[b]])
            m0 = P.arena_off
            for b in range(NCH):
                P.arena_off = m0
                hn = P.alloc(D, BF16, "hn")
                rms_block(xT[b], gB1, hn)
                pst = P.psum()
                pstb = pst.bitcast(BF16)
                for kc in range(8):
                    TR(P, pstb[:, kc * 128:(kc + 1) * 128], hn[:, kc * 128:(kc + 1) * 128], identb)
                CP(P, "act", hnF3[:, :, b * 128:(b + 1) * 128], pstb.re("p (k t) -> p k t", k=8))
            P.arena_off = m0

            def proj(c0, ncols, evac):
                ps = P.psum()
                for kc in range(8):
                    MM(P, ps[0:ncols, 0:T], w_in_sb[:, kc * PIN + c0: kc * PIN + c0 + ncols], hnF3[:, kc, :],
                       start=(kc == 0), stop=(kc == 7))
                evac(ps[0:ncols, 0:T])


            base_thr = P.arena_off

            class Region:
                def __init__(self, start, size):
                    self.off = start
                    self.end = start + size

                def alloc(self, n, dt=F32, name=""):
                    save = P.arena_off
                    P.arena_off = self.off
                    t = P.alloc(n, dt, name)
                    self.off = P.arena_off
                    P.arena_off = save
                    if self.off > self.end:
                        raise RuntimeError(f"region overflow {name} {self.off}>{self.end}")
                    return t

            SZ_RW, SZ_LRU, SZ_GLA = 15000, 5000, 5800

            def rsqrt_act(out, in_, scale, bias):
                ACT(P, out, in_, AF.Ln, bias=bias, scale=scale)
                ACT(P, out, out, AF.Exp, scale=-0.5)

            def rw_thread():
                R = Region(base_thr, SZ_RW)
                A = R.alloc
                pm = {}
                ptmp = A(1 + T, F32, "ptmp")
                ltmp = A(T, F32, "ltmp")
                for gi, gname in enumerate(["r0", "r1", "k0", "k1", "v0", "v1", "wa", "glo"]):
                    pm[gname] = A(T, F32, "pm_" + gname)

                    def ev(ps, gi=gi, gname=gname):
                        CP(P, "act", ptmp[:, 1:1 + T], ps)
                        CP(P, "pool", ptmp[:, 0:1], rw_carry[:, gi:gi + 1])
                        TS(P, "dve", ltmp, ptmp[:, 0:T], col("mu", gi), ALU.mult)
                        STT(P, pm[gname], ptmp[:, 1:1 + T], omm[:, gi:gi + 1], ltmp, ALU.mult, ALU.add)
                        CP(P, "pool", rw_carry[:, gi:gi + 1], ptmp[:, T:T + 1])
                    proj(gi * 128, 128, ev)
                    yield
                wab = A(T, BF16, "wab")
                ACT(P, wab[0:64, :], pm["wa"][0:64, :], AF.Tanh)
                CP(P, "pool", wab[64:128, :], pm["wa"][64:128, :])
                sgl = A(T, BF16, "sgl")
                ACT(P, sgl, pm["glo"], AF.Sigmoid)
                yield
                sgs, avs, gSs = [], [], []
                for hp in range(2):
                    ps_w, ps_a, ps_g = P.psum(), P.psum(), P.psum()
                    MM(P, ps_w[:, 0:T], wa_up[0:64, hp * 128:(hp + 1) * 128], wab[0:64, :])
                    MM(P, ps_a[:, 0:T], wa_up[64:128, hp * 128:(hp + 1) * 128], wab[64:128, :])
                    MM(P, ps_g[:, 0:T], g_up[:, hp * 128:(hp + 1) * 128], sgl)
                    sg = A(T, F32, f"sg{hp}")
                    a = A(T, F32, f"a{hp}")
                    gS = A(T, F32, f"gS{hp}")
                    ACT(P, sg, ps_w[:, 0:T], AF.Sigmoid, bias=col("w0", hp))
                    ACT(P, a, ps_a[:, 0:T], AF.Sigmoid, bias=col("a0", hp))
                    CP(P, "dve", gS, ps_g[:, 0:T])
                    sgs.append(sg)
                    avs.append(a)
                    gSs.append(gS)
                    yield
                yield "B"
                subs = [rw_hp(hp, R, pm, sgs[hp], avs[hp], gSs[hp]) for hp in range(2)]
                SEQ_HP = os.environ.get("K_SEQHP", "1") == "1"
                while subs:
                    for sub in (list(subs[:1]) if SEQ_HP else list(subs)):
                        try:
                            next(sub)
                        except StopIteration:
                            subs.remove(sub)
                    yield

            def rw_hp(hp, R, pm, sg, a, gS):
                A = R.alloc
                if True:
                    r, k, v = pm[f"r{hp}"], pm[f"k{hp}"], pm[f"v{hp}"]
                    kk = A(T, F32, "kk")
                    sqk = A(T, BF16, "sqk")
                    TS(P, "dve", kk, k, col("k_k", hp), ALU.mult)
                    TT(P, "pool", sqk, kk, kk, ALU.mult)
                    ps_n = P.psum()
                    MM(P, ps_n[:, 0:T], ones64b, sqk)
                    rn = A(T, F32, "rn")
                    rsqrt_act(rn, ps_n[:, 0:T], 1.0, 1e-24)
                    yield
                    TT(P, "dve", kk, kk, rn, ALU.mult)
                    kmod = A(T, F32, "kmod")
                    TS(P, "pool", kmod, a, col("k_a", hp), ALU.mult, omka[:, hp:hp + 1], ALU.add)
                    TT(P, "dve", kmod, kmod, k, ALU.mult)
                    yield
                    bvec = A(T, F32, "bvec")
                    TT(P, "dve", bvec, kk, a, ALU.mult)
                    rk = rn
                    TT(P, "dve", rk, r, kmod, ALU.mult)
                    rkb = sqk
                    TS(P, "pool", rkb, rk, col("r_k", hp), ALU.mult)
                    ps_b = P.psum()
                    MM(P, ps_b[:, 0:T], ones64b, rkb)
                    bonus = A(T, F32, "bonus")
                    TT(P, "dve", bonus, ps_b[:, 0:T], v, ALU.mult)
                    yield
                    css = A(T, F32, "css")
                    SCAN(P, css, rmask, sg, 0.0)
                    cse = A(T, F32, "cse")
                    TT(P, "pool", cse, css, sg, ALU.subtract)
                    E1 = A(T, F32, "E1")
                    E0 = cse
                    Einv = A(T, F32, "Einv")
                    Eend = sg
                    nb = A(NCH, F32, "nb")
                    TS(P, "pool", nb, css.re("p (c t) -> p c t", c=NCH)[:, :, 127], -DEC, ALU.mult)
                    ACT(P, E1, css, AF.Exp, scale=-DEC)
                    ACT(P, Einv, css, AF.Exp, scale=DEC)
                    yield
                    for c in range(NCH):
                        ACT(P, Eend[:, c * 128:(c + 1) * 128], css[:, c * 128:(c + 1) * 128], AF.Exp,
                            bias=nb[:, c:c + 1], scale=DEC)
                    ACT(P, E0, cse, AF.Exp, scale=-DEC)
                    gC = A(NCH, F32, "gC")
                    CP(P, "pool", gC, E1.re("p (c t) -> p c t", c=NCH)[:, :, 127])
                    yield
                    Rt = A(T, BF16, "Rt")
                    At = A(T, BF16, "At")
                    Bt = A(T, BF16, "Bt")
                    Kt = A(T, BF16, "Kt")
                    Kh = A(T, BF16, "Kh")
                    Bh = A(T, BF16, "Bh")
                    vb = A(T, BF16, "vb")
                    TT(P, "dve", Rt, r, E1, ALU.mult)
                    STT(P, At, kk, -1.0, E0, ALU.mult, ALU.mult)
                    TT(P, "dve", Bt, bvec, Einv, ALU.mult)
                    TT(P, "dve", Kt, kmod, Einv, ALU.mult)
                    yield
                    TT(P, "dve", Kh, kmod, Eend, ALU.mult)
                    TT(P, "pool", Bh, bvec, Eend, ALU.mult)
                    CP(P, "pool", vb, v)
                    yield
                    TM = []
                    for c in range(NCH):
                        pst = P.psum()
                        pstb = pst.bitcast(BF16)
                        for i, src_ in enumerate([vb, Kh, Bh, At]):
                            TR(P, pstb[:, i * 128:(i + 1) * 128], src_[:, c * 128:(c + 1) * 128], identb)
                        tm = A(512, BF16, f"TM{c}")
                        CP(P, "act", tm, pstb[:, 0:512])
                        TM.append(tm)
                        yield
                    NJ = 2 * NCH
                    Aall = []
                    al = [t_.bitcast(BF16) for t_ in (kk, kmod, bvec, rn, css, cse, E1)]
                    P0, P0T = al[0], al[1]
                    for s in range(2):
                        rows = slice(64 * s, 64 * s + 64)
                        ps_p0 = P.psum()
                        for c in range(NCH):
                            j = s * NCH + c
                            cs = slice(c * 128, (c + 1) * 128)
                            ps = P.psum()
                            MM(P, ps[:, 0:128], Bt[rows, cs], At[rows, cs])
                            MM(P, ps[:, 128:256], Kt[rows, cs], At[rows, cs])
                            MM(P, ps[:, 256:384], Bt[rows, cs], Rt[rows, cs])
                            MM(P, ps[:, 384:512], Kt[rows, cs], Rt[rows, cs])
                            aa = A(512, BF16, f"Aall{j}")
                            TT(P, "dve", aa, ps, cmask, ALU.mult)
                            Aall.append(aa)
                            CP(P, "act", P0T[:, j * 128:(j + 1) * 128], aa[:, 0:128])
                            MM(P, ps_p0[:, c * 128:(c + 1) * 128], At[rows, cs], Bt[rows, cs])
                        TT(P, "dve", P0[:, s * T:(s + 1) * T], ps_p0[:, 0:T], sl4[:, 0:T], ALU.mult)
                        yield
                    G = al[2]
                    TT(P, "pool", G, P0T, ident4b[:, 0:NJ * 128], ALU.add)
                    Pk, PkT = P0, P0T
                    Pn = [al[3], al[4]]
                    PnT = [al[5], al[6]]
                    NLEV = 6
                    for lev in range(NLEV):
                        nP, nPT = Pn[lev % 2], PnT[lev % 2]
                        ps1 = P.psum()
                        for j in range(NJ):
                            js = slice(j * 128, (j + 1) * 128)
                            MM(P, ps1[:, js], PkT[:, js], Pk[:, js])
                        CP(P, "act", nP, ps1[:, 0:NJ * 128])
                        if lev < NLEV - 1:
                            ps2 = P.psum()
                            for j in range(NJ):
                                js = slice(j * 128, (j + 1) * 128)
                                MM(P, ps2[:, js], Pk[:, js], PkT[:, js])
                            CP(P, "dve", nPT, ps2[:, 0:NJ * 128])
                        yield
                        ps3 = P.psum()
                        for j in range(NJ):
                            js = slice(j * 128, (j + 1) * 128)
                            MM(P, ps3[:, js], nP[:, js], G[:, js])
                        TT(P, "dve", G, G, ps3[:, 0:NJ * 128], ALU.add)
                        Pk, PkT = nP, nPT
                        yield
                    XW = Einv.bitcast(BF16)
                    ps = P.psum()
                    for s in range(2):
                        for c in range(NCH):
                            j = s * NCH + c
                            MM(P, ps[:, j * 128:j * 128 + 64], Aall[j][:, 128:256], TM[c][:, 64 * s:64 * s + 64])
                            MM(P, ps[:, j * 128 + 64:j * 128 + 128], G[:, j * 128:(j + 1) * 128],
                               TM[c][:, 384 + 64 * s:384 + 64 * s + 64])
                    CP(P, "act", XW, ps[:, 0:NJ * 128])
                    yield
                    U0 = r.bitcast(BF16)[:, 0:NJ * 64]
                    ps = P.psum()
                    for j in range(NJ):
                        MM(P, ps[:, j * 64:(j + 1) * 64], G[:, j * 128:(j + 1) * 128], XW[:, j * 128:j * 128 + 64])
                    CP(P, "act", U0, ps[:, 0:NJ * 64])
                    yield
                    ps = P.psum()
                    for s in range(2):
                        orow = slice(64 * s, 64 * s + 64)
                        for c in range(NCH):
                            j = s * NCH + c
                            MM(P, ps[orow, c * 128:c * 128 + 64], XW[:, j * 128 + 64:j * 128 + 128],
                               TM[c][:, 256 + 64 * s:256 + 64 * s + 64])
                            MM(P, ps[orow, c * 128 + 64:c * 128 + 128], TM[c][:, 256 + 64 * s:256 + 64 * s + 64],
                               U0[:, j * 64:(j + 1) * 64], start=True, stop=False)
                            MM(P, ps[orow, c * 128 + 64:c * 128 + 128], TM[c][:, 128 + 64 * s:128 + 64 * s + 64],
                               TM[c][:, 64 * s:64 * s + 64], start=False, stop=True)
                    Nn = k[:, 0:NCH * 64]
                    for c in range(NCH):
                        CP(P, "act", mbbd[hp][c][0:64, 0:64], ps[0:64, c * 128:c * 128 + 64])
                        CP(P, "act", mbbd[hp][c][64:128, 64:128], ps[64:128, c * 128:c * 128 + 64])
                        CP(P, "dve", Nn[:, c * 64:(c + 1) * 64], ps[:, c * 128 + 64:c * 128 + 128])
                    yield
                    RhT = k[:, NCH * 64:NCH * 64 + T // 2].bitcast(BF16)
                    ps = P.psum()
                    for s in range(2):
                        orow = slice(64 * s, 64 * s + 64)
                        for c in range(NCH):
                            j = s * NCH + c
                            MM(P, ps[orow, c * 128:(c + 1) * 128], XW[:, j * 128 + 64:j * 128 + 128],
                               Aall[j][:, 256:384])
                    TT(P, "dve", RhT, ps[:, 0:T], Rt, ALU.add)
                    yield
                    Y0 = v
                    ps = P.psum()
                    for s in range(2):
                        orow = slice(64 * s, 64 * s + 64)
                        for c in range(NCH):
                            j = s * NCH + c
                            MM(P, ps[orow, c * 128:(c + 1) * 128], U0[:, j * 64:(j + 1) * 64], Aall[j][:, 256:384],
                               start=True, stop=False)
                            MM(P, ps[orow, c * 128:(c + 1) * 128], TM[c][:, 64 * s:64 * s + 64],
                               Aall[j][:, 384:512], start=False, stop=True)
                    CP(P, "act", Y0, ps[:, 0:T])
                    yield
                    Hs = sg[:, 0:(NCH + 1) * 64]
                    Hb = r[:, 128:128 + NCH * 32].bitcast(BF16)
                    CP(P, "pool", Hs[:, 0:64], rw_H[hp])
                    for c in range(NCH):
                        CP(P, "pool", Hb[:, c * 64:(c + 1) * 64], Hs[:, c * 64:(c + 1) * 64])
                        ps = P.psum()
                        MM(P, ps[:, 0:64], mbbd[hp][c], Hs[:, c * 64:(c + 1) * 64], start=True, stop=False)
                        MM(P, ps[:, 0:64], identf, Nn[:, c * 64:(c + 1) * 64], start=False, stop=True)
                        STT(P, Hs[:, (c + 1) * 64:(c + 2) * 64], Hs[:, c * 64:(c + 1) * 64], gC[:, c:c + 1],
                            ps[:, 0:64], ALU.mult, ALU.add)
                        yield
                    CP(P, "pool", rw_H[hp], Hs[:, NCH * 64:(NCH + 1) * 64])
                    y = a
                    pse, pso = P.psum(), P.psum()
                    for c in range(NCH):
                        cs = slice(c * 128, (c + 1) * 128)
                        MM(P, pse[0:64, cs], Hb[0:64, c * 64:(c + 1) * 64], RhT[0:64, cs])
                        MM(P, pso[64:128, cs], Hb[64:128, c * 64:(c + 1) * 64], RhT[64:128, cs])
                    TT(P, "dve", y[0:64, :], pse[0:64, 0:T], Y0[0:64, :], ALU.add)
                    TT(P, "dve", y[64:128, :], pso[64:128, 0:T], Y0[64:128, :], ALU.add)
                    dbg(f"rw_y{hp}", y, 128, T, tok0, ntok)
                    yield
                    yb = A(T, BF16, "yb")
                    CP(P, "pool", yb, y)
                    ps_m = P.psum()
                    MM(P, ps_m[:, 0:T], ones64b, yb)
                    yc = A(T, F32, "yc")
                    STT(P, yc, ps_m[:, 0:T], -1.0 / 64, y, ALU.mult, ALU.add)
                    yield
                    TT(P, "pool", yb, yc, yc, ALU.mult)
                    ps_v = P.psum()
                    MM(P, ps_v[:, 0:T], ones64b, yb)
                    rs = y
                    rsqrt_act(rs, ps_v[:, 0:T], 1.0 / 64, epsc[:, 1:2])
                    yield
                    TT(P, "dve", yc, yc, rs, ALU.mult)
                    TS(P, "pool", yc, yc, col("ln_g", hp), ALU.mult, col("ln_b", hp), ALU.add)
                    TT(P, "dve", yc, yc, bonus, ALU.add)
                    TT(P, "dve", ycat3[:, hp, :], yc, gS, ALU.mult)
                    yield

            def lru_thread():
                R = Region(base_thr + SZ_RW, SZ_LRU)
                A = R.alloc
                xbuf = A(3 + T, F32, "xbuf")
                gt = A(T, F32, "gt")
                xc = A(T, F32, "xc")
                xcb = A(T, BF16, "xcb")
                st = []
                for j in range(4):
                    CP(P, "pool", xbuf[:, 0:3], lru_xc[j])
                    proj(1024 + j * 128, 128, lambda ps: CP(P, "act", xbuf[:, 3:3 + T], ps))
                    yield
                    proj(1536 + j * 128, 128, lambda ps: CP(P, "act", gt, ps))
                    CP(P, "pool", lru_xc[j], xbuf[:, T:T + 3])
                    yield
                    TS(P, "pool", xc, xbuf[:, 0:T], col("cw0", j), ALU.mult, col("cb", j), ALU.add)
                    for tap in range(1, 4):
                        STT(P, xc, xbuf[:, tap:tap + T], col(f"cw{tap}", j), xc, ALU.mult, ALU.add)
                    CP(P, "pool", xcb, xc)
                    yield
                    ps_r, ps_i = P.psum(), P.psum()
                    MM(P, ps_r[:, 0:T], wabd[:, j * 128:(j + 1) * 128], xcb)
                    MM(P, ps_i[:, 0:T], wxbd[:, j * 128:(j + 1) * 128], xcb)
                    gr = A(T, F32, f"gr{j}")
                    uu = A(T, F32, f"uu{j}")
                    ge = A(T, F32, f"ge{j}")
                    ACT(P, gr, ps_r[:, 0:T], AF.Sigmoid, bias=col("ba", j))
                    ACT(P, uu, ps_i[:, 0:T], AF.Sigmoid, bias=col("bx", j))
                    yield
                    TT(P, "dve", uu, uu, xc, ALU.mult)
                    TT(P, "pool", ge, gt, gt, ALU.mult)
                    TS(P, "pool", ge, ge, 0.044715, ALU.mult, 1.0, ALU.add)
                    TT(P, "dve", ge, ge, gt, ALU.mult)
                    ACT(P, ge, ge, AF.Sigmoid, scale=1.5957691216057308)
                    TT(P, "dve", ge, ge, gt, ALU.mult)
                    st.append((gr, uu, ge))
                    yield
                yield "B"
                av = A(T, F32, "av")
                a2 = A(T, F32, "a2")
                hh = A(T, F32, "hh")
                sqb = A(T, BF16, "sqb")
                for j in range(4):
                    gr, uu, ge = st[j]
                    ACT(P, av, gr, AF.Exp, scale=c1[:, j:j + 1])
                    ACT(P, a2, gr, AF.Exp, scale=c2[:, j:j + 1])
                    ACT(P, a2, a2, AF.Ln, bias=1.0, scale=-1.0)
                    ACT(P, a2, a2, AF.Exp, scale=0.5)
                    yield
                    TT(P, "dve", uu, uu, a2, ALU.mult)
                    SCAN(P, hh, av, uu, lru_h[:, j:j + 1])
                    CP(P, "pool", lru_h[:, j:j + 1], hh[:, T - 1:T])
                    yl = av
                    TT(P, "dve", yl, hh, ge, ALU.mult)
                    dbg(f"lru_y{j}", yl, 128, T, tok0, ntok)
                    yield
                    TT(P, "pool", sqb, yl, yl, ALU.mult)
                    ps_m = P.psum()
                    MM(P, ps_m[:, 0:T], ones64b, sqb)
                    rs = a2
                    rsqrt_act(rs, ps_m[:, 0:T], 1.0 / 64, epsc[:, 0:1])
                    STT(P, ycat3[:, 2 + j, :], yl, col("lng", j), rs, ALU.mult, ALU.mult)
                    yield

            def gla_thread():
                R = Region(base_thr + SZ_RW + SZ_LRU, SZ_GLA)
                A = R.alloc
                q = A(T, F32, "q")
                kg = A(T, F32, "kg")
                vg = [A(T, BF16, f"vg{i}") for i in range(2)]
                gg = [A(T, F32, f"gg{i}") for i in range(2)]
                gklo = A(T, BF16, "gklo")
                sgm = A(T, F32, "sgm")
                proj(2048, 128, lambda ps: CP(P, "act", q, ps))
                yield
                proj(2176, 128, lambda ps: CP(P, "act", kg, ps))
                yield
                proj(2304, 128, lambda ps: CP(P, "act", vg[0], ps))
                yield
                proj(2432, 128, lambda ps: CP(P, "act", vg[1], ps))
                yield
                proj(2560, 16, lambda ps: CP(P, "act", gklo[0:16, :], ps))
                yield
                for i in range(2):
                    def evg(ps, i=i):
                        ACT(P, sgm, ps, AF.Sigmoid)
                        TT(P, "dve", gg[i], ps, sgm, ALU.mult)
                    proj(2576 + 128 * i, 128, evg)
                    yield
                ps_gk = P.psum()
                MM(P, ps_gk[:, 0:T], gkup[0:16, :], gklo[0:16, :])
                la = A(T, F32, "la")
                ACT(P, la, ps_gk[:, 0:T], AF.Sigmoid, bias=col("gkb"))
                yield "B"
                ACT(P, la, la, AF.Ln)
                bc = A(T, F32, "bc")
                SCAN(P, bc, rmask, la, 0.0)
                Eq = A(T, F32, "Eq")
                Ek = A(T, F32, "Ek")
                Ee = la
                ACT(P, Eq, bc, AF.Exp, scale=1.0 / 16)
                ACT(P, Ek, bc, AF.Exp, scale=-1.0 / 16)
                yield
                nb = A(NCH, F32, "nbg")
                TS(P, "pool", nb, bc.re("p (c t) -> p c t", c=NCH)[:, :, 127], 1.0 / 16, ALU.mult)
                for c in range(NCH):
                    ACT(P, Ee[:, c * 128:(c + 1) * 128], bc[:, c * 128:(c + 1) * 128], AF.Exp,
                        bias=nb[:, c:c + 1], scale=-1.0 / 16)
                gCg = A(NCH, F32, "gCg")
                CP(P, "pool", gCg, Eq.re("p (c t) -> p c t", c=NCH)[:, :, 127])
                yield
                qin = A(T, BF16, "qin")
                STT(P, qin, q, 32.0 ** -0.5, Eq, ALU.mult, ALU.mult)
                kin = [A(T, BF16, f"kin{i}") for i in range(2)]
                for i in range(2):
                    STT(P, kin[i], kg, par[:, i:i + 1], Ek, ALU.mult, ALU.mult)
                kend = A(T, BF16, "kend")
                TT(P, "pool", kend, kg, Ee, ALU.mult)
                yield
                GT = []
                for c in range(NCH):
                    pst = P.psum()
                    pstb = pst.bitcast(BF16)
                    cs = slice(c * 128, (c + 1) * 128)
                    TR(P, pstb[:, 0:128], vg[0][:, cs], identb)
                    TR(P, pstb[:, 128:256], vg[1][:, cs], identb)
                    TR(P, pstb[:, 256:384], kend[:, cs], identb)
                    gt_ = A(384, BF16, f"GT{c}")
                    CP(P, "act", gt_, pstb[:, 0:384])
                    GT.append(gt_)
                    yield
                ST = []
                for h in range(4):
                    rows = slice(64 * (h // 2), 64 * (h // 2) + 64)
                    ps = P.psum()
                    for c in range(NCH):
                        cs = slice(c * 128, (c + 1) * 128)
                        MM(P, ps[:, cs], kin[h % 2][rows, cs], qin[rows, cs])
                    st_ = A(T, BF16, f"ST{h}")
                    TT(P, "dve", st_, ps[:, 0:T], ui4[:, 0:T], ALU.mult)
                    ST.append(st_)
                    yield
                Sb = []
                Scur = A((NCH + 1) * 256, F32, "Scur")
                CP(P, "pool", Scur[:, 0:256], gla_S)
                for c in range(NCH):
                    sb_ = A(256, BF16, f"Sb{c}")
                    TT(P, "pool", sb_, Scur[:, c * 256:(c + 1) * 256], bmask, ALU.mult)
                    Sb.append(sb_)
                    ps = P.psum()
                    MM(P, ps[:, 0:256], GT[c][:, 256:384], GT[c][:, 0:256])
                    STT(P, Scur[:, (c + 1) * 256:(c + 2) * 256], Scur[:, c * 256:(c + 1) * 256], gCg[:, c:c + 1],
                        ps[:, 0:256], ALU.mult, ALU.add)
                    yield
                CP(P, "pool", gla_S, Scur[:, NCH * 256:(NCH + 1) * 256])
                o = A(T, F32, "o")
                ob = A(T, BF16, "ob")
                rs = A(T, F32, "rsg")
                for vp in range(2):
                    ps_in, ps_it = P.psum(), P.psum()
                    for s in range(2):
                        h = 2 * vp + s
                        orow = slice(64 * s, 64 * s + 64)
                        krow = slice(64 * vp, 64 * vp + 64)
                        for c in range(NCH):
                            cs = slice(c * 128, (c + 1) * 128)
                            MM(P, ps_in[orow, cs], GT[c][:, 64 * h:64 * h + 64], ST[h][:, cs])
                            MM(P, ps_it[orow, cs], Sb[c][krow, 64 * h:64 * h + 64], qin[krow, cs])
                    CP(P, "act", o, ps_it[:, 0:T])
                    TT(P, "dve", o, o, ps_in[:, 0:T], ALU.add)
                    dbg(f"gla_o{vp}", o, 128, T, tok0, ntok)
                    yield
                    TT(P, "pool", ob, o, o, ALU.mult)
                    ps_m = P.psum()
                    MM(P, ps_m[:, 0:T], ones64b, ob)
                    rsqrt_act(rs, ps_m[:, 0:T], 1.0 / 64, epsc[:, 0:1])
                    STT(P, o, o, col("gng"), rs, ALU.mult, ALU.mult)
                    TT(P, "dve", ycat3[:, 6 + vp, :], o, gg[vp], ALU.mult)
                    yield

            threads = [rw_thread(), lru_thread(), gla_thread()]
            atB = []
            while threads:
                for th in list(threads):
                    try:
                        v_ = next(th)
                    except StopIteration:
                        threads.remove(th)
                        continue
                    if v_ == "B":
                        threads.remove(th)
                        atB.append(th)
            threads = atB
            while threads:
                for th in list(threads):
                    try:
                        next(th)
                    except StopIteration:
                        threads.remove(th)
            P.arena_off = base_thr

            for kc in range(8):
                dbg(f"ycat{kc}", ycat3[:, kc, :], 128, T, tok0, ntok)

            for b in range(NCH):
                for hf in range(2):
                    ps = P.psum()
                    for kc in range(8):
                        MM(P, ps, ycat3[:, kc, b * 128:(b + 1) * 128],
                           w_out_sb[:, kc * D + hf * 512: kc * D + (hf + 1) * 512], start=(kc == 0), stop=(kc == 7))
                    TT(P, "dve", xT[b][:, hf * 512:(hf + 1) * 512], xT[b][:, hf * 512:(hf + 1) * 512], ps, ALU.add)
                P.dma("sp", xa_d[tok0 + b * 128: tok0 + (b + 1) * 128, :], xT[b].ap, reads=[xT[b]],
                      writes=[xa_buf])

        P.barrier()
        P.arena_off = persist_mark
        if last:
            gBf = P.alloc(D, F32, "gBf")
            P.dma("sp", gBf.ap, final_g.partition_broadcast(128), writes=[gBf])
        gB2 = P.alloc(D, F32, "gB2")
        P.dma("sp", gB2.ap, norm2_g[l:l + 1, :].partition_broadcast(128), writes=[gB2])
        wg_sb = P.alloc(8 * DFF, BF16, "wg")
        wu_sb = P.alloc(8 * DFF, BF16, "wu")
        wd_sb = P.alloc(NFC * D, BF16, "wd")
        for kc in range(8):
            for hf in range(2):
                sl_ = slice(hf * 1408, (hf + 1) * 1408)
                P.dma("pool", wg_sb.ap[:, kc * DFF + hf * 1408: kc * DFF + (hf + 1) * 1408],
                      w_gate[l, kc * 128:(kc + 1) * 128, sl_], writes=[wg_sb])
                P.dma("pool", wu_sb.ap[:, kc * DFF + hf * 1408: kc * DFF + (hf + 1) * 1408],
                      w_up[l, kc * 128:(kc + 1) * 128, sl_], writes=[wu_sb])
        for fc in range(NFC):
            P.dma("pool", wd_sb.ap[:, fc * D:(fc + 1) * D], w_down[l, fc * 128:(fc + 1) * 128, :], writes=[wd_sb])
        xTs = [[P.alloc(D, F32, f"fxT{p_}{b}") for b in range(NCH)] for p_ in range(2)]
        hnFs = [P.alloc(8 * T, BF16, f"fhnF{p_}") for p_ in range(2)]
        hF = P.alloc(NFC * T, BF16, "hF")
        hF3 = hF.re("p (k t) -> p k t", k=NFC)
        fhn = P.alloc(D, BF16, "fhn")
        sgt = [P.alloc(T, F32, f"sgt{i}") for i in range(2)]
        scs = [P.alloc(2, F32, f"fsc{i}") for i in range(4)]
        yos = [P.alloc(D, F32, f"yo{i}") for i in range(2)] if last else []
        sci = [0]

        def nsc():
            sci[0] += 1
            return scs[sci[0] % 4]

        def f_pro(ti):
            tok0 = ti * T
            xT = xTs[ti % 2]
            hnF3 = hnFs[ti % 2].re("p (k t) -> p k t", k=8)
            for b in range(NCH):
                P.dma("sp", xT[b].ap, xa_d[tok0 + b * 128: tok0 + (b + 1) * 128, :], reads=[xa_buf], writes=[xT[b]])
            for b in range(NCH):
                rms_block(xT[b], gB2, fhn, nsc())
                pst = P.psum()
                pstb = pst.bitcast(BF16)
                for kc in range(8):
                    TR(P, pstb[:, kc * 128:(kc + 1) * 128], fhn[:, kc * 128:(kc + 1) * 128], identb)
                CP(P, "act", hnF3[:, :, b * 128:(b + 1) * 128], pstb.re("p (k t) -> p k t", k=8))

        def f_gateup(ti):
            hnF3 = hnFs[ti % 2].re("p (k t) -> p k t", k=8)
            for fc in range(NFC):
                ps = P.psum()
                for kc in range(8):
                    MM(P, ps[:, 0:T], wg_sb[:, kc * DFF + fc * 128: kc * DFF + (fc + 1) * 128], hnF3[:, kc, :],
                       start=(kc == 0), stop=(kc == 7))
                for kc in range(8):
                    MM(P, ps[:, T:2 * T], wu_sb[:, kc * DFF + fc * 128: kc * DFF + (fc + 1) * 128], hnF3[:, kc, :],
                       start=(kc == 0), stop=(kc == 7))
                s_ = sgt[fc % 2]
                ACT(P, s_, ps[:, 0:T], AF.Silu)
                TT(P, "dve", hF3[:, fc, :], s_, ps[:, T:2 * T], ALU.mult)

        def f_down(ti):
            tok0 = ti * T
            xT = xTs[ti % 2]
            for b in range(NCH):
                for hf in range(2):
                    ps = P.psum()
                    for fc in range(NFC):
                        MM(P, ps, hF3[:, fc, b * 128:(b + 1) * 128],
                           wd_sb[:, fc * D + hf * 512: fc * D + (hf + 1) * 512], start=(fc == 0), stop=(fc == NFC - 1))
                    TT(P, "dve", xT[b][:, hf * 512:(hf + 1) * 512], xT[b][:, hf * 512:(hf + 1) * 512], ps, ALU.add)
                if last:
                    yo = yos[b % 2]
                    rms_block(xT[b], gBf, yo, nsc())
                    fin.append(P.dma("sp", out_d[tok0 + b * 128: tok0 + (b + 1) * 128, :], yo.ap, reads=[yo]))
                else:
                    P.dma("sp", xb_d[tok0 + b * 128: tok0 + (b + 1) * 128, :], xT[b].ap, reads=[xT[b]],
                          writes=[xb_buf])

        f_pro(0)
        for ti in range(NT):
            f_gateup(ti)
            if ti + 1 < NT:
                f_pro(ti + 1)
            f_down(ti)
    P.finish(fin)
    P.build()
    return nc, list(dbg_out.keys())


def make_in_map(inputs, xs, nlayers):
    m = {"x": np.ascontiguousarray(xs, dtype=np.float32)}
    for k in ["w_in", "w_out", "ffn_w_gate", "ffn_w_up", "ffn_w_down", "rw_w_up", "rw_a_up", "rw_g_up",
              "lru_wa", "lru_wx", "gla_gk_up", "norm1_g", "norm2_g"]:
        m[k] = np.ascontiguousarray(np.asarray(inputs[k], np.float32)[:nlayers])
    m["cols"] = np.stack([pack_cols(inputs, l) for l in range(nlayers)])
    m["final_norm_g"] = np.asarray(inputs["final_norm_g"], np.float32).reshape(1, D)
    for k, v in make_consts().items():
        m["c_" + k] = v
    return m


_CACHE = {}


def kernel(**inputs):
    x = np.asarray(inputs["x"], np.float32)
    B, S, _ = x.shape
    L = np.asarray(inputs["w_in"]).shape[0]
    key = (S, L)
    if key not in _CACHE:
        _CACHE[key] = build_program(S, L)[0]
    nc = _CACHE[key]
    in_maps = [make_in_map(inputs, x[c % B], L) for c in range(8)]
    res = run_bass_kernel_spmd(nc, in_maps, core_ids=list(range(8)))
    return np.stack([res.results[b]["out"] for b in range(B)], axis=0)
```

```python
import contextlib
import math
import os
import numpy as np
import concourse.bass as bass
import concourse.mybir as mybir
from concourse.bass_utils import run_bass_kernel_spmd

F32 = mybir.dt.float32
BF16 = mybir.dt.bfloat16
AF = mybir.ActivationFunctionType
ALU = mybir.AluOpType

CHUNK = 8000
SAMEQ = os.environ.get('K_SAMEQ', '1') == '1'
NDMASEM = 12

D = 1024
PIN = 2832
DFF = 2816
NFC = DFF // 128
EPS = 1e-6
RW_EPS = 64e-5
DEC = math.exp(-0.5)
T = 256
NCH = T // 128


class Buf:
    __slots__ = ("name", "writers", "readers", "t")

    def __init__(self, name=""):
        self.name = name
        self.writers = {}
        self.readers = {}
        self.t = 0.0


def _dep_kv(d):
    if d[0] == "e":
        return ("e", d[1], d[2] // CHUNK), d[2] % CHUNK + 1
    return ("d", d[1], d[2]), d[3]


def _merge(dst, src):
    for k, v in src.items():
        if dst.get(k, 0) < v:
            dst[k] = v


class Tile:
    __slots__ = ("ap", "buf")

    def __init__(self, ap, buf=None):
        self.ap = ap
        self.buf = buf if buf is not None else Buf()

    def __getitem__(self, k):
        return Tile(self.ap[k], self.buf)

    def bitcast(self, dt):
        return Tile(self.ap.bitcast(dt), self.buf)

    def re(self, s, **kw):
        return Tile(self.ap.rearrange(s, **kw), self.buf)


class Prog:
    ENG = ("pe", "act", "dve", "pool", "sp")

    def __init__(self, nc):
        self.nc = nc
        self.stack = contextlib.ExitStack()
        self.streams = {e: [] for e in self.ENG}
        self.count = {e: 0 for e in self.ENG}
        self.esems = {e: [] for e in self.ENG}
        self.dsems = {}
        self.dma_n = {e: 0 for e in self.ENG}
        self.dma_hist = {e: {} for e in self.ENG}
        self.waited = {e: {} for e in self.ENG}
        self.n_t = 0
        self.final_deps = []
        self.arena = None
        self.arena_off = 0
        self.arena_size = 0
        self.psb = []
        self.ps_i = 0
        self.live = []
        self.t_eng = {e: 0.0 for e in self.ENG}
        self.step_t = 0.0

    def init_mem(self, arena_f32_cols):
        self.arena_size = arena_f32_cols
        self.arena = self.stack.enter_context(
            self.nc.sbuf_tensor("arena", [128, arena_f32_cols], F32))
        for i in range(8):
            t = self.stack.enter_context(self.nc.psum_tensor(f"psb{i}", [128, 512], F32))
            self.psb.append(Tile(t[:, :], Buf(f"ps{i}")))

    def alloc(self, free_elems, dt=F32, name=""):
        ncol = free_elems if dt == F32 else (free_elems + 1) // 2
        if self.arena_off + ncol > self.arena_size:
            raise RuntimeError(f"arena overflow at {name}: {self.arena_off}+{ncol}>{self.arena_size}")
        s0, s1 = self.arena_off, self.arena_off + ncol
        self.arena_off += ncol
        self.hi = max(getattr(self, "hi", 0), s1)
        keep, over = [], []
        for ent in self.live:
            (over if (ent[0] < s1 and s0 < ent[1]) else keep).append(ent)
        if len(over) == 1 and over[0][0] == s0 and over[0][1] == s1 and over[0][2] == (dt, free_elems):
            return over[0][3]
        ap = self.arena[:, s0:s1]
        if dt != F32:
            ap = ap.bitcast(dt)[:, 0:free_elems]
        buf = Buf(name)
        for ent in over:
            _merge(buf.writers, ent[3].buf.writers)
            _merge(buf.readers, ent[3].buf.readers)
        t = Tile(ap, buf)
        keep.append((s0, s1, (dt, free_elems), t))
        self.live = keep
        return t

    def psum(self):
        t = self.psb[self.ps_i]
        self.ps_i = (self.ps_i + 1) % 8
        return t

    def _deps(self, e, reads, writes, is_dma=False):
        need = {}
        for r in reads:
            _merge(need, r.writers)
        for w in writes:
            _merge(need, w.writers)
            _merge(need, w.readers)
        out = []
        for key, val in need.items():
            if key[0] == "e" and key[1] == e and (e == "pe" or not SAMEQ):
                continue
            if self.waited[e].get(key, 0) >= val:
                continue
            self.waited[e][key] = val
            out.append((key, val))
        return out

    def _record(self, d, reads, writes, is_dma):
        k, v = _dep_kv(d)
        for w in writes:
            if is_dma:
                w.writers = {kk: vv for kk, vv in w.writers.items() if kk[0] == "d"}
            else:
                w.writers = {}
            w.writers[k] = v
            w.readers = {}
        for r in reads:
            if r.readers.get(k, 0) < v:
                r.readers[k] = v

    def _sem(self, key):
        if key[0] == "e":
            return self.esems[key[1]][key[2]]
        return self.dsems[(key[1], key[2])]

    def _est(self, e, reads, writes, cost):
        t0 = self.t_eng[e]
        for b in reads:
            if b.t > t0:
                t0 = b.t
        for b in writes:
            if b.t > t0:
                t0 = b.t
        self.t_eng[e] = t0 + cost
        fin = t0 + cost + 0.25
        for b in writes:
            b.t = fin
        if fin > self.step_t:
            self.step_t = fin

    def op(self, e, fn, reads=(), writes=(), cost=None):
        if cost is None:
            n = 256
            for w in writes:
                if isinstance(w, Tile):
                    n = 1
                    for d_ in w.ap.shape[1:]:
                        n *= d_
                    break
            cost = {"act": 0.2 + n / 1150.0, "dve": 0.15 + n / 960.0, "pool": 0.2 + n / 480.0,
                    "pe": 0.05 + n / 2400.0}.get(e, 1.0)
        reads = [r.buf if isinstance(r, Tile) else r for r in reads]
        writes = [w.buf if isinstance(w, Tile) else w for w in writes]
        self._est(e, reads, writes, cost)
        waits = self._deps(e, reads, writes)
        idx = self.count[e]
        self.count[e] += 1
        mykey = ("e", e, idx // CHUNK)

        def emit(eng, waits=waits, fn=fn, mykey=mykey):
            for k, v in waits:
                eng.wait_ge(self._sem(k), v)
            fn(eng).then_inc(self._sem(mykey), 1)

        self.streams[e].append(emit)
        d = ("e", e, idx)
        self._record(d, reads, writes, False)
        return d

    def dma(self, q, out, in_, reads=(), writes=(), **kw):
        reads = [r.buf if isinstance(r, Tile) else r for r in reads]
        writes = [w.buf if isinstance(w, Tile) else w for w in writes]
        self._est(q, reads, writes, 2.0)
        self.t_eng[q] -= 1.9
        waits = self._deps(q, reads, writes, True)
        n = self.dma_n[q]
        self.dma_n[q] += 1
        slot = n % NDMASEM
        prev = self.dma_hist[q].get(slot, 0)
        val = prev + 16
        self.dma_hist[q][slot] = val
        key = ("d", q, slot)
        if prev > 0 and self.waited[q].get(key, 0) < prev:
            waits = waits + [(key, prev)]
            self.waited[q][key] = prev

        def emit(eng, waits=waits, key=key):
            for k, v in waits:
                eng.wait_ge(self._sem(k), v)
            eng.dma_start(out=out, in_=in_, **kw).then_inc(self._sem(key), 16)

        self.streams[q].append(emit)
        d = ("d", q, slot, val)
        self._record(d, reads, writes, True)
        return d

    def barrier(self):
        keys = []
        for e in self.ENG:
            if self.count[e] > 0:
                idx = self.count[e] - 1
                keys.append((("e", e, idx // CHUNK), idx % CHUNK + 1))
            for slot, val in self.dma_hist[e].items():
                keys.append((("d", e, slot), val))
        for f in self.ENG:
            mine = []
            for k, v in keys:
                if k[0] == "e" and k[1] == f:
                    continue
                if self.waited[f].get(k, 0) >= v:
                    continue
                self.waited[f][k] = v
                mine.append((k, v))

            def emit(eng, mine=mine):
                for k, v in mine:
                    eng.wait_ge(self._sem(k), v)

            self.streams[f].append(emit)

    def finish(self, deps):
        self.final_deps = list(deps)

    def build(self):
        nc = self.nc
        st = self.stack
        for e in self.ENG:
            nsem = (self.count[e] + CHUNK - 1) // CHUNK
            self.esems[e] = [st.enter_context(nc.semaphore(f"s_{e}_{i}")) for i in range(nsem)]
            nd = min(self.dma_n[e], NDMASEM)
            for s in range(nd):
                self.dsems[(e, s)] = st.enter_context(nc.semaphore(f"d_{e}_{s}"))
        fin = []
        for d in self.final_deps:
            if d[0] == "e":
                fin.append((("e", d[1], d[2] // CHUNK), d[2] % CHUNK + 1))
            else:
                fin.append((("d", d[1], d[2]), d[3]))
        block = st.enter_context(nc.Block())
        streams = self.streams

        @block.tensor
        def _(eng):
            for f in streams["pe"]:
                f(eng)

        @block.scalar
        def _(eng):
            for f in streams["act"]:
                f(eng)

        @block.vector
        def _(eng):
            for f in streams["dve"]:
                f(eng)

        @block.gpsimd
        def _(eng):
            for f in streams["pool"]:
                f(eng)

        @block.sync
        def _(eng):
            for f in streams["sp"]:
                f(eng)
            for k, v in fin:
                eng.wait_ge(self._sem(k), v)

        st.close()


def _ap(x):
    return x.ap if isinstance(x, Tile) else x


def _tl(*xs):
    return [x for x in xs if isinstance(x, Tile)]


def ACT(P, out, in_, func, bias=None, scale=None, accum=None):
    kw = {}
    if bias is not None:
        kw["bias"] = _ap(bias)
    if scale is not None:
        kw["scale"] = _ap(scale)
    if accum is not None:
        kw["accum_out"] = _ap(accum)
    P.op("act", lambda e: e.activation(out=out.ap, in_=in_.ap, func=func, **kw),
         reads=_tl(in_, bias, scale), writes=_tl(out, accum))


def TT(P, eng, out, a, b, op):
    P.op(eng, lambda e: e.tensor_tensor(out=out.ap, in0=a.ap, in1=b.ap, op=op),
         reads=_tl(a, b), writes=[out])


def TS(P, eng, out, a, s1, op0, s2=None, op1=None):
    if op1 is None:
        P.op(eng, lambda e: e.tensor_scalar(out=out.ap, in0=a.ap, scalar1=_ap(s1), scalar2=None, op0=op0),
             reads=_tl(a, s1), writes=[out])
    else:
        P.op(eng, lambda e: e.tensor_scalar(out=out.ap, in0=a.ap, scalar1=_ap(s1), scalar2=_ap(s2),
                                            op0=op0, op1=op1),
             reads=_tl(a, s1, s2), writes=[out])


def STT(P, out, in0, scalar, in1, op0, op1):
    P.op("dve", lambda e: e.scalar_tensor_tensor(out=out.ap, in0=in0.ap, scalar=_ap(scalar), in1=in1.ap,
                                                 op0=op0, op1=op1),
         reads=_tl(in0, scalar, in1), writes=[out])


def CP(P, eng, out, in_):
    if eng == "act":
        P.op("act", lambda e: e.activation(out=out.ap, in_=in_.ap, func=AF.Copy), reads=[in_], writes=[out])
    else:
        P.op(eng, lambda e: e.tensor_copy(out=out.ap, in_=in_.ap), reads=[in_], writes=[out])


def MM(P, out, lhsT, rhs, start=True, stop=True):
    P.op("pe", lambda e: e.matmul(out.ap, lhsT=lhsT.ap, rhs=rhs.ap, start=start, stop=stop),
         reads=[lhsT, rhs], writes=[out])


def TR(P, out, in_, ident):
    P.op("pe", lambda e: e.transpose(out=out.ap, in_=in_.ap, identity=ident.ap),
         reads=[in_, ident], writes=[out])


def SCAN(P, out, d0, d1, init):
    P.op("dve", lambda e: e.tensor_tensor_scan(out=out.ap, data0=d0.ap, data1=d1.ap, initial=_ap(init),
                                               op0=ALU.mult, op1=ALU.add),
         reads=_tl(d0, d1, init), writes=[out])


def MEMSET(P, eng, out, val):
    P.op(eng, lambda e: e.memset(out.ap, val), writes=[out])


COLS = {}


def _col_layout():
    names = [("mu", 8), ("w0", 2), ("a0", 2), ("k_k", 2), ("k_a", 2), ("r_k", 2), ("ln_g", 2), ("ln_b", 2),
             ("cw0", 4), ("cw1", 4), ("cw2", 4), ("cw3", 4), ("cb", 4), ("ba", 4), ("bx", 4), ("lam", 4),
             ("lng", 4), ("gkb", 1), ("gng", 1)]
    off = 0
    for n, c in names:
        COLS[n] = (off, c)
        off += c
    return off


NCOL = _col_layout()


def pack_cols(inp, l):
    out = np.zeros((128, NCOL), np.float32)

    def put(name, vec):
        o, c = COLS[name]
        out[:, o:o + c] = np.asarray(vec, np.float32).reshape(c, 128).T

    put("mu", inp["rw_mu"][l])
    put("w0", inp["rw_w0"][l])
    put("a0", inp["rw_a0"][l])
    put("k_k", inp["rw_k_k"][l])
    put("k_a", inp["rw_k_a"][l])
    put("r_k", inp["rw_r_k"][l].reshape(-1))
    put("ln_g", inp["rw_ln_g"][l])
    put("ln_b", inp["rw_ln_b"][l])
    for j in range(4):
        put(f"cw{j}", inp["lru_conv_w"][l, j])
    put("cb", inp["lru_conv_b"][l])
    put("ba", inp["lru_ba"][l])
    put("bx", inp["lru_bx"][l])
    put("lam", inp["lru_lam"][l])
    put("lng", inp["lru_norm_g"][l])
    put("gkb", inp["gla_gk_b"][l])
    put("gng", np.concatenate([inp["gla_norm_g"][l], inp["gla_norm_g"][l]]))
    return out


def make_consts():
    c = {}
    idx = np.arange(128)
    su = (idx[:, None] < idx[None, :]).astype(np.float32)
    ui = (idx[:, None] <= idx[None, :]).astype(np.float32)
    sl = (idx[:, None] > idx[None, :]).astype(np.float32)
    c["ident"] = np.eye(128, dtype=np.float32)
    c["cmask"] = np.concatenate([su, su, ui, ui], axis=1)
    c["sl4"] = np.concatenate([sl] * 4, axis=1)
    c["ui4"] = np.concatenate([ui] * NCH, axis=1)
    ob = np.zeros((128, 128), np.float32)
    ob[:64, :64] = 1
    ob[64:, 64:] = 1
    c["ones64"] = ob
    rm = np.ones((128, 2 * T), np.float32)
    rm[:, ::128] = 0
    c["rmask"] = rm
    bm = np.zeros((128, 256), np.float32)
    for h in range(4):
        bm[32 * h:32 * h + 32, 64 * h:64 * h + 64] = 1
    c["bmask"] = bm
    par = np.zeros((128, 2), np.float32)
    for h in range(4):
        par[32 * h:32 * h + 32, h % 2] = 1
    c["par"] = par
    return c


CONST_SHAPES = {"ident": 128, "cmask": 512, "sl4": 512, "ui4": T, "ones64": 128,
                "rmask": 2 * T, "bmask": 256, "par": 2}


def build_program(ntok, nlayers, debug=()):
    nc = bass.Bass("TRN2", target_bir_lowering=False)
    NT = ntok // T
    L = nlayers

    def din(name, shape):
        return nc.dram_tensor(name, list(shape), F32, kind="ExternalInput").ap()

    x_in = din("x", [ntok, D])
    w_in = din("w_in", [L, D, PIN])
    w_out = din("w_out", [L, D, D])
    w_gate = din("ffn_w_gate", [L, D, DFF])
    w_up = din("ffn_w_up", [L, D, DFF])
    w_down = din("ffn_w_down", [L, DFF, D])
    rw_w_up = din("rw_w_up", [L, 64, 256])
    rw_a_up = din("rw_a_up", [L, 64, 256])
    rw_g_up = din("rw_g_up", [L, 128, 256])
    lru_wa = din("lru_wa", [L, 8, 64, 64])
    lru_wx = din("lru_wx", [L, 8, 64, 64])
    gk_up = din("gla_gk_up", [L, 16, 128])
    cols_d = din("cols", [L, 128, NCOL])
    norm1_g = din("norm1_g", [L, D])
    norm2_g = din("norm2_g", [L, D])
    final_g = din("final_norm_g", [1, D])
    cdram = {k: din("c_" + k, [128, n]) for k, n in CONST_SHAPES.items()}
    out_d = nc.dram_tensor("out", [ntok, D], F32, kind="ExternalOutput").ap()
    xa_d = nc.dram_tensor("xa_scr", [ntok, D], F32).ap()
    xb_d = nc.dram_tensor("xb_scr", [ntok, D], F32).ap()
    xa_buf, xb_buf = Buf("xa"), Buf("xb")
    dbg_out = {}

    P = Prog(nc)
    P.init_mem(52500)
    fin = []

    def dbg(name, tile, rows, cols, tok0=None, total_cols=None):
        if name not in debug:
            return
        if name not in dbg_out:
            tc = total_cols if total_cols is not None else cols
            dbg_out[name] = nc.dram_tensor("dbg_" + name, [rows, tc], F32, kind="ExternalOutput").ap()
        dst = dbg_out[name]
        c0 = tok0 if tok0 is not None else 0
        fin.append(P.dma("pool", dst[0:rows, c0:c0 + cols], tile.ap, reads=[tile]))

    cst = {}
    for k, n in CONST_SHAPES.items():
        if k in ("rmask",):
            cst[k] = P.alloc(n, BF16, "c_" + k)
            P.dma("pool", cst[k].ap, cdram[k][:, :], writes=[cst[k]])
        else:
            cst[k] = P.alloc(n, F32, "c_" + k)
            P.dma("sp", cst[k].ap, cdram[k][:, :], writes=[cst[k]])
    identf = cst["ident"]
    identb = P.alloc(128, BF16, "identb")
    CP(P, "pool", identb, identf)
    ident4b = P.alloc(512, BF16, "ident4b")
    for i in range(4):
        CP(P, "pool", ident4b[:, i * 128:(i + 1) * 128], identf)
    ones64b = P.alloc(128, BF16, "ones64b")
    CP(P, "pool", ones64b, cst["ones64"])
    cmask, sl4, ui4, rmask, bmask, par = (cst[k] for k in ("cmask", "sl4", "ui4", "rmask", "bmask", "par"))
    epsc = P.alloc(2, F32, "epsc")
    MEMSET(P, "pool", epsc[:, 0:1], EPS)
    MEMSET(P, "pool", epsc[:, 1:2], RW_EPS)

    rw_carry = P.alloc(8, F32, "rw_carry")
    lru_xc = [P.alloc(3, F32, f"lru_xc{j}") for j in range(4)]
    lru_h = P.alloc(4, F32, "lru_h")
    rw_H = [P.alloc(64, F32, f"rwH{hp}") for hp in range(2)]
    gla_S = P.alloc(256, F32, "glaS")
    persist_mark = P.arena_off

    def rms_block(xblk, gB, hn_out, sc=None):
        if sc is None:
            ssq = P.alloc(1, F32, "ssq")
            rstd = P.alloc(1, F32, "rstd")
        else:
            ssq, rstd = sc[:, 0:1], sc[:, 1:2]
        ACT(P, hn_out, xblk, AF.Square, accum=ssq)
        ACT(P, rstd, ssq, AF.Ln, bias=epsc[:, 0:1], scale=1.0 / D)
        ACT(P, rstd, rstd, AF.Exp, scale=-0.5)
        STT(P, hn_out, xblk, rstd[:, 0:1], gB, ALU.mult, ALU.mult)

    for l in range(L):
        src_d, src_buf = (x_in, None) if l == 0 else (xb_d, xb_buf)
        last = l == L - 1
        P.barrier()
        P.arena_off = persist_mark
        colsT = P.alloc(NCOL, F32, "cols")
        P.dma("sp", colsT.ap, cols_d[l], writes=[colsT])

        def col(name, j=0, rows=slice(0, 128)):
            o, c = COLS[name]
            return colsT[rows, o + j:o + j + 1]

        dcol = P.alloc(24, F32, "dcol")
        o_mu = COLS["mu"][0]
        omm = dcol[:, 0:8]
        TS(P, "pool", omm, colsT[:, o_mu:o_mu + 8], -1.0, ALU.mult, 1.0, ALU.add)
        o_ka = COLS["k_a"][0]
        omka = dcol[:, 8:10]
        TS(P, "pool", omka, colsT[:, o_ka:o_ka + 2], -1.0, ALU.mult, 1.0, ALU.add)
        o_lam = COLS["lam"][0]
        c1 = dcol[:, 10:14]
        c2 = dcol[:, 14:18]
        ACT(P, c1, colsT[:, o_lam:o_lam + 4], AF.Exp, scale=-1.0)
        ACT(P, c1, c1, AF.Ln, bias=1.0)
        TS(P, "pool", c2, c1, -16.0, ALU.mult)
        TS(P, "pool", c1, c1, -8.0, ALU.mult)
        MEMSET(P, "pool", rw_carry, 0.0)
        for j in range(4):
            MEMSET(P, "pool", lru_xc[j], 0.0)
        MEMSET(P, "pool", lru_h, 0.0)
        for hp in range(2):
            MEMSET(P, "pool", rw_H[hp], 0.0)
        MEMSET(P, "pool", gla_S, 0.0)

        gB1 = P.alloc(D, F32, "gB1")
        P.dma("sp", gB1.ap, norm1_g[l:l + 1, :].partition_broadcast(128), writes=[gB1])
        w_in_sb = P.alloc(8 * PIN, BF16, "w_in")
        for kc in range(8):
            for hf in range(2):
                P.dma("pool", w_in_sb.ap[:, kc * PIN + hf * 1416: kc * PIN + (hf + 1) * 1416],
                      w_in[l, kc * 128:(kc + 1) * 128, hf * 1416:(hf + 1) * 1416], writes=[w_in_sb])
        w_out_sb = P.alloc(8 * D, BF16, "w_out")
        for kc in range(8):
            P.dma("pool", w_out_sb.ap[:, kc * D:(kc + 1) * D], w_out[l, kc * 128:(kc + 1) * 128, :],
                  writes=[w_out_sb])
        wa_up = P.alloc(256, BF16, "wa_up")
        P.dma("pool", wa_up.ap[0:64, :], rw_w_up[l], writes=[wa_up])
        P.dma("pool", wa_up.ap[64:128, :], rw_a_up[l], writes=[wa_up])
        g_up = P.alloc(256, BF16, "g_up")
        P.dma("pool", g_up.ap, rw_g_up[l], writes=[g_up])
        gkup = P.alloc(128, BF16, "gkup")
        P.dma("pool", gkup.ap[0:16, :], gk_up[l], writes=[gkup])
        wabd = P.alloc(4 * 128, BF16, "wabd")
        wxbd = P.alloc(4 * 128, BF16, "wxbd")
        MEMSET(P, "pool", wabd, 0.0)
        MEMSET(P, "pool", wxbd, 0.0)
        for j in range(4):
            for s in range(2):
                P.dma("pool", wabd.ap[64 * s:64 * s + 64, j * 128 + 64 * s: j * 128 + 64 * s + 64],
                      lru_wa[l, 2 * j + s], writes=[wabd])
                P.dma("pool", wxbd.ap[64 * s:64 * s + 64, j * 128 + 64 * s: j * 128 + 64 * s + 64],
                      lru_wx[l, 2 * j + s], writes=[wxbd])
        mbbd = [[P.alloc(128, F32, f"mbbd{hp}{c}") for c in range(NCH)] for hp in range(2)]
        for hp in range(2):
            for c in range(NCH):
                MEMSET(P, "pool", mbbd[hp][c], 0.0)
        xTs = [[P.alloc(D, F32, f"xT{p_}{b}") for b in range(NCH)] for p_ in range(2)]
        hnF = P.alloc(8 * T, BF16, "hnF")
        hnF3 = hnF.re("p (k t) -> p k t", k=8)
        ycat = P.alloc(8 * T, BF16, "ycat")
        ycat3 = ycat.re("p (k t) -> p k t", k=8)
        mscs = [P.alloc(2, F32, f"msc{i}") for i in range(4)]
        tile_mark = P.arena_off

        def m_load(ti_):
            rd = [src_buf] if src_buf is not None else []
            for b in range(NCH):
                P.dma("sp", xTs[ti_ % 2][b].ap, src_d[ti_ * T + b * 128: ti_ * T + (b + 1) * 128, :], reads=rd,
                      writes=[xTs[ti_ % 2][b]])

        def m_norm(ti_):
            save = P.arena_off
            P.arena_off = tile_mark
            hn = P.alloc(D, BF16, "hn")
            P.arena_off = save
            for b in range(NCH):
                rms_block(xTs[ti_ % 2][b], gB1, hn, mscs[(2 * ti_ + b) % 4])
                yield
                pst = P.psum()
                pstb = pst.bitcast(BF16)
                for kc in range(8):
                    TR(P, pstb[:, kc * 128:(kc + 1) * 128], hn[:, kc * 128:(kc + 1) * 128], identb)
                CP(P, "act", hnF3[:, :, b * 128:(b + 1) * 128], pstb.re("p (k t) -> p k t", k=8))
                yield

        m_load(0)
        for _ in m_norm(0):
            pass

        for ti in range(NT):
            P.arena_off = tile_mark
            tok0 = ti * T
            xT = xTs[ti % 2]

            def proj(c0, ncols, evac):
                ps = P.psum()
                for kc in range(8):
                    MM(P, ps[0:ncols, 0:T], w_in_sb[:, kc * PIN + c0: kc * PIN + c0 + ncols], hnF3[:, kc, :],
                       start=(kc == 0), stop=(kc == 7))
                evac(ps[0:ncols, 0:T])


            base_thr = P.arena_off

            class Region:
                def __init__(self, start, size):
                    self.off = start
                    self.end = start + size

                def alloc(self, n, dt=F32, name=""):
                    save = P.arena_off
                    P.arena_off = self.off
                    t = P.alloc(n, dt, name)
                    self.off = P.arena_off
                    P.arena_off = save
                    if self.off > self.end:
                        raise RuntimeError(f"region overflow {name} {self.off}>{self.end}")
                    return t

            SZ_RW, SZ_LRU, SZ_GLA = 14900, 4900, 5780

            def rsqrt_act(out, in_, scale, bias):
                ACT(P, out, in_, AF.Ln, bias=bias, scale=scale)
                ACT(P, out, out, AF.Exp, scale=-0.5)

            def rw_thread():
                R = Region(base_thr, SZ_RW)
                A = R.alloc
                W2 = 2 * T
                NC2 = 2 * NCH
                ptmp = A(1 + T, F32, "ptmp")
                ltmp = A(T, F32, "ltmp")
                rW = A(W2, F32, "rW")
                kW = A(W2, F32, "kW")
                vW = A(W2, F32, "vW")
                waT = A(T, F32, "waT")
                gloT = A(T, F32, "gloT")
                dest = [rW[:, 0:T], rW[:, T:W2], kW[:, 0:T], kW[:, T:W2], vW[:, 0:T], vW[:, T:W2], waT, gloT]
                for gi in range(8):
                    def ev(ps, gi=gi):
                        CP(P, "act", ptmp[:, 1:1 + T], ps)
                        CP(P, "pool", ptmp[:, 0:1], rw_carry[:, gi:gi + 1])
                        TS(P, "dve", ltmp, ptmp[:, 0:T], col("mu", gi), ALU.mult)
                        STT(P, dest[gi], ptmp[:, 1:1 + T], omm[:, gi:gi + 1], ltmp, ALU.mult, ALU.add)
                        CP(P, "pool", rw_carry[:, gi:gi + 1], ptmp[:, T:T + 1])
                    proj(gi * 128, 128, ev)
                    yield
                wab = A(T, BF16, "wab")
                ACT(P, wab[0:64, :], waT[0:64, :], AF.Tanh)
                CP(P, "pool", wab[64:128, :], waT[64:128, :])
                sgl = A(T, BF16, "sgl")
                ACT(P, sgl, gloT, AF.Sigmoid)
                yield
                sgW = A(W2, F32, "sgW")
                aW = A(W2, F32, "aW")
                gSW = A(W2, F32, "gSW")
                for hp in range(2):
                    hsl = slice(hp * T, (hp + 1) * T)
                    ps_w, ps_a, ps_g = P.psum(), P.psum(), P.psum()
                    MM(P, ps_w[:, 0:T], wa_up[0:64, hp * 128:(hp + 1) * 128], wab[0:64, :])
                    MM(P, ps_a[:, 0:T], wa_up[64:128, hp * 128:(hp + 1) * 128], wab[64:128, :])
                    MM(P, ps_g[:, 0:T], g_up[:, hp * 128:(hp + 1) * 128], sgl)
                    ACT(P, sgW[:, hsl], ps_w[:, 0:T], AF.Sigmoid, bias=col("w0", hp))
                    ACT(P, aW[:, hsl], ps_a[:, 0:T], AF.Sigmoid, bias=col("a0", hp))
                    CP(P, "dve", gSW[:, hsl], ps_g[:, 0:T])
                    yield
                yield "B"
                css = A(W2, F32, "css")
                SCAN(P, css, rmask, sgW, 0.0)
                cse = A(W2, F32, "cse")
                TT(P, "pool", cse, css, sgW, ALU.subtract)
                E1 = A(W2, F32, "E1")
                E0 = cse
                Einv = A(W2, F32, "Einv")
                Eend = sgW
                nb = A(NC2, F32, "nb")
                TS(P, "pool", nb, css.re("p (c t) -> p c t", c=NC2)[:, :, 127], -DEC, ALU.mult)
                ACT(P, E1, css, AF.Exp, scale=-DEC)
                ACT(P, Einv, css, AF.Exp, scale=DEC)
                yield
                for c in range(NC2):
                    ACT(P, Eend[:, c * 128:(c + 1) * 128], css[:, c * 128:(c + 1) * 128], AF.Exp,
                        bias=nb[:, c:c + 1], scale=DEC)
                ACT(P, E0, cse, AF.Exp, scale=-DEC)
                gC = A(NC2, F32, "gC")
                CP(P, "pool", gC, E1.re("p (c t) -> p c t", c=NC2)[:, :, 127])
                yield
                kk = A(W2, F32, "kk")
                sqk = A(W2, BF16, "sqk")
                for hp in range(2):
                    hsl = slice(hp * T, (hp + 1) * T)
                    TS(P, "dve", kk[:, hsl], kW[:, hsl], col("k_k", hp), ALU.mult)
                TT(P, "pool", sqk, kk, kk, ALU.mult)
                ps_n = P.psum()
                MM(P, ps_n, ones64b, sqk)
                rn = A(W2, F32, "rn")
                rsqrt_act(rn, ps_n, 1.0, 1e-24)
                yield
                TT(P, "dve", kk, kk, rn, ALU.mult)
                kmod = A(W2, F32, "kmod")
                for hp in range(2):
                    hsl = slice(hp * T, (hp + 1) * T)
                    TS(P, "pool", kmod[:, hsl], aW[:, hsl], col("k_a", hp), ALU.mult, omka[:, hp:hp + 1], ALU.add)
                TT(P, "dve", kmod, kmod, kW, ALU.mult)
                yield
                bvec = A(W2, F32, "bvec")
                TT(P, "dve", bvec, kk, aW, ALU.mult)
                rk = rn
                TT(P, "dve", rk, rW, kmod, ALU.mult)
                rkb = sqk
                for hp in range(2):
                    hsl = slice(hp * T, (hp + 1) * T)
                    TS(P, "pool", rkb[:, hsl], rk[:, hsl], col("r_k", hp), ALU.mult)
                ps_b = P.psum()
                MM(P, ps_b, ones64b, rkb)
                bonus = A(W2, F32, "bonus")
                TT(P, "dve", bonus, ps_b, vW, ALU.mult)
                yield
                Rt = A(W2, BF16, "Rt")
                At = A(W2, BF16, "At")
                Bt = A(W2, BF16, "Bt")
                Kt = A(W2, BF16, "Kt")
                Kh = A(W2, BF16, "Kh")
                Bh = A(W2, BF16, "Bh")
                vb = A(W2, BF16, "vb")
                TT(P, "dve", Rt, rW, E1, ALU.mult)
                STT(P, At, kk, -1.0, E0, ALU.mult, ALU.mult)
                TT(P, "dve", Bt, bvec, Einv, ALU.mult)
                TT(P, "dve", Kt, kmod, Einv, ALU.mult)
                yield
                TT(P, "dve", Kh, kmod, Eend, ALU.mult)
                TT(P, "pool", Bh, bvec, Eend, ALU.mult)
                CP(P, "pool", vb, vW)
                yield
                TMs = []
                for hp in range(2):
                    pst = P.psum()
                    pstb = pst.bitcast(BF16)
                    for c in range(NCH):
                        for i, src_ in enumerate([vb, Kh, Bh, At]):
                            TR(P, pstb[:, (c * 4 + i) * 128:(c * 4 + i + 1) * 128],
                               src_[:, hp * T + c * 128: hp * T + (c + 1) * 128], identb)
                    tm = A(NCH * 512, BF16, f"TM{hp}")
                    CP(P, "act", tm, pstb[:, 0:NCH * 512])
                    TMs.append(tm)
                    yield

                def TMc(hp, c):
                    return TMs[hp][:, c * 512:(c + 1) * 512]
                NJ = 2 * NCH
                NJ2 = 2 * NJ
                al = [t_.bitcast(BF16) for t_ in (kk, kmod, bvec, rn, css, cse, E1)]
                P0, P0T = al[0], al[1]
                Aall = [None] * NJ2
                for hp in range(2):
                    for s in range(2):
                        rows = slice(64 * s, 64 * s + 64)
                        ps_p0 = P.psum()
                        for c in range(NCH):
                            j = hp * NJ + s * NCH + c
                            cs = slice(hp * T + c * 128, hp * T + (c + 1) * 128)
                            ps = P.psum()
                            MM(P, ps[:, 0:128], Bt[rows, cs], At[rows, cs])
                            MM(P, ps[:, 128:256], Kt[rows, cs], At[rows, cs])
                            MM(P, ps[:, 256:384], Bt[rows, cs], Rt[rows, cs])
                            MM(P, ps[:, 384:512], Kt[rows, cs], Rt[rows, cs])
                            aa = A(512, BF16, f"Aall{j}")
                            TT(P, "dve", aa, ps, cmask, ALU.mult)
                            Aall[j] = aa
                            CP(P, "act", P0T[:, j * 128:(j + 1) * 128], aa[:, 0:128])
                            MM(P, ps_p0[:, c * 128:(c + 1) * 128], At[rows, cs], Bt[rows, cs])
                        j0 = hp * NJ + s * NCH
                        TT(P, "dve", P0[:, j0 * 128:(j0 + NCH) * 128], ps_p0[:, 0:T], sl4[:, 0:T], ALU.mult)
                        yield
                G = al[2]
                for hp in range(2):
                    hw = slice(hp * NJ * 128, (hp + 1) * NJ * 128)
                    TT(P, "pool", G[:, hw], P0T[:, hw], ident4b, ALU.add)
                Pk, PkT = P0, P0T
                Pn = [al[3], al[4]]
                PnT = [al[5], al[6]]
                NLEV = 6
                for lev in range(NLEV):
                    nP, nPT = Pn[lev % 2], PnT[lev % 2]
                    for hp in range(2):
                        hw = slice(hp * NJ * 128, (hp + 1) * NJ * 128)
                        ps1 = P.psum()
                        for j in range(NJ):
                            js = slice((hp * NJ + j) * 128, (hp * NJ + j + 1) * 128)
                            MM(P, ps1[:, j * 128:(j + 1) * 128], PkT[:, js], Pk[:, js])
                        CP(P, "act", nP[:, hw], ps1[:, 0:NJ * 128])
                        if lev < NLEV - 1:
                            ps2 = P.psum()
                            for j in range(NJ):
                                js = slice((hp * NJ + j) * 128, (hp * NJ + j + 1) * 128)
                                MM(P, ps2[:, j * 128:(j + 1) * 128], Pk[:, js], PkT[:, js])
                            CP(P, "dve", nPT[:, hw], ps2[:, 0:NJ * 128])
                    yield
                    for hp in range(2):
                        hw = slice(hp * NJ * 128, (hp + 1) * NJ * 128)
                        ps3 = P.psum()
                        for j in range(NJ):
                            js = slice((hp * NJ + j) * 128, (hp * NJ + j + 1) * 128)
                            MM(P, ps3[:, j * 128:(j + 1) * 128], nP[:, js], G[:, js])
                        TT(P, "dve", G[:, hw], G[:, hw], ps3[:, 0:NJ * 128], ALU.add)
                    Pk, PkT = nP, nPT
                    yield
                XW = Einv.bitcast(BF16)
                for hp in range(2):
                    ps = P.psum()
                    for s in range(2):
                        for c in range(NCH):
                            jl = s * NCH + c
                            j = hp * NJ + jl
                            MM(P, ps[:, jl * 128:jl * 128 + 64], Aall[j][:, 128:256], TMc(hp, c)[:, 64 * s:64 * s + 64])
                            MM(P, ps[:, jl * 128 + 64:jl * 128 + 128], G[:, j * 128:(j + 1) * 128],
                               TMc(hp, c)[:, 384 + 64 * s:384 + 64 * s + 64])
                    CP(P, "act", XW[:, hp * NJ * 128:(hp + 1) * NJ * 128], ps[:, 0:NJ * 128])
                yield
                U0 = rW.bitcast(BF16)[:, 0:NJ2 * 64]
                ps = P.psum()
                for j in range(NJ2):
                    MM(P, ps[:, j * 64:(j + 1) * 64], G[:, j * 128:(j + 1) * 128], XW[:, j * 128:j * 128 + 64])
                CP(P, "act", U0, ps[:, 0:NJ2 * 64])
                yield
                Nn = kW[:, 0:NC2 * 64]
                for hp in range(2):
                    ps = P.psum()
                    for s in range(2):
                        orow = slice(64 * s, 64 * s + 64)
                        for c in range(NCH):
                            j = hp * NJ + s * NCH + c
                            tmc = TMc(hp, c)
                            MM(P, ps[orow, c * 128:c * 128 + 64], XW[:, j * 128 + 64:j * 128 + 128],
                               tmc[:, 256 + 64 * s:256 + 64 * s + 64])
                            MM(P, ps[orow, c * 128 + 64:c * 128 + 128], tmc[:, 256 + 64 * s:256 + 64 * s + 64],
                               U0[:, j * 64:(j + 1) * 64], start=True, stop=False)
                            MM(P, ps[orow, c * 128 + 64:c * 128 + 128], tmc[:, 128 + 64 * s:128 + 64 * s + 64],
                               tmc[:, 64 * s:64 * s + 64], start=False, stop=True)
                    for c in range(NCH):
                        CP(P, "act", mbbd[hp][c][0:64, 0:64], ps[0:64, c * 128:c * 128 + 64])
                        CP(P, "act", mbbd[hp][c][64:128, 64:128], ps[64:128, c * 128:c * 128 + 64])
                        CP(P, "dve", Nn[:, (hp * NCH + c) * 64:(hp * NCH + c + 1) * 64], ps[:, c * 128 + 64:c * 128 + 128])
                yield
                RhT = kW[:, NC2 * 64:NC2 * 64 + W2 // 2].bitcast(BF16)
                Y0 = vW
                for hp in range(2):
                    hsl = slice(hp * T, (hp + 1) * T)
                    ps = P.psum()
                    for s in range(2):
                        orow = slice(64 * s, 64 * s + 64)
                        for c in range(NCH):
                            j = hp * NJ + s * NCH + c
                            MM(P, ps[orow, c * 128:(c + 1) * 128], XW[:, j * 128 + 64:j * 128 + 128],
                               Aall[j][:, 256:384])
                    TT(P, "dve", RhT[:, hsl], ps[:, 0:T], Rt[:, hsl], ALU.add)
                    ps = P.psum()
                    for s in range(2):
                        orow = slice(64 * s, 64 * s + 64)
                        for c in range(NCH):
                            j = hp * NJ + s * NCH + c
                            MM(P, ps[orow, c * 128:(c + 1) * 128], U0[:, j * 64:(j + 1) * 64], Aall[j][:, 256:384],
                               start=True, stop=False)
                            MM(P, ps[orow, c * 128:(c + 1) * 128], TMc(hp, c)[:, 64 * s:64 * s + 64],
                               Aall[j][:, 384:512], start=False, stop=True)
                    CP(P, "act", Y0[:, hsl], ps[:, 0:T])
                yield
                Hs = sgW[:, 0:2 * (NCH + 1) * 64]
                Hb = rW[:, NJ2 * 32:NJ2 * 32 + NC2 * 32].bitcast(BF16)

                def Hsl(hp, c):
                    o_ = (hp * (NCH + 1) + c) * 64
                    return Hs[:, o_:o_ + 64]

                def Hbl(hp, c):
                    o_ = (hp * NCH + c) * 64
                    return Hb[:, o_:o_ + 64]
                for hp in range(2):
                    CP(P, "pool", Hsl(hp, 0), rw_H[hp])
                for c in range(NCH):
                    for hp in range(2):
                        CP(P, "pool", Hbl(hp, c), Hsl(hp, c))
                        ps = P.psum()
                        MM(P, ps[:, 0:64], mbbd[hp][c], Hsl(hp, c), start=True, stop=False)
                        MM(P, ps[:, 0:64], identf, Nn[:, (hp * NCH + c) * 64:(hp * NCH + c + 1) * 64], start=False, stop=True)
                        STT(P, Hsl(hp, c + 1), Hsl(hp, c), gC[:, hp * NCH + c:hp * NCH + c + 1],
                            ps[:, 0:64], ALU.mult, ALU.add)
                    yield
                for hp in range(2):
                    CP(P, "pool", rw_H[hp], Hsl(hp, NCH))
                y = aW
                for hp in range(2):
                    hsl = slice(hp * T, (hp + 1) * T)
                    pse, pso = P.psum(), P.psum()
                    for c in range(NCH):
                        cs = slice(c * 128, (c + 1) * 128)
                        gcs = slice(hp * T + c * 128, hp * T + (c + 1) * 128)
                        MM(P, pse[0:64, cs], Hbl(hp, c)[0:64, :], RhT[0:64, gcs])
                        MM(P, pso[64:128, cs], Hbl(hp, c)[64:128, :], RhT[64:128, gcs])
                    TT(P, "dve", y[0:64, hsl], pse[0:64, 0:T], Y0[0:64, hsl], ALU.add)
                    TT(P, "dve", y[64:128, hsl], pso[64:128, 0:T], Y0[64:128, hsl], ALU.add)
                    dbg(f"rw_y{hp}", y[:, hsl], 128, T, tok0, ntok)
                yield
                yb = A(W2, BF16, "yb")
                CP(P, "pool", yb, y)
                ps_m = P.psum()
                MM(P, ps_m, ones64b, yb)
                yc = A(W2, F32, "yc")
                STT(P, yc, ps_m, -1.0 / 64, y, ALU.mult, ALU.add)
                yield
                TT(P, "pool", yb, yc, yc, ALU.mult)
                ps_v = P.psum()
                MM(P, ps_v, ones64b, yb)
                rs = y
                rsqrt_act(rs, ps_v, 1.0 / 64, epsc[:, 1:2])
                yield
                TT(P, "dve", yc, yc, rs, ALU.mult)
                for hp in range(2):
                    hsl = slice(hp * T, (hp + 1) * T)
                    TS(P, "pool", yc[:, hsl], yc[:, hsl], col("ln_g", hp), ALU.mult, col("ln_b", hp), ALU.add)
                TT(P, "dve", yc, yc, bonus, ALU.add)
                TT(P, "dve", ycat[:, 0:W2], yc, gSW, ALU.mult)
                yield

            def lru_thread():
                R = Region(base_thr + SZ_RW, SZ_LRU)
                A = R.alloc
                xbuf = A(3 + T, F32, "xbuf")
                gt = A(T, F32, "gt")
                xc = A(T, F32, "xc")
                xcb = A(T, BF16, "xcb")
                st = []
                for j in range(4):
                    CP(P, "pool", xbuf[:, 0:3], lru_xc[j])
                    proj(1024 + j * 128, 128, lambda ps: CP(P, "act", xbuf[:, 3:3 + T], ps))
                    yield
                    proj(1536 + j * 128, 128, lambda ps: CP(P, "act", gt, ps))
                    CP(P, "pool", lru_xc[j], xbuf[:, T:T + 3])
                    yield
                    TS(P, "pool", xc, xbuf[:, 0:T], col("cw0", j), ALU.mult, col("cb", j), ALU.add)
                    for tap in range(1, 4):
                        STT(P, xc, xbuf[:, tap:tap + T], col(f"cw{tap}", j), xc, ALU.mult, ALU.add)
                    CP(P, "pool", xcb, xc)
                    yield
                    ps_r, ps_i = P.psum(), P.psum()
                    MM(P, ps_r[:, 0:T], wabd[:, j * 128:(j + 1) * 128], xcb)
                    MM(P, ps_i[:, 0:T], wxbd[:, j * 128:(j + 1) * 128], xcb)
                    gr = A(T, F32, f"gr{j}")
                    uu = A(T, F32, f"uu{j}")
                    ge = A(T, F32, f"ge{j}")
                    ACT(P, gr, ps_r[:, 0:T], AF.Sigmoid, bias=col("ba", j))
                    ACT(P, uu, ps_i[:, 0:T], AF.Sigmoid, bias=col("bx", j))
                    yield
                    TT(P, "dve", uu, uu, xc, ALU.mult)
                    TT(P, "pool", ge, gt, gt, ALU.mult)
                    TS(P, "pool", ge, ge, 0.044715, ALU.mult, 1.0, ALU.add)
                    TT(P, "dve", ge, ge, gt, ALU.mult)
                    ACT(P, ge, ge, AF.Sigmoid, scale=1.5957691216057308)
                    TT(P, "dve", ge, ge, gt, ALU.mult)
                    st.append((gr, uu, ge))
                    yield
                yield "B"
                av = A(T, F32, "av")
                a2 = A(T, F32, "a2")
                hh = A(T, F32, "hh")
                sqb = A(T, BF16, "sqb")
                for j in range(4):
                    gr, uu, ge = st[j]
                    ACT(P, av, gr, AF.Exp, scale=c1[:, j:j + 1])
                    ACT(P, a2, gr, AF.Exp, scale=c2[:, j:j + 1])
                    ACT(P, a2, a2, AF.Ln, bias=1.0, scale=-1.0)
                    ACT(P, a2, a2, AF.Exp, scale=0.5)
                    yield
                    TT(P, "dve", uu, uu, a2, ALU.mult)
                    SCAN(P, hh, av, uu, lru_h[:, j:j + 1])
                    CP(P, "pool", lru_h[:, j:j + 1], hh[:, T - 1:T])
                    yl = av
                    TT(P, "dve", yl, hh, ge, ALU.mult)
                    dbg(f"lru_y{j}", yl, 128, T, tok0, ntok)
                    yield
                    TT(P, "pool", sqb, yl, yl, ALU.mult)
                    ps_m = P.psum()
                    MM(P, ps_m[:, 0:T], ones64b, sqb)
                    rs = a2
                    rsqrt_act(rs, ps_m[:, 0:T], 1.0 / 64, epsc[:, 0:1])
                    STT(P, ycat3[:, 2 + j, :], yl, col("lng", j), rs, ALU.mult, ALU.mult)
                    yield

            def gla_thread():
                R = Region(base_thr + SZ_RW + SZ_LRU, SZ_GLA)
                A = R.alloc
                q = A(T, F32, "q")
                kg = A(T, F32, "kg")
                vg = [A(T, BF16, f"vg{i}") for i in range(2)]
                gg = [A(T, F32, f"gg{i}") for i in range(2)]
                gklo = A(T, BF16, "gklo")
                sgm = A(T, F32, "sgm")
                proj(2048, 128, lambda ps: CP(P, "act", q, ps))
                yield
                proj(2176, 128, lambda ps: CP(P, "act", kg, ps))
                yield
                proj(2304, 128, lambda ps: CP(P, "act", vg[0], ps))
                yield
                proj(2432, 128, lambda ps: CP(P, "act", vg[1], ps))
                yield
                proj(2560, 16, lambda ps: CP(P, "act", gklo[0:16, :], ps))
                yield
                for i in range(2):
                    def evg(ps, i=i):
                        ACT(P, sgm, ps, AF.Sigmoid)
                        TT(P, "dve", gg[i], ps, sgm, ALU.mult)
                    proj(2576 + 128 * i, 128, evg)
                    yield
                ps_gk = P.psum()
                MM(P, ps_gk[:, 0:T], gkup[0:16, :], gklo[0:16, :])
                la = A(T, F32, "la")
                ACT(P, la, ps_gk[:, 0:T], AF.Sigmoid, bias=col("gkb"))
                yield "B"
                ACT(P, la, la, AF.Ln)
                bc = A(T, F32, "bc")
                SCAN(P, bc, rmask[:, 0:T], la, 0.0)
                Eq = A(T, F32, "Eq")
                Ek = A(T, F32, "Ek")
                Ee = la
                ACT(P, Eq, bc, AF.Exp, scale=1.0 / 16)
                ACT(P, Ek, bc, AF.Exp, scale=-1.0 / 16)
                yield
                nb = A(NCH, F32, "nbg")
                TS(P, "pool", nb, bc.re("p (c t) -> p c t", c=NCH)[:, :, 127], 1.0 / 16, ALU.mult)
                for c in range(NCH):
                    ACT(P, Ee[:, c * 128:(c + 1) * 128], bc[:, c * 128:(c + 1) * 128], AF.Exp,
                        bias=nb[:, c:c + 1], scale=-1.0 / 16)
                gCg = A(NCH, F32, "gCg")
                CP(P, "pool", gCg, Eq.re("p (c t) -> p c t", c=NCH)[:, :, 127])
                yield
                qin = A(T, BF16, "qin")
                STT(P, qin, q, 32.0 ** -0.5, Eq, ALU.mult, ALU.mult)
                kin = [A(T, BF16, f"kin{i}") for i in range(2)]
                for i in range(2):
                    STT(P, kin[i], kg, par[:, i:i + 1], Ek, ALU.mult, ALU.mult)
                kend = A(T, BF16, "kend")
                TT(P, "pool", kend, kg, Ee, ALU.mult)
                yield
                GT = []
                for c in range(NCH):
                    pst = P.psum()
                    pstb = pst.bitcast(BF16)
                    cs = slice(c * 128, (c + 1) * 128)
                    TR(P, pstb[:, 0:128], vg[0][:, cs], identb)
                    TR(P, pstb[:, 128:256], vg[1][:, cs], identb)
                    TR(P, pstb[:, 256:384], kend[:, cs], identb)
                    gt_ = A(384, BF16, f"GT{c}")
                    CP(P, "act", gt_, pstb[:, 0:384])
                    GT.append(gt_)
                    yield
                ST = []
                for h in range(4):
                    rows = slice(64 * (h // 2), 64 * (h // 2) + 64)
                    ps = P.psum()
                    for c in range(NCH):
                        cs = slice(c * 128, (c + 1) * 128)
                        MM(P, ps[:, cs], kin[h % 2][rows, cs], qin[rows, cs])
                    st_ = A(T, BF16, f"ST{h}")
                    TT(P, "dve", st_, ps[:, 0:T], ui4[:, 0:T], ALU.mult)
                    ST.append(st_)
                    yield
                Sb = []
                Scur = A((NCH + 1) * 256, F32, "Scur")
                CP(P, "pool", Scur[:, 0:256], gla_S)
                for c in range(NCH):
                    sb_ = A(256, BF16, f"Sb{c}")
                    TT(P, "pool", sb_, Scur[:, c * 256:(c + 1) * 256], bmask, ALU.mult)
                    Sb.append(sb_)
                    ps = P.psum()
                    MM(P, ps[:, 0:256], GT[c][:, 256:384], GT[c][:, 0:256])
                    STT(P, Scur[:, (c + 1) * 256:(c + 2) * 256], Scur[:, c * 256:(c + 1) * 256], gCg[:, c:c + 1],
                        ps[:, 0:256], ALU.mult, ALU.add)
                    yield
                CP(P, "pool", gla_S, Scur[:, NCH * 256:(NCH + 1) * 256])
                o = A(T, F32, "o")
                ob = A(T, BF16, "ob")
                rs = A(T, F32, "rsg")
                for vp in range(2):
                    ps_in, ps_it = P.psum(), P.psum()
                    for s in range(2):
                        h = 2 * vp + s
                        orow = slice(64 * s, 64 * s + 64)
                        krow = slice(64 * vp, 64 * vp + 64)
                        for c in range(NCH):
                            cs = slice(c * 128, (c + 1) * 128)
                            MM(P, ps_in[orow, cs], GT[c][:, 64 * h:64 * h + 64], ST[h][:, cs])
                            MM(P, ps_it[orow, cs], Sb[c][krow, 64 * h:64 * h + 64], qin[krow, cs])
                    CP(P, "act", o, ps_it[:, 0:T])
                    TT(P, "dve", o, o, ps_in[:, 0:T], ALU.add)
                    dbg(f"gla_o{vp}", o, 128, T, tok0, ntok)
                    yield
                    TT(P, "pool", ob, o, o, ALU.mult)
                    ps_m = P.psum()
                    MM(P, ps_m[:, 0:T], ones64b, ob)
                    rsqrt_act(rs, ps_m[:, 0:T], 1.0 / 64, epsc[:, 0:1])
                    STT(P, o, o, col("gng"), rs, ALU.mult, ALU.mult)
                    TT(P, "dve", ycat3[:, 6 + vp, :], o, gg[vp], ALU.mult)
                    yield

            _skip = os.environ.get("K_SKIP", "").split(",")
            threads = [g_ for n_, g_ in (("rw", rw_thread), ("lru", lru_thread), ("gla", gla_thread)) if n_ not in _skip]
            threads = [g_() for g_ in threads]
            def run_greedy(ths):
                ready = {id(t_): 0.0 for t_ in ths}
                out_ = []
                while ths:
                    th = min(ths, key=lambda t_: ready[id(t_)])
                    P.step_t = 0.0
                    try:
                        v_ = next(th)
                    except StopIteration:
                        ths.remove(th)
                        continue
                    if P.step_t > 0:
                        ready[id(th)] = P.step_t
                    if v_ == "B":
                        ths.remove(th)
                        out_.append(th)
                return out_

            if os.environ.get("K_GREEDY", "0") == "1":
                atB = run_greedy(threads)
                if ti + 1 < NT:
                    m_load(ti + 1)
                    atB.append(m_norm(ti + 1))
                run_greedy(atB)
            else:
                atB = []
                while threads:
                    for th in list(threads):
                        try:
                            v_ = next(th)
                        except StopIteration:
                            threads.remove(th)
                            continue
                        if v_ == "B":
                            threads.remove(th)
                            atB.append(th)
                threads = atB
                if ti + 1 < NT:
                    m_load(ti + 1)
                    threads.append(m_norm(ti + 1))
                while threads:
                    for th in list(threads):
                        try:
                            next(th)
                        except StopIteration:
                            threads.remove(th)
            P.arena_off = base_thr

            for kc in range(8):
                dbg(f"ycat{kc}", ycat3[:, kc, :], 128, T, tok0, ntok)

            for b in range(NCH):
                for hf in range(2):
                    ps = P.psum()
                    for kc in range(8):
                        MM(P, ps, ycat3[:, kc, b * 128:(b + 1) * 128],
                           w_out_sb[:, kc * D + hf * 512: kc * D + (hf + 1) * 512], start=(kc == 0), stop=(kc == 7))
                    TT(P, "dve", xT[b][:, hf * 512:(hf + 1) * 512], xT[b][:, hf * 512:(hf + 1) * 512], ps, ALU.add)
                P.dma("sp", xa_d[tok0 + b * 128: tok0 + (b + 1) * 128, :], xT[b].ap, reads=[xT[b]],
                      writes=[xa_buf])

        P.barrier()
        P.arena_off = persist_mark
        if last:
            gBf = P.alloc(D, F32, "gBf")
            P.dma("sp", gBf.ap, final_g.partition_broadcast(128), writes=[gBf])
        gB2 = P.alloc(D, F32, "gB2")
        P.dma("sp", gB2.ap, norm2_g[l:l + 1, :].partition_broadcast(128), writes=[gB2])
        wg_sb = P.alloc(8 * DFF, BF16, "wg")
        wu_sb = P.alloc(8 * DFF, BF16, "wu")
        wd_sb = P.alloc(NFC * D, BF16, "wd")
        for kc in range(8):
            for hf in range(2):
                sl_ = slice(hf * 1408, (hf + 1) * 1408)
                P.dma("pool", wg_sb.ap[:, kc * DFF + hf * 1408: kc * DFF + (hf + 1) * 1408],
                      w_gate[l, kc * 128:(kc + 1) * 128, sl_], writes=[wg_sb])
                P.dma("pool", wu_sb.ap[:, kc * DFF + hf * 1408: kc * DFF + (hf + 1) * 1408],
                      w_up[l, kc * 128:(kc + 1) * 128, sl_], writes=[wu_sb])
        for fc in range(NFC):
            P.dma("pool", wd_sb.ap[:, fc * D:(fc + 1) * D], w_down[l, fc * 128:(fc + 1) * 128, :], writes=[wd_sb])
        xTs = [[P.alloc(D, F32, f"fxT{p_}{b}") for b in range(NCH)] for p_ in range(2)]
        hnFs = [P.alloc(8 * T, BF16, f"fhnF{p_}") for p_ in range(2)]
        hF = P.alloc(NFC * T, BF16, "hF")
        hF3 = hF.re("p (k t) -> p k t", k=NFC)
        fhn = P.alloc(D, BF16, "fhn")
        sgt = [P.alloc(T, F32, f"sgt{i}") for i in range(2)]
        scs = [P.alloc(2, F32, f"fsc{i}") for i in range(4)]
        yos = [P.alloc(D, F32, f"yo{i}") for i in range(2)] if last else []
        sci = [0]

        def nsc():
            sci[0] += 1
            return scs[sci[0] % 4]

        def f_pro(ti):
            tok0 = ti * T
            xT = xTs[ti % 2]
            hnF3 = hnFs[ti % 2].re("p (k t) -> p k t", k=8)
            for b in range(NCH):
                P.dma("sp", xT[b].ap, xa_d[tok0 + b * 128: tok0 + (b + 1) * 128, :], reads=[xa_buf], writes=[xT[b]])
            for b in range(NCH):
                rms_block(xT[b], gB2, fhn, nsc())
                pst = P.psum()
                pstb = pst.bitcast(BF16)
                for kc in range(8):
                    TR(P, pstb[:, kc * 128:(kc + 1) * 128], fhn[:, kc * 128:(kc + 1) * 128], identb)
                CP(P, "act", hnF3[:, :, b * 128:(b + 1) * 128], pstb.re("p (k t) -> p k t", k=8))

        def f_gateup(ti):
            hnF3 = hnFs[ti % 2].re("p (k t) -> p k t", k=8)
            for fc in range(NFC):
                ps = P.psum()
                for kc in range(8):
                    MM(P, ps[:, 0:T], wg_sb[:, kc * DFF + fc * 128: kc * DFF + (fc + 1) * 128], hnF3[:, kc, :],
                       start=(kc == 0), stop=(kc == 7))
                for kc in range(8):
                    MM(P, ps[:, T:2 * T], wu_sb[:, kc * DFF + fc * 128: kc * DFF + (fc + 1) * 128], hnF3[:, kc, :],
                       start=(kc == 0), stop=(kc == 7))
                s_ = sgt[fc % 2]
                ACT(P, s_, ps[:, 0:T], AF.Silu)
                TT(P, "dve", hF3[:, fc, :], s_, ps[:, T:2 * T], ALU.mult)

        def f_down(ti):
            tok0 = ti * T
            xT = xTs[ti % 2]
            for b in range(NCH):
                for hf in range(2):
                    ps = P.psum()
                    for fc in range(NFC):
                        MM(P, ps, hF3[:, fc, b * 128:(b + 1) * 128],
                           wd_sb[:, fc * D + hf * 512: fc * D + (hf + 1) * 512], start=(fc == 0), stop=(fc == NFC - 1))
                    TT(P, "dve", xT[b][:, hf * 512:(hf + 1) * 512], xT[b][:, hf * 512:(hf + 1) * 512], ps, ALU.add)
                if last:
                    yo = yos[b % 2]
                    rms_block(xT[b], gBf, yo, nsc())
                    fin.append(P.dma("sp", out_d[tok0 + b * 128: tok0 + (b + 1) * 128, :], yo.ap, reads=[yo]))
                else:
                    P.dma("sp", xb_d[tok0 + b * 128: tok0 + (b + 1) * 128, :], xT[b].ap, reads=[xT[b]],
                          writes=[xb_buf])

        f_pro(0)
        for ti in range(NT):
            f_gateup(ti)
            if ti + 1 < NT:
                f_pro(ti + 1)
            f_down(ti)
    P.finish(fin)
    P.build()
    return nc, list(dbg_out.keys())


def make_in_map(inputs, xs, nlayers):
    m = {"x": np.ascontiguousarray(xs, dtype=np.float32)}
    for k in ["w_in", "w_out", "ffn_w_gate", "ffn_w_up", "ffn_w_down", "rw_w_up", "rw_a_up", "rw_g_up",
              "lru_wa", "lru_wx", "gla_gk_up", "norm1_g", "norm2_g"]:
        m[k] = np.ascontiguousarray(np.asarray(inputs[k], np.float32)[:nlayers])
    m["cols"] = np.stack([pack_cols(inputs, l) for l in range(nlayers)])
    m["final_norm_g"] = np.asarray(inputs["final_norm_g"], np.float32).reshape(1, D)
    for k, v in make_consts().items():
        m["c_" + k] = v
    return m


_CACHE = {}


def kernel(**inputs):
    x = np.asarray(inputs["x"], np.float32)
    B, S, _ = x.shape
    L = np.asarray(inputs["w_in"]).shape[0]
    key = (S, L)
    if key not in _CACHE:
        _CACHE[key] = build_program(S, L)[0]
    nc = _CACHE[key]
    in_maps = [make_in_map(inputs, x[c % B], L) for c in range(8)]
    res = run_bass_kernel_spmd(nc, in_maps, core_ids=list(range(8)))
    return np.stack([res.results[b]["out"] for b in range(B)], axis=0)
```

```python
import contextlib
import math
import os
import numpy as np
import concourse.bass as bass
import concourse.mybir as mybir
from concourse.bass_utils import run_bass_kernel_spmd

F32 = mybir.dt.float32
BF16 = mybir.dt.bfloat16
AF = mybir.ActivationFunctionType
ALU = mybir.AluOpType

CHUNK = 8000
SAMEQ = os.environ.get('K_SAMEQ', '1') == '1'
NDMASEM = 12

D = 1024
PIN = 2832
DFF = 2816
NFC = DFF // 128
EPS = 1e-6
RW_EPS = 64e-5
DEC = math.exp(-0.5)
T = 256
NCH = T // 128


class Buf:
    __slots__ = ("name", "writers", "readers", "t")

    def __init__(self, name=""):
        self.name = name
        self.writers = {}
        self.readers = {}
        self.t = 0.0


def _dep_kv(d):
    if d[0] == "e":
        return ("e", d[1], d[2] // CHUNK), d[2] % CHUNK + 1
    return ("d", d[1], d[2]), d[3]


def _merge(dst, src):
    for k, v in src.items():
        if dst.get(k, 0) < v:
            dst[k] = v


class Tile:
    __slots__ = ("ap", "buf")

    def __init__(self, ap, buf=None):
        self.ap = ap
        self.buf = buf if buf is not None else Buf()

    def __getitem__(self, k):
        return Tile(self.ap[k], self.buf)

    def bitcast(self, dt):
        return Tile(self.ap.bitcast(dt), self.buf)

    def re(self, s, **kw):
        return Tile(self.ap.rearrange(s, **kw), self.buf)


class Prog:
    ENG = ("pe", "act", "dve", "pool", "sp")

    def __init__(self, nc):
        self.nc = nc
        self.stack = contextlib.ExitStack()
        self.streams = {e: [] for e in self.ENG}
        self.count = {e: 0 for e in self.ENG}
        self.esems = {e: [] for e in self.ENG}
        self.dsems = {}
        self.dma_n = {e: 0 for e in self.ENG}
        self.dma_hist = {e: {} for e in self.ENG}
        self.waited = {e: {} for e in self.ENG}
        self.n_t = 0
        self.final_deps = []
        self.arena = None
        self.arena_off = 0
        self.arena_size = 0
        self.psb = []
        self.ps_i = 0
        self.live = []
        self.t_eng = {e: 0.0 for e in self.ENG}
        self.step_t = 0.0

    def init_mem(self, arena_f32_cols):
        self.arena_size = arena_f32_cols
        self.arena = self.stack.enter_context(
            self.nc.sbuf_tensor("arena", [128, arena_f32_cols], F32))
        for i in range(8):
            t = self.stack.enter_context(self.nc.psum_tensor(f"psb{i}", [128, 512], F32))
            self.psb.append(Tile(t[:, :], Buf(f"ps{i}")))

    def alloc(self, free_elems, dt=F32, name=""):
        ncol = free_elems if dt == F32 else (free_elems + 1) // 2
        if self.arena_off + ncol > self.arena_size:
            raise RuntimeError(f"arena overflow at {name}: {self.arena_off}+{ncol}>{self.arena_size}")
        s0, s1 = self.arena_off, self.arena_off + ncol
        self.arena_off += ncol
        self.hi = max(getattr(self, "hi", 0), s1)
        keep, over = [], []
        for ent in self.live:
            (over if (ent[0] < s1 and s0 < ent[1]) else keep).append(ent)
        if len(over) == 1 and over[0][0] == s0 and over[0][1] == s1 and over[0][2] == (dt, free_elems):
            return over[0][3]
        ap = self.arena[:, s0:s1]
        if dt != F32:
            ap = ap.bitcast(dt)[:, 0:free_elems]
        buf = Buf(name)
        for ent in over:
            _merge(buf.writers, ent[3].buf.writers)
            _merge(buf.readers, ent[3].buf.readers)
        t = Tile(ap, buf)
        keep.append((s0, s1, (dt, free_elems), t))
        self.live = keep
        return t

    def psum(self):
        t = self.psb[self.ps_i]
        self.ps_i = (self.ps_i + 1) % 8
        return t

    def _deps(self, e, reads, writes, is_dma=False):
        need = {}
        for r in reads:
            _merge(need, r.writers)
        for w in writes:
            _merge(need, w.writers)
            _merge(need, w.readers)
        out = []
        for key, val in need.items():
            if key[0] == "e" and key[1] == e and (e == "pe" or not SAMEQ):
                continue
            if self.waited[e].get(key, 0) >= val:
                continue
            self.waited[e][key] = val
            out.append((key, val))
        return out

    def _record(self, d, reads, writes, is_dma):
        k, v = _dep_kv(d)
        for w in writes:
            if is_dma:
                w.writers = {kk: vv for kk, vv in w.writers.items() if kk[0] == "d"}
            else:
                w.writers = {}
            w.writers[k] = v
            w.readers = {}
        for r in reads:
            if r.readers.get(k, 0) < v:
                r.readers[k] = v

    def _sem(self, key):
        if key[0] == "e":
            return self.esems[key[1]][key[2]]
        return self.dsems[(key[1], key[2])]

    def _est(self, e, reads, writes, cost):
        t0 = self.t_eng[e]
        for b in reads:
            if b.t > t0:
                t0 = b.t
        for b in writes:
            if b.t > t0:
                t0 = b.t
        self.t_eng[e] = t0 + cost
        fin = t0 + cost + 0.25
        for b in writes:
            b.t = fin
        if fin > self.step_t:
            self.step_t = fin

    def op(self, e, fn, reads=(), writes=(), cost=None):
        if cost is None:
            n = 256
            for w in writes:
                if isinstance(w, Tile):
                    n = 1
                    for d_ in w.ap.shape[1:]:
                        n *= d_
                    break
            cost = {"act": 0.2 + n / 1150.0, "dve": 0.15 + n / 960.0, "pool": 0.2 + n / 480.0,
                    "pe": 0.05 + n / 2400.0}.get(e, 1.0)
        reads = [r.buf if isinstance(r, Tile) else r for r in reads]
        writes = [w.buf if isinstance(w, Tile) else w for w in writes]
        self._est(e, reads, writes, cost)
        waits = self._deps(e, reads, writes)
        idx = self.count[e]
        self.count[e] += 1
        mykey = ("e", e, idx // CHUNK)

        def emit(eng, waits=waits, fn=fn, mykey=mykey):
            for k, v in waits:
                eng.wait_ge(self._sem(k), v)
            fn(eng).then_inc(self._sem(mykey), 1)

        self.streams[e].append(emit)
        d = ("e", e, idx)
        self._record(d, reads, writes, False)
        return d

    def dma(self, q, out, in_, reads=(), writes=(), **kw):
        reads = [r.buf if isinstance(r, Tile) else r for r in reads]
        writes = [w.buf if isinstance(w, Tile) else w for w in writes]
        self._est(q, reads, writes, 2.0)
        self.t_eng[q] -= 1.9
        waits = self._deps(q, reads, writes, True)
        n = self.dma_n[q]
        self.dma_n[q] += 1
        slot = n % NDMASEM
        prev = self.dma_hist[q].get(slot, 0)
        val = prev + 16
        self.dma_hist[q][slot] = val
        key = ("d", q, slot)
        if prev > 0 and self.waited[q].get(key, 0) < prev:
            waits = waits + [(key, prev)]
            self.waited[q][key] = prev

        def emit(eng, waits=waits, key=key):
            for k, v in waits:
                eng.wait_ge(self._sem(k), v)
            eng.dma_start(out=out, in_=in_, **kw).then_inc(self._sem(key), 16)

        self.streams[q].append(emit)
        d = ("d", q, slot, val)
        self._record(d, reads, writes, True)
        return d

    def barrier(self):
        keys = []
        for e in self.ENG:
            if self.count[e] > 0:
                idx = self.count[e] - 1
                keys.append((("e", e, idx // CHUNK), idx % CHUNK + 1))
            for slot, val in self.dma_hist[e].items():
                keys.append((("d", e, slot), val))
        for f in self.ENG:
            mine = []
            for k, v in keys:
                if k[0] == "e" and k[1] == f:
                    continue
                if self.waited[f].get(k, 0) >= v:
                    continue
                self.waited[f][k] = v
                mine.append((k, v))

            def emit(eng, mine=mine):
                for k, v in mine:
                    eng.wait_ge(self._sem(k), v)

            self.streams[f].append(emit)

    def finish(self, deps):
        self.final_deps = list(deps)

    def build(self):
        nc = self.nc
        st = self.stack
        for e in self.ENG:
            nsem = (self.count[e] + CHUNK - 1) // CHUNK
            self.esems[e] = [st.enter_context(nc.semaphore(f"s_{e}_{i}")) for i in range(nsem)]
            nd = min(self.dma_n[e], NDMASEM)
            for s in range(nd):
                self.dsems[(e, s)] = st.enter_context(nc.semaphore(f"d_{e}_{s}"))
        fin = []
        for d in self.final_deps:
            if d[0] == "e":
                fin.append((("e", d[1], d[2] // CHUNK), d[2] % CHUNK + 1))
            else:
                fin.append((("d", d[1], d[2]), d[3]))
        block = st.enter_context(nc.Block())
        streams = self.streams

        @block.tensor
        def _(eng):
            for f in streams["pe"]:
                f(eng)

        @block.scalar
        def _(eng):
            for f in streams["act"]:
                f(eng)

        @block.vector
        def _(eng):
            for f in streams["dve"]:
                f(eng)

        @block.gpsimd
        def _(eng):
            for f in streams["pool"]:
                f(eng)

        @block.sync
        def _(eng):
            for f in streams["sp"]:
                f(eng)
            for k, v in fin:
                eng.wait_ge(self._sem(k), v)

        st.close()


def _ap(x):
    return x.ap if isinstance(x, Tile) else x


def _tl(*xs):
    return [x for x in xs if isinstance(x, Tile)]


def ACT(P, out, in_, func, bias=None, scale=None, accum=None):
    kw = {}
    if bias is not None:
        kw["bias"] = _ap(bias)
    if scale is not None:
        kw["scale"] = _ap(scale)
    if accum is not None:
        kw["accum_out"] = _ap(accum)
    P.op("act", lambda e: e.activation(out=out.ap, in_=in_.ap, func=func, **kw),
         reads=_tl(in_, bias, scale), writes=_tl(out, accum))


def TT(P, eng, out, a, b, op):
    P.op(eng, lambda e: e.tensor_tensor(out=out.ap, in0=a.ap, in1=b.ap, op=op),
         reads=_tl(a, b), writes=[out])


def TS(P, eng, out, a, s1, op0, s2=None, op1=None):
    if op1 is None:
        P.op(eng, lambda e: e.tensor_scalar(out=out.ap, in0=a.ap, scalar1=_ap(s1), scalar2=None, op0=op0),
             reads=_tl(a, s1), writes=[out])
    else:
        P.op(eng, lambda e: e.tensor_scalar(out=out.ap, in0=a.ap, scalar1=_ap(s1), scalar2=_ap(s2),
                                            op0=op0, op1=op1),
             reads=_tl(a, s1, s2), writes=[out])


def STT(P, out, in0, scalar, in1, op0, op1):
    P.op("dve", lambda e: e.scalar_tensor_tensor(out=out.ap, in0=in0.ap, scalar=_ap(scalar), in1=in1.ap,
                                                 op0=op0, op1=op1),
         reads=_tl(in0, scalar, in1), writes=[out])


def CP(P, eng, out, in_):
    if eng == "act":
        P.op("act", lambda e: e.activation(out=out.ap, in_=in_.ap, func=AF.Copy), reads=[in_], writes=[out])
    else:
        P.op(eng, lambda e: e.tensor_copy(out=out.ap, in_=in_.ap), reads=[in_], writes=[out])


def MM(P, out, lhsT, rhs, start=True, stop=True):
    P.op("pe", lambda e: e.matmul(out.ap, lhsT=lhsT.ap, rhs=rhs.ap, start=start, stop=stop),
         reads=[lhsT, rhs], writes=[out])


def TR(P, out, in_, ident):
    P.op("pe", lambda e: e.transpose(out=out.ap, in_=in_.ap, identity=ident.ap),
         reads=[in_, ident], writes=[out])


def SCAN(P, out, d0, d1, init):
    P.op("dve", lambda e: e.tensor_tensor_scan(out=out.ap, data0=d0.ap, data1=d1.ap, initial=_ap(init),
                                               op0=ALU.mult, op1=ALU.add),
         reads=_tl(d0, d1, init), writes=[out])


def MEMSET(P, eng, out, val):
    P.op(eng, lambda e: e.memset(out.ap, val), writes=[out])


COLS = {}


def _col_layout():
    names = [("mu", 8), ("w0", 2), ("a0", 2), ("k_k", 2), ("k_a", 2), ("r_k", 2), ("ln_g", 2), ("ln_b", 2),
             ("cw0", 4), ("cw1", 4), ("cw2", 4), ("cw3", 4), ("cb", 4), ("ba", 4), ("bx", 4), ("lam", 4),
             ("lng", 4), ("gkb", 1), ("gng", 1)]
    off = 0
    for n, c in names:
        COLS[n] = (off, c)
        off += c
    return off


NCOL = _col_layout()


def pack_cols(inp, l):
    out = np.zeros((128, NCOL), np.float32)

    def put(name, vec):
        o, c = COLS[name]
        out[:, o:o + c] = np.asarray(vec, np.float32).reshape(c, 128).T

    put("mu", inp["rw_mu"][l])
    put("w0", inp["rw_w0"][l])
    put("a0", inp["rw_a0"][l])
    put("k_k", inp["rw_k_k"][l])
    put("k_a", inp["rw_k_a"][l])
    put("r_k", inp["rw_r_k"][l].reshape(-1))
    put("ln_g", inp["rw_ln_g"][l])
    put("ln_b", inp["rw_ln_b"][l])
    for j in range(4):
        put(f"cw{j}", inp["lru_conv_w"][l, j])
    put("cb", inp["lru_conv_b"][l])
    put("ba", inp["lru_ba"][l])
    put("bx", inp["lru_bx"][l])
    put("lam", inp["lru_lam"][l])
    put("lng", inp["lru_norm_g"][l])
    put("gkb", inp["gla_gk_b"][l])
    put("gng", np.concatenate([inp["gla_norm_g"][l], inp["gla_norm_g"][l]]))
    return out


def make_consts():
    c = {}
    idx = np.arange(128)
    su = (idx[:, None] < idx[None, :]).astype(np.float32)
    ui = (idx[:, None] <= idx[None, :]).astype(np.float32)
    sl = (idx[:, None] > idx[None, :]).astype(np.float32)
    c["ident"] = np.eye(128, dtype=np.float32)
    c["cmask"] = np.concatenate([su, su, ui, ui], axis=1)
    c["sl4"] = np.concatenate([sl] * 4, axis=1)
    c["ui4"] = np.concatenate([ui] * NCH, axis=1)
    ob = np.zeros((128, 128), np.float32)
    ob[:64, :64] = 1
    ob[64:, 64:] = 1
    c["ones64"] = ob
    rm = np.ones((128, 2 * T), np.float32)
    rm[:, ::128] = 0
    c["rmask"] = rm
    bm = np.zeros((128, 256), np.float32)
    for h in range(4):
        bm[32 * h:32 * h + 32, 64 * h:64 * h + 64] = 1
    c["bmask"] = bm
    par = np.zeros((128, 2), np.float32)
    for h in range(4):
        par[32 * h:32 * h + 32, h % 2] = 1
    c["par"] = par
    return c


CONST_SHAPES = {"ident": 128, "cmask": 512, "sl4": 512, "ui4": T, "ones64": 128,
                "rmask": 2 * T, "bmask": 256, "par": 2}


def build_program(ntok, nlayers, debug=()):
    nc = bass.Bass("TRN2", target_bir_lowering=False)
    NT = ntok // T
    L = nlayers

    def din(name, shape):
        return nc.dram_tensor(name, list(shape), F32, kind="ExternalInput").ap()

    x_in = din("x", [ntok, D])
    w_in = din("w_in", [L, D, PIN])
    w_out = din("w_out", [L, D, D])
    w_gate = din("ffn_w_gate", [L, D, DFF])
    w_up = din("ffn_w_up", [L, D, DFF])
    w_down = din("ffn_w_down", [L, DFF, D])
    rw_w_up = din("rw_w_up", [L, 64, 256])
    rw_a_up = din("rw_a_up", [L, 64, 256])
    rw_g_up = din("rw_g_up", [L, 128, 256])
    lru_wa = din("lru_wa", [L, 8, 64, 64])
    lru_wx = din("lru_wx", [L, 8, 64, 64])
    gk_up = din("gla_gk_up", [L, 16, 128])
    cols_d = din("cols", [L, 128, NCOL])
    norm1_g = din("norm1_g", [L, D])
    norm2_g = din("norm2_g", [L, D])
    final_g = din("final_norm_g", [1, D])
    cdram = {k: din("c_" + k, [128, n]) for k, n in CONST_SHAPES.items()}
    out_d = nc.dram_tensor("out", [ntok, D], F32, kind="ExternalOutput").ap()
    xa_d = nc.dram_tensor("xa_scr", [ntok, D], F32).ap()
    xb_d = nc.dram_tensor("xb_scr", [ntok, D], F32).ap()
    xa_buf, xb_buf = Buf("xa"), Buf("xb")
    dbg_out = {}

    P = Prog(nc)
    P.init_mem(52500)
    fin = []

    def dbg(name, tile, rows, cols, tok0=None, total_cols=None):
        if name not in debug:
            return
        if name not in dbg_out:
            tc = total_cols if total_cols is not None else cols
            dbg_out[name] = nc.dram_tensor("dbg_" + name, [rows, tc], F32, kind="ExternalOutput").ap()
        dst = dbg_out[name]
        c0 = tok0 if tok0 is not None else 0
        fin.append(P.dma("pool", dst[0:rows, c0:c0 + cols], tile.ap, reads=[tile]))

    cst = {}
    for k, n in CONST_SHAPES.items():
        if k in ("rmask",):
            cst[k] = P.alloc(n, BF16, "c_" + k)
            P.dma("pool", cst[k].ap, cdram[k][:, :], writes=[cst[k]])
        else:
            cst[k] = P.alloc(n, F32, "c_" + k)
            P.dma("sp", cst[k].ap, cdram[k][:, :], writes=[cst[k]])
    identf = cst["ident"]
    identb = P.alloc(128, BF16, "identb")
    CP(P, "pool", identb, identf)
    ident4b = P.alloc(512, BF16, "ident4b")
    for i in range(4):
        CP(P, "pool", ident4b[:, i * 128:(i + 1) * 128], identf)
    ones64b = P.alloc(128, BF16, "ones64b")
    CP(P, "pool", ones64b, cst["ones64"])
    cmask, sl4, ui4, rmask, bmask, par = (cst[k] for k in ("cmask", "sl4", "ui4", "rmask", "bmask", "par"))
    epsc = P.alloc(2, F32, "epsc")
    MEMSET(P, "pool", epsc[:, 0:1], EPS)
    MEMSET(P, "pool", epsc[:, 1:2], RW_EPS)

    rw_carry = P.alloc(8, F32, "rw_carry")
    lru_xc = [P.alloc(3, F32, f"lru_xc{j}") for j in range(4)]
    lru_h = P.alloc(4, F32, "lru_h")
    rw_H = [P.alloc(64, F32, f"rwH{hp}") for hp in range(2)]
    gla_S = P.alloc(256, F32, "glaS")
    persist_mark = P.arena_off

    def rms_block(xblk, gB, hn_out, sc=None):
        if sc is None:
            ssq = P.alloc(1, F32, "ssq")
            rstd = P.alloc(1, F32, "rstd")
        else:
            ssq, rstd = sc[:, 0:1], sc[:, 1:2]
        ACT(P, hn_out, xblk, AF.Square, accum=ssq)
        ACT(P, rstd, ssq, AF.Ln, bias=epsc[:, 0:1], scale=1.0 / D)
        ACT(P, rstd, rstd, AF.Exp, scale=-0.5)
        STT(P, hn_out, xblk, rstd[:, 0:1], gB, ALU.mult, ALU.mult)

    for l in range(L):
        src_d, src_buf = (x_in, None) if l == 0 else (xb_d, xb_buf)
        last = l == L - 1
        if os.environ.get("K_BARRIER", "1") == "1":
            P.barrier()
        P.arena_off = persist_mark
        colsT = P.alloc(NCOL, F32, "cols")
        P.dma("sp", colsT.ap, cols_d[l], writes=[colsT])

        def col(name, j=0, rows=slice(0, 128)):
            o, c = COLS[name]
            return colsT[rows, o + j:o + j + 1]

        dcol = P.alloc(24, F32, "dcol")
        o_mu = COLS["mu"][0]
        omm = dcol[:, 0:8]
        TS(P, "pool", omm, colsT[:, o_mu:o_mu + 8], -1.0, ALU.mult, 1.0, ALU.add)
        o_ka = COLS["k_a"][0]
        omka = dcol[:, 8:10]
        TS(P, "pool", omka, colsT[:, o_ka:o_ka + 2], -1.0, ALU.mult, 1.0, ALU.add)
        o_lam = COLS["lam"][0]
        c1 = dcol[:, 10:14]
        c2 = dcol[:, 14:18]
        ACT(P, c1, colsT[:, o_lam:o_lam + 4], AF.Exp, scale=-1.0)
        ACT(P, c1, c1, AF.Ln, bias=1.0)
        TS(P, "pool", c2, c1, -16.0, ALU.mult)
        TS(P, "pool", c1, c1, -8.0, ALU.mult)
        MEMSET(P, "pool", rw_carry, 0.0)
        for j in range(4):
            MEMSET(P, "pool", lru_xc[j], 0.0)
        MEMSET(P, "pool", lru_h, 0.0)
        for hp in range(2):
            MEMSET(P, "pool", rw_H[hp], 0.0)
        MEMSET(P, "pool", gla_S, 0.0)

        gB1 = P.alloc(D, F32, "gB1")
        P.dma("sp", gB1.ap, norm1_g[l:l + 1, :].partition_broadcast(128), writes=[gB1])
        w_in_sb = P.alloc(8 * PIN, BF16, "w_in")
        for kc in range(8):
            for hf in range(2):
                P.dma("pool", w_in_sb.ap[:, kc * PIN + hf * 1416: kc * PIN + (hf + 1) * 1416],
                      w_in[l, kc * 128:(kc + 1) * 128, hf * 1416:(hf + 1) * 1416], writes=[w_in_sb])
        w_out_sb = P.alloc(8 * D, BF16, "w_out")
        for kc in range(8):
            P.dma("pool", w_out_sb.ap[:, kc * D:(kc + 1) * D], w_out[l, kc * 128:(kc + 1) * 128, :],
                  writes=[w_out_sb])
        wa_up = P.alloc(256, BF16, "wa_up")
        P.dma("pool", wa_up.ap[0:64, :], rw_w_up[l], writes=[wa_up])
        P.dma("pool", wa_up.ap[64:128, :], rw_a_up[l], writes=[wa_up])
        g_up = P.alloc(256, BF16, "g_up")
        P.dma("pool", g_up.ap, rw_g_up[l], writes=[g_up])
        gkup = P.alloc(128, BF16, "gkup")
        P.dma("pool", gkup.ap[0:16, :], gk_up[l], writes=[gkup])
        wabd = P.alloc(4 * 128, BF16, "wabd")
        wxbd = P.alloc(4 * 128, BF16, "wxbd")
        MEMSET(P, "pool", wabd, 0.0)
        MEMSET(P, "pool", wxbd, 0.0)
        for j in range(4):
            for s in range(2):
                P.dma("pool", wabd.ap[64 * s:64 * s + 64, j * 128 + 64 * s: j * 128 + 64 * s + 64],
                      lru_wa[l, 2 * j + s], writes=[wabd])
                P.dma("pool", wxbd.ap[64 * s:64 * s + 64, j * 128 + 64 * s: j * 128 + 64 * s + 64],
                      lru_wx[l, 2 * j + s], writes=[wxbd])
        mbbd = [[P.alloc(128, F32, f"mbbd{hp}{c}") for c in range(NCH)] for hp in range(2)]
        for hp in range(2):
            for c in range(NCH):
                MEMSET(P, "pool", mbbd[hp][c], 0.0)
        xTs = [[P.alloc(D, F32, f"xT{p_}{b}") for b in range(NCH)] for p_ in range(2)]
        hnF = P.alloc(8 * T, BF16, "hnF")
        hnF3 = hnF.re("p (k t) -> p k t", k=8)
        ycat = P.alloc(8 * T, BF16, "ycat")
        ycat3 = ycat.re("p (k t) -> p k t", k=8)
        mscs = [P.alloc(2, F32, f"msc{i}") for i in range(4)]
        tile_mark = P.arena_off

        def m_load(ti_):
            rd = [src_buf] if src_buf is not None else []
            for b in range(NCH):
                P.dma("sp", xTs[ti_ % 2][b].ap, src_d[ti_ * T + b * 128: ti_ * T + (b + 1) * 128, :], reads=rd,
                      writes=[xTs[ti_ % 2][b]])

        def m_norm(ti_):
            save = P.arena_off
            P.arena_off = tile_mark
            hn = P.alloc(D, BF16, "hn")
            P.arena_off = save
            for b in range(NCH):
                rms_block(xTs[ti_ % 2][b], gB1, hn, mscs[(2 * ti_ + b) % 4])
                yield
                pst = P.psum()
                pstb = pst.bitcast(BF16)
                for kc in range(8):
                    TR(P, pstb[:, kc * 128:(kc + 1) * 128], hn[:, kc * 128:(kc + 1) * 128], identb)
                CP(P, "act", hnF3[:, :, b * 128:(b + 1) * 128], pstb.re("p (k t) -> p k t", k=8))
                yield

        def m_outproj(ti_):
            xT_ = xTs[ti_ % 2]
            for b in range(NCH):
                for hf in range(2):
                    ps = P.psum()
                    for kc in range(8):
                        MM(P, ps, ycat3[:, kc, b * 128:(b + 1) * 128],
                           w_out_sb[:, kc * D + hf * 512: kc * D + (hf + 1) * 512], start=(kc == 0), stop=(kc == 7))
                    TT(P, "dve", xT_[b][:, hf * 512:(hf + 1) * 512], xT_[b][:, hf * 512:(hf + 1) * 512], ps, ALU.add)
                P.dma("sp", xa_d[ti_ * T + b * 128: ti_ * T + (b + 1) * 128, :], xT_[b].ap, reads=[xT_[b]],
                      writes=[xa_buf])

        m_load(0)
        for _ in m_norm(0):
            pass

        for ti in range(NT):
            P.arena_off = tile_mark
            tok0 = ti * T
            xT = xTs[ti % 2]

            def proj(c0, ncols, evac):
                ps = P.psum()
                for kc in range(8):
                    MM(P, ps[0:ncols, 0:T], w_in_sb[:, kc * PIN + c0: kc * PIN + c0 + ncols], hnF3[:, kc, :],
                       start=(kc == 0), stop=(kc == 7))
                evac(ps[0:ncols, 0:T])


            base_thr = P.arena_off

            class Region:
                def __init__(self, start, size):
                    self.off = start
                    self.end = start + size

                def alloc(self, n, dt=F32, name=""):
                    save = P.arena_off
                    P.arena_off = self.off
                    t = P.alloc(n, dt, name)
                    self.off = P.arena_off
                    P.arena_off = save
                    if self.off > self.end:
                        raise RuntimeError(f"region overflow {name} {self.off}>{self.end}")
                    return t

            SZ_RW, SZ_LRU, SZ_GLA = 14900, 4900, 5780

            def rsqrt_act(out, in_, scale, bias):
                ACT(P, out, in_, AF.Ln, bias=bias, scale=scale)
                ACT(P, out, out, AF.Exp, scale=-0.5)

            def rw_thread():
                R = Region(base_thr, SZ_RW)
                A = R.alloc
                W2 = 2 * T
                NC2 = 2 * NCH
                ptmp = A(1 + T, F32, "ptmp")
                ltmp = A(T, F32, "ltmp")
                rW = A(W2, F32, "rW")
                kW = A(W2, F32, "kW")
                vW = A(W2, F32, "vW")
                waT = A(T, F32, "waT")
                gloT = A(T, F32, "gloT")
                dest = [rW[:, 0:T], rW[:, T:W2], kW[:, 0:T], kW[:, T:W2], vW[:, 0:T], vW[:, T:W2], waT, gloT]
                for gi in range(8):
                    def ev(ps, gi=gi):
                        CP(P, "act", ptmp[:, 1:1 + T], ps)
                        CP(P, "pool", ptmp[:, 0:1], rw_carry[:, gi:gi + 1])
                        TS(P, "dve", ltmp, ptmp[:, 0:T], col("mu", gi), ALU.mult)
                        STT(P, dest[gi], ptmp[:, 1:1 + T], omm[:, gi:gi + 1], ltmp, ALU.mult, ALU.add)
                        CP(P, "pool", rw_carry[:, gi:gi + 1], ptmp[:, T:T + 1])
                    proj(gi * 128, 128, ev)
                    yield
                wab = A(T, BF16, "wab")
                ACT(P, wab[0:64, :], waT[0:64, :], AF.Tanh)
                CP(P, "pool", wab[64:128, :], waT[64:128, :])
                sgl = A(T, BF16, "sgl")
                ACT(P, sgl, gloT, AF.Sigmoid)
                yield
                sgW = A(W2, F32, "sgW")
                aW = A(W2, F32, "aW")
                gSW = A(W2, F32, "gSW")
                for hp in range(2):
                    hsl = slice(hp * T, (hp + 1) * T)
                    ps_w, ps_a, ps_g = P.psum(), P.psum(), P.psum()
                    MM(P, ps_w[:, 0:T], wa_up[0:64, hp * 128:(hp + 1) * 128], wab[0:64, :])
                    MM(P, ps_a[:, 0:T], wa_up[64:128, hp * 128:(hp + 1) * 128], wab[64:128, :])
                    MM(P, ps_g[:, 0:T], g_up[:, hp * 128:(hp + 1) * 128], sgl)
                    ACT(P, sgW[:, hsl], ps_w[:, 0:T], AF.Sigmoid, bias=col("w0", hp))
                    ACT(P, aW[:, hsl], ps_a[:, 0:T], AF.Sigmoid, bias=col("a0", hp))
                    CP(P, "dve", gSW[:, hsl], ps_g[:, 0:T])
                    yield
                yield "B"
                css = A(W2, F32, "css")
                SCAN(P, css, rmask, sgW, 0.0)
                cse = A(W2, F32, "cse")
                TT(P, "pool", cse, css, sgW, ALU.subtract)
                E1 = A(W2, F32, "E1")
                E0 = cse
                Einv = A(W2, F32, "Einv")
                Eend = sgW
                nb = A(NC2, F32, "nb")
                TS(P, "pool", nb, css.re("p (c t) -> p c t", c=NC2)[:, :, 127], -DEC, ALU.mult)
                ACT(P, E1, css, AF.Exp, scale=-DEC)
                ACT(P, Einv, css, AF.Exp, scale=DEC)
                yield
                for c in range(NC2):
                    ACT(P, Eend[:, c * 128:(c + 1) * 128], css[:, c * 128:(c + 1) * 128], AF.Exp,
                        bias=nb[:, c:c + 1], scale=DEC)
                ACT(P, E0, cse, AF.Exp, scale=-DEC)
                gC = A(NC2, F32, "gC")
                CP(P, "pool", gC, E1.re("p (c t) -> p c t", c=NC2)[:, :, 127])
                yield
                kk = A(W2, F32, "kk")
                sqk = A(W2, BF16, "sqk")
                for hp in range(2):
                    hsl = slice(hp * T, (hp + 1) * T)
                    TS(P, "dve", kk[:, hsl], kW[:, hsl], col("k_k", hp), ALU.mult)
                TT(P, "pool", sqk, kk, kk, ALU.mult)
                ps_n = P.psum()
                MM(P, ps_n, ones64b, sqk)
                rn = A(W2, F32, "rn")
                rsqrt_act(rn, ps_n, 1.0, 1e-24)
                yield
                TT(P, "dve", kk, kk, rn, ALU.mult)
                kmod = A(W2, F32, "kmod")
                for hp in range(2):
                    hsl = slice(hp * T, (hp + 1) * T)
                    TS(P, "pool", kmod[:, hsl], aW[:, hsl], col("k_a", hp), ALU.mult, omka[:, hp:hp + 1], ALU.add)
                TT(P, "dve", kmod, kmod, kW, ALU.mult)
                yield
                bvec = A(W2, F32, "bvec")
                TT(P, "dve", bvec, kk, aW, ALU.mult)
                rk = rn
                TT(P, "dve", rk, rW, kmod, ALU.mult)
                rkb = sqk
                for hp in range(2):
                    hsl = slice(hp * T, (hp + 1) * T)
                    TS(P, "pool", rkb[:, hsl], rk[:, hsl], col("r_k", hp), ALU.mult)
                ps_b = P.psum()
                MM(P, ps_b, ones64b, rkb)
                bonus = A(W2, F32, "bonus")
                TT(P, "dve", bonus, ps_b, vW, ALU.mult)
                yield
                Rt = A(W2, BF16, "Rt")
                At = A(W2, BF16, "At")
                Bt = A(W2, BF16, "Bt")
                Kt = A(W2, BF16, "Kt")
                Kh = A(W2, BF16, "Kh")
                Bh = A(W2, BF16, "Bh")
                vb = A(W2, BF16, "vb")
                TT(P, "dve", Rt, rW, E1, ALU.mult)
                STT(P, At, kk, -1.0, E0, ALU.mult, ALU.mult)
                TT(P, "dve", Bt, bvec, Einv, ALU.mult)
                TT(P, "dve", Kt, kmod, Einv, ALU.mult)
                yield
                TT(P, "dve", Kh, kmod, Eend, ALU.mult)
                TT(P, "pool", Bh, bvec, Eend, ALU.mult)
                CP(P, "pool", vb, vW)
                yield
                TMs = []
                for hp in range(2):
                    pst = P.psum()
                    pstb = pst.bitcast(BF16)
                    for c in range(NCH):
                        for i, src_ in enumerate([vb, Kh, Bh, At]):
                            TR(P, pstb[:, (c * 4 + i) * 128:(c * 4 + i + 1) * 128],
                               src_[:, hp * T + c * 128: hp * T + (c + 1) * 128], identb)
                    tm = A(NCH * 512, BF16, f"TM{hp}")
                    CP(P, "act", tm, pstb[:, 0:NCH * 512])
                    TMs.append(tm)
                    yield

                def TMc(hp, c):
                    return TMs[hp][:, c * 512:(c + 1) * 512]
                NJ = 2 * NCH
                NJ2 = 2 * NJ
                al = [t_.bitcast(BF16) for t_ in (kk, kmod, bvec, rn, css, cse, E1)]
                P0, P0T = al[0], al[1]
                Aall = [None] * NJ2
                for hp in range(2):
                    for s in range(2):
                        rows = slice(64 * s, 64 * s + 64)
                        ps_p0 = P.psum()
                        for c in range(NCH):
                            j = hp * NJ + s * NCH + c
                            cs = slice(hp * T + c * 128, hp * T + (c + 1) * 128)
                            ps = P.psum()
                            MM(P, ps[:, 0:128], Bt[rows, cs], At[rows, cs])
                            MM(P, ps[:, 128:256], Kt[rows, cs], At[rows, cs])
                            MM(P, ps[:, 256:384], Bt[rows, cs], Rt[rows, cs])
                            MM(P, ps[:, 384:512], Kt[rows, cs], Rt[rows, cs])
                            aa = A(512, BF16, f"Aall{j}")
                            TT(P, "dve", aa, ps, cmask, ALU.mult)
                            Aall[j] = aa
                            CP(P, "act", P0T[:, j * 128:(j + 1) * 128], aa[:, 0:128])
                            MM(P, ps_p0[:, c * 128:(c + 1) * 128], At[rows, cs], Bt[rows, cs])
                        j0 = hp * NJ + s * NCH
                        TT(P, "dve", P0[:, j0 * 128:(j0 + NCH) * 128], ps_p0[:, 0:T], sl4[:, 0:T], ALU.mult)
                        yield
                G = al[2]
                for hp in range(2):
                    hw = slice(hp * NJ * 128, (hp + 1) * NJ * 128)
                    TT(P, "pool", G[:, hw], P0T[:, hw], ident4b, ALU.add)
                Pk, PkT = P0, P0T
                Pn = [al[3], al[4]]
                PnT = [al[5], al[6]]
                NLEV = 6
                for lev in range(NLEV):
                    nP, nPT = Pn[lev % 2], PnT[lev % 2]
                    for hp in range(2):
                        hw = slice(hp * NJ * 128, (hp + 1) * NJ * 128)
                        ps1 = P.psum()
                        for j in range(NJ):
                            js = slice((hp * NJ + j) * 128, (hp * NJ + j + 1) * 128)
                            MM(P, ps1[:, j * 128:(j + 1) * 128], PkT[:, js], Pk[:, js])
                        CP(P, "act", nP[:, hw], ps1[:, 0:NJ * 128])
                        if lev < NLEV - 1:
                            ps2 = P.psum()
                            for j in range(NJ):
                                js = slice((hp * NJ + j) * 128, (hp * NJ + j + 1) * 128)
                                MM(P, ps2[:, j * 128:(j + 1) * 128], Pk[:, js], PkT[:, js])
                            CP(P, "dve", nPT[:, hw], ps2[:, 0:NJ * 128])
                    yield
                    for hp in range(2):
                        hw = slice(hp * NJ * 128, (hp + 1) * NJ * 128)
                        ps3 = P.psum()
                        for j in range(NJ):
                            js = slice((hp * NJ + j) * 128, (hp * NJ + j + 1) * 128)
                            MM(P, ps3[:, j * 128:(j + 1) * 128], nP[:, js], G[:, js])
                        TT(P, "dve", G[:, hw], G[:, hw], ps3[:, 0:NJ * 128], ALU.add)
                    Pk, PkT = nP, nPT
                    yield
                XW = Einv.bitcast(BF16)
                for hp in range(2):
                    ps = P.psum()
                    for s in range(2):
                        for c in range(NCH):
                            jl = s * NCH + c
                            j = hp * NJ + jl
                            MM(P, ps[:, jl * 128:jl * 128 + 64], Aall[j][:, 128:256], TMc(hp, c)[:, 64 * s:64 * s + 64])
                            MM(P, ps[:, jl * 128 + 64:jl * 128 + 128], G[:, j * 128:(j + 1) * 128],
                               TMc(hp, c)[:, 384 + 64 * s:384 + 64 * s + 64])
                    CP(P, "act", XW[:, hp * NJ * 128:(hp + 1) * NJ * 128], ps[:, 0:NJ * 128])
                yield
                U0 = rW.bitcast(BF16)[:, 0:NJ2 * 64]
                ps = P.psum()
                for j in range(NJ2):
                    MM(P, ps[:, j * 64:(j + 1) * 64], G[:, j * 128:(j + 1) * 128], XW[:, j * 128:j * 128 + 64])
                CP(P, "act", U0, ps[:, 0:NJ2 * 64])
                yield
                Nn = kW[:, 0:NC2 * 64]
                for hp in range(2):
                    ps = P.psum()
                    for s in range(2):
                        orow = slice(64 * s, 64 * s + 64)
                        for c in range(NCH):
                            j = hp * NJ + s * NCH + c
                            tmc = TMc(hp, c)
                            MM(P, ps[orow, c * 128:c * 128 + 64], XW[:, j * 128 + 64:j * 128 + 128],
                               tmc[:, 256 + 64 * s:256 + 64 * s + 64])
                            MM(P, ps[orow, c * 128 + 64:c * 128 + 128], tmc[:, 256 + 64 * s:256 + 64 * s + 64],
                               U0[:, j * 64:(j + 1) * 64], start=True, stop=False)
                            MM(P, ps[orow, c * 128 + 64:c * 128 + 128], tmc[:, 128 + 64 * s:128 + 64 * s + 64],
                               tmc[:, 64 * s:64 * s + 64], start=False, stop=True)
                    for c in range(NCH):
                        CP(P, "act", mbbd[hp][c][0:64, 0:64], ps[0:64, c * 128:c * 128 + 64])
                        CP(P, "act", mbbd[hp][c][64:128, 64:128], ps[64:128, c * 128:c * 128 + 64])
                        CP(P, "dve", Nn[:, (hp * NCH + c) * 64:(hp * NCH + c + 1) * 64], ps[:, c * 128 + 64:c * 128 + 128])
                yield
                RhT = kW[:, NC2 * 64:NC2 * 64 + W2 // 2].bitcast(BF16)
                Y0 = vW
                for hp in range(2):
                    hsl = slice(hp * T, (hp + 1) * T)
                    ps = P.psum()
                    for s in range(2):
                        orow = slice(64 * s, 64 * s + 64)
                        for c in range(NCH):
                            j = hp * NJ + s * NCH + c
                            MM(P, ps[orow, c * 128:(c + 1) * 128], XW[:, j * 128 + 64:j * 128 + 128],
                               Aall[j][:, 256:384])
                    TT(P, "dve", RhT[:, hsl], ps[:, 0:T], Rt[:, hsl], ALU.add)
                    ps = P.psum()
                    for s in range(2):
                        orow = slice(64 * s, 64 * s + 64)
                        for c in range(NCH):
                            j = hp * NJ + s * NCH + c
                            MM(P, ps[orow, c * 128:(c + 1) * 128], U0[:, j * 64:(j + 1) * 64], Aall[j][:, 256:384],
                               start=True, stop=False)
                            MM(P, ps[orow, c * 128:(c + 1) * 128], TMc(hp, c)[:, 64 * s:64 * s + 64],
                               Aall[j][:, 384:512], start=False, stop=True)
                    CP(P, "act", Y0[:, hsl], ps[:, 0:T])
                yield
                Hs = sgW[:, 0:2 * (NCH + 1) * 64]
                Hb = rW[:, NJ2 * 32:NJ2 * 32 + NC2 * 32].bitcast(BF16)

                def Hsl(hp, c):
                    o_ = (hp * (NCH + 1) + c) * 64
                    return Hs[:, o_:o_ + 64]

                def Hbl(hp, c):
                    o_ = (hp * NCH + c) * 64
                    return Hb[:, o_:o_ + 64]
                for hp in range(2):
                    CP(P, "pool", Hsl(hp, 0), rw_H[hp])
                for c in range(NCH):
                    for hp in range(2):
                        CP(P, "pool", Hbl(hp, c), Hsl(hp, c))
                        ps = P.psum()
                        MM(P, ps[:, 0:64], mbbd[hp][c], Hsl(hp, c), start=True, stop=False)
                        MM(P, ps[:, 0:64], identf, Nn[:, (hp * NCH + c) * 64:(hp * NCH + c + 1) * 64], start=False, stop=True)
                        STT(P, Hsl(hp, c + 1), Hsl(hp, c), gC[:, hp * NCH + c:hp * NCH + c + 1],
                            ps[:, 0:64], ALU.mult, ALU.add)
                    yield
                for hp in range(2):
                    CP(P, "pool", rw_H[hp], Hsl(hp, NCH))
                y = aW
                for hp in range(2):
                    hsl = slice(hp * T, (hp + 1) * T)
                    pse, pso = P.psum(), P.psum()
                    for c in range(NCH):
                        cs = slice(c * 128, (c + 1) * 128)
                        gcs = slice(hp * T + c * 128, hp * T + (c + 1) * 128)
                        MM(P, pse[0:64, cs], Hbl(hp, c)[0:64, :], RhT[0:64, gcs])
                        MM(P, pso[64:128, cs], Hbl(hp, c)[64:128, :], RhT[64:128, gcs])
                    TT(P, "dve", y[0:64, hsl], pse[0:64, 0:T], Y0[0:64, hsl], ALU.add)
                    TT(P, "dve", y[64:128, hsl], pso[64:128, 0:T], Y0[64:128, hsl], ALU.add)
                    dbg(f"rw_y{hp}", y[:, hsl], 128, T, tok0, ntok)
                yield
                yb = A(W2, BF16, "yb")
                CP(P, "pool", yb, y)
                ps_m = P.psum()
                MM(P, ps_m, ones64b, yb)
                yc = A(W2, F32, "yc")
                STT(P, yc, ps_m, -1.0 / 64, y, ALU.mult, ALU.add)
                yield
                TT(P, "pool", yb, yc, yc, ALU.mult)
                ps_v = P.psum()
                MM(P, ps_v, ones64b, yb)
                rs = y
                rsqrt_act(rs, ps_v, 1.0 / 64, epsc[:, 1:2])
                yield
                TT(P, "dve", yc, yc, rs, ALU.mult)
                for hp in range(2):
                    hsl = slice(hp * T, (hp + 1) * T)
                    TS(P, "pool", yc[:, hsl], yc[:, hsl], col("ln_g", hp), ALU.mult, col("ln_b", hp), ALU.add)
                TT(P, "dve", yc, yc, bonus, ALU.add)
                TT(P, "dve", ycat[:, 0:W2], yc, gSW, ALU.mult)
                yield

            def lru_thread():
                R = Region(base_thr + SZ_RW, SZ_LRU)
                A = R.alloc
                xbuf = A(3 + T, F32, "xbuf")
                gt = A(T, F32, "gt")
                xc = A(T, F32, "xc")
                xcb = A(T, BF16, "xcb")
                st = []
                for j in range(4):
                    CP(P, "pool", xbuf[:, 0:3], lru_xc[j])
                    proj(1024 + j * 128, 128, lambda ps: CP(P, "act", xbuf[:, 3:3 + T], ps))
                    yield
                    proj(1536 + j * 128, 128, lambda ps: CP(P, "act", gt, ps))
                    CP(P, "pool", lru_xc[j], xbuf[:, T:T + 3])
                    yield
                    TS(P, "pool", xc, xbuf[:, 0:T], col("cw0", j), ALU.mult, col("cb", j), ALU.add)
                    for tap in range(1, 4):
                        STT(P, xc, xbuf[:, tap:tap + T], col(f"cw{tap}", j), xc, ALU.mult, ALU.add)
                    CP(P, "pool", xcb, xc)
                    yield
                    ps_r, ps_i = P.psum(), P.psum()
                    MM(P, ps_r[:, 0:T], wabd[:, j * 128:(j + 1) * 128], xcb)
                    MM(P, ps_i[:, 0:T], wxbd[:, j * 128:(j + 1) * 128], xcb)
                    gr = A(T, F32, f"gr{j}")
                    uu = A(T, F32, f"uu{j}")
                    ge = A(T, F32, f"ge{j}")
                    ACT(P, gr, ps_r[:, 0:T], AF.Sigmoid, bias=col("ba", j))
                    ACT(P, uu, ps_i[:, 0:T], AF.Sigmoid, bias=col("bx", j))
                    yield
                    TT(P, "dve", uu, uu, xc, ALU.mult)
                    TT(P, "pool", ge, gt, gt, ALU.mult)
                    TS(P, "pool", ge, ge, 0.044715, ALU.mult, 1.0, ALU.add)
                    TT(P, "dve", ge, ge, gt, ALU.mult)
                    ACT(P, ge, ge, AF.Sigmoid, scale=1.5957691216057308)
                    TT(P, "dve", ge, ge, gt, ALU.mult)
                    st.append((gr, uu, ge))
                    yield
                yield "B"
                av = A(T, F32, "av")
                a2 = A(T, F32, "a2")
                hh = A(T, F32, "hh")
                sqb = A(T, BF16, "sqb")
                for j in range(4):
                    gr, uu, ge = st[j]
                    ACT(P, av, gr, AF.Exp, scale=c1[:, j:j + 1])
                    ACT(P, a2, gr, AF.Exp, scale=c2[:, j:j + 1])
                    ACT(P, a2, a2, AF.Ln, bias=1.0, scale=-1.0)
                    ACT(P, a2, a2, AF.Exp, scale=0.5)
                    yield
                    TT(P, "dve", uu, uu, a2, ALU.mult)
                    SCAN(P, hh, av, uu, lru_h[:, j:j + 1])
                    CP(P, "pool", lru_h[:, j:j + 1], hh[:, T - 1:T])
                    yl = av
                    TT(P, "dve", yl, hh, ge, ALU.mult)
                    dbg(f"lru_y{j}", yl, 128, T, tok0, ntok)
                    yield
                    TT(P, "pool", sqb, yl, yl, ALU.mult)
                    ps_m = P.psum()
                    MM(P, ps_m[:, 0:T], ones64b, sqb)
                    rs = a2
                    rsqrt_act(rs, ps_m[:, 0:T], 1.0 / 64, epsc[:, 0:1])
                    STT(P, ycat3[:, 2 + j, :], yl, col("lng", j), rs, ALU.mult, ALU.mult)
                    yield

            def gla_thread():
                R = Region(base_thr + SZ_RW + SZ_LRU, SZ_GLA)
                A = R.alloc
                q = A(T, F32, "q")
                kg = A(T, F32, "kg")
                vg = [A(T, BF16, f"vg{i}") for i in range(2)]
                gg = [A(T, F32, f"gg{i}") for i in range(2)]
                gklo = A(T, BF16, "gklo")
                sgm = A(T, F32, "sgm")
                proj(2048, 128, lambda ps: CP(P, "act", q, ps))
                yield
                proj(2176, 128, lambda ps: CP(P, "act", kg, ps))
                yield
                proj(2304, 128, lambda ps: CP(P, "act", vg[0], ps))
                yield
                proj(2432, 128, lambda ps: CP(P, "act", vg[1], ps))
                yield
                proj(2560, 16, lambda ps: CP(P, "act", gklo[0:16, :], ps))
                yield
                for i in range(2):
                    def evg(ps, i=i):
                        ACT(P, sgm, ps, AF.Sigmoid)
                        TT(P, "dve", gg[i], ps, sgm, ALU.mult)
                    proj(2576 + 128 * i, 128, evg)
                    yield
                ps_gk = P.psum()
                MM(P, ps_gk[:, 0:T], gkup[0:16, :], gklo[0:16, :])
                la = A(T, F32, "la")
                ACT(P, la, ps_gk[:, 0:T], AF.Sigmoid, bias=col("gkb"))
                yield "B"
                ACT(P, la, la, AF.Ln)
                bc = A(T, F32, "bc")
                SCAN(P, bc, rmask[:, 0:T], la, 0.0)
                Eq = A(T, F32, "Eq")
                Ek = A(T, F32, "Ek")
                Ee = la
                ACT(P, Eq, bc, AF.Exp, scale=1.0 / 16)
                ACT(P, Ek, bc, AF.Exp, scale=-1.0 / 16)
                yield
                nb = A(NCH, F32, "nbg")
                TS(P, "pool", nb, bc.re("p (c t) -> p c t", c=NCH)[:, :, 127], 1.0 / 16, ALU.mult)
                for c in range(NCH):
                    ACT(P, Ee[:, c * 128:(c + 1) * 128], bc[:, c * 128:(c + 1) * 128], AF.Exp,
                        bias=nb[:, c:c + 1], scale=-1.0 / 16)
                gCg = A(NCH, F32, "gCg")
                CP(P, "pool", gCg, Eq.re("p (c t) -> p c t", c=NCH)[:, :, 127])
                yield
                qin = A(T, BF16, "qin")
                STT(P, qin, q, 32.0 ** -0.5, Eq, ALU.mult, ALU.mult)
                kin = [A(T, BF16, f"kin{i}") for i in range(2)]
                for i in range(2):
                    STT(P, kin[i], kg, par[:, i:i + 1], Ek, ALU.mult, ALU.mult)
                kend = A(T, BF16, "kend")
                TT(P, "pool", kend, kg, Ee, ALU.mult)
                yield
                GT = []
                for c in range(NCH):
                    pst = P.psum()
                    pstb = pst.bitcast(BF16)
                    cs = slice(c * 128, (c + 1) * 128)
                    TR(P, pstb[:, 0:128], vg[0][:, cs], identb)
                    TR(P, pstb[:, 128:256], vg[1][:, cs], identb)
                    TR(P, pstb[:, 256:384], kend[:, cs], identb)
                    gt_ = A(384, BF16, f"GT{c}")
                    CP(P, "act", gt_, pstb[:, 0:384])
                    GT.append(gt_)
                    yield
                ST = []
                for h in range(4):
                    rows = slice(64 * (h // 2), 64 * (h // 2) + 64)
                    ps = P.psum()
                    for c in range(NCH):
                        cs = slice(c * 128, (c + 1) * 128)
                        MM(P, ps[:, cs], kin[h % 2][rows, cs], qin[rows, cs])
                    st_ = A(T, BF16, f"ST{h}")
                    TT(P, "dve", st_, ps[:, 0:T], ui4[:, 0:T], ALU.mult)
                    ST.append(st_)
                    yield
                Sb = []
                Scur = A((NCH + 1) * 256, F32, "Scur")
                CP(P, "pool", Scur[:, 0:256], gla_S)
                for c in range(NCH):
                    sb_ = A(256, BF16, f"Sb{c}")
                    TT(P, "pool", sb_, Scur[:, c * 256:(c + 1) * 256], bmask, ALU.mult)
                    Sb.append(sb_)
                    ps = P.psum()
                    MM(P, ps[:, 0:256], GT[c][:, 256:384], GT[c][:, 0:256])
                    STT(P, Scur[:, (c + 1) * 256:(c + 2) * 256], Scur[:, c * 256:(c + 1) * 256], gCg[:, c:c + 1],
                        ps[:, 0:256], ALU.mult, ALU.add)
                    yield
                CP(P, "pool", gla_S, Scur[:, NCH * 256:(NCH + 1) * 256])
                o = A(T, F32, "o")
                ob = A(T, BF16, "ob")
                rs = A(T, F32, "rsg")
                for vp in range(2):
                    ps_in, ps_it = P.psum(), P.psum()
                    for s in range(2):
                        h = 2 * vp + s
                        orow = slice(64 * s, 64 * s + 64)
                        krow = slice(64 * vp, 64 * vp + 64)
                        for c in range(NCH):
                            cs = slice(c * 128, (c + 1) * 128)
                            MM(P, ps_in[orow, cs], GT[c][:, 64 * h:64 * h + 64], ST[h][:, cs])
                            MM(P, ps_it[orow, cs], Sb[c][krow, 64 * h:64 * h + 64], qin[krow, cs])
                    CP(P, "act", o, ps_it[:, 0:T])
                    TT(P, "dve", o, o, ps_in[:, 0:T], ALU.add)
                    dbg(f"gla_o{vp}", o, 128, T, tok0, ntok)
                    yield
                    TT(P, "pool", ob, o, o, ALU.mult)
                    ps_m = P.psum()
                    MM(P, ps_m[:, 0:T], ones64b, ob)
                    rsqrt_act(rs, ps_m[:, 0:T], 1.0 / 64, epsc[:, 0:1])
                    STT(P, o, o, col("gng"), rs, ALU.mult, ALU.mult)
                    TT(P, "dve", ycat3[:, 6 + vp, :], o, gg[vp], ALU.mult)
                    yield

            _skip = os.environ.get("K_SKIP", "").split(",")
            threads = [g_ for n_, g_ in (("rw", rw_thread), ("lru", lru_thread), ("gla", gla_thread)) if n_ not in _skip]
            threads = [g_() for g_ in threads]
            def run_greedy(ths):
                ready = {id(t_): 0.0 for t_ in ths}
                out_ = []
                while ths:
                    th = min(ths, key=lambda t_: ready[id(t_)])
                    P.step_t = 0.0
                    try:
                        v_ = next(th)
                    except StopIteration:
                        ths.remove(th)
                        continue
                    if P.step_t > 0:
                        ready[id(th)] = P.step_t
                    if v_ == "B":
                        ths.remove(th)
                        out_.append(th)
                return out_

            if os.environ.get("K_GREEDY", "0") == "1":
                atB = run_greedy(threads)
                if ti + 1 < NT:
                    m_load(ti + 1)
                    atB.append(m_norm(ti + 1))
                run_greedy(atB)
            else:
                atB = []
                while threads:
                    for th in list(threads):
                        try:
                            v_ = next(th)
                        except StopIteration:
                            threads.remove(th)
                            continue
                        if v_ == "B":
                            threads.remove(th)
                            atB.append(th)
                threads = atB
                if ti > 0 and os.environ.get("K_DEFER", "1") == "1":
                    m_outproj(ti - 1)
                if ti + 1 < NT:
                    m_load(ti + 1)
                    threads.append(m_norm(ti + 1))
                while threads:
                    for th in list(threads):
                        try:
                            next(th)
                        except StopIteration:
                            threads.remove(th)
            P.arena_off = base_thr
            if os.environ.get("K_DEFER", "1") != "1" and ti < NT - 1:
                m_outproj(ti)

            for kc in range(8):
                dbg(f"ycat{kc}", ycat3[:, kc, :], 128, T, tok0, ntok)

        m_outproj(NT - 1)

        if os.environ.get("K_BARRIER", "1") == "1":
            P.barrier()
        P.arena_off = persist_mark
        if last:
            gBf = P.alloc(D, F32, "gBf")
            P.dma("sp", gBf.ap, final_g.partition_broadcast(128), writes=[gBf])
        gB2 = P.alloc(D, F32, "gB2")
        P.dma("sp", gB2.ap, norm2_g[l:l + 1, :].partition_broadcast(128), writes=[gB2])
        wg_sb = P.alloc(8 * DFF, BF16, "wg")
        wu_sb = P.alloc(8 * DFF, BF16, "wu")
        wd_sb = P.alloc(NFC * D, BF16, "wd")
        for kc in range(8):
            for hf in range(2):
                sl_ = slice(hf * 1408, (hf + 1) * 1408)
                P.dma("pool", wg_sb.ap[:, kc * DFF + hf * 1408: kc * DFF + (hf + 1) * 1408],
                      w_gate[l, kc * 128:(kc + 1) * 128, sl_], writes=[wg_sb])
                P.dma("pool", wu_sb.ap[:, kc * DFF + hf * 1408: kc * DFF + (hf + 1) * 1408],
                      w_up[l, kc * 128:(kc + 1) * 128, sl_], writes=[wu_sb])
        for fc in range(NFC):
            P.dma("pool", wd_sb.ap[:, fc * D:(fc + 1) * D], w_down[l, fc * 128:(fc + 1) * 128, :], writes=[wd_sb])
        xTs = [[P.alloc(D, F32, f"fxT{p_}{b}") for b in range(NCH)] for p_ in range(2)]
        hnFs = [P.alloc(8 * T, BF16, f"fhnF{p_}") for p_ in range(2)]
        hF = P.alloc(NFC * T, BF16, "hF")
        hF3 = hF.re("p (k t) -> p k t", k=NFC)
        fhn = P.alloc(D, BF16, "fhn")
        sgt = [P.alloc(T, F32, f"sgt{i}") for i in range(2)]
        scs = [P.alloc(2, F32, f"fsc{i}") for i in range(4)]
        yos = [P.alloc(D, F32, f"yo{i}") for i in range(2)] if last else []
        sci = [0]

        def nsc():
            sci[0] += 1
            return scs[sci[0] % 4]

        def f_pro(ti):
            tok0 = ti * T
            xT = xTs[ti % 2]
            hnF3 = hnFs[ti % 2].re("p (k t) -> p k t", k=8)
            for b in range(NCH):
                P.dma("sp", xT[b].ap, xa_d[tok0 + b * 128: tok0 + (b + 1) * 128, :], reads=[xa_buf], writes=[xT[b]])
            for b in range(NCH):
                rms_block(xT[b], gB2, fhn, nsc())
                pst = P.psum()
                pstb = pst.bitcast(BF16)
                for kc in range(8):
                    TR(P, pstb[:, kc * 128:(kc + 1) * 128], fhn[:, kc * 128:(kc + 1) * 128], identb)
                CP(P, "act", hnF3[:, :, b * 128:(b + 1) * 128], pstb.re("p (k t) -> p k t", k=8))

        def f_gateup(ti):
            hnF3 = hnFs[ti % 2].re("p (k t) -> p k t", k=8)
            for fc in range(NFC):
                ps = P.psum()
                for kc in range(8):
                    MM(P, ps[:, 0:T], wg_sb[:, kc * DFF + fc * 128: kc * DFF + (fc + 1) * 128], hnF3[:, kc, :],
                       start=(kc == 0), stop=(kc == 7))
                for kc in range(8):
                    MM(P, ps[:, T:2 * T], wu_sb[:, kc * DFF + fc * 128: kc * DFF + (fc + 1) * 128], hnF3[:, kc, :],
                       start=(kc == 0), stop=(kc == 7))
                s_ = sgt[fc % 2]
                ACT(P, s_, ps[:, 0:T], AF.Silu)
                TT(P, "dve", hF3[:, fc, :], s_, ps[:, T:2 * T], ALU.mult)

        def f_down(ti):
            tok0 = ti * T
            xT = xTs[ti % 2]
            for b in range(NCH):
                for hf in range(2):
                    ps = P.psum()
                    for fc in range(NFC):
                        MM(P, ps, hF3[:, fc, b * 128:(b + 1) * 128],
                           wd_sb[:, fc * D + hf * 512: fc * D + (hf + 1) * 512], start=(fc == 0), stop=(fc == NFC - 1))
                    TT(P, "dve", xT[b][:, hf * 512:(hf + 1) * 512], xT[b][:, hf * 512:(hf + 1) * 512], ps, ALU.add)
                if last:
                    yo = yos[b % 2]
                    rms_block(xT[b], gBf, yo, nsc())
                    fin.append(P.dma("sp", out_d[tok0 + b * 128: tok0 + (b + 1) * 128, :], yo.ap, reads=[yo]))
                else:
                    P.dma("sp", xb_d[tok0 + b * 128: tok0 + (b + 1) * 128, :], xT[b].ap, reads=[xT[b]],
                          writes=[xb_buf])

        f_pro(0)
        for ti in range(NT):
            f_gateup(ti)
            if ti + 1 < NT:
                f_pro(ti + 1)
            f_down(ti)
    P.finish(fin)
    P.build()
    return nc, list(dbg_out.keys())


def make_in_map(inputs, xs, nlayers):
    m = {"x": np.ascontiguousarray(xs, dtype=np.float32)}
    for k in ["w_in", "w_out", "ffn_w_gate", "ffn_w_up", "ffn_w_down", "rw_w_up", "rw_a_up", "rw_g_up",
              "lru_wa", "lru_wx", "gla_gk_up", "norm1_g", "norm2_g"]:
        m[k] = np.ascontiguousarray(np.asarray(inputs[k], np.float32)[:nlayers])
    m["cols"] = np.stack([pack_cols(inputs, l) for l in range(nlayers)])
    m["final_norm_g"] = np.asarray(inputs["final_norm_g"], np.float32).reshape(1, D)
    for k, v in make_consts().items():
        m["c_" + k] = v
    return m


_CACHE = {}


def kernel(**inputs):
    x = np.asarray(inputs["x"], np.float32)
    B, S, _ = x.shape
    L = np.asarray(inputs["w_in"]).shape[0]
    key = (S, L)
    if key not in _CACHE:
        _CACHE[key] = build_program(S, L)[0]
    nc = _CACHE[key]
    in_maps = [make_in_map(inputs, x[c % B], L) for c in range(8)]
    res = run_bass_kernel_spmd(nc, in_maps, core_ids=list(range(8)))
    return np.stack([res.results[b]["out"] for b in range(B)], axis=0)
```

```python
import contextlib
import math
import os
import numpy as np
import concourse.bass as bass
import concourse.mybir as mybir
from concourse.bass_utils import run_bass_kernel_spmd

F32 = mybir.dt.float32
BF16 = mybir.dt.bfloat16
AF = mybir.ActivationFunctionType
ALU = mybir.AluOpType

CHUNK = 8000
SAMEQ = os.environ.get('K_SAMEQ', '1') == '1'
NDMASEM = 12

D = 1024
PIN = 2832
DFF = 2816
NFC = DFF // 128
EPS = 1e-6
RW_EPS = 64e-5
DEC = math.exp(-0.5)
T = 256
NCH = T // 128


class Buf:
    __slots__ = ("name", "writers", "readers", "t")

    def __init__(self, name=""):
        self.name = name
        self.writers = {}
        self.readers = {}
        self.t = 0.0


def _dep_kv(d):
    if d[0] == "e":
        return ("e", d[1], d[2] // CHUNK), d[2] % CHUNK + 1
    return ("d", d[1], d[2]), d[3]


def _merge(dst, src):
    for k, v in src.items():
        if dst.get(k, 0) < v:
            dst[k] = v


class Tile:
    __slots__ = ("ap", "buf")

    def __init__(self, ap, buf=None):
        self.ap = ap
        self.buf = buf if buf is not None else Buf()

    def __getitem__(self, k):
        return Tile(self.ap[k], self.buf)

    def bitcast(self, dt):
        return Tile(self.ap.bitcast(dt), self.buf)

    def re(self, s, **kw):
        return Tile(self.ap.rearrange(s, **kw), self.buf)


class Prog:
    ENG = ("pe", "act", "dve", "pool", "sp")

    def __init__(self, nc):
        self.nc = nc
        self.stack = contextlib.ExitStack()
        self.streams = {e: [] for e in self.ENG}
        self.count = {e: 0 for e in self.ENG}
        self.esems = {e: [] for e in self.ENG}
        self.dsems = {}
        self.dma_n = {e: 0 for e in self.ENG}
        self.dma_hist = {e: {} for e in self.ENG}
        self.waited = {e: {} for e in self.ENG}
        self.n_t = 0
        self.final_deps = []
        self.arena = None
        self.arena_off = 0
        self.arena_size = 0
        self.psb = []
        self.ps_i = 0
        self.live = []
        self.t_eng = {e: 0.0 for e in self.ENG}
        self.step_t = 0.0

    def init_mem(self, arena_f32_cols):
        self.arena_size = arena_f32_cols
        self.arena = self.stack.enter_context(
            self.nc.sbuf_tensor("arena", [128, arena_f32_cols], F32))
        for i in range(8):
            t = self.stack.enter_context(self.nc.psum_tensor(f"psb{i}", [128, 512], F32))
            self.psb.append(Tile(t[:, :], Buf(f"ps{i}")))

    def alloc(self, free_elems, dt=F32, name=""):
        ncol = free_elems if dt == F32 else (free_elems + 1) // 2
        if self.arena_off + ncol > self.arena_size:
            raise RuntimeError(f"arena overflow at {name}: {self.arena_off}+{ncol}>{self.arena_size}")
        s0, s1 = self.arena_off, self.arena_off + ncol
        self.arena_off += ncol
        self.hi = max(getattr(self, "hi", 0), s1)
        keep, over = [], []
        for ent in self.live:
            (over if (ent[0] < s1 and s0 < ent[1]) else keep).append(ent)
        if len(over) == 1 and over[0][0] == s0 and over[0][1] == s1 and over[0][2] == (dt, free_elems):
            return over[0][3]
        ap = self.arena[:, s0:s1]
        if dt != F32:
            ap = ap.bitcast(dt)[:, 0:free_elems]
        buf = Buf(name)
        for ent in over:
            _merge(buf.writers, ent[3].buf.writers)
            _merge(buf.readers, ent[3].buf.readers)
        t = Tile(ap, buf)
        keep.append((s0, s1, (dt, free_elems), t))
        self.live = keep
        return t

    def psum(self):
        t = self.psb[self.ps_i]
        self.ps_i = (self.ps_i + 1) % 8
        return t

    def _deps(self, e, reads, writes, is_dma=False):
        need = {}
        for r in reads:
            _merge(need, r.writers)
        for w in writes:
            _merge(need, w.writers)
            _merge(need, w.readers)
        out = []
        for key, val in need.items():
            if key[0] == "e" and key[1] == e and (e == "pe" or not SAMEQ):
                continue
            if self.waited[e].get(key, 0) >= val:
                continue
            self.waited[e][key] = val
            out.append((key, val))
        return out

    def _record(self, d, reads, writes, is_dma):
        k, v = _dep_kv(d)
        for w in writes:
            if is_dma:
                w.writers = {kk: vv for kk, vv in w.writers.items() if kk[0] == "d"}
            else:
                w.writers = {}
            w.writers[k] = v
            w.readers = {}
        for r in reads:
            if r.readers.get(k, 0) < v:
                r.readers[k] = v

    def _sem(self, key):
        if key[0] == "e":
            return self.esems[key[1]][key[2]]
        return self.dsems[(key[1], key[2])]

    def _est(self, e, reads, writes, cost):
        t0 = self.t_eng[e]
        for b in reads:
            if b.t > t0:
                t0 = b.t
        for b in writes:
            if b.t > t0:
                t0 = b.t
        self.t_eng[e] = t0 + cost
        fin = t0 + cost + 0.25
        for b in writes:
            b.t = fin
        if fin > self.step_t:
            self.step_t = fin

    def op(self, e, fn, reads=(), writes=(), cost=None):
        if cost is None:
            n = 256
            for w in writes:
                if isinstance(w, Tile):
                    n = 1
                    for d_ in w.ap.shape[1:]:
                        n *= d_
                    break
            cost = {"act": 0.2 + n / 1150.0, "dve": 0.15 + n / 960.0, "pool": 0.2 + n / 480.0,
                    "pe": 0.05 + n / 2400.0}.get(e, 1.0)
        reads = [r.buf if isinstance(r, Tile) else r for r in reads]
        writes = [w.buf if isinstance(w, Tile) else w for w in writes]
        self._est(e, reads, writes, cost)
        waits = self._deps(e, reads, writes)
        idx = self.count[e]
        self.count[e] += 1
        mykey = ("e", e, idx // CHUNK)

        def emit(eng, waits=waits, fn=fn, mykey=mykey):
            for k, v in waits:
                eng.wait_ge(self._sem(k), v)
            fn(eng).then_inc(self._sem(mykey), 1)

        self.streams[e].append(emit)
        d = ("e", e, idx)
        self._record(d, reads, writes, False)
        return d

    def dma(self, q, out, in_, reads=(), writes=(), **kw):
        reads = [r.buf if isinstance(r, Tile) else r for r in reads]
        writes = [w.buf if isinstance(w, Tile) else w for w in writes]
        self._est(q, reads, writes, 2.0)
        self.t_eng[q] -= 1.9
        waits = self._deps(q, reads, writes, True)
        n = self.dma_n[q]
        self.dma_n[q] += 1
        slot = n % NDMASEM
        prev = self.dma_hist[q].get(slot, 0)
        val = prev + 16
        self.dma_hist[q][slot] = val
        key = ("d", q, slot)
        if prev > 0 and self.waited[q].get(key, 0) < prev:
            waits = waits + [(key, prev)]
            self.waited[q][key] = prev

        def emit(eng, waits=waits, key=key):
            for k, v in waits:
                eng.wait_ge(self._sem(k), v)
            eng.dma_start(out=out, in_=in_, **kw).then_inc(self._sem(key), 16)

        self.streams[q].append(emit)
        d = ("d", q, slot, val)
        self._record(d, reads, writes, True)
        return d

    def barrier(self):
        keys = []
        for e in self.ENG:
            if self.count[e] > 0:
                idx = self.count[e] - 1
                keys.append((("e", e, idx // CHUNK), idx % CHUNK + 1))
            for slot, val in self.dma_hist[e].items():
                keys.append((("d", e, slot), val))
        for f in self.ENG:
            mine = []
            for k, v in keys:
                if k[0] == "e" and k[1] == f:
                    continue
                if self.waited[f].get(k, 0) >= v:
                    continue
                self.waited[f][k] = v
                mine.append((k, v))

            def emit(eng, mine=mine):
                for k, v in mine:
                    eng.wait_ge(self._sem(k), v)

            self.streams[f].append(emit)

    def finish(self, deps):
        self.final_deps = list(deps)

    def build(self):
        nc = self.nc
        st = self.stack
        for e in self.ENG:
            nsem = (self.count[e] + CHUNK - 1) // CHUNK
            self.esems[e] = [st.enter_context(nc.semaphore(f"s_{e}_{i}")) for i in range(nsem)]
            nd = min(self.dma_n[e], NDMASEM)
            for s in range(nd):
                self.dsems[(e, s)] = st.enter_context(nc.semaphore(f"d_{e}_{s}"))
        fin = []
        for d in self.final_deps:
            if d[0] == "e":
                fin.append((("e", d[1], d[2] // CHUNK), d[2] % CHUNK + 1))
            else:
                fin.append((("d", d[1], d[2]), d[3]))
        block = st.enter_context(nc.Block())
        streams = self.streams

        @block.tensor
        def _(eng):
            for f in streams["pe"]:
                f(eng)

        @block.scalar
        def _(eng):
            for f in streams["act"]:
                f(eng)

        @block.vector
        def _(eng):
            for f in streams["dve"]:
                f(eng)

        @block.gpsimd
        def _(eng):
            for f in streams["pool"]:
                f(eng)

        @block.sync
        def _(eng):
            for f in streams["sp"]:
                f(eng)
            for k, v in fin:
                eng.wait_ge(self._sem(k), v)

        st.close()


def _ap(x):
    return x.ap if isinstance(x, Tile) else x


def _tl(*xs):
    return [x for x in xs if isinstance(x, Tile)]


def ACT(P, out, in_, func, bias=None, scale=None, accum=None):
    kw = {}
    if bias is not None:
        kw["bias"] = _ap(bias)
    if scale is not None:
        kw["scale"] = _ap(scale)
    if accum is not None:
        kw["accum_out"] = _ap(accum)
    P.op("act", lambda e: e.activation(out=out.ap, in_=in_.ap, func=func, **kw),
         reads=_tl(in_, bias, scale), writes=_tl(out, accum))


def TT(P, eng, out, a, b, op):
    P.op(eng, lambda e: e.tensor_tensor(out=out.ap, in0=a.ap, in1=b.ap, op=op),
         reads=_tl(a, b), writes=[out])


def TS(P, eng, out, a, s1, op0, s2=None, op1=None):
    if op1 is None:
        P.op(eng, lambda e: e.tensor_scalar(out=out.ap, in0=a.ap, scalar1=_ap(s1), scalar2=None, op0=op0),
             reads=_tl(a, s1), writes=[out])
    else:
        P.op(eng, lambda e: e.tensor_scalar(out=out.ap, in0=a.ap, scalar1=_ap(s1), scalar2=_ap(s2),
                                            op0=op0, op1=op1),
             reads=_tl(a, s1, s2), writes=[out])


def STT(P, out, in0, scalar, in1, op0, op1):
    P.op("dve", lambda e: e.scalar_tensor_tensor(out=out.ap, in0=in0.ap, scalar=_ap(scalar), in1=in1.ap,
                                                 op0=op0, op1=op1),
         reads=_tl(in0, scalar, in1), writes=[out])


def CP(P, eng, out, in_):
    if eng == "act":
        P.op("act", lambda e: e.activation(out=out.ap, in_=in_.ap, func=AF.Copy), reads=[in_], writes=[out])
    else:
        P.op(eng, lambda e: e.tensor_copy(out=out.ap, in_=in_.ap), reads=[in_], writes=[out])


def MM(P, out, lhsT, rhs, start=True, stop=True):
    P.op("pe", lambda e: e.matmul(out.ap, lhsT=lhsT.ap, rhs=rhs.ap, start=start, stop=stop),
         reads=[lhsT, rhs], writes=[out])


def TR(P, out, in_, ident):
    P.op("pe", lambda e: e.transpose(out=out.ap, in_=in_.ap, identity=ident.ap),
         reads=[in_, ident], writes=[out])


def SCAN(P, out, d0, d1, init):
    P.op("dve", lambda e: e.tensor_tensor_scan(out=out.ap, data0=d0.ap, data1=d1.ap, initial=_ap(init),
                                               op0=ALU.mult, op1=ALU.add),
         reads=_tl(d0, d1, init), writes=[out])


def MEMSET(P, eng, out, val):
    P.op(eng, lambda e: e.memset(out.ap, val), writes=[out])


COLS = {}


def _col_layout():
    names = [("mu", 8), ("w0", 2), ("a0", 2), ("k_k", 2), ("k_a", 2), ("r_k", 2), ("ln_g", 2), ("ln_b", 2),
             ("cw0", 4), ("cw1", 4), ("cw2", 4), ("cw3", 4), ("cb", 4), ("ba", 4), ("bx", 4), ("lam", 4),
             ("lng", 4), ("gkb", 1), ("gng", 1)]
    off = 0
    for n, c in names:
        COLS[n] = (off, c)
        off += c
    return off


NCOL = _col_layout()


def pack_cols(inp, l):
    out = np.zeros((128, NCOL), np.float32)

    def put(name, vec):
        o, c = COLS[name]
        out[:, o:o + c] = np.asarray(vec, np.float32).reshape(c, 128).T

    put("mu", inp["rw_mu"][l])
    put("w0", inp["rw_w0"][l])
    put("a0", inp["rw_a0"][l])
    put("k_k", inp["rw_k_k"][l])
    put("k_a", inp["rw_k_a"][l])
    put("r_k", inp["rw_r_k"][l].reshape(-1))
    put("ln_g", inp["rw_ln_g"][l])
    put("ln_b", inp["rw_ln_b"][l])
    for j in range(4):
        put(f"cw{j}", inp["lru_conv_w"][l, j])
    put("cb", inp["lru_conv_b"][l])
    put("ba", inp["lru_ba"][l])
    put("bx", inp["lru_bx"][l])
    put("lam", inp["lru_lam"][l])
    put("lng", inp["lru_norm_g"][l])
    put("gkb", inp["gla_gk_b"][l])
    put("gng", np.concatenate([inp["gla_norm_g"][l], inp["gla_norm_g"][l]]))
    return out


def make_consts():
    c = {}
    idx = np.arange(128)
    su = (idx[:, None] < idx[None, :]).astype(np.float32)
    ui = (idx[:, None] <= idx[None, :]).astype(np.float32)
    sl = (idx[:, None] > idx[None, :]).astype(np.float32)
    c["ident"] = np.eye(128, dtype=np.float32)
    c["cmask"] = np.concatenate([su, su, ui, ui], axis=1)
    c["sl4"] = np.concatenate([sl] * 4, axis=1)
    c["ui4"] = np.concatenate([ui] * NCH, axis=1)
    ob = np.zeros((128, 128), np.float32)
    ob[:64, :64] = 1
    ob[64:, 64:] = 1
    c["ones64"] = ob
    rm = np.ones((128, 2 * T), np.float32)
    rm[:, ::128] = 0
    c["rmask"] = rm
    bm = np.zeros((128, 256), np.float32)
    for h in range(4):
        bm[32 * h:32 * h + 32, 64 * h:64 * h + 64] = 1
    c["bmask"] = bm
    par = np.zeros((128, 2), np.float32)
    for h in range(4):
        par[32 * h:32 * h + 32, h % 2] = 1
    c["par"] = par
    return c


CONST_SHAPES = {"ident": 128, "cmask": 512, "sl4": 512, "ui4": T, "ones64": 128,
                "rmask": 2 * T, "bmask": 256, "par": 2}


def build_program(ntok, nlayers, debug=()):
    nc = bass.Bass("TRN2", target_bir_lowering=False)
    NT = ntok // T
    L = nlayers

    def din(name, shape):
        return nc.dram_tensor(name, list(shape), F32, kind="ExternalInput").ap()

    x_in = din("x", [ntok, D])
    w_in = din("w_in", [L, D, PIN])
    w_out = din("w_out", [L, D, D])
    w_gate = din("ffn_w_gate", [L, D, DFF])
    w_up = din("ffn_w_up", [L, D, DFF])
    w_down = din("ffn_w_down", [L, DFF, D])
    rw_w_up = din("rw_w_up", [L, 64, 256])
    rw_a_up = din("rw_a_up", [L, 64, 256])
    rw_g_up = din("rw_g_up", [L, 128, 256])
    lru_wa = din("lru_wa", [L, 8, 64, 64])
    lru_wx = din("lru_wx", [L, 8, 64, 64])
    gk_up = din("gla_gk_up", [L, 16, 128])
    cols_d = din("cols", [L, 128, NCOL])
    norm1_g = din("norm1_g", [L, D])
    norm2_g = din("norm2_g", [L, D])
    final_g = din("final_norm_g", [1, D])
    cdram = {k: din("c_" + k, [128, n]) for k, n in CONST_SHAPES.items()}
    out_d = nc.dram_tensor("out", [ntok, D], F32, kind="ExternalOutput").ap()
    xa_d = nc.dram_tensor("xa_scr", [ntok, D], F32).ap()
    xb_d = nc.dram_tensor("xb_scr", [ntok, D], F32).ap()
    xa_buf, xb_buf = Buf("xa"), Buf("xb")
    dbg_out = {}

    P = Prog(nc)
    P.init_mem(52500)
    fin = []

    def dbg(name, tile, rows, cols, tok0=None, total_cols=None):
        if name not in debug:
            return
        if name not in dbg_out:
            tc = total_cols if total_cols is not None else cols
            dbg_out[name] = nc.dram_tensor("dbg_" + name, [rows, tc], F32, kind="ExternalOutput").ap()
        dst = dbg_out[name]
        c0 = tok0 if tok0 is not None else 0
        fin.append(P.dma("pool", dst[0:rows, c0:c0 + cols], tile.ap, reads=[tile]))

    cst = {}
    for k, n in CONST_SHAPES.items():
        if k in ("rmask",):
            cst[k] = P.alloc(n, BF16, "c_" + k)
            P.dma("pool", cst[k].ap, cdram[k][:, :], writes=[cst[k]])
        else:
            cst[k] = P.alloc(n, F32, "c_" + k)
            P.dma("sp", cst[k].ap, cdram[k][:, :], writes=[cst[k]])
    identf = cst["ident"]
    identb = P.alloc(128, BF16, "identb")
    CP(P, "pool", identb, identf)
    ident4b = P.alloc(512, BF16, "ident4b")
    for i in range(4):
        CP(P, "pool", ident4b[:, i * 128:(i + 1) * 128], identf)
    ones64b = P.alloc(128, BF16, "ones64b")
    CP(P, "pool", ones64b, cst["ones64"])
    cmask, sl4, ui4, rmask, bmask, par = (cst[k] for k in ("cmask", "sl4", "ui4", "rmask", "bmask", "par"))
    epsc = P.alloc(2, F32, "epsc")
    MEMSET(P, "pool", epsc[:, 0:1], EPS)
    MEMSET(P, "pool", epsc[:, 1:2], RW_EPS)

    rw_carry = P.alloc(8, F32, "rw_carry")
    lru_xc = [P.alloc(3, F32, f"lru_xc{j}") for j in range(4)]
    lru_h = P.alloc(4, F32, "lru_h")
    rw_H = [P.alloc(64, F32, f"rwH{hp}") for hp in range(2)]
    gla_S = P.alloc(256, F32, "glaS")
    persist_mark = P.arena_off

    def rms_block(xblk, gB, hn_out, sc=None):
        if sc is None:
            ssq = P.alloc(1, F32, "ssq")
            rstd = P.alloc(1, F32, "rstd")
        else:
            ssq, rstd = sc[:, 0:1], sc[:, 1:2]
        ACT(P, hn_out, xblk, AF.Square, accum=ssq)
        ACT(P, rstd, ssq, AF.Ln, bias=epsc[:, 0:1], scale=1.0 / D)
        ACT(P, rstd, rstd, AF.Exp, scale=-0.5)
        STT(P, hn_out, xblk, rstd[:, 0:1], gB, ALU.mult, ALU.mult)

    for l in range(L):
        src_d, src_buf = (x_in, None) if l == 0 else (xb_d, xb_buf)
        last = l == L - 1
        if os.environ.get("K_BARRIER", "1") == "1":
            P.barrier()
        P.arena_off = persist_mark
        colsT = P.alloc(NCOL, F32, "cols")
        P.dma("sp", colsT.ap, cols_d[l], writes=[colsT])

        def col(name, j=0, rows=slice(0, 128)):
            o, c = COLS[name]
            return colsT[rows, o + j:o + j + 1]

        dcol = P.alloc(24, F32, "dcol")
        o_mu = COLS["mu"][0]
        omm = dcol[:, 0:8]
        TS(P, "pool", omm, colsT[:, o_mu:o_mu + 8], -1.0, ALU.mult, 1.0, ALU.add)
        o_ka = COLS["k_a"][0]
        omka = dcol[:, 8:10]
        TS(P, "pool", omka, colsT[:, o_ka:o_ka + 2], -1.0, ALU.mult, 1.0, ALU.add)
        o_lam = COLS["lam"][0]
        c1 = dcol[:, 10:14]
        c2 = dcol[:, 14:18]
        ACT(P, c1, colsT[:, o_lam:o_lam + 4], AF.Exp, scale=-1.0)
        ACT(P, c1, c1, AF.Ln, bias=1.0)
        TS(P, "pool", c2, c1, -16.0, ALU.mult)
        TS(P, "pool", c1, c1, -8.0, ALU.mult)
        MEMSET(P, "pool", rw_carry, 0.0)
        for j in range(4):
            MEMSET(P, "pool", lru_xc[j], 0.0)
        MEMSET(P, "pool", lru_h, 0.0)
        for hp in range(2):
            MEMSET(P, "pool", rw_H[hp], 0.0)
        MEMSET(P, "pool", gla_S, 0.0)

        gB1 = P.alloc(D, F32, "gB1")
        P.dma("sp", gB1.ap, norm1_g[l:l + 1, :].partition_broadcast(128), writes=[gB1])
        w_in_sb = P.alloc(8 * PIN, BF16, "w_in")
        for kc in range(8):
            for hf in range(2):
                P.dma("pool", w_in_sb.ap[:, kc * PIN + hf * 1416: kc * PIN + (hf + 1) * 1416],
                      w_in[l, kc * 128:(kc + 1) * 128, hf * 1416:(hf + 1) * 1416], writes=[w_in_sb])
        w_out_sb = P.alloc(8 * D, BF16, "w_out")
        for kc in range(8):
            P.dma("pool", w_out_sb.ap[:, kc * D:(kc + 1) * D], w_out[l, kc * 128:(kc + 1) * 128, :],
                  writes=[w_out_sb])
        wa_up = P.alloc(256, BF16, "wa_up")
        P.dma("pool", wa_up.ap[0:64, :], rw_w_up[l], writes=[wa_up])
        P.dma("pool", wa_up.ap[64:128, :], rw_a_up[l], writes=[wa_up])
        g_up = P.alloc(256, BF16, "g_up")
        P.dma("pool", g_up.ap, rw_g_up[l], writes=[g_up])
        gkup = P.alloc(128, BF16, "gkup")
        P.dma("pool", gkup.ap[0:16, :], gk_up[l], writes=[gkup])
        wabd = P.alloc(4 * 128, BF16, "wabd")
        wxbd = P.alloc(4 * 128, BF16, "wxbd")
        MEMSET(P, "pool", wabd, 0.0)
        MEMSET(P, "pool", wxbd, 0.0)
        for j in range(4):
            for s in range(2):
                P.dma("pool", wabd.ap[64 * s:64 * s + 64, j * 128 + 64 * s: j * 128 + 64 * s + 64],
                      lru_wa[l, 2 * j + s], writes=[wabd])
                P.dma("pool", wxbd.ap[64 * s:64 * s + 64, j * 128 + 64 * s: j * 128 + 64 * s + 64],
                      lru_wx[l, 2 * j + s], writes=[wxbd])
        mbbd = [[P.alloc(128, F32, f"mbbd{hp}{c}") for c in range(NCH)] for hp in range(2)]
        for hp in range(2):
            for c in range(NCH):
                MEMSET(P, "pool", mbbd[hp][c], 0.0)
        xTs = [[P.alloc(D, F32, f"xT{p_}{b}") for b in range(NCH)] for p_ in range(2)]
        hnF = P.alloc(8 * T, BF16, "hnF")
        hnF3 = hnF.re("p (k t) -> p k t", k=8)
        ycat = P.alloc(8 * T, BF16, "ycat")
        ycat3 = ycat.re("p (k t) -> p k t", k=8)
        mscs = [P.alloc(2, F32, f"msc{i}") for i in range(4)]
        tile_mark = P.arena_off

        def m_load(ti_):
            rd = [src_buf] if src_buf is not None else []
            for b in range(NCH):
                P.dma("sp", xTs[ti_ % 2][b].ap, src_d[ti_ * T + b * 128: ti_ * T + (b + 1) * 128, :], reads=rd,
                      writes=[xTs[ti_ % 2][b]])

        def m_norm(ti_):
            save = P.arena_off
            P.arena_off = tile_mark
            hn = P.alloc(D, BF16, "hn")
            P.arena_off = save
            for b in range(NCH):
                rms_block(xTs[ti_ % 2][b], gB1, hn, mscs[(2 * ti_ + b) % 4])
                yield
                pst = P.psum()
                pstb = pst.bitcast(BF16)
                for kc in range(8):
                    TR(P, pstb[:, kc * 128:(kc + 1) * 128], hn[:, kc * 128:(kc + 1) * 128], identb)
                CP(P, "act", hnF3[:, :, b * 128:(b + 1) * 128], pstb.re("p (k t) -> p k t", k=8))
                yield

        def m_outproj(ti_):
            xT_ = xTs[ti_ % 2]
            for b in range(NCH):
                for hf in range(2):
                    ps = P.psum()
                    for kc in range(8):
                        MM(P, ps, ycat3[:, kc, b * 128:(b + 1) * 128],
                           w_out_sb[:, kc * D + hf * 512: kc * D + (hf + 1) * 512], start=(kc == 0), stop=(kc == 7))
                    TT(P, "dve", xT_[b][:, hf * 512:(hf + 1) * 512], xT_[b][:, hf * 512:(hf + 1) * 512], ps, ALU.add)
                P.dma("sp", xa_d[ti_ * T + b * 128: ti_ * T + (b + 1) * 128, :], xT_[b].ap, reads=[xT_[b]],
                      writes=[xa_buf])

        m_load(0)
        for _ in m_norm(0):
            pass

        for ti in range(NT):
            P.arena_off = tile_mark
            tok0 = ti * T
            xT = xTs[ti % 2]

            def proj(c0, ncols, evac):
                ps = P.psum()
                for kc in range(8):
                    MM(P, ps[0:ncols, 0:T], w_in_sb[:, kc * PIN + c0: kc * PIN + c0 + ncols], hnF3[:, kc, :],
                       start=(kc == 0), stop=(kc == 7))
                evac(ps[0:ncols, 0:T])


            base_thr = P.arena_off

            class Region:
                def __init__(self, start, size):
                    self.off = start
                    self.end = start + size

                def alloc(self, n, dt=F32, name=""):
                    save = P.arena_off
                    P.arena_off = self.off
                    t = P.alloc(n, dt, name)
                    self.off = P.arena_off
                    P.arena_off = save
                    if self.off > self.end:
                        raise RuntimeError(f"region overflow {name} {self.off}>{self.end}")
                    return t

            SZ_RW, SZ_LRU, SZ_GLA = 14900, 4900, 5780

            def rsqrt_act(out, in_, scale, bias):
                ACT(P, out, in_, AF.Ln, bias=bias, scale=scale)
                ACT(P, out, out, AF.Exp, scale=-0.5)

            def rw_thread():
                R = Region(base_thr, SZ_RW)
                A = R.alloc
                W2 = 2 * T
                NC2 = 2 * NCH
                ptmp = A(1 + T, F32, "ptmp")
                ltmp = A(T, F32, "ltmp")
                rW = A(W2, F32, "rW")
                kW = A(W2, F32, "kW")
                vW = A(W2, F32, "vW")
                waT = A(T, F32, "waT")
                gloT = A(T, F32, "gloT")
                dest = [rW[:, 0:T], rW[:, T:W2], kW[:, 0:T], kW[:, T:W2], vW[:, 0:T], vW[:, T:W2], waT, gloT]
                for gi in range(8):
                    def ev(ps, gi=gi):
                        CP(P, "act", ptmp[:, 1:1 + T], ps)
                        CP(P, "pool", ptmp[:, 0:1], rw_carry[:, gi:gi + 1])
                        TS(P, "dve", ltmp, ptmp[:, 0:T], col("mu", gi), ALU.mult)
                        STT(P, dest[gi], ptmp[:, 1:1 + T], omm[:, gi:gi + 1], ltmp, ALU.mult, ALU.add)
                        CP(P, "pool", rw_carry[:, gi:gi + 1], ptmp[:, T:T + 1])
                    proj(gi * 128, 128, ev)
                    yield
                wab = A(T, BF16, "wab")
                ACT(P, wab[0:64, :], waT[0:64, :], AF.Tanh)
                CP(P, "pool", wab[64:128, :], waT[64:128, :])
                sgl = A(T, BF16, "sgl")
                ACT(P, sgl, gloT, AF.Sigmoid)
                yield
                sgW = A(W2, F32, "sgW")
                aW = A(W2, F32, "aW")
                gSW = A(W2, F32, "gSW")
                for hp in range(2):
                    hsl = slice(hp * T, (hp + 1) * T)
                    ps_w, ps_a, ps_g = P.psum(), P.psum(), P.psum()
                    MM(P, ps_w[:, 0:T], wa_up[0:64, hp * 128:(hp + 1) * 128], wab[0:64, :])
                    MM(P, ps_a[:, 0:T], wa_up[64:128, hp * 128:(hp + 1) * 128], wab[64:128, :])
                    MM(P, ps_g[:, 0:T], g_up[:, hp * 128:(hp + 1) * 128], sgl)
                    ACT(P, sgW[:, hsl], ps_w[:, 0:T], AF.Sigmoid, bias=col("w0", hp))
                    ACT(P, aW[:, hsl], ps_a[:, 0:T], AF.Sigmoid, bias=col("a0", hp))
                    CP(P, "dve", gSW[:, hsl], ps_g[:, 0:T])
                    yield
                yield "B"
                css = A(W2, F32, "css")
                SCAN(P, css, rmask, sgW, 0.0)
                cse = A(W2, F32, "cse")
                TT(P, "pool", cse, css, sgW, ALU.subtract)
                E1 = A(W2, F32, "E1")
                E0 = cse
                Einv = A(W2, F32, "Einv")
                Eend = sgW
                nb = A(NC2, F32, "nb")
                TS(P, "pool", nb, css.re("p (c t) -> p c t", c=NC2)[:, :, 127], -DEC, ALU.mult)
                ACT(P, E1, css, AF.Exp, scale=-DEC)
                ACT(P, Einv, css, AF.Exp, scale=DEC)
                yield
                for c in range(NC2):
                    ACT(P, Eend[:, c * 128:(c + 1) * 128], css[:, c * 128:(c + 1) * 128], AF.Exp,
                        bias=nb[:, c:c + 1], scale=DEC)
                ACT(P, E0, cse, AF.Exp, scale=-DEC)
                gC = A(NC2, F32, "gC")
                CP(P, "pool", gC, E1.re("p (c t) -> p c t", c=NC2)[:, :, 127])
                yield
                kk = A(W2, F32, "kk")
                sqk = A(W2, BF16, "sqk")
                for hp in range(2):
                    hsl = slice(hp * T, (hp + 1) * T)
                    ACT(P, sqk[:, hsl], kW[:, hsl], AF.Square, scale=col("k_k", hp))
                ps_n = P.psum()
                MM(P, ps_n, ones64b, sqk)
                rn = A(W2, F32, "rn")
                rsqrt_act(rn, ps_n, 1.0, 1e-24)
                yield
                for hp in range(2):
                    hsl = slice(hp * T, (hp + 1) * T)
                    STT(P, kk[:, hsl], kW[:, hsl], col("k_k", hp), rn[:, hsl], ALU.mult, ALU.mult)
                kmod = A(W2, F32, "kmod")
                for hp in range(2):
                    hsl = slice(hp * T, (hp + 1) * T)
                    TS(P, "dve", kmod[:, hsl], aW[:, hsl], col("k_a", hp), ALU.mult, omka[:, hp:hp + 1], ALU.add)
                TT(P, "dve", kmod, kmod, kW, ALU.mult)
                yield
                bvec = A(W2, F32, "bvec")
                TT(P, "dve", bvec, kk, aW, ALU.mult)
                rkb = sqk
                for hp in range(2):
                    hsl = slice(hp * T, (hp + 1) * T)
                    STT(P, rkb[:, hsl], rW[:, hsl], col("r_k", hp), kmod[:, hsl], ALU.mult, ALU.mult)
                ps_b = P.psum()
                MM(P, ps_b, ones64b, rkb)
                bonus = A(W2, F32, "bonus")
                TT(P, "dve", bonus, ps_b, vW, ALU.mult)
                for hp in range(2):
                    hsl = slice(hp * T, (hp + 1) * T)
                    TS(P, "pool", bonus[:, hsl], bonus[:, hsl], col("ln_b", hp), ALU.add)
                yield
                Rt = A(W2, BF16, "Rt")
                At = A(W2, BF16, "At")
                Bt = A(W2, BF16, "Bt")
                Kt = A(W2, BF16, "Kt")
                Kh = A(W2, BF16, "Kh")
                Bh = A(W2, BF16, "Bh")
                vb = A(W2, BF16, "vb")
                TT(P, "dve", Rt, rW, E1, ALU.mult)
                STT(P, At, kk, -1.0, E0, ALU.mult, ALU.mult)
                TT(P, "dve", Bt, bvec, Einv, ALU.mult)
                TT(P, "dve", Kt, kmod, Einv, ALU.mult)
                yield
                TT(P, "dve", Kh, kmod, Eend, ALU.mult)
                TT(P, "pool", Bh, bvec, Eend, ALU.mult)
                CP(P, "pool", vb, vW)
                yield
                TMs = []
                for hp in range(2):
                    pst = P.psum()
                    pstb = pst.bitcast(BF16)
                    for c in range(NCH):
                        for i, src_ in enumerate([vb, Kh, Bh, At]):
                            TR(P, pstb[:, (c * 4 + i) * 128:(c * 4 + i + 1) * 128],
                               src_[:, hp * T + c * 128: hp * T + (c + 1) * 128], identb)
                    tm = A(NCH * 512, BF16, f"TM{hp}")
                    CP(P, "act", tm, pstb[:, 0:NCH * 512])
                    TMs.append(tm)
                    yield

                def TMc(hp, c):
                    return TMs[hp][:, c * 512:(c + 1) * 512]
                NJ = 2 * NCH
                NJ2 = 2 * NJ
                al = [t_.bitcast(BF16) for t_ in (kk, kmod, bvec, rn, css, cse, E1)]
                P0, P0T = al[0], al[1]
                Aall = [None] * NJ2
                for hp in range(2):
                    for s in range(2):
                        rows = slice(64 * s, 64 * s + 64)
                        ps_p0 = P.psum()
                        for c in range(NCH):
                            j = hp * NJ + s * NCH + c
                            cs = slice(hp * T + c * 128, hp * T + (c + 1) * 128)
                            ps = P.psum()
                            MM(P, ps[:, 0:128], Bt[rows, cs], At[rows, cs])
                            MM(P, ps[:, 128:256], Kt[rows, cs], At[rows, cs])
                            MM(P, ps[:, 256:384], Bt[rows, cs], Rt[rows, cs])
                            MM(P, ps[:, 384:512], Kt[rows, cs], Rt[rows, cs])
                            aa = A(512, BF16, f"Aall{j}")
                            TT(P, "dve", aa, ps, cmask, ALU.mult)
                            Aall[j] = aa
                            CP(P, "act", P0T[:, j * 128:(j + 1) * 128], aa[:, 0:128])
                            MM(P, ps_p0[:, c * 128:(c + 1) * 128], At[rows, cs], Bt[rows, cs])
                        j0 = hp * NJ + s * NCH
                        TT(P, "dve", P0[:, j0 * 128:(j0 + NCH) * 128], ps_p0[:, 0:T], sl4[:, 0:T], ALU.mult)
                        yield
                G = al[2]
                for hp in range(2):
                    hw = slice(hp * NJ * 128, (hp + 1) * NJ * 128)
                    TT(P, "pool", G[:, hw], P0T[:, hw], ident4b, ALU.add)
                Pk, PkT = P0, P0T
                Pn = [al[3], al[4]]
                PnT = [al[5], al[6]]
                NLEV = 6
                for lev in range(NLEV):
                    nP, nPT = Pn[lev % 2], PnT[lev % 2]
                    for hp in range(2):
                        hw = slice(hp * NJ * 128, (hp + 1) * NJ * 128)
                        ps1 = P.psum()
                        for j in range(NJ):
                            js = slice((hp * NJ + j) * 128, (hp * NJ + j + 1) * 128)
                            MM(P, ps1[:, j * 128:(j + 1) * 128], PkT[:, js], Pk[:, js])
                        CP(P, "act", nP[:, hw], ps1[:, 0:NJ * 128])
                        if lev < NLEV - 1:
                            ps2 = P.psum()
                            for j in range(NJ):
                                js = slice((hp * NJ + j) * 128, (hp * NJ + j + 1) * 128)
                                MM(P, ps2[:, j * 128:(j + 1) * 128], Pk[:, js], PkT[:, js])
                            CP(P, "dve", nPT[:, hw], ps2[:, 0:NJ * 128])
                    yield
                    for hp in range(2):
                        hw = slice(hp * NJ * 128, (hp + 1) * NJ * 128)
                        ps3 = P.psum()
                        for j in range(NJ):
                            js = slice((hp * NJ + j) * 128, (hp * NJ + j + 1) * 128)
                            MM(P, ps3[:, j * 128:(j + 1) * 128], nP[:, js], G[:, js])
                        TT(P, "dve", G[:, hw], G[:, hw], ps3[:, 0:NJ * 128], ALU.add)
                    Pk, PkT = nP, nPT
                    yield
                XW = Einv.bitcast(BF16)
                for hp in range(2):
                    ps = P.psum()
                    for s in range(2):
                        for c in range(NCH):
                            jl = s * NCH + c
                            j = hp * NJ + jl
                            MM(P, ps[:, jl * 128:jl * 128 + 64], Aall[j][:, 128:256], TMc(hp, c)[:, 64 * s:64 * s + 64])
                            MM(P, ps[:, jl * 128 + 64:jl * 128 + 128], G[:, j * 128:(j + 1) * 128],
                               TMc(hp, c)[:, 384 + 64 * s:384 + 64 * s + 64])
                    CP(P, "act", XW[:, hp * NJ * 128:(hp + 1) * NJ * 128], ps[:, 0:NJ * 128])
                yield
                U0 = rW.bitcast(BF16)[:, 0:NJ2 * 64]
                ps = P.psum()
                for j in range(NJ2):
                    MM(P, ps[:, j * 64:(j + 1) * 64], G[:, j * 128:(j + 1) * 128], XW[:, j * 128:j * 128 + 64])
                CP(P, "act", U0, ps[:, 0:NJ2 * 64])
                yield
                Nn = kW[:, 0:NC2 * 64]
                for hp in range(2):
                    ps = P.psum()
                    for s in range(2):
                        orow = slice(64 * s, 64 * s + 64)
                        for c in range(NCH):
                            j = hp * NJ + s * NCH + c
                            tmc = TMc(hp, c)
                            MM(P, ps[orow, c * 128:c * 128 + 64], XW[:, j * 128 + 64:j * 128 + 128],
                               tmc[:, 256 + 64 * s:256 + 64 * s + 64])
                            MM(P, ps[orow, c * 128 + 64:c * 128 + 128], tmc[:, 256 + 64 * s:256 + 64 * s + 64],
                               U0[:, j * 64:(j + 1) * 64], start=True, stop=False)
                            MM(P, ps[orow, c * 128 + 64:c * 128 + 128], tmc[:, 128 + 64 * s:128 + 64 * s + 64],
                               tmc[:, 64 * s:64 * s + 64], start=False, stop=True)
                    for c in range(NCH):
                        CP(P, "act", mbbd[hp][c][0:64, 0:64], ps[0:64, c * 128:c * 128 + 64])
                        CP(P, "act", mbbd[hp][c][64:128, 64:128], ps[64:128, c * 128:c * 128 + 64])
                        CP(P, "dve", Nn[:, (hp * NCH + c) * 64:(hp * NCH + c + 1) * 64], ps[:, c * 128 + 64:c * 128 + 128])
                yield
                RhT = kW[:, NC2 * 64:NC2 * 64 + W2 // 2].bitcast(BF16)
                Y0 = vW
                for hp in range(2):
                    hsl = slice(hp * T, (hp + 1) * T)
                    ps = P.psum()
                    for s in range(2):
                        orow = slice(64 * s, 64 * s + 64)
                        for c in range(NCH):
                            j = hp * NJ + s * NCH + c
                            MM(P, ps[orow, c * 128:(c + 1) * 128], XW[:, j * 128 + 64:j * 128 + 128],
                               Aall[j][:, 256:384])
                    TT(P, "dve", RhT[:, hsl], ps[:, 0:T], Rt[:, hsl], ALU.add)
                    ps = P.psum()
                    for s in range(2):
                        orow = slice(64 * s, 64 * s + 64)
                        for c in range(NCH):
                            j = hp * NJ + s * NCH + c
                            MM(P, ps[orow, c * 128:(c + 1) * 128], U0[:, j * 64:(j + 1) * 64], Aall[j][:, 256:384],
                               start=True, stop=False)
                            MM(P, ps[orow, c * 128:(c + 1) * 128], TMc(hp, c)[:, 64 * s:64 * s + 64],
                               Aall[j][:, 384:512], start=False, stop=True)
                    CP(P, "act", Y0[:, hsl], ps[:, 0:T])
                yield
                Hs = sgW[:, 0:2 * (NCH + 1) * 64]
                Hb = rW[:, NJ2 * 32:NJ2 * 32 + NC2 * 32].bitcast(BF16)

                def Hsl(hp, c):
                    o_ = (hp * (NCH + 1) + c) * 64
                    return Hs[:, o_:o_ + 64]

                def Hbl(hp, c):
                    o_ = (hp * NCH + c) * 64
                    return Hb[:, o_:o_ + 64]
                for hp in range(2):
                    CP(P, "pool", Hsl(hp, 0), rw_H[hp])
                for c in range(NCH):
                    for hp in range(2):
                        CP(P, "pool", Hbl(hp, c), Hsl(hp, c))
                        ps = P.psum()
                        MM(P, ps[:, 0:64], mbbd[hp][c], Hsl(hp, c), start=True, stop=False)
                        MM(P, ps[:, 0:64], identf, Nn[:, (hp * NCH + c) * 64:(hp * NCH + c + 1) * 64], start=False, stop=True)
                        STT(P, Hsl(hp, c + 1), Hsl(hp, c), gC[:, hp * NCH + c:hp * NCH + c + 1],
                            ps[:, 0:64], ALU.mult, ALU.add)
                    yield
                for hp in range(2):
                    CP(P, "pool", rw_H[hp], Hsl(hp, NCH))
                y = aW
                for hp in range(2):
                    hsl = slice(hp * T, (hp + 1) * T)
                    pse, pso = P.psum(), P.psum()
                    for c in range(NCH):
                        cs = slice(c * 128, (c + 1) * 128)
                        gcs = slice(hp * T + c * 128, hp * T + (c + 1) * 128)
                        MM(P, pse[0:64, cs], Hbl(hp, c)[0:64, :], RhT[0:64, gcs])
                        MM(P, pso[64:128, cs], Hbl(hp, c)[64:128, :], RhT[64:128, gcs])
                    TT(P, "dve", y[0:64, hsl], pse[0:64, 0:T], Y0[0:64, hsl], ALU.add)
                    TT(P, "dve", y[64:128, hsl], pso[64:128, 0:T], Y0[64:128, hsl], ALU.add)
                    dbg(f"rw_y{hp}", y[:, hsl], 128, T, tok0, ntok)
                yield
                yb = A(W2, BF16, "yb")
                CP(P, "pool", yb, y)
                ps_m = P.psum()
                MM(P, ps_m, ones64b, yb)
                yc = A(W2, F32, "yc")
                STT(P, yc, ps_m, -1.0 / 64, y, ALU.mult, ALU.add)
                yield
                TT(P, "pool", yb, yc, yc, ALU.mult)
                ps_v = P.psum()
                MM(P, ps_v, ones64b, yb)
                rs = y
                rsqrt_act(rs, ps_v, 1.0 / 64, epsc[:, 1:2])
                yield
                TT(P, "dve", yc, yc, rs, ALU.mult)
                for hp in range(2):
                    hsl = slice(hp * T, (hp + 1) * T)
                    STT(P, yc[:, hsl], yc[:, hsl], col("ln_g", hp), bonus[:, hsl], ALU.mult, ALU.add)
                TT(P, "dve", ycat[:, 0:W2], yc, gSW, ALU.mult)
                yield

            def lru_thread():
                R = Region(base_thr + SZ_RW, SZ_LRU)
                A = R.alloc
                lsets = [(A(3 + T, F32, f"xbuf{i}"), A(T, F32, f"gt{i}"), A(T, F32, f"xc{i}"), A(T, BF16, f"xcb{i}"))
                         for i in range(2)]
                st = []
                for j in range(4):
                    xbuf, gt, xc, xcb = lsets[j % 2]
                    CP(P, "pool", xbuf[:, 0:3], lru_xc[j])
                    proj(1024 + j * 128, 128, lambda ps: CP(P, "act", xbuf[:, 3:3 + T], ps))
                    yield
                    proj(1536 + j * 128, 128, lambda ps: CP(P, "act", gt, ps))
                    CP(P, "pool", lru_xc[j], xbuf[:, T:T + 3])
                    yield
                    TS(P, "pool", xc, xbuf[:, 0:T], col("cw0", j), ALU.mult, col("cb", j), ALU.add)
                    for tap in range(1, 4):
                        STT(P, xc, xbuf[:, tap:tap + T], col(f"cw{tap}", j), xc, ALU.mult, ALU.add)
                    CP(P, "pool", xcb, xc)
                    yield
                    ps_r, ps_i = P.psum(), P.psum()
                    MM(P, ps_r[:, 0:T], wabd[:, j * 128:(j + 1) * 128], xcb)
                    MM(P, ps_i[:, 0:T], wxbd[:, j * 128:(j + 1) * 128], xcb)
                    gr = A(T, F32, f"gr{j}")
                    uu = A(T, F32, f"uu{j}")
                    ge = A(T, F32, f"ge{j}")
                    ACT(P, gr, ps_r[:, 0:T], AF.Sigmoid, bias=col("ba", j))
                    ACT(P, uu, ps_i[:, 0:T], AF.Sigmoid, bias=col("bx", j))
                    yield
                    TT(P, "dve", uu, uu, xc, ALU.mult)
                    TT(P, "pool", ge, gt, gt, ALU.mult)
                    TS(P, "pool", ge, ge, 0.044715, ALU.mult, 1.0, ALU.add)
                    TT(P, "dve", ge, ge, gt, ALU.mult)
                    ACT(P, ge, ge, AF.Sigmoid, scale=1.5957691216057308)
                    TT(P, "dve", ge, ge, gt, ALU.mult)
                    st.append((gr, uu, ge))
                    yield
                yield "B"
                for j in range(4):
                    gr, uu, ge = st[j]
                    hh, av, a2, sqb = lsets[j % 2][0][:, 0:T], lsets[j % 2][1], lsets[j % 2][2], lsets[j % 2][3]
                    ACT(P, av, gr, AF.Exp, scale=c1[:, j:j + 1])
                    ACT(P, a2, gr, AF.Exp, scale=c2[:, j:j + 1])
                    ACT(P, a2, a2, AF.Ln, bias=1.0, scale=-1.0)
                    ACT(P, a2, a2, AF.Exp, scale=0.5)
                    yield
                    TT(P, "dve", uu, uu, a2, ALU.mult)
                    SCAN(P, hh, av, uu, lru_h[:, j:j + 1])
                    CP(P, "pool", lru_h[:, j:j + 1], hh[:, T - 1:T])
                    yl = av
                    TT(P, "dve", yl, hh, ge, ALU.mult)
                    dbg(f"lru_y{j}", yl, 128, T, tok0, ntok)
                    yield
                    TT(P, "pool", sqb, yl, yl, ALU.mult)
                    ps_m = P.psum()
                    MM(P, ps_m[:, 0:T], ones64b, sqb)
                    rs = a2
                    rsqrt_act(rs, ps_m[:, 0:T], 1.0 / 64, epsc[:, 0:1])
                    STT(P, ycat3[:, 2 + j, :], yl, col("lng", j), rs, ALU.mult, ALU.mult)
                    yield

            def gla_thread():
                R = Region(base_thr + SZ_RW + SZ_LRU, SZ_GLA)
                A = R.alloc
                q = A(T, F32, "q")
                kg = A(T, F32, "kg")
                vg = [A(T, BF16, f"vg{i}") for i in range(2)]
                gg = [A(T, F32, f"gg{i}") for i in range(2)]
                gklo = A(T, BF16, "gklo")
                sgm = A(T, F32, "sgm")
                proj(2048, 128, lambda ps: CP(P, "act", q, ps))
                yield
                proj(2176, 128, lambda ps: CP(P, "act", kg, ps))
                yield
                proj(2304, 128, lambda ps: CP(P, "act", vg[0], ps))
                yield
                proj(2432, 128, lambda ps: CP(P, "act", vg[1], ps))
                yield
                proj(2560, 16, lambda ps: CP(P, "act", gklo[0:16, :], ps))
                yield
                for i in range(2):
                    def evg(ps, i=i):
                        ACT(P, sgm, ps, AF.Sigmoid)
                        TT(P, "dve", gg[i], ps, sgm, ALU.mult)
                    proj(2576 + 128 * i, 128, evg)
                    yield
                ps_gk = P.psum()
                MM(P, ps_gk[:, 0:T], gkup[0:16, :], gklo[0:16, :])
                la = A(T, F32, "la")
                ACT(P, la, ps_gk[:, 0:T], AF.Sigmoid, bias=col("gkb"))
                yield "B"
                ACT(P, la, la, AF.Ln)
                bc = A(T, F32, "bc")
                SCAN(P, bc, rmask[:, 0:T], la, 0.0)
                Eq = A(T, F32, "Eq")
                Ek = A(T, F32, "Ek")
                Ee = la
                ACT(P, Eq, bc, AF.Exp, scale=1.0 / 16)
                ACT(P, Ek, bc, AF.Exp, scale=-1.0 / 16)
                yield
                nb = A(NCH, F32, "nbg")
                TS(P, "pool", nb, bc.re("p (c t) -> p c t", c=NCH)[:, :, 127], 1.0 / 16, ALU.mult)
                for c in range(NCH):
                    ACT(P, Ee[:, c * 128:(c + 1) * 128], bc[:, c * 128:(c + 1) * 128], AF.Exp,
                        bias=nb[:, c:c + 1], scale=-1.0 / 16)
                gCg = A(NCH, F32, "gCg")
                CP(P, "pool", gCg, Eq.re("p (c t) -> p c t", c=NCH)[:, :, 127])
                yield
                qin = A(T, BF16, "qin")
                STT(P, qin, q, 32.0 ** -0.5, Eq, ALU.mult, ALU.mult)
                kin = [A(T, BF16, f"kin{i}") for i in range(2)]
                for i in range(2):
                    STT(P, kin[i], kg, par[:, i:i + 1], Ek, ALU.mult, ALU.mult)
                kend = A(T, BF16, "kend")
                TT(P, "pool", kend, kg, Ee, ALU.mult)
                yield
                GT = []
                for c in range(NCH):
                    pst = P.psum()
                    pstb = pst.bitcast(BF16)
                    cs = slice(c * 128, (c + 1) * 128)
                    TR(P, pstb[:, 0:128], vg[0][:, cs], identb)
                    TR(P, pstb[:, 128:256], vg[1][:, cs], identb)
                    TR(P, pstb[:, 256:384], kend[:, cs], identb)
                    gt_ = A(384, BF16, f"GT{c}")
                    CP(P, "act", gt_, pstb[:, 0:384])
                    GT.append(gt_)
                    yield
                ST = []
                for h in range(4):
                    rows = slice(64 * (h // 2), 64 * (h // 2) + 64)
                    ps = P.psum()
                    for c in range(NCH):
                        cs = slice(c * 128, (c + 1) * 128)
                        MM(P, ps[:, cs], kin[h % 2][rows, cs], qin[rows, cs])
                    st_ = A(T, BF16, f"ST{h}")
                    TT(P, "dve", st_, ps[:, 0:T], ui4[:, 0:T], ALU.mult)
                    ST.append(st_)
                    yield
                Sb = []
                Scur = A((NCH + 1) * 256, F32, "Scur")
                CP(P, "pool", Scur[:, 0:256], gla_S)
                for c in range(NCH):
                    sb_ = A(256, BF16, f"Sb{c}")
                    TT(P, "pool", sb_, Scur[:, c * 256:(c + 1) * 256], bmask, ALU.mult)
                    Sb.append(sb_)
                    ps = P.psum()
                    MM(P, ps[:, 0:256], GT[c][:, 256:384], GT[c][:, 0:256])
                    STT(P, Scur[:, (c + 1) * 256:(c + 2) * 256], Scur[:, c * 256:(c + 1) * 256], gCg[:, c:c + 1],
                        ps[:, 0:256], ALU.mult, ALU.add)
                    yield
                CP(P, "pool", gla_S, Scur[:, NCH * 256:(NCH + 1) * 256])
                o = A(T, F32, "o")
                ob = A(T, BF16, "ob")
                rs = A(T, F32, "rsg")
                for vp in range(2):
                    ps_in, ps_it = P.psum(), P.psum()
                    for s in range(2):
                        h = 2 * vp + s
                        orow = slice(64 * s, 64 * s + 64)
                        krow = slice(64 * vp, 64 * vp + 64)
                        for c in range(NCH):
                            cs = slice(c * 128, (c + 1) * 128)
                            MM(P, ps_in[orow, cs], GT[c][:, 64 * h:64 * h + 64], ST[h][:, cs])
                            MM(P, ps_it[orow, cs], Sb[c][krow, 64 * h:64 * h + 64], qin[krow, cs])
                    CP(P, "act", o, ps_it[:, 0:T])
                    TT(P, "dve", o, o, ps_in[:, 0:T], ALU.add)
                    dbg(f"gla_o{vp}", o, 128, T, tok0, ntok)
                    yield
                    TT(P, "pool", ob, o, o, ALU.mult)
                    ps_m = P.psum()
                    MM(P, ps_m[:, 0:T], ones64b, ob)
                    rsqrt_act(rs, ps_m[:, 0:T], 1.0 / 64, epsc[:, 0:1])
                    STT(P, o, o, col("gng"), rs, ALU.mult, ALU.mult)
                    TT(P, "dve", ycat3[:, 6 + vp, :], o, gg[vp], ALU.mult)
                    yield

            _skip = os.environ.get("K_SKIP", "").split(",")
            threads = [g_ for n_, g_ in (("rw", rw_thread), ("lru", lru_thread), ("gla", gla_thread)) if n_ not in _skip]
            threads = [g_() for g_ in threads]
            def run_greedy(ths):
                ready = {id(t_): 0.0 for t_ in ths}
                out_ = []
                while ths:
                    th = min(ths, key=lambda t_: ready[id(t_)])
                    P.step_t = 0.0
                    try:
                        v_ = next(th)
                    except StopIteration:
                        ths.remove(th)
                        continue
                    if P.step_t > 0:
                        ready[id(th)] = P.step_t
                    if v_ == "B":
                        ths.remove(th)
                        out_.append(th)
                return out_

            if os.environ.get("K_GREEDY", "0") == "1":
                atB = run_greedy(threads)
                if ti + 1 < NT:
                    m_load(ti + 1)
                    atB.append(m_norm(ti + 1))
                run_greedy(atB)
            else:
                atB = []
                while threads:
                    for th in list(threads):
                        try:
                            v_ = next(th)
                        except StopIteration:
                            threads.remove(th)
                            continue
                        if v_ == "B":
                            threads.remove(th)
                            atB.append(th)
                threads = atB
                if ti > 0 and os.environ.get("K_DEFER", "1") == "1":
                    m_outproj(ti - 1)
                if ti + 1 < NT:
                    m_load(ti + 1)
                    threads.append(m_norm(ti + 1))
                rr_w = int(os.environ.get("K_RWW", "1"))
                while threads:
                    for ith, th in enumerate(list(threads)):
                        for _rep in range(rr_w if ith == 0 else 1):
                            try:
                                next(th)
                            except StopIteration:
                                threads.remove(th)
                                break
            P.arena_off = base_thr
            if os.environ.get("K_DEFER", "1") != "1" and ti < NT - 1:
                m_outproj(ti)

            for kc in range(8):
                dbg(f"ycat{kc}", ycat3[:, kc, :], 128, T, tok0, ntok)

        m_outproj(NT - 1)

        if os.environ.get("K_BARRIER", "1") == "1":
            P.barrier()
        P.arena_off = persist_mark
        if last:
            gBf = P.alloc(D, F32, "gBf")
            P.dma("sp", gBf.ap, final_g.partition_broadcast(128), writes=[gBf])
        gB2 = P.alloc(D, F32, "gB2")
        P.dma("sp", gB2.ap, norm2_g[l:l + 1, :].partition_broadcast(128), writes=[gB2])
        wg_sb = P.alloc(8 * DFF, BF16, "wg")
        wu_sb = P.alloc(8 * DFF, BF16, "wu")
        wd_sb = P.alloc(NFC * D, BF16, "wd")
        for kc in range(8):
            for hf in range(2):
                sl_ = slice(hf * 1408, (hf + 1) * 1408)
                P.dma("pool", wg_sb.ap[:, kc * DFF + hf * 1408: kc * DFF + (hf + 1) * 1408],
                      w_gate[l, kc * 128:(kc + 1) * 128, sl_], writes=[wg_sb])
                P.dma("pool", wu_sb.ap[:, kc * DFF + hf * 1408: kc * DFF + (hf + 1) * 1408],
                      w_up[l, kc * 128:(kc + 1) * 128, sl_], writes=[wu_sb])
        for fc in range(NFC):
            P.dma("pool", wd_sb.ap[:, fc * D:(fc + 1) * D], w_down[l, fc * 128:(fc + 1) * 128, :], writes=[wd_sb])
        xTs = [[P.alloc(D, F32, f"fxT{p_}{b}") for b in range(NCH)] for p_ in range(2)]
        hnFs = [P.alloc(8 * T, BF16, f"fhnF{p_}") for p_ in range(2)]
        hF = P.alloc(NFC * T, BF16, "hF")
        hF3 = hF.re("p (k t) -> p k t", k=NFC)
        fhn = P.alloc(D, BF16, "fhn")
        sgt = [P.alloc(T, F32, f"sgt{i}") for i in range(2)]
        scs = [P.alloc(2, F32, f"fsc{i}") for i in range(4)]
        yos = [P.alloc(D, F32, f"yo{i}") for i in range(2)] if last else []
        sci = [0]

        def nsc():
            sci[0] += 1
            return scs[sci[0] % 4]

        def f_pro(ti):
            tok0 = ti * T
            xT = xTs[ti % 2]
            hnF3 = hnFs[ti % 2].re("p (k t) -> p k t", k=8)
            for b in range(NCH):
                P.dma("sp", xT[b].ap, xa_d[tok0 + b * 128: tok0 + (b + 1) * 128, :], reads=[xa_buf], writes=[xT[b]])
            for b in range(NCH):
                rms_block(xT[b], gB2, fhn, nsc())
                pst = P.psum()
                pstb = pst.bitcast(BF16)
                for kc in range(8):
                    TR(P, pstb[:, kc * 128:(kc + 1) * 128], fhn[:, kc * 128:(kc + 1) * 128], identb)
                CP(P, "act", hnF3[:, :, b * 128:(b + 1) * 128], pstb.re("p (k t) -> p k t", k=8))

        def f_gateup(ti):
            hnF3 = hnFs[ti % 2].re("p (k t) -> p k t", k=8)
            for fc in range(NFC):
                ps = P.psum()
                for kc in range(8):
                    MM(P, ps[:, 0:T], wg_sb[:, kc * DFF + fc * 128: kc * DFF + (fc + 1) * 128], hnF3[:, kc, :],
                       start=(kc == 0), stop=(kc == 7))
                for kc in range(8):
                    MM(P, ps[:, T:2 * T], wu_sb[:, kc * DFF + fc * 128: kc * DFF + (fc + 1) * 128], hnF3[:, kc, :],
                       start=(kc == 0), stop=(kc == 7))
                s_ = sgt[fc % 2]
                ACT(P, s_, ps[:, 0:T], AF.Silu)
                TT(P, "dve", hF3[:, fc, :], s_, ps[:, T:2 * T], ALU.mult)

        def f_down(ti):
            tok0 = ti * T
            xT = xTs[ti % 2]
            for b in range(NCH):
                for hf in range(2):
                    ps = P.psum()
                    for fc in range(NFC):
                        MM(P, ps, hF3[:, fc, b * 128:(b + 1) * 128],
                           wd_sb[:, fc * D + hf * 512: fc * D + (hf + 1) * 512], start=(fc == 0), stop=(fc == NFC - 1))
                    TT(P, "dve", xT[b][:, hf * 512:(hf + 1) * 512], xT[b][:, hf * 512:(hf + 1) * 512], ps, ALU.add)
                if last:
                    yo = yos[b % 2]
                    rms_block(xT[b], gBf, yo, nsc())
                    fin.append(P.dma("sp", out_d[tok0 + b * 128: tok0 + (b + 1) * 128, :], yo.ap, reads=[yo]))
                else:
                    P.dma("sp", xb_d[tok0 + b * 128: tok0 + (b + 1) * 128, :], xT[b].ap, reads=[xT[b]],
                          writes=[xb_buf])

        f_pro(0)
        for ti in range(NT):
            f_gateup(ti)
            if ti + 1 < NT:
                f_pro(ti + 1)
            f_down(ti)
    P.finish(fin)
    P.build()
    return nc, list(dbg_out.keys())


def make_in_map(inputs, xs, nlayers):
    m = {"x": np.ascontiguousarray(xs, dtype=np.float32)}
    for k in ["w_in", "w_out", "ffn_w_gate", "ffn_w_up", "ffn_w_down", "rw_w_up", "rw_a_up", "rw_g_up",
              "lru_wa", "lru_wx", "gla_gk_up", "norm1_g", "norm2_g"]:
        m[k] = np.ascontiguousarray(np.asarray(inputs[k], np.float32)[:nlayers])
    m["cols"] = np.stack([pack_cols(inputs, l) for l in range(nlayers)])
    m["final_norm_g"] = np.asarray(inputs["final_norm_g"], np.float32).reshape(1, D)
    for k, v in make_consts().items():
        m["c_" + k] = v
    return m


_CACHE = {}


def kernel(**inputs):
    x = np.asarray(inputs["x"], np.float32)
    B, S, _ = x.shape
    L = np.asarray(inputs["w_in"]).shape[0]
    key = (S, L)
    if key not in _CACHE:
        _CACHE[key] = build_program(S, L)[0]
    nc = _CACHE[key]
    in_maps = [make_in_map(inputs, x[c % B], L) for c in range(8)]
    res = run_bass_kernel_spmd(nc, in_maps, core_ids=list(range(8)))
    return np.stack([res.results[b]["out"] for b in range(B)], axis=0)
```

```python
import contextlib
import math
import os
import numpy as np
import concourse.bass as bass
import concourse.mybir as mybir
from concourse.bass_utils import run_bass_kernel_spmd

F32 = mybir.dt.float32
BF16 = mybir.dt.bfloat16
AF = mybir.ActivationFunctionType
ALU = mybir.AluOpType

CHUNK = 8000
SAMEQ = os.environ.get('K_SAMEQ', '1') == '1'
NDMASEM = 12

D = 1024
PIN = 2832
DFF = 2816
NFC = DFF // 128
EPS = 1e-6
RW_EPS = 64e-5
DEC = math.exp(-0.5)
T = 256
NCH = T // 128


class Buf:
    __slots__ = ("name", "writers", "readers", "t")

    def __init__(self, name=""):
        self.name = name
        self.writers = {}
        self.readers = {}
        self.t = 0.0


def _dep_kv(d):
    if d[0] == "e":
        return ("e", d[1], d[2] // CHUNK), d[2] % CHUNK + 1
    return ("d", d[1], d[2]), d[3]


def _merge(dst, src):
    for k, v in src.items():
        if dst.get(k, 0) < v:
            dst[k] = v


class Tile:
    __slots__ = ("ap", "buf")

    def __init__(self, ap, buf=None):
        self.ap = ap
        self.buf = buf if buf is not None else Buf()

    def __getitem__(self, k):
        return Tile(self.ap[k], self.buf)

    def bitcast(self, dt):
        return Tile(self.ap.bitcast(dt), self.buf)

    def re(self, s, **kw):
        return Tile(self.ap.rearrange(s, **kw), self.buf)


class Prog:
    ENG = ("pe", "act", "dve", "pool", "sp")

    def __init__(self, nc):
        self.nc = nc
        self.stack = contextlib.ExitStack()
        self.streams = {e: [] for e in self.ENG}
        self.count = {e: 0 for e in self.ENG}
        self.esems = {e: [] for e in self.ENG}
        self.dsems = {}
        self.dma_n = {e: 0 for e in self.ENG}
        self.dma_hist = {e: {} for e in self.ENG}
        self.waited = {e: {} for e in self.ENG}
        self.n_t = 0
        self.final_deps = []
        self.arena = None
        self.arena_off = 0
        self.arena_size = 0
        self.psb = []
        self.ps_i = 0
        self.live = []
        self.t_eng = {e: 0.0 for e in self.ENG}
        self.step_t = 0.0

    def init_mem(self, arena_f32_cols):
        self.arena_size = arena_f32_cols
        self.arena = self.stack.enter_context(
            self.nc.sbuf_tensor("arena", [128, arena_f32_cols], F32))
        for i in range(8):
            t = self.stack.enter_context(self.nc.psum_tensor(f"psb{i}", [128, 512], F32))
            self.psb.append(Tile(t[:, :], Buf(f"ps{i}")))

    def alloc(self, free_elems, dt=F32, name=""):
        ncol = free_elems if dt == F32 else (free_elems + 1) // 2
        if self.arena_off + ncol > self.arena_size:
            raise RuntimeError(f"arena overflow at {name}: {self.arena_off}+{ncol}>{self.arena_size}")
        s0, s1 = self.arena_off, self.arena_off + ncol
        self.arena_off += ncol
        self.hi = max(getattr(self, "hi", 0), s1)
        keep, over = [], []
        for ent in self.live:
            (over if (ent[0] < s1 and s0 < ent[1]) else keep).append(ent)
        if len(over) == 1 and over[0][0] == s0 and over[0][1] == s1 and over[0][2] == (dt, free_elems):
            return over[0][3]
        ap = self.arena[:, s0:s1]
        if dt != F32:
            ap = ap.bitcast(dt)[:, 0:free_elems]
        buf = Buf(name)
        for ent in over:
            _merge(buf.writers, ent[3].buf.writers)
            _merge(buf.readers, ent[3].buf.readers)
        t = Tile(ap, buf)
        keep.append((s0, s1, (dt, free_elems), t))
        self.live = keep
        return t

    def psum(self):
        t = self.psb[self.ps_i]
        self.ps_i = (self.ps_i + 1) % 8
        return t

    def _deps(self, e, reads, writes, is_dma=False):
        need = {}
        for r in reads:
            _merge(need, r.writers)
        for w in writes:
            _merge(need, w.writers)
            _merge(need, w.readers)
        out = []
        for key, val in need.items():
            if key[0] == "e" and key[1] == e and (e == "pe" or not SAMEQ):
                continue
            if self.waited[e].get(key, 0) >= val:
                continue
            self.waited[e][key] = val
            out.append((key, val))
        return out

    def _record(self, d, reads, writes, is_dma):
        k, v = _dep_kv(d)
        for w in writes:
            if is_dma:
                w.writers = {kk: vv for kk, vv in w.writers.items() if kk[0] == "d"}
            else:
                w.writers = {}
            w.writers[k] = v
            w.readers = {}
        for r in reads:
            if r.readers.get(k, 0) < v:
                r.readers[k] = v

    def _sem(self, key):
        if key[0] == "e":
            return self.esems[key[1]][key[2]]
        return self.dsems[(key[1], key[2])]

    def _est(self, e, reads, writes, cost):
        t0 = self.t_eng[e]
        for b in reads:
            if b.t > t0:
                t0 = b.t
        for b in writes:
            if b.t > t0:
                t0 = b.t
        self.t_eng[e] = t0 + cost
        fin = t0 + cost + 0.25
        for b in writes:
            b.t = fin
        if fin > self.step_t:
            self.step_t = fin

    def op(self, e, fn, reads=(), writes=(), cost=None):
        if cost is None:
            n = 256
            for w in writes:
                if isinstance(w, Tile):
                    n = 1
                    for d_ in w.ap.shape[1:]:
                        n *= d_
                    break
            cost = {"act": 0.2 + n / 1150.0, "dve": 0.15 + n / 960.0, "pool": 0.2 + n / 480.0,
                    "pe": 0.05 + n / 2400.0}.get(e, 1.0)
        reads = [r.buf if isinstance(r, Tile) else r for r in reads]
        writes = [w.buf if isinstance(w, Tile) else w for w in writes]
        self._est(e, reads, writes, cost)
        waits = self._deps(e, reads, writes)
        idx = self.count[e]
        self.count[e] += 1
        mykey = ("e", e, idx // CHUNK)

        def emit(eng, waits=waits, fn=fn, mykey=mykey):
            for k, v in waits:
                eng.wait_ge(self._sem(k), v)
            fn(eng).then_inc(self._sem(mykey), 1)

        self.streams[e].append(emit)
        d = ("e", e, idx)
        self._record(d, reads, writes, False)
        return d

    def dma(self, q, out, in_, reads=(), writes=(), **kw):
        reads = [r.buf if isinstance(r, Tile) else r for r in reads]
        writes = [w.buf if isinstance(w, Tile) else w for w in writes]
        self._est(q, reads, writes, 2.0)
        self.t_eng[q] -= 1.9
        waits = self._deps(q, reads, writes, True)
        n = self.dma_n[q]
        self.dma_n[q] += 1
        slot = n % NDMASEM
        prev = self.dma_hist[q].get(slot, 0)
        val = prev + 16
        self.dma_hist[q][slot] = val
        key = ("d", q, slot)
        if prev > 0 and self.waited[q].get(key, 0) < prev:
            waits = waits + [(key, prev)]
            self.waited[q][key] = prev

        def emit(eng, waits=waits, key=key):
            for k, v in waits:
                eng.wait_ge(self._sem(k), v)
            eng.dma_start(out=out, in_=in_, **kw).then_inc(self._sem(key), 16)

        self.streams[q].append(emit)
        d = ("d", q, slot, val)
        self._record(d, reads, writes, True)
        return d

    def barrier(self):
        keys = []
        for e in self.ENG:
            if self.count[e] > 0:
                idx = self.count[e] - 1
                keys.append((("e", e, idx // CHUNK), idx % CHUNK + 1))
            for slot, val in self.dma_hist[e].items():
                keys.append((("d", e, slot), val))
        for f in self.ENG:
            mine = []
            for k, v in keys:
                if k[0] == "e" and k[1] == f:
                    continue
                if self.waited[f].get(k, 0) >= v:
                    continue
                self.waited[f][k] = v
                mine.append((k, v))

            def emit(eng, mine=mine):
                for k, v in mine:
                    eng.wait_ge(self._sem(k), v)

            self.streams[f].append(emit)

    def finish(self, deps):
        self.final_deps = list(deps)

    def build(self):
        nc = self.nc
        st = self.stack
        for e in self.ENG:
            nsem = (self.count[e] + CHUNK - 1) // CHUNK
            self.esems[e] = [st.enter_context(nc.semaphore(f"s_{e}_{i}")) for i in range(nsem)]
            nd = min(self.dma_n[e], NDMASEM)
            for s in range(nd):
                self.dsems[(e, s)] = st.enter_context(nc.semaphore(f"d_{e}_{s}"))
        fin = []
        for d in self.final_deps:
            if d[0] == "e":
                fin.append((("e", d[1], d[2] // CHUNK), d[2] % CHUNK + 1))
            else:
                fin.append((("d", d[1], d[2]), d[3]))
        block = st.enter_context(nc.Block())
        streams = self.streams

        @block.tensor
        def _(eng):
            for f in streams["pe"]:
                f(eng)

        @block.scalar
        def _(eng):
            for f in streams["act"]:
                f(eng)

        @block.vector
        def _(eng):
            for f in streams["dve"]:
                f(eng)

        @block.gpsimd
        def _(eng):
            for f in streams["pool"]:
                f(eng)

        @block.sync
        def _(eng):
            for f in streams["sp"]:
                f(eng)
            for k, v in fin:
                eng.wait_ge(self._sem(k), v)

        st.close()


def _ap(x):
    return x.ap if isinstance(x, Tile) else x


def _tl(*xs):
    return [x for x in xs if isinstance(x, Tile)]


def ACT(P, out, in_, func, bias=None, scale=None, accum=None):
    kw = {}
    if bias is not None:
        kw["bias"] = _ap(bias)
    if scale is not None:
        kw["scale"] = _ap(scale)
    if accum is not None:
        kw["accum_out"] = _ap(accum)
    P.op("act", lambda e: e.activation(out=out.ap, in_=in_.ap, func=func, **kw),
         reads=_tl(in_, bias, scale), writes=_tl(out, accum))


def TT(P, eng, out, a, b, op):
    P.op(eng, lambda e: e.tensor_tensor(out=out.ap, in0=a.ap, in1=b.ap, op=op),
         reads=_tl(a, b), writes=[out])


def TS(P, eng, out, a, s1, op0, s2=None, op1=None):
    if op1 is None:
        P.op(eng, lambda e: e.tensor_scalar(out=out.ap, in0=a.ap, scalar1=_ap(s1), scalar2=None, op0=op0),
             reads=_tl(a, s1), writes=[out])
    else:
        P.op(eng, lambda e: e.tensor_scalar(out=out.ap, in0=a.ap, scalar1=_ap(s1), scalar2=_ap(s2),
                                            op0=op0, op1=op1),
             reads=_tl(a, s1, s2), writes=[out])


def STT(P, out, in0, scalar, in1, op0, op1):
    P.op("dve", lambda e: e.scalar_tensor_tensor(out=out.ap, in0=in0.ap, scalar=_ap(scalar), in1=in1.ap,
                                                 op0=op0, op1=op1),
         reads=_tl(in0, scalar, in1), writes=[out])


def CP(P, eng, out, in_):
    if eng == "act":
        P.op("act", lambda e: e.activation(out=out.ap, in_=in_.ap, func=AF.Copy), reads=[in_], writes=[out])
    else:
        P.op(eng, lambda e: e.tensor_copy(out=out.ap, in_=in_.ap), reads=[in_], writes=[out])


def MM(P, out, lhsT, rhs, start=True, stop=True):
    P.op("pe", lambda e: e.matmul(out.ap, lhsT=lhsT.ap, rhs=rhs.ap, start=start, stop=stop),
         reads=[lhsT, rhs], writes=[out])


def TR(P, out, in_, ident):
    P.op("pe", lambda e: e.transpose(out=out.ap, in_=in_.ap, identity=ident.ap),
         reads=[in_, ident], writes=[out])


def SCAN(P, out, d0, d1, init):
    P.op("dve", lambda e: e.tensor_tensor_scan(out=out.ap, data0=d0.ap, data1=d1.ap, initial=_ap(init),
                                               op0=ALU.mult, op1=ALU.add),
         reads=_tl(d0, d1, init), writes=[out])


def MEMSET(P, eng, out, val):
    P.op(eng, lambda e: e.memset(out.ap, val), writes=[out])


COLS = {}


def _col_layout():
    names = [("mu", 8), ("w0", 2), ("a0", 2), ("k_k", 2), ("k_a", 2), ("r_k", 2), ("ln_g", 2), ("ln_b", 2),
             ("cw0", 4), ("cw1", 4), ("cw2", 4), ("cw3", 4), ("cb", 4), ("ba", 4), ("bx", 4), ("lam", 4),
             ("lng", 4), ("gkb", 1), ("gng", 1)]
    off = 0
    for n, c in names:
        COLS[n] = (off, c)
        off += c
    return off


NCOL = _col_layout()


def pack_cols(inp, l):
    out = np.zeros((128, NCOL), np.float32)

    def put(name, vec):
        o, c = COLS[name]
        out[:, o:o + c] = np.asarray(vec, np.float32).reshape(c, 128).T

    put("mu", inp["rw_mu"][l])
    put("w0", inp["rw_w0"][l])
    put("a0", inp["rw_a0"][l])
    put("k_k", inp["rw_k_k"][l])
    put("k_a", inp["rw_k_a"][l])
    put("r_k", inp["rw_r_k"][l].reshape(-1))
    put("ln_g", inp["rw_ln_g"][l])
    put("ln_b", inp["rw_ln_b"][l])
    for j in range(4):
        put(f"cw{j}", inp["lru_conv_w"][l, j])
    put("cb", inp["lru_conv_b"][l])
    put("ba", inp["lru_ba"][l])
    put("bx", inp["lru_bx"][l])
    put("lam", inp["lru_lam"][l])
    put("lng", inp["lru_norm_g"][l])
    put("gkb", inp["gla_gk_b"][l])
    put("gng", np.concatenate([inp["gla_norm_g"][l], inp["gla_norm_g"][l]]))
    return out


def make_consts():
    c = {}
    idx = np.arange(128)
    su = (idx[:, None] < idx[None, :]).astype(np.float32)
    ui = (idx[:, None] <= idx[None, :]).astype(np.float32)
    sl = (idx[:, None] > idx[None, :]).astype(np.float32)
    c["ident"] = np.eye(128, dtype=np.float32)
    c["cmask"] = np.concatenate([su, su, ui, ui], axis=1)
    c["sl4"] = np.concatenate([sl] * 4, axis=1)
    c["ui4"] = np.concatenate([ui] * NCH, axis=1)
    ob = np.zeros((128, 128), np.float32)
    ob[:64, :64] = 1
    ob[64:, 64:] = 1
    c["ones64"] = ob
    rm = np.ones((128, 2 * T), np.float32)
    rm[:, ::128] = 0
    c["rmask"] = rm
    bm = np.zeros((128, 256), np.float32)
    for h in range(4):
        bm[32 * h:32 * h + 32, 64 * h:64 * h + 64] = 1
    c["bmask"] = bm
    par = np.zeros((128, 2), np.float32)
    for h in range(4):
        par[32 * h:32 * h + 32, h % 2] = 1
    c["par"] = par
    return c


CONST_SHAPES = {"ident": 128, "cmask": 512, "sl4": 512, "ui4": T, "ones64": 128,
                "rmask": 2 * T, "bmask": 256, "par": 2}


def build_program(ntok, nlayers, debug=()):
    nc = bass.Bass("TRN2", target_bir_lowering=False)
    NT = ntok // T
    L = nlayers

    def din(name, shape):
        return nc.dram_tensor(name, list(shape), F32, kind="ExternalInput").ap()

    x_in = din("x", [ntok, D])
    w_in = din("w_in", [L, D, PIN])
    w_out = din("w_out", [L, D, D])
    w_gate = din("ffn_w_gate", [L, D, DFF])
    w_up = din("ffn_w_up", [L, D, DFF])
    w_down = din("ffn_w_down", [L, DFF, D])
    rw_w_up = din("rw_w_up", [L, 64, 256])
    rw_a_up = din("rw_a_up", [L, 64, 256])
    rw_g_up = din("rw_g_up", [L, 128, 256])
    lru_wa = din("lru_wa", [L, 8, 64, 64])
    lru_wx = din("lru_wx", [L, 8, 64, 64])
    gk_up = din("gla_gk_up", [L, 16, 128])
    cols_d = din("cols", [L, 128, NCOL])
    norm1_g = din("norm1_g", [L, D])
    norm2_g = din("norm2_g", [L, D])
    final_g = din("final_norm_g", [1, D])
    cdram = {k: din("c_" + k, [128, n]) for k, n in CONST_SHAPES.items()}
    out_d = nc.dram_tensor("out", [ntok, D], F32, kind="ExternalOutput").ap()
    xa_d = nc.dram_tensor("xa_scr", [ntok, D], F32).ap()
    xb_d = nc.dram_tensor("xb_scr", [ntok, D], F32).ap()
    xa_buf, xb_buf = Buf("xa"), Buf("xb")
    dbg_out = {}

    P = Prog(nc)
    P.init_mem(52500)
    fin = []

    def dbg(name, tile, rows, cols, tok0=None, total_cols=None):
        if name not in debug:
            return
        if name not in dbg_out:
            tc = total_cols if total_cols is not None else cols
            dbg_out[name] = nc.dram_tensor("dbg_" + name, [rows, tc], F32, kind="ExternalOutput").ap()
        dst = dbg_out[name]
        c0 = tok0 if tok0 is not None else 0
        fin.append(P.dma("pool", dst[0:rows, c0:c0 + cols], tile.ap, reads=[tile]))

    cst = {}
    for k, n in CONST_SHAPES.items():
        if k in ("rmask",):
            cst[k] = P.alloc(n, BF16, "c_" + k)
            P.dma("pool", cst[k].ap, cdram[k][:, :], writes=[cst[k]])
        else:
            cst[k] = P.alloc(n, F32, "c_" + k)
            P.dma("sp", cst[k].ap, cdram[k][:, :], writes=[cst[k]])
    identf = cst["ident"]
    identb = P.alloc(128, BF16, "identb")
    CP(P, "pool", identb, identf)
    ident4b = P.alloc(512, BF16, "ident4b")
    for i in range(4):
        CP(P, "pool", ident4b[:, i * 128:(i + 1) * 128], identf)
    ones64b = P.alloc(128, BF16, "ones64b")
    CP(P, "pool", ones64b, cst["ones64"])
    cmask, sl4, ui4, rmask, bmask, par = (cst[k] for k in ("cmask", "sl4", "ui4", "rmask", "bmask", "par"))
    epsc = P.alloc(2, F32, "epsc")
    MEMSET(P, "pool", epsc[:, 0:1], EPS)
    MEMSET(P, "pool", epsc[:, 1:2], RW_EPS)

    rw_carry = P.alloc(8, F32, "rw_carry")
    lru_xc = [P.alloc(3, F32, f"lru_xc{j}") for j in range(4)]
    lru_h = P.alloc(4, F32, "lru_h")
    rw_H = [P.alloc(64, F32, f"rwH{hp}") for hp in range(2)]
    gla_S = P.alloc(256, F32, "glaS")
    persist_mark = P.arena_off

    def rms_block(xblk, gB, hn_out, sc=None):
        if sc is None:
            ssq = P.alloc(1, F32, "ssq")
            rstd = P.alloc(1, F32, "rstd")
        else:
            ssq, rstd = sc[:, 0:1], sc[:, 1:2]
        ACT(P, hn_out, xblk, AF.Square, accum=ssq)
        ACT(P, rstd, ssq, AF.Ln, bias=epsc[:, 0:1], scale=1.0 / D)
        ACT(P, rstd, rstd, AF.Exp, scale=-0.5)
        STT(P, hn_out, xblk, rstd[:, 0:1], gB, ALU.mult, ALU.mult)

    for l in range(L):
        src_d, src_buf = (x_in, None) if l == 0 else (xb_d, xb_buf)
        last = l == L - 1
        if os.environ.get("K_BARRIER", "1") == "1":
            P.barrier()
        P.arena_off = persist_mark
        colsT = P.alloc(NCOL, F32, "cols")
        P.dma("sp", colsT.ap, cols_d[l], writes=[colsT])

        def col(name, j=0, rows=slice(0, 128)):
            o, c = COLS[name]
            return colsT[rows, o + j:o + j + 1]

        dcol = P.alloc(24, F32, "dcol")
        o_mu = COLS["mu"][0]
        omm = dcol[:, 0:8]
        TS(P, "pool", omm, colsT[:, o_mu:o_mu + 8], -1.0, ALU.mult, 1.0, ALU.add)
        o_ka = COLS["k_a"][0]
        omka = dcol[:, 8:10]
        TS(P, "pool", omka, colsT[:, o_ka:o_ka + 2], -1.0, ALU.mult, 1.0, ALU.add)
        o_lam = COLS["lam"][0]
        c1 = dcol[:, 10:14]
        c2 = dcol[:, 14:18]
        ACT(P, c1, colsT[:, o_lam:o_lam + 4], AF.Exp, scale=-1.0)
        ACT(P, c1, c1, AF.Ln, bias=1.0)
        TS(P, "pool", c2, c1, -16.0, ALU.mult)
        TS(P, "pool", c1, c1, -8.0, ALU.mult)
        MEMSET(P, "pool", rw_carry, 0.0)
        for j in range(4):
            MEMSET(P, "pool", lru_xc[j], 0.0)
        MEMSET(P, "pool", lru_h, 0.0)
        for hp in range(2):
            MEMSET(P, "pool", rw_H[hp], 0.0)
        MEMSET(P, "pool", gla_S, 0.0)

        gB1 = P.alloc(D, F32, "gB1")
        P.dma("sp", gB1.ap, norm1_g[l:l + 1, :].partition_broadcast(128), writes=[gB1])
        w_in_sb = P.alloc(8 * PIN, BF16, "w_in")
        w_in_v = w_in_sb.ap.rearrange("p (k f) -> p k f", k=8)
        for kp in range(4):
            for hf in range(2):
                P.dma("pool", w_in_v[:, 2 * kp:2 * kp + 2, hf * 1416:(hf + 1) * 1416],
                      w_in[l, 2 * kp * 128:(2 * kp + 2) * 128, hf * 1416:(hf + 1) * 1416].rearrange(
                          "(k p) f -> p k f", p=128), writes=[w_in_sb])
        w_out_sb = P.alloc(8 * D, BF16, "w_out")
        w_out_v = w_out_sb.ap.rearrange("p (k f) -> p k f", k=8)
        for kp in range(4):
            P.dma("pool", w_out_v[:, 2 * kp:2 * kp + 2, :],
                  w_out[l, 2 * kp * 128:(2 * kp + 2) * 128, :].rearrange("(k p) f -> p k f", p=128),
                  writes=[w_out_sb])
        wa_up = P.alloc(256, BF16, "wa_up")
        P.dma("pool", wa_up.ap[0:64, :], rw_w_up[l], writes=[wa_up])
        P.dma("pool", wa_up.ap[64:128, :], rw_a_up[l], writes=[wa_up])
        g_up = P.alloc(256, BF16, "g_up")
        P.dma("pool", g_up.ap, rw_g_up[l], writes=[g_up])
        gkup = P.alloc(128, BF16, "gkup")
        P.dma("pool", gkup.ap[0:16, :], gk_up[l], writes=[gkup])
        wabd = P.alloc(4 * 128, BF16, "wabd")
        wxbd = P.alloc(4 * 128, BF16, "wxbd")
        MEMSET(P, "pool", wabd, 0.0)
        MEMSET(P, "pool", wxbd, 0.0)
        for j in range(4):
            for s in range(2):
                P.dma("pool", wabd.ap[64 * s:64 * s + 64, j * 128 + 64 * s: j * 128 + 64 * s + 64],
                      lru_wa[l, 2 * j + s], writes=[wabd])
                P.dma("pool", wxbd.ap[64 * s:64 * s + 64, j * 128 + 64 * s: j * 128 + 64 * s + 64],
                      lru_wx[l, 2 * j + s], writes=[wxbd])
        mbbd = [[P.alloc(128, F32, f"mbbd{hp}{c}") for c in range(NCH)] for hp in range(2)]
        for hp in range(2):
            for c in range(NCH):
                MEMSET(P, "pool", mbbd[hp][c], 0.0)
        xTs = [[P.alloc(D, F32, f"xT{p_}{b}") for b in range(NCH)] for p_ in range(2)]
        hnF = P.alloc(8 * T, BF16, "hnF")
        hnF3 = hnF.re("p (k t) -> p k t", k=8)
        ycat = P.alloc(8 * T, BF16, "ycat")
        ycat3 = ycat.re("p (k t) -> p k t", k=8)
        mscs = [P.alloc(2, F32, f"msc{i}") for i in range(4)]
        tile_mark = P.arena_off

        def m_load(ti_):
            rd = [src_buf] if src_buf is not None else []
            for b in range(NCH):
                P.dma("sp", xTs[ti_ % 2][b].ap, src_d[ti_ * T + b * 128: ti_ * T + (b + 1) * 128, :], reads=rd,
                      writes=[xTs[ti_ % 2][b]])

        def m_norm(ti_):
            save = P.arena_off
            P.arena_off = tile_mark
            hn = P.alloc(D, BF16, "hn")
            P.arena_off = save
            for b in range(NCH):
                rms_block(xTs[ti_ % 2][b], gB1, hn, mscs[(2 * ti_ + b) % 4])
                yield
                pst = P.psum()
                pstb = pst.bitcast(BF16)
                for kc in range(8):
                    TR(P, pstb[:, kc * 128:(kc + 1) * 128], hn[:, kc * 128:(kc + 1) * 128], identb)
                CP(P, "act", hnF3[:, :, b * 128:(b + 1) * 128], pstb.re("p (k t) -> p k t", k=8))
                yield

        def m_outproj(ti_):
            xT_ = xTs[ti_ % 2]
            for b in range(NCH):
                for hf in range(2):
                    ps = P.psum()
                    for kc in range(8):
                        MM(P, ps, ycat3[:, kc, b * 128:(b + 1) * 128],
                           w_out_sb[:, kc * D + hf * 512: kc * D + (hf + 1) * 512], start=(kc == 0), stop=(kc == 7))
                    TT(P, "dve", xT_[b][:, hf * 512:(hf + 1) * 512], xT_[b][:, hf * 512:(hf + 1) * 512], ps, ALU.add)
                P.dma("sp", xa_d[ti_ * T + b * 128: ti_ * T + (b + 1) * 128, :], xT_[b].ap, reads=[xT_[b]],
                      writes=[xa_buf])

        m_load(0)
        for _ in m_norm(0):
            pass

        for ti in range(NT):
            P.arena_off = tile_mark
            tok0 = ti * T
            xT = xTs[ti % 2]

            def proj(c0, ncols, evac):
                ps = P.psum()
                for kc in range(8):
                    MM(P, ps[0:ncols, 0:T], w_in_sb[:, kc * PIN + c0: kc * PIN + c0 + ncols], hnF3[:, kc, :],
                       start=(kc == 0), stop=(kc == 7))
                evac(ps[0:ncols, 0:T])


            base_thr = P.arena_off

            class Region:
                def __init__(self, start, size):
                    self.off = start
                    self.end = start + size

                def alloc(self, n, dt=F32, name=""):
                    save = P.arena_off
                    P.arena_off = self.off
                    t = P.alloc(n, dt, name)
                    self.off = P.arena_off
                    P.arena_off = save
                    if self.off > self.end:
                        raise RuntimeError(f"region overflow {name} {self.off}>{self.end}")
                    return t

            SZ_RW, SZ_LRU, SZ_GLA = 14900, 4900, 5780

            def rsqrt_act(out, in_, scale, bias):
                ACT(P, out, in_, AF.Ln, bias=bias, scale=scale)
                ACT(P, out, out, AF.Exp, scale=-0.5)

            def rw_thread():
                R = Region(base_thr, SZ_RW)
                A = R.alloc
                W2 = 2 * T
                NC2 = 2 * NCH
                ptmp = A(1 + T, F32, "ptmp")
                ltmp = A(T, F32, "ltmp")
                rW = A(W2, F32, "rW")
                kW = A(W2, F32, "kW")
                vW = A(W2, F32, "vW")
                waT = A(T, F32, "waT")
                gloT = A(T, F32, "gloT")
                dest = [rW[:, 0:T], rW[:, T:W2], kW[:, 0:T], kW[:, T:W2], vW[:, 0:T], vW[:, T:W2], waT, gloT]
                for gi in range(8):
                    def ev(ps, gi=gi):
                        CP(P, "act", ptmp[:, 1:1 + T], ps)
                        CP(P, "pool", ptmp[:, 0:1], rw_carry[:, gi:gi + 1])
                        TS(P, "dve", ltmp, ptmp[:, 0:T], col("mu", gi), ALU.mult)
                        STT(P, dest[gi], ptmp[:, 1:1 + T], omm[:, gi:gi + 1], ltmp, ALU.mult, ALU.add)
                        CP(P, "pool", rw_carry[:, gi:gi + 1], ptmp[:, T:T + 1])
                    proj(gi * 128, 128, ev)
                    yield
                wab = A(T, BF16, "wab")
                ACT(P, wab[0:64, :], waT[0:64, :], AF.Tanh)
                CP(P, "pool", wab[64:128, :], waT[64:128, :])
                sgl = A(T, BF16, "sgl")
                ACT(P, sgl, gloT, AF.Sigmoid)
                yield
                sgW = A(W2, F32, "sgW")
                aW = A(W2, F32, "aW")
                gSW = A(W2, F32, "gSW")
                for hp in range(2):
                    hsl = slice(hp * T, (hp + 1) * T)
                    ps_w, ps_a, ps_g = P.psum(), P.psum(), P.psum()
                    MM(P, ps_w[:, 0:T], wa_up[0:64, hp * 128:(hp + 1) * 128], wab[0:64, :])
                    MM(P, ps_a[:, 0:T], wa_up[64:128, hp * 128:(hp + 1) * 128], wab[64:128, :])
                    MM(P, ps_g[:, 0:T], g_up[:, hp * 128:(hp + 1) * 128], sgl)
                    ACT(P, sgW[:, hsl], ps_w[:, 0:T], AF.Sigmoid, bias=col("w0", hp))
                    ACT(P, aW[:, hsl], ps_a[:, 0:T], AF.Sigmoid, bias=col("a0", hp))
                    CP(P, "dve", gSW[:, hsl], ps_g[:, 0:T])
                    yield
                yield "B"
                css = A(W2, F32, "css")
                SCAN(P, css, rmask, sgW, 0.0)
                cse = A(W2, F32, "cse")
                TT(P, "pool", cse, css, sgW, ALU.subtract)
                E1 = A(W2, F32, "E1")
                E0 = cse
                Einv = A(W2, F32, "Einv")
                Eend = sgW
                nb = A(NC2, F32, "nb")
                TS(P, "pool", nb, css.re("p (c t) -> p c t", c=NC2)[:, :, 127], -DEC, ALU.mult)
                ACT(P, E1, css, AF.Exp, scale=-DEC)
                ACT(P, Einv, css, AF.Exp, scale=DEC)
                yield
                for c in range(NC2):
                    ACT(P, Eend[:, c * 128:(c + 1) * 128], css[:, c * 128:(c + 1) * 128], AF.Exp,
                        bias=nb[:, c:c + 1], scale=DEC)
                ACT(P, E0, cse, AF.Exp, scale=-DEC)
                gC = A(NC2, F32, "gC")
                CP(P, "pool", gC, E1.re("p (c t) -> p c t", c=NC2)[:, :, 127])
                yield
                kk = A(W2, F32, "kk")
                sqk = A(W2, BF16, "sqk")
                for hp in range(2):
                    hsl = slice(hp * T, (hp + 1) * T)
                    ACT(P, sqk[:, hsl], kW[:, hsl], AF.Square, scale=col("k_k", hp))
                ps_n = P.psum()
                MM(P, ps_n, ones64b, sqk)
                rn = A(W2, F32, "rn")
                rsqrt_act(rn, ps_n, 1.0, 1e-24)
                yield
                for hp in range(2):
                    hsl = slice(hp * T, (hp + 1) * T)
                    STT(P, kk[:, hsl], kW[:, hsl], col("k_k", hp), rn[:, hsl], ALU.mult, ALU.mult)
                kmod = A(W2, F32, "kmod")
                for hp in range(2):
                    hsl = slice(hp * T, (hp + 1) * T)
                    TS(P, "dve", kmod[:, hsl], aW[:, hsl], col("k_a", hp), ALU.mult, omka[:, hp:hp + 1], ALU.add)
                TT(P, "dve", kmod, kmod, kW, ALU.mult)
                yield
                bvec = A(W2, F32, "bvec")
                TT(P, "dve", bvec, kk, aW, ALU.mult)
                rkb = sqk
                for hp in range(2):
                    hsl = slice(hp * T, (hp + 1) * T)
                    STT(P, rkb[:, hsl], rW[:, hsl], col("r_k", hp), kmod[:, hsl], ALU.mult, ALU.mult)
                ps_b = P.psum()
                MM(P, ps_b, ones64b, rkb)
                bonus = A(W2, F32, "bonus")
                TT(P, "dve", bonus, ps_b, vW, ALU.mult)
                for hp in range(2):
                    hsl = slice(hp * T, (hp + 1) * T)
                    TS(P, "pool", bonus[:, hsl], bonus[:, hsl], col("ln_b", hp), ALU.add)
                yield
                Rt = A(W2, BF16, "Rt")
                At = A(W2, BF16, "At")
                Bt = A(W2, BF16, "Bt")
                Kt = A(W2, BF16, "Kt")
                Kh = A(W2, BF16, "Kh")
                Bh = A(W2, BF16, "Bh")
                vb = A(W2, BF16, "vb")
                TT(P, "dve", Rt, rW, E1, ALU.mult)
                STT(P, At, kk, -1.0, E0, ALU.mult, ALU.mult)
                TT(P, "dve", Bt, bvec, Einv, ALU.mult)
                TT(P, "dve", Kt, kmod, Einv, ALU.mult)
                yield
                TT(P, "dve", Kh, kmod, Eend, ALU.mult)
                TT(P, "pool", Bh, bvec, Eend, ALU.mult)
                CP(P, "pool", vb, vW)
                yield
                TMs = []
                for hp in range(2):
                    pst = P.psum()
                    pstb = pst.bitcast(BF16)
                    for c in range(NCH):
                        for i, src_ in enumerate([vb, Kh, Bh, At]):
                            TR(P, pstb[:, (c * 4 + i) * 128:(c * 4 + i + 1) * 128],
                               src_[:, hp * T + c * 128: hp * T + (c + 1) * 128], identb)
                    tm = A(NCH * 512, BF16, f"TM{hp}")
                    CP(P, "act", tm, pstb[:, 0:NCH * 512])
                    TMs.append(tm)
                    yield

                def TMc(hp, c):
                    return TMs[hp][:, c * 512:(c + 1) * 512]
                NJ = 2 * NCH
                NJ2 = 2 * NJ
                al = [t_.bitcast(BF16) for t_ in (kk, kmod, bvec, rn, css, cse, E1)]
                P0, P0T = al[0], al[1]
                Aall = [None] * NJ2
                for hp in range(2):
                    for s in range(2):
                        rows = slice(64 * s, 64 * s + 64)
                        ps_p0 = P.psum()
                        for c in range(NCH):
                            j = hp * NJ + s * NCH + c
                            cs = slice(hp * T + c * 128, hp * T + (c + 1) * 128)
                            ps = P.psum()
                            MM(P, ps[:, 0:128], Bt[rows, cs], At[rows, cs])
                            MM(P, ps[:, 128:256], Kt[rows, cs], At[rows, cs])
                            MM(P, ps[:, 256:384], Bt[rows, cs], Rt[rows, cs])
                            MM(P, ps[:, 384:512], Kt[rows, cs], Rt[rows, cs])
                            aa = A(512, BF16, f"Aall{j}")
                            TT(P, "dve", aa, ps, cmask, ALU.mult)
                            Aall[j] = aa
                            CP(P, "act", P0T[:, j * 128:(j + 1) * 128], aa[:, 0:128])
                            MM(P, ps_p0[:, c * 128:(c + 1) * 128], At[rows, cs], Bt[rows, cs])
                        j0 = hp * NJ + s * NCH
                        TT(P, "dve", P0[:, j0 * 128:(j0 + NCH) * 128], ps_p0[:, 0:T], sl4[:, 0:T], ALU.mult)
                        yield
                G = al[2]
                for hp in range(2):
                    hw = slice(hp * NJ * 128, (hp + 1) * NJ * 128)
                    TT(P, "pool", G[:, hw], P0T[:, hw], ident4b, ALU.add)
                Pk, PkT = P0, P0T
                Pn = [al[3], al[4]]
                PnT = [al[5], al[6]]
                NLEV = 6
                for lev in range(NLEV):
                    nP, nPT = Pn[lev % 2], PnT[lev % 2]
                    for hp in range(2):
                        hw = slice(hp * NJ * 128, (hp + 1) * NJ * 128)
                        ps1 = P.psum()
                        for j in range(NJ):
                            js = slice((hp * NJ + j) * 128, (hp * NJ + j + 1) * 128)
                            MM(P, ps1[:, j * 128:(j + 1) * 128], PkT[:, js], Pk[:, js])
                        CP(P, "act", nP[:, hw], ps1[:, 0:NJ * 128])
                        if lev < NLEV - 1:
                            ps2 = P.psum()
                            for j in range(NJ):
                                js = slice((hp * NJ + j) * 128, (hp * NJ + j + 1) * 128)
                                MM(P, ps2[:, j * 128:(j + 1) * 128], Pk[:, js], PkT[:, js])
                            CP(P, "dve", nPT[:, hw], ps2[:, 0:NJ * 128])
                    yield
                    for hp in range(2):
                        hw = slice(hp * NJ * 128, (hp + 1) * NJ * 128)
                        ps3 = P.psum()
                        for j in range(NJ):
                            js = slice((hp * NJ + j) * 128, (hp * NJ + j + 1) * 128)
                            MM(P, ps3[:, j * 128:(j + 1) * 128], nP[:, js], G[:, js])
                        TT(P, "dve", G[:, hw], G[:, hw], ps3[:, 0:NJ * 128], ALU.add)
                    Pk, PkT = nP, nPT
                    yield
                XW = Einv.bitcast(BF16)
                for hp in range(2):
                    ps = P.psum()
                    for s in range(2):
                        for c in range(NCH):
                            jl = s * NCH + c
                            j = hp * NJ + jl
                            MM(P, ps[:, jl * 128:jl * 128 + 64], Aall[j][:, 128:256], TMc(hp, c)[:, 64 * s:64 * s + 64])
                            MM(P, ps[:, jl * 128 + 64:jl * 128 + 128], G[:, j * 128:(j + 1) * 128],
                               TMc(hp, c)[:, 384 + 64 * s:384 + 64 * s + 64])
                    CP(P, "act", XW[:, hp * NJ * 128:(hp + 1) * NJ * 128], ps[:, 0:NJ * 128])
                yield
                U0 = rW.bitcast(BF16)[:, 0:NJ2 * 64]
                ps = P.psum()
                for j in range(NJ2):
                    MM(P, ps[:, j * 64:(j + 1) * 64], G[:, j * 128:(j + 1) * 128], XW[:, j * 128:j * 128 + 64])
                CP(P, "act", U0, ps[:, 0:NJ2 * 64])
                yield
                Nn = kW[:, 0:NC2 * 64]
                for hp in range(2):
                    ps = P.psum()
                    for s in range(2):
                        orow = slice(64 * s, 64 * s + 64)
                        for c in range(NCH):
                            j = hp * NJ + s * NCH + c
                            tmc = TMc(hp, c)
                            MM(P, ps[orow, c * 128:c * 128 + 64], XW[:, j * 128 + 64:j * 128 + 128],
                               tmc[:, 256 + 64 * s:256 + 64 * s + 64])
                            MM(P, ps[orow, c * 128 + 64:c * 128 + 128], tmc[:, 256 + 64 * s:256 + 64 * s + 64],
                               U0[:, j * 64:(j + 1) * 64], start=True, stop=False)
                            MM(P, ps[orow, c * 128 + 64:c * 128 + 128], tmc[:, 128 + 64 * s:128 + 64 * s + 64],
                               tmc[:, 64 * s:64 * s + 64], start=False, stop=True)
                    for c in range(NCH):
                        CP(P, "act", mbbd[hp][c][0:64, 0:64], ps[0:64, c * 128:c * 128 + 64])
                        CP(P, "act", mbbd[hp][c][64:128, 64:128], ps[64:128, c * 128:c * 128 + 64])
                        CP(P, "dve", Nn[:, (hp * NCH + c) * 64:(hp * NCH + c + 1) * 64], ps[:, c * 128 + 64:c * 128 + 128])
                yield
                RhT = kW[:, NC2 * 64:NC2 * 64 + W2 // 2].bitcast(BF16)
                Y0 = vW
                for hp in range(2):
                    hsl = slice(hp * T, (hp + 1) * T)
                    ps = P.psum()
                    for s in range(2):
                        orow = slice(64 * s, 64 * s + 64)
                        for c in range(NCH):
                            j = hp * NJ + s * NCH + c
                            MM(P, ps[orow, c * 128:(c + 1) * 128], XW[:, j * 128 + 64:j * 128 + 128],
                               Aall[j][:, 256:384])
                    TT(P, "dve", RhT[:, hsl], ps[:, 0:T], Rt[:, hsl], ALU.add)
                    ps = P.psum()
                    for s in range(2):
                        orow = slice(64 * s, 64 * s + 64)
                        for c in range(NCH):
                            j = hp * NJ + s * NCH + c
                            MM(P, ps[orow, c * 128:(c + 1) * 128], U0[:, j * 64:(j + 1) * 64], Aall[j][:, 256:384],
                               start=True, stop=False)
                            MM(P, ps[orow, c * 128:(c + 1) * 128], TMc(hp, c)[:, 64 * s:64 * s + 64],
                               Aall[j][:, 384:512], start=False, stop=True)
                    CP(P, "act", Y0[:, hsl], ps[:, 0:T])
                yield
                Hs = sgW[:, 0:2 * (NCH + 1) * 64]
                Hb = rW[:, NJ2 * 32:NJ2 * 32 + NC2 * 32].bitcast(BF16)

                def Hsl(hp, c):
                    o_ = (hp * (NCH + 1) + c) * 64
                    return Hs[:, o_:o_ + 64]

                def Hbl(hp, c):
                    o_ = (hp * NCH + c) * 64
                    return Hb[:, o_:o_ + 64]
                for hp in range(2):
                    CP(P, "pool", Hsl(hp, 0), rw_H[hp])
                for c in range(NCH):
                    for hp in range(2):
                        CP(P, "pool", Hbl(hp, c), Hsl(hp, c))
                        ps = P.psum()
                        MM(P, ps[:, 0:64], mbbd[hp][c], Hsl(hp, c), start=True, stop=False)
                        MM(P, ps[:, 0:64], identf, Nn[:, (hp * NCH + c) * 64:(hp * NCH + c + 1) * 64], start=False, stop=True)
                        STT(P, Hsl(hp, c + 1), Hsl(hp, c), gC[:, hp * NCH + c:hp * NCH + c + 1],
                            ps[:, 0:64], ALU.mult, ALU.add)
                    yield
                for hp in range(2):
                    CP(P, "pool", rw_H[hp], Hsl(hp, NCH))
                y = aW
                for hp in range(2):
                    hsl = slice(hp * T, (hp + 1) * T)
                    pse, pso = P.psum(), P.psum()
                    for c in range(NCH):
                        cs = slice(c * 128, (c + 1) * 128)
                        gcs = slice(hp * T + c * 128, hp * T + (c + 1) * 128)
                        MM(P, pse[0:64, cs], Hbl(hp, c)[0:64, :], RhT[0:64, gcs])
                        MM(P, pso[64:128, cs], Hbl(hp, c)[64:128, :], RhT[64:128, gcs])
                    TT(P, "dve", y[0:64, hsl], pse[0:64, 0:T], Y0[0:64, hsl], ALU.add)
                    TT(P, "dve", y[64:128, hsl], pso[64:128, 0:T], Y0[64:128, hsl], ALU.add)
                    dbg(f"rw_y{hp}", y[:, hsl], 128, T, tok0, ntok)
                yield
                yb = A(W2, BF16, "yb")
                CP(P, "pool", yb, y)
                ps_m = P.psum()
                MM(P, ps_m, ones64b, yb)
                yc = A(W2, F32, "yc")
                STT(P, yc, ps_m, -1.0 / 64, y, ALU.mult, ALU.add)
                yield
                TT(P, "pool", yb, yc, yc, ALU.mult)
                ps_v = P.psum()
                MM(P, ps_v, ones64b, yb)
                rs = y
                rsqrt_act(rs, ps_v, 1.0 / 64, epsc[:, 1:2])
                yield
                TT(P, "dve", yc, yc, rs, ALU.mult)
                for hp in range(2):
                    hsl = slice(hp * T, (hp + 1) * T)
                    STT(P, yc[:, hsl], yc[:, hsl], col("ln_g", hp), bonus[:, hsl], ALU.mult, ALU.add)
                TT(P, "dve", ycat[:, 0:W2], yc, gSW, ALU.mult)
                yield

            def lru_thread():
                R = Region(base_thr + SZ_RW, SZ_LRU)
                A = R.alloc
                lsets = [(A(3 + T, F32, f"xbuf{i}"), A(T, F32, f"gt{i}"), A(T, F32, f"xc{i}"), A(T, BF16, f"xcb{i}"))
                         for i in range(2)]
                st = []
                for j in range(4):
                    xbuf, gt, xc, xcb = lsets[j % 2]
                    CP(P, "pool", xbuf[:, 0:3], lru_xc[j])
                    proj(1024 + j * 128, 128, lambda ps: CP(P, "act", xbuf[:, 3:3 + T], ps))
                    yield
                    proj(1536 + j * 128, 128, lambda ps: CP(P, "act", gt, ps))
                    CP(P, "pool", lru_xc[j], xbuf[:, T:T + 3])
                    yield
                    TS(P, "pool", xc, xbuf[:, 0:T], col("cw0", j), ALU.mult, col("cb", j), ALU.add)
                    for tap in range(1, 4):
                        STT(P, xc, xbuf[:, tap:tap + T], col(f"cw{tap}", j), xc, ALU.mult, ALU.add)
                    CP(P, "pool", xcb, xc)
                    yield
                    ps_r, ps_i = P.psum(), P.psum()
                    MM(P, ps_r[:, 0:T], wabd[:, j * 128:(j + 1) * 128], xcb)
                    MM(P, ps_i[:, 0:T], wxbd[:, j * 128:(j + 1) * 128], xcb)
                    gr = A(T, F32, f"gr{j}")
                    uu = A(T, F32, f"uu{j}")
                    ge = A(T, F32, f"ge{j}")
                    ACT(P, gr, ps_r[:, 0:T], AF.Sigmoid, bias=col("ba", j))
                    ACT(P, uu, ps_i[:, 0:T], AF.Sigmoid, bias=col("bx", j))
                    yield
                    TT(P, "dve", uu, uu, xc, ALU.mult)
                    TT(P, "pool", ge, gt, gt, ALU.mult)
                    TS(P, "pool", ge, ge, 0.044715, ALU.mult, 1.0, ALU.add)
                    TT(P, "dve", ge, ge, gt, ALU.mult)
                    ACT(P, ge, ge, AF.Sigmoid, scale=1.5957691216057308)
                    TT(P, "dve", ge, ge, gt, ALU.mult)
                    st.append((gr, uu, ge))
                    yield
                yield "B"
                for j in range(4):
                    gr, uu, ge = st[j]
                    hh, av, a2, sqb = lsets[j % 2][0][:, 0:T], lsets[j % 2][1], lsets[j % 2][2], lsets[j % 2][3]
                    ACT(P, av, gr, AF.Exp, scale=c1[:, j:j + 1])
                    ACT(P, a2, gr, AF.Exp, scale=c2[:, j:j + 1])
                    ACT(P, a2, a2, AF.Ln, bias=1.0, scale=-1.0)
                    ACT(P, a2, a2, AF.Exp, scale=0.5)
                    yield
                    TT(P, "dve", uu, uu, a2, ALU.mult)
                    SCAN(P, hh, av, uu, lru_h[:, j:j + 1])
                    CP(P, "pool", lru_h[:, j:j + 1], hh[:, T - 1:T])
                    yl = av
                    TT(P, "dve", yl, hh, ge, ALU.mult)
                    dbg(f"lru_y{j}", yl, 128, T, tok0, ntok)
                    yield
                    TT(P, "pool", sqb, yl, yl, ALU.mult)
                    ps_m = P.psum()
                    MM(P, ps_m[:, 0:T], ones64b, sqb)
                    rs = a2
                    rsqrt_act(rs, ps_m[:, 0:T], 1.0 / 64, epsc[:, 0:1])
                    STT(P, ycat3[:, 2 + j, :], yl, col("lng", j), rs, ALU.mult, ALU.mult)
                    yield

            def gla_thread():
                R = Region(base_thr + SZ_RW + SZ_LRU, SZ_GLA)
                A = R.alloc
                q = A(T, F32, "q")
                kg = A(T, F32, "kg")
                vg = [A(T, BF16, f"vg{i}") for i in range(2)]
                gg = [A(T, F32, f"gg{i}") for i in range(2)]
                gklo = A(T, BF16, "gklo")
                sgm = A(T, F32, "sgm")
                proj(2048, 128, lambda ps: CP(P, "act", q, ps))
                yield
                proj(2176, 128, lambda ps: CP(P, "act", kg, ps))
                yield
                proj(2304, 128, lambda ps: CP(P, "act", vg[0], ps))
                yield
                proj(2432, 128, lambda ps: CP(P, "act", vg[1], ps))
                yield
                proj(2560, 16, lambda ps: CP(P, "act", gklo[0:16, :], ps))
                yield
                for i in range(2):
                    def evg(ps, i=i):
                        ACT(P, sgm, ps, AF.Sigmoid)
                        TT(P, "dve", gg[i], ps, sgm, ALU.mult)
                    proj(2576 + 128 * i, 128, evg)
                    yield
                ps_gk = P.psum()
                MM(P, ps_gk[:, 0:T], gkup[0:16, :], gklo[0:16, :])
                la = A(T, F32, "la")
                ACT(P, la, ps_gk[:, 0:T], AF.Sigmoid, bias=col("gkb"))
                yield "B"
                ACT(P, la, la, AF.Ln)
                bc = A(T, F32, "bc")
                SCAN(P, bc, rmask[:, 0:T], la, 0.0)
                Eq = A(T, F32, "Eq")
                Ek = A(T, F32, "Ek")
                Ee = la
                ACT(P, Eq, bc, AF.Exp, scale=1.0 / 16)
                ACT(P, Ek, bc, AF.Exp, scale=-1.0 / 16)
                yield
                nb = A(NCH, F32, "nbg")
                TS(P, "pool", nb, bc.re("p (c t) -> p c t", c=NCH)[:, :, 127], 1.0 / 16, ALU.mult)
                for c in range(NCH):
                    ACT(P, Ee[:, c * 128:(c + 1) * 128], bc[:, c * 128:(c + 1) * 128], AF.Exp,
                        bias=nb[:, c:c + 1], scale=-1.0 / 16)
                gCg = A(NCH, F32, "gCg")
                CP(P, "pool", gCg, Eq.re("p (c t) -> p c t", c=NCH)[:, :, 127])
                yield
                qin = A(T, BF16, "qin")
                STT(P, qin, q, 32.0 ** -0.5, Eq, ALU.mult, ALU.mult)
                kin = [A(T, BF16, f"kin{i}") for i in range(2)]
                for i in range(2):
                    STT(P, kin[i], kg, par[:, i:i + 1], Ek, ALU.mult, ALU.mult)
                kend = A(T, BF16, "kend")
                TT(P, "pool", kend, kg, Ee, ALU.mult)
                yield
                GT = []
                for c in range(NCH):
                    pst = P.psum()
                    pstb = pst.bitcast(BF16)
                    cs = slice(c * 128, (c + 1) * 128)
                    TR(P, pstb[:, 0:128], vg[0][:, cs], identb)
                    TR(P, pstb[:, 128:256], vg[1][:, cs], identb)
                    TR(P, pstb[:, 256:384], kend[:, cs], identb)
                    gt_ = A(384, BF16, f"GT{c}")
                    CP(P, "act", gt_, pstb[:, 0:384])
                    GT.append(gt_)
                    yield
                ST = []
                for h in range(4):
                    rows = slice(64 * (h // 2), 64 * (h // 2) + 64)
                    ps = P.psum()
                    for c in range(NCH):
                        cs = slice(c * 128, (c + 1) * 128)
                        MM(P, ps[:, cs], kin[h % 2][rows, cs], qin[rows, cs])
                    st_ = A(T, BF16, f"ST{h}")
                    TT(P, "dve", st_, ps[:, 0:T], ui4[:, 0:T], ALU.mult)
                    ST.append(st_)
                    yield
                Sb = []
                Scur = A((NCH + 1) * 256, F32, "Scur")
                CP(P, "pool", Scur[:, 0:256], gla_S)
                for c in range(NCH):
                    sb_ = A(256, BF16, f"Sb{c}")
                    TT(P, "pool", sb_, Scur[:, c * 256:(c + 1) * 256], bmask, ALU.mult)
                    Sb.append(sb_)
                    ps = P.psum()
                    MM(P, ps[:, 0:256], GT[c][:, 256:384], GT[c][:, 0:256])
                    STT(P, Scur[:, (c + 1) * 256:(c + 2) * 256], Scur[:, c * 256:(c + 1) * 256], gCg[:, c:c + 1],
                        ps[:, 0:256], ALU.mult, ALU.add)
                    yield
                CP(P, "pool", gla_S, Scur[:, NCH * 256:(NCH + 1) * 256])
                o = A(T, F32, "o")
                ob = A(T, BF16, "ob")
                rs = A(T, F32, "rsg")
                for vp in range(2):
                    ps_in, ps_it = P.psum(), P.psum()
                    for s in range(2):
                        h = 2 * vp + s
                        orow = slice(64 * s, 64 * s + 64)
                        krow = slice(64 * vp, 64 * vp + 64)
                        for c in range(NCH):
                            cs = slice(c * 128, (c + 1) * 128)
                            MM(P, ps_in[orow, cs], GT[c][:, 64 * h:64 * h + 64], ST[h][:, cs])
                            MM(P, ps_it[orow, cs], Sb[c][krow, 64 * h:64 * h + 64], qin[krow, cs])
                    CP(P, "act", o, ps_it[:, 0:T])
                    TT(P, "dve", o, o, ps_in[:, 0:T], ALU.add)
                    dbg(f"gla_o{vp}", o, 128, T, tok0, ntok)
                    yield
                    TT(P, "pool", ob, o, o, ALU.mult)
                    ps_m = P.psum()
                    MM(P, ps_m[:, 0:T], ones64b, ob)
                    rsqrt_act(rs, ps_m[:, 0:T], 1.0 / 64, epsc[:, 0:1])
                    STT(P, o, o, col("gng"), rs, ALU.mult, ALU.mult)
                    TT(P, "dve", ycat3[:, 6 + vp, :], o, gg[vp], ALU.mult)
                    yield

            _skip = os.environ.get("K_SKIP", "").split(",")
            threads = [g_ for n_, g_ in (("rw", rw_thread), ("lru", lru_thread), ("gla", gla_thread)) if n_ not in _skip]
            threads = [g_() for g_ in threads]
            def run_greedy(ths):
                ready = {id(t_): 0.0 for t_ in ths}
                out_ = []
                while ths:
                    th = min(ths, key=lambda t_: ready[id(t_)])
                    P.step_t = 0.0
                    try:
                        v_ = next(th)
                    except StopIteration:
                        ths.remove(th)
                        continue
                    if P.step_t > 0:
                        ready[id(th)] = P.step_t
                    if v_ == "B":
                        ths.remove(th)
                        out_.append(th)
                return out_

            if os.environ.get("K_GREEDY", "0") == "1":
                atB = run_greedy(threads)
                if ti + 1 < NT:
                    m_load(ti + 1)
                    atB.append(m_norm(ti + 1))
                run_greedy(atB)
            else:
                atB = []
                while threads:
                    for th in list(threads):
                        try:
                            v_ = next(th)
                        except StopIteration:
                            threads.remove(th)
                            continue
                        if v_ == "B":
                            threads.remove(th)
                            atB.append(th)
                threads = atB
                if ti > 0 and os.environ.get("K_DEFER", "1") == "1":
                    m_outproj(ti - 1)
                if ti + 1 < NT:
                    m_load(ti + 1)
                    threads.append(m_norm(ti + 1))
                rr_w = int(os.environ.get("K_RWW", "1"))
                while threads:
                    for ith, th in enumerate(list(threads)):
                        for _rep in range(rr_w if ith == 0 else 1):
                            try:
                                next(th)
                            except StopIteration:
                                threads.remove(th)
                                break
            P.arena_off = base_thr
            if os.environ.get("K_DEFER", "1") != "1" and ti < NT - 1:
                m_outproj(ti)

            for kc in range(8):
                dbg(f"ycat{kc}", ycat3[:, kc, :], 128, T, tok0, ntok)

        m_outproj(NT - 1)

        if os.environ.get("K_BARRIER", "1") == "1":
            P.barrier()
        P.arena_off = persist_mark
        if last:
            gBf = P.alloc(D, F32, "gBf")
            P.dma("sp", gBf.ap, final_g.partition_broadcast(128), writes=[gBf])
        gB2 = P.alloc(D, F32, "gB2")
        P.dma("sp", gB2.ap, norm2_g[l:l + 1, :].partition_broadcast(128), writes=[gB2])
        wg_sb = P.alloc(8 * DFF, BF16, "wg")
        wu_sb = P.alloc(8 * DFF, BF16, "wu")
        wd_sb = P.alloc(NFC * D, BF16, "wd")
        wg_v = wg_sb.ap.rearrange("p (k f) -> p k f", k=8)
        wu_v = wu_sb.ap.rearrange("p (k f) -> p k f", k=8)
        for kp in range(4):
            for hf in range(2):
                sl_ = slice(hf * 1408, (hf + 1) * 1408)
                P.dma("pool", wg_v[:, 2 * kp:2 * kp + 2, sl_],
                      w_gate[l, 2 * kp * 128:(2 * kp + 2) * 128, sl_].rearrange("(k p) f -> p k f", p=128),
                      writes=[wg_sb])
                P.dma("pool", wu_v[:, 2 * kp:2 * kp + 2, sl_],
                      w_up[l, 2 * kp * 128:(2 * kp + 2) * 128, sl_].rearrange("(k p) f -> p k f", p=128),
                      writes=[wu_sb])
        wd_v = wd_sb.ap.rearrange("p (i d) -> p i d", i=NFC)
        for fp in range(NFC // 2):
            P.dma("pool", wd_v[:, 2 * fp:2 * fp + 2, :],
                  w_down[l, 2 * fp * 128:(2 * fp + 2) * 128, :].rearrange("(i p) d -> p i d", p=128), writes=[wd_sb])
        xTs = [[P.alloc(D, F32, f"fxT{p_}{b}") for b in range(NCH)] for p_ in range(2)]
        hnFs = [P.alloc(8 * T, BF16, f"fhnF{p_}") for p_ in range(2)]
        hF = P.alloc(NFC * T, BF16, "hF")
        hF3 = hF.re("p (k t) -> p k t", k=NFC)
        fhn = P.alloc(D, BF16, "fhn")
        sgt = [P.alloc(T, F32, f"sgt{i}") for i in range(2)]
        scs = [P.alloc(2, F32, f"fsc{i}") for i in range(4)]
        yos = [P.alloc(D, F32, f"yo{i}") for i in range(2)] if last else []
        sci = [0]

        def nsc():
            sci[0] += 1
            return scs[sci[0] % 4]

        def f_pro(ti):
            tok0 = ti * T
            xT = xTs[ti % 2]
            hnF3 = hnFs[ti % 2].re("p (k t) -> p k t", k=8)
            for b in range(NCH):
                P.dma("sp", xT[b].ap, xa_d[tok0 + b * 128: tok0 + (b + 1) * 128, :], reads=[xa_buf], writes=[xT[b]])
            for b in range(NCH):
                rms_block(xT[b], gB2, fhn, nsc())
                pst = P.psum()
                pstb = pst.bitcast(BF16)
                for kc in range(8):
                    TR(P, pstb[:, kc * 128:(kc + 1) * 128], fhn[:, kc * 128:(kc + 1) * 128], identb)
                CP(P, "act", hnF3[:, :, b * 128:(b + 1) * 128], pstb.re("p (k t) -> p k t", k=8))

        def f_gateup(ti):
            hnF3 = hnFs[ti % 2].re("p (k t) -> p k t", k=8)
            for fc in range(NFC):
                ps = P.psum()
                for kc in range(8):
                    MM(P, ps[:, 0:T], wg_sb[:, kc * DFF + fc * 128: kc * DFF + (fc + 1) * 128], hnF3[:, kc, :],
                       start=(kc == 0), stop=(kc == 7))
                for kc in range(8):
                    MM(P, ps[:, T:2 * T], wu_sb[:, kc * DFF + fc * 128: kc * DFF + (fc + 1) * 128], hnF3[:, kc, :],
                       start=(kc == 0), stop=(kc == 7))
                s_ = sgt[fc % 2]
                ACT(P, s_, ps[:, 0:T], AF.Silu)
                TT(P, "dve", hF3[:, fc, :], s_, ps[:, T:2 * T], ALU.mult)

        def f_down(ti):
            tok0 = ti * T
            xT = xTs[ti % 2]
            for b in range(NCH):
                for hf in range(2):
                    ps = P.psum()
                    for fc in range(NFC):
                        MM(P, ps, hF3[:, fc, b * 128:(b + 1) * 128],
                           wd_sb[:, fc * D + hf * 512: fc * D + (hf + 1) * 512], start=(fc == 0), stop=(fc == NFC - 1))
                    TT(P, "dve", xT[b][:, hf * 512:(hf + 1) * 512], xT[b][:, hf * 512:(hf + 1) * 512], ps, ALU.add)
                if last:
                    yo = yos[b % 2]
                    rms_block(xT[b], gBf, yo, nsc())
                    fin.append(P.dma("sp", out_d[tok0 + b * 128: tok0 + (b + 1) * 128, :], yo.ap, reads=[yo]))
                else:
                    P.dma("sp", xb_d[tok0 + b * 128: tok0 + (b + 1) * 128, :], xT[b].ap, reads=[xT[b]],
                          writes=[xb_buf])

        f_pro(0)
        for ti in range(NT):
            f_gateup(ti)
            if ti + 1 < NT:
                f_pro(ti + 1)
            f_down(ti)
    P.finish(fin)
    P.build()
    return nc, list(dbg_out.keys())


def make_in_map(inputs, xs, nlayers):
    m = {"x": np.ascontiguousarray(xs, dtype=np.float32)}
    for k in ["w_in", "w_out", "ffn_w_gate", "ffn_w_up", "ffn_w_down", "rw_w_up", "rw_a_up", "rw_g_up",
              "lru_wa", "lru_wx", "gla_gk_up", "norm1_g", "norm2_g"]:
        m[k] = np.ascontiguousarray(np.asarray(inputs[k], np.float32)[:nlayers])
    m["cols"] = np.stack([pack_cols(inputs, l) for l in range(nlayers)])
    m["final_norm_g"] = np.asarray(inputs["final_norm_g"], np.float32).reshape(1, D)
    for k, v in make_consts().items():
        m["c_" + k] = v
    return m


_CACHE = {}


def kernel(**inputs):
    x = np.asarray(inputs["x"], np.float32)
    B, S, _ = x.shape
    L = np.asarray(inputs["w_in"]).shape[0]
    key = (S, L)
    if key not in _CACHE:
        _CACHE[key] = build_program(S, L)[0]
    nc = _CACHE[key]
    in_maps = [make_in_map(inputs, x[c % B], L) for c in range(8)]
    res = run_bass_kernel_spmd(nc, in_maps, core_ids=list(range(8)))
    return np.stack([res.results[b]["out"] for b in range(B)], axis=0)
```

```python
import contextlib
import math
import os
import numpy as np
import concourse.bass as bass
import concourse.mybir as mybir
from concourse.bass_utils import run_bass_kernel_spmd

F32 = mybir.dt.float32
BF16 = mybir.dt.bfloat16
AF = mybir.ActivationFunctionType
ALU = mybir.AluOpType

CHUNK = 8000
SAMEQ = os.environ.get('K_SAMEQ', '1') == '1'
NDMASEM = 12

D = 1024
PIN = 2832
DFF = 2816
NFC = DFF // 128
EPS = 1e-6
RW_EPS = 64e-5
DEC = math.exp(-0.5)
T = 256
NCH = T // 128


class Buf:
    __slots__ = ("name", "writers", "readers", "t")

    def __init__(self, name=""):
        self.name = name
        self.writers = {}
        self.readers = {}
        self.t = 0.0


def _dep_kv(d):
    if d[0] == "e":
        return ("e", d[1], d[2] // CHUNK), d[2] % CHUNK + 1
    return ("d", d[1], d[2]), d[3]


def _merge(dst, src):
    for k, v in src.items():
        if dst.get(k, 0) < v:
            dst[k] = v


class Tile:
    __slots__ = ("ap", "buf")

    def __init__(self, ap, buf=None):
        self.ap = ap
        self.buf = buf if buf is not None else Buf()

    def __getitem__(self, k):
        return Tile(self.ap[k], self.buf)

    def bitcast(self, dt):
        return Tile(self.ap.bitcast(dt), self.buf)

    def re(self, s, **kw):
        return Tile(self.ap.rearrange(s, **kw), self.buf)


class Prog:
    ENG = ("pe", "act", "dve", "pool", "sp")

    def __init__(self, nc):
        self.nc = nc
        self.stack = contextlib.ExitStack()
        self.streams = {e: [] for e in self.ENG}
        self.count = {e: 0 for e in self.ENG}
        self.esems = {e: [] for e in self.ENG}
        self.dsems = {}
        self.dma_n = {e: 0 for e in self.ENG}
        self.dma_hist = {e: {} for e in self.ENG}
        self.waited = {e: {} for e in self.ENG}
        self.n_t = 0
        self.final_deps = []
        self.arena = None
        self.arena_off = 0
        self.arena_size = 0
        self.psb = []
        self.ps_i = 0
        self.live = []
        self.t_eng = {e: 0.0 for e in self.ENG}
        self.step_t = 0.0

    def init_mem(self, arena_f32_cols):
        self.arena_size = arena_f32_cols
        self.arena = self.stack.enter_context(
            self.nc.sbuf_tensor("arena", [128, arena_f32_cols], F32))
        for i in range(8):
            t = self.stack.enter_context(self.nc.psum_tensor(f"psb{i}", [128, 512], F32))
            self.psb.append(Tile(t[:, :], Buf(f"ps{i}")))

    def alloc(self, free_elems, dt=F32, name=""):
        ncol = free_elems if dt == F32 else (free_elems + 1) // 2
        if self.arena_off + ncol > self.arena_size:
            raise RuntimeError(f"arena overflow at {name}: {self.arena_off}+{ncol}>{self.arena_size}")
        s0, s1 = self.arena_off, self.arena_off + ncol
        self.arena_off += ncol
        self.hi = max(getattr(self, "hi", 0), s1)
        keep, over = [], []
        for ent in self.live:
            (over if (ent[0] < s1 and s0 < ent[1]) else keep).append(ent)
        if len(over) == 1 and over[0][0] == s0 and over[0][1] == s1 and over[0][2] == (dt, free_elems):
            return over[0][3]
        ap = self.arena[:, s0:s1]
        if dt != F32:
            ap = ap.bitcast(dt)[:, 0:free_elems]
        buf = Buf(name)
        for ent in over:
            _merge(buf.writers, ent[3].buf.writers)
            _merge(buf.readers, ent[3].buf.readers)
        t = Tile(ap, buf)
        keep.append((s0, s1, (dt, free_elems), t))
        self.live = keep
        return t

    def psum(self):
        t = self.psb[self.ps_i]
        self.ps_i = (self.ps_i + 1) % 8
        return t

    def _deps(self, e, reads, writes, is_dma=False):
        need = {}
        for r in reads:
            _merge(need, r.writers)
        for w in writes:
            _merge(need, w.writers)
            _merge(need, w.readers)
        out = []
        for key, val in need.items():
            if key[0] == "e" and key[1] == e and (e == "pe" or not SAMEQ):
                continue
            if self.waited[e].get(key, 0) >= val:
                continue
            self.waited[e][key] = val
            out.append((key, val))
        return out

    def _record(self, d, reads, writes, is_dma):
        k, v = _dep_kv(d)
        for w in writes:
            if is_dma:
                w.writers = {kk: vv for kk, vv in w.writers.items() if kk[0] == "d"}
            else:
                w.writers = {}
            w.writers[k] = v
            w.readers = {}
        for r in reads:
            if r.readers.get(k, 0) < v:
                r.readers[k] = v

    def _sem(self, key):
        if key[0] == "e":
            return self.esems[key[1]][key[2]]
        return self.dsems[(key[1], key[2])]

    def _est(self, e, reads, writes, cost):
        t0 = self.t_eng[e]
        for b in reads:
            if b.t > t0:
                t0 = b.t
        for b in writes:
            if b.t > t0:
                t0 = b.t
        self.t_eng[e] = t0 + cost
        fin = t0 + cost + 0.25
        for b in writes:
            b.t = fin
        if fin > self.step_t:
            self.step_t = fin

    def op(self, e, fn, reads=(), writes=(), cost=None):
        if cost is None:
            n = 256
            for w in writes:
                if isinstance(w, Tile):
                    n = 1
                    for d_ in w.ap.shape[1:]:
                        n *= d_
                    break
            cost = {"act": 0.2 + n / 1150.0, "dve": 0.15 + n / 960.0, "pool": 0.2 + n / 480.0,
                    "pe": 0.05 + n / 2400.0}.get(e, 1.0)
        reads = [r.buf if isinstance(r, Tile) else r for r in reads]
        writes = [w.buf if isinstance(w, Tile) else w for w in writes]
        self._est(e, reads, writes, cost)
        waits = self._deps(e, reads, writes)
        idx = self.count[e]
        self.count[e] += 1
        mykey = ("e", e, idx // CHUNK)

        def emit(eng, waits=waits, fn=fn, mykey=mykey):
            for k, v in waits:
                eng.wait_ge(self._sem(k), v)
            fn(eng).then_inc(self._sem(mykey), 1)

        self.streams[e].append(emit)
        d = ("e", e, idx)
        self._record(d, reads, writes, False)
        return d

    def dma(self, q, out, in_, reads=(), writes=(), **kw):
        reads = [r.buf if isinstance(r, Tile) else r for r in reads]
        writes = [w.buf if isinstance(w, Tile) else w for w in writes]
        self._est(q, reads, writes, 2.0)
        self.t_eng[q] -= 1.9
        waits = self._deps(q, reads, writes, True)
        n = self.dma_n[q]
        self.dma_n[q] += 1
        slot = n % NDMASEM
        prev = self.dma_hist[q].get(slot, 0)
        val = prev + 16
        self.dma_hist[q][slot] = val
        key = ("d", q, slot)
        if prev > 0 and self.waited[q].get(key, 0) < prev:
            waits = waits + [(key, prev)]
            self.waited[q][key] = prev

        def emit(eng, waits=waits, key=key):
            for k, v in waits:
                eng.wait_ge(self._sem(k), v)
            eng.dma_start(out=out, in_=in_, **kw).then_inc(self._sem(key), 16)

        self.streams[q].append(emit)
        d = ("d", q, slot, val)
        self._record(d, reads, writes, True)
        return d

    def barrier(self):
        keys = []
        for e in self.ENG:
            if self.count[e] > 0:
                idx = self.count[e] - 1
                keys.append((("e", e, idx // CHUNK), idx % CHUNK + 1))
            for slot, val in self.dma_hist[e].items():
                keys.append((("d", e, slot), val))
        for f in self.ENG:
            mine = []
            for k, v in keys:
                if k[0] == "e" and k[1] == f:
                    continue
                if self.waited[f].get(k, 0) >= v:
                    continue
                self.waited[f][k] = v
                mine.append((k, v))

            def emit(eng, mine=mine):
                for k, v in mine:
                    eng.wait_ge(self._sem(k), v)

            self.streams[f].append(emit)

    def finish(self, deps):
        self.final_deps = list(deps)

    def build(self):
        nc = self.nc
        st = self.stack
        for e in self.ENG:
            nsem = (self.count[e] + CHUNK - 1) // CHUNK
            self.esems[e] = [st.enter_context(nc.semaphore(f"s_{e}_{i}")) for i in range(nsem)]
            nd = min(self.dma_n[e], NDMASEM)
            for s in range(nd):
                self.dsems[(e, s)] = st.enter_context(nc.semaphore(f"d_{e}_{s}"))
        fin = []
        for d in self.final_deps:
            if d[0] == "e":
                fin.append((("e", d[1], d[2] // CHUNK), d[2] % CHUNK + 1))
            else:
                fin.append((("d", d[1], d[2]), d[3]))
        block = st.enter_context(nc.Block())
        streams = self.streams

        @block.tensor
        def _(eng):
            for f in streams["pe"]:
                f(eng)

        @block.scalar
        def _(eng):
            for f in streams["act"]:
                f(eng)

        @block.vector
        def _(eng):
            for f in streams["dve"]:
                f(eng)

        @block.gpsimd
        def _(eng):
            for f in streams["pool"]:
                f(eng)

        @block.sync
        def _(eng):
            for f in streams["sp"]:
                f(eng)
            for k, v in fin:
                eng.wait_ge(self._sem(k), v)

        st.close()


def _ap(x):
    return x.ap if isinstance(x, Tile) else x


def _tl(*xs):
    return [x for x in xs if isinstance(x, Tile)]


def ACT(P, out, in_, func, bias=None, scale=None, accum=None):
    kw = {}
    if bias is not None:
        kw["bias"] = _ap(bias)
    if scale is not None:
        kw["scale"] = _ap(scale)
    if accum is not None:
        kw["accum_out"] = _ap(accum)
    P.op("act", lambda e: e.activation(out=out.ap, in_=in_.ap, func=func, **kw),
         reads=_tl(in_, bias, scale), writes=_tl(out, accum))


def TT(P, eng, out, a, b, op):
    P.op(eng, lambda e: e.tensor_tensor(out=out.ap, in0=a.ap, in1=b.ap, op=op),
         reads=_tl(a, b), writes=[out])


def TS(P, eng, out, a, s1, op0, s2=None, op1=None):
    if op1 is None:
        P.op(eng, lambda e: e.tensor_scalar(out=out.ap, in0=a.ap, scalar1=_ap(s1), scalar2=None, op0=op0),
             reads=_tl(a, s1), writes=[out])
    else:
        P.op(eng, lambda e: e.tensor_scalar(out=out.ap, in0=a.ap, scalar1=_ap(s1), scalar2=_ap(s2),
                                            op0=op0, op1=op1),
             reads=_tl(a, s1, s2), writes=[out])


def STT(P, out, in0, scalar, in1, op0, op1):
    P.op("dve", lambda e: e.scalar_tensor_tensor(out=out.ap, in0=in0.ap, scalar=_ap(scalar), in1=in1.ap,
                                                 op0=op0, op1=op1),
         reads=_tl(in0, scalar, in1), writes=[out])


def CP(P, eng, out, in_):
    if eng == "act":
        P.op("act", lambda e: e.activation(out=out.ap, in_=in_.ap, func=AF.Copy), reads=[in_], writes=[out])
    else:
        P.op(eng, lambda e: e.tensor_copy(out=out.ap, in_=in_.ap), reads=[in_], writes=[out])


def MM(P, out, lhsT, rhs, start=True, stop=True):
    P.op("pe", lambda e: e.matmul(out.ap, lhsT=lhsT.ap, rhs=rhs.ap, start=start, stop=stop),
         reads=[lhsT, rhs], writes=[out])


def TR(P, out, in_, ident):
    P.op("pe", lambda e: e.transpose(out=out.ap, in_=in_.ap, identity=ident.ap),
         reads=[in_, ident], writes=[out])


def SCAN(P, out, d0, d1, init):
    P.op("dve", lambda e: e.tensor_tensor_scan(out=out.ap, data0=d0.ap, data1=d1.ap, initial=_ap(init),
                                               op0=ALU.mult, op1=ALU.add),
         reads=_tl(d0, d1, init), writes=[out])


def MEMSET(P, eng, out, val):
    P.op(eng, lambda e: e.memset(out.ap, val), writes=[out])


COLS = {}


def _col_layout():
    names = [("mu", 8), ("w0", 2), ("a0", 2), ("k_k", 2), ("k_a", 2), ("r_k", 2), ("ln_g", 2), ("ln_b", 2),
             ("cw0", 4), ("cw1", 4), ("cw2", 4), ("cw3", 4), ("cb", 4), ("ba", 4), ("bx", 4), ("lam", 4),
             ("lng", 4), ("gkb", 1), ("gng", 1)]
    off = 0
    for n, c in names:
        COLS[n] = (off, c)
        off += c
    return off


NCOL = _col_layout()


def pack_cols(inp, l):
    out = np.zeros((128, NCOL), np.float32)

    def put(name, vec):
        o, c = COLS[name]
        out[:, o:o + c] = np.asarray(vec, np.float32).reshape(c, 128).T

    put("mu", inp["rw_mu"][l])
    put("w0", inp["rw_w0"][l])
    put("a0", inp["rw_a0"][l])
    put("k_k", inp["rw_k_k"][l])
    put("k_a", inp["rw_k_a"][l])
    put("r_k", inp["rw_r_k"][l].reshape(-1))
    put("ln_g", inp["rw_ln_g"][l])
    put("ln_b", inp["rw_ln_b"][l])
    for j in range(4):
        put(f"cw{j}", inp["lru_conv_w"][l, j])
    put("cb", inp["lru_conv_b"][l])
    put("ba", inp["lru_ba"][l])
    put("bx", inp["lru_bx"][l])
    put("lam", inp["lru_lam"][l])
    put("lng", inp["lru_norm_g"][l])
    put("gkb", inp["gla_gk_b"][l])
    put("gng", np.concatenate([inp["gla_norm_g"][l], inp["gla_norm_g"][l]]))
    return out


def make_consts():
    c = {}
    idx = np.arange(128)
    su = (idx[:, None] < idx[None, :]).astype(np.float32)
    ui = (idx[:, None] <= idx[None, :]).astype(np.float32)
    sl = (idx[:, None] > idx[None, :]).astype(np.float32)
    c["ident"] = np.eye(128, dtype=np.float32)
    c["cmask"] = np.concatenate([su, su, ui, ui], axis=1)
    c["sl4"] = np.concatenate([sl] * 4, axis=1)
    c["ui4"] = np.concatenate([ui] * NCH, axis=1)
    ob = np.zeros((128, 128), np.float32)
    ob[:64, :64] = 1
    ob[64:, 64:] = 1
    c["ones64"] = ob
    rm = np.ones((128, 2 * T), np.float32)
    rm[:, ::128] = 0
    c["rmask"] = rm
    bm = np.zeros((128, 256), np.float32)
    for h in range(4):
        bm[32 * h:32 * h + 32, 64 * h:64 * h + 64] = 1
    c["bmask"] = bm
    par = np.zeros((128, 2), np.float32)
    for h in range(4):
        par[32 * h:32 * h + 32, h % 2] = 1
    c["par"] = par
    return c


CONST_SHAPES = {"ident": 128, "cmask": 512, "sl4": 512, "ui4": T, "ones64": 128,
                "rmask": 2 * T, "bmask": 256, "par": 2}


def build_program(ntok, nlayers, debug=()):
    nc = bass.Bass("TRN2", target_bir_lowering=False)
    NT = ntok // T
    L = nlayers

    def din(name, shape):
        return nc.dram_tensor(name, list(shape), F32, kind="ExternalInput").ap()

    x_in = din("x", [ntok, D])
    w_in = din("w_in", [L, D, PIN])
    w_out = din("w_out", [L, D, D])
    w_gate = din("ffn_w_gate", [L, D, DFF])
    w_up = din("ffn_w_up", [L, D, DFF])
    w_down = din("ffn_w_down", [L, DFF, D])
    rw_w_up = din("rw_w_up", [L, 64, 256])
    rw_a_up = din("rw_a_up", [L, 64, 256])
    rw_g_up = din("rw_g_up", [L, 128, 256])
    lru_wa = din("lru_wa", [L, 8, 64, 64])
    lru_wx = din("lru_wx", [L, 8, 64, 64])
    gk_up = din("gla_gk_up", [L, 16, 128])
    cols_d = din("cols", [L, 128, NCOL])
    norm1_g = din("norm1_g", [L, D])
    norm2_g = din("norm2_g", [L, D])
    final_g = din("final_norm_g", [1, D])
    cdram = {k: din("c_" + k, [128, n]) for k, n in CONST_SHAPES.items()}
    out_d = nc.dram_tensor("out", [ntok, D], F32, kind="ExternalOutput").ap()
    xa_d = nc.dram_tensor("xa_scr", [ntok, D], F32).ap()
    xb_d = nc.dram_tensor("xb_scr", [ntok, D], F32).ap()
    xa_buf, xb_buf = Buf("xa"), Buf("xb")
    dbg_out = {}

    P = Prog(nc)
    P.init_mem(52500)
    fin = []

    def dbg(name, tile, rows, cols, tok0=None, total_cols=None):
        if name not in debug:
            return
        if name not in dbg_out:
            tc = total_cols if total_cols is not None else cols
            dbg_out[name] = nc.dram_tensor("dbg_" + name, [rows, tc], F32, kind="ExternalOutput").ap()
        dst = dbg_out[name]
        c0 = tok0 if tok0 is not None else 0
        fin.append(P.dma("pool", dst[0:rows, c0:c0 + cols], tile.ap, reads=[tile]))

    cst = {}
    for k, n in CONST_SHAPES.items():
        if k in ("rmask",):
            cst[k] = P.alloc(n, BF16, "c_" + k)
            P.dma("pool", cst[k].ap, cdram[k][:, :], writes=[cst[k]])
        else:
            cst[k] = P.alloc(n, F32, "c_" + k)
            P.dma("sp", cst[k].ap, cdram[k][:, :], writes=[cst[k]])
    identf = cst["ident"]
    identb = P.alloc(128, BF16, "identb")
    CP(P, "pool", identb, identf)
    ident4b = P.alloc(512, BF16, "ident4b")
    for i in range(4):
        CP(P, "pool", ident4b[:, i * 128:(i + 1) * 128], identf)
    ones64b = P.alloc(128, BF16, "ones64b")
    CP(P, "pool", ones64b, cst["ones64"])
    cmask, sl4, ui4, rmask, bmask, par = (cst[k] for k in ("cmask", "sl4", "ui4", "rmask", "bmask", "par"))
    epsc = P.alloc(2, F32, "epsc")
    MEMSET(P, "pool", epsc[:, 0:1], EPS)
    MEMSET(P, "pool", epsc[:, 1:2], RW_EPS)

    rw_carry = P.alloc(8, F32, "rw_carry")
    lru_xc = [P.alloc(3, F32, f"lru_xc{j}") for j in range(4)]
    lru_h = P.alloc(4, F32, "lru_h")
    rw_H = [P.alloc(64, F32, f"rwH{hp}") for hp in range(2)]
    gla_S = P.alloc(256, F32, "glaS")
    persist_mark = P.arena_off

    def rms_block(xblk, gB, hn_out, sc=None):
        if sc is None:
            ssq = P.alloc(1, F32, "ssq")
            rstd = P.alloc(1, F32, "rstd")
        else:
            ssq, rstd = sc[:, 0:1], sc[:, 1:2]
        ACT(P, hn_out, xblk, AF.Square, accum=ssq)
        ACT(P, rstd, ssq, AF.Ln, bias=epsc[:, 0:1], scale=1.0 / D)
        ACT(P, rstd, rstd, AF.Exp, scale=-0.5)
        STT(P, hn_out, xblk, rstd[:, 0:1], gB, ALU.mult, ALU.mult)

    for l in range(L):
        src_d, src_buf = (x_in, None) if l == 0 else (xb_d, xb_buf)
        last = l == L - 1
        if os.environ.get("K_BARRIER", "1") == "1":
            P.barrier()
        P.arena_off = persist_mark
        colsT = P.alloc(NCOL, F32, "cols")
        P.dma("sp", colsT.ap, cols_d[l], writes=[colsT])

        def col(name, j=0, rows=slice(0, 128)):
            o, c = COLS[name]
            return colsT[rows, o + j:o + j + 1]

        dcol = P.alloc(24, F32, "dcol")
        o_mu = COLS["mu"][0]
        omm = dcol[:, 0:8]
        TS(P, "pool", omm, colsT[:, o_mu:o_mu + 8], -1.0, ALU.mult, 1.0, ALU.add)
        o_ka = COLS["k_a"][0]
        omka = dcol[:, 8:10]
        TS(P, "pool", omka, colsT[:, o_ka:o_ka + 2], -1.0, ALU.mult, 1.0, ALU.add)
        o_lam = COLS["lam"][0]
        c1 = dcol[:, 10:14]
        c2 = dcol[:, 14:18]
        ACT(P, c1, colsT[:, o_lam:o_lam + 4], AF.Exp, scale=-1.0)
        ACT(P, c1, c1, AF.Ln, bias=1.0)
        TS(P, "pool", c2, c1, -16.0, ALU.mult)
        TS(P, "pool", c1, c1, -8.0, ALU.mult)
        MEMSET(P, "pool", rw_carry, 0.0)
        for j in range(4):
            MEMSET(P, "pool", lru_xc[j], 0.0)
        MEMSET(P, "pool", lru_h, 0.0)
        for hp in range(2):
            MEMSET(P, "pool", rw_H[hp], 0.0)
        MEMSET(P, "pool", gla_S, 0.0)

        gB1 = P.alloc(D, F32, "gB1")
        P.dma("sp", gB1.ap, norm1_g[l:l + 1, :].partition_broadcast(128), writes=[gB1])
        w_in_sb = P.alloc(8 * PIN, BF16, "w_in")
        w_in_v = w_in_sb.ap.rearrange("p (k f) -> p k f", k=8)
        for kp in range(4):
            for hf in range(2):
                P.dma("pool", w_in_v[:, 2 * kp:2 * kp + 2, hf * 1416:(hf + 1) * 1416],
                      w_in[l, 2 * kp * 128:(2 * kp + 2) * 128, hf * 1416:(hf + 1) * 1416].rearrange(
                          "(k p) f -> p k f", p=128), writes=[w_in_sb])
        w_out_sb = P.alloc(8 * D, BF16, "w_out")
        w_out_v = w_out_sb.ap.rearrange("p (k f) -> p k f", k=8)
        for kp in range(4):
            P.dma("pool", w_out_v[:, 2 * kp:2 * kp + 2, :],
                  w_out[l, 2 * kp * 128:(2 * kp + 2) * 128, :].rearrange("(k p) f -> p k f", p=128),
                  writes=[w_out_sb])
        wa_up = P.alloc(256, BF16, "wa_up")
        P.dma("pool", wa_up.ap[0:64, :], rw_w_up[l], writes=[wa_up])
        P.dma("pool", wa_up.ap[64:128, :], rw_a_up[l], writes=[wa_up])
        g_up = P.alloc(256, BF16, "g_up")
        P.dma("pool", g_up.ap, rw_g_up[l], writes=[g_up])
        gkup = P.alloc(128, BF16, "gkup")
        P.dma("pool", gkup.ap[0:16, :], gk_up[l], writes=[gkup])
        wabd = P.alloc(4 * 128, BF16, "wabd")
        wxbd = P.alloc(4 * 128, BF16, "wxbd")
        MEMSET(P, "pool", wabd, 0.0)
        MEMSET(P, "pool", wxbd, 0.0)
        for s in range(2):
            for dst_, src_ in ((wabd, lru_wa), (wxbd, lru_wx)):
                P.dma("pool",
                      dst_.ap[64 * s:64 * s + 64, :].rearrange("p (j c) -> p j c", j=4)[:, :, 64 * s:64 * s + 64],
                      src_[l].rearrange("(j s) r c -> s r j c", s=2)[s], writes=[dst_])
        mbbd = [[P.alloc(128, F32, f"mbbd{hp}{c}") for c in range(NCH)] for hp in range(2)]
        for hp in range(2):
            for c in range(NCH):
                MEMSET(P, "pool", mbbd[hp][c], 0.0)
        xTs = [[P.alloc(D, F32, f"xT{p_}{b}") for b in range(NCH)] for p_ in range(2)]
        hnF = P.alloc(8 * T, BF16, "hnF")
        hnF3 = hnF.re("p (k t) -> p k t", k=8)
        ycat = P.alloc(8 * T, BF16, "ycat")
        ycat3 = ycat.re("p (k t) -> p k t", k=8)
        mscs = [P.alloc(2, F32, f"msc{i}") for i in range(4)]
        tile_mark = P.arena_off

        def m_load(ti_):
            rd = [src_buf] if src_buf is not None else []
            for b in range(NCH):
                P.dma("sp", xTs[ti_ % 2][b].ap, src_d[ti_ * T + b * 128: ti_ * T + (b + 1) * 128, :], reads=rd,
                      writes=[xTs[ti_ % 2][b]])

        def m_norm(ti_):
            save = P.arena_off
            P.arena_off = tile_mark
            hn = P.alloc(D, BF16, "hn")
            P.arena_off = save
            for b in range(NCH):
                rms_block(xTs[ti_ % 2][b], gB1, hn, mscs[(2 * ti_ + b) % 4])
                yield
                pst = P.psum()
                pstb = pst.bitcast(BF16)
                for kc in range(8):
                    TR(P, pstb[:, kc * 128:(kc + 1) * 128], hn[:, kc * 128:(kc + 1) * 128], identb)
                CP(P, "act", hnF3[:, :, b * 128:(b + 1) * 128], pstb.re("p (k t) -> p k t", k=8))
                yield

        def m_outproj(ti_):
            xT_ = xTs[ti_ % 2]
            for b in range(NCH):
                for hf in range(2):
                    ps = P.psum()
                    for kc in range(8):
                        MM(P, ps, ycat3[:, kc, b * 128:(b + 1) * 128],
                           w_out_sb[:, kc * D + hf * 512: kc * D + (hf + 1) * 512], start=(kc == 0), stop=(kc == 7))
                    TT(P, "dve", xT_[b][:, hf * 512:(hf + 1) * 512], xT_[b][:, hf * 512:(hf + 1) * 512], ps, ALU.add)
                P.dma("sp", xa_d[ti_ * T + b * 128: ti_ * T + (b + 1) * 128, :], xT_[b].ap, reads=[xT_[b]],
                      writes=[xa_buf])

        m_load(0)
        for _ in m_norm(0):
            pass

        for ti in range(NT):
            P.arena_off = tile_mark
            tok0 = ti * T
            xT = xTs[ti % 2]

            def proj(c0, ncols, evac):
                ps = P.psum()
                for kc in range(8):
                    MM(P, ps[0:ncols, 0:T], w_in_sb[:, kc * PIN + c0: kc * PIN + c0 + ncols], hnF3[:, kc, :],
                       start=(kc == 0), stop=(kc == 7))
                evac(ps[0:ncols, 0:T])


            base_thr = P.arena_off

            class Region:
                def __init__(self, start, size):
                    self.off = start
                    self.end = start + size

                def alloc(self, n, dt=F32, name=""):
                    save = P.arena_off
                    P.arena_off = self.off
                    t = P.alloc(n, dt, name)
                    self.off = P.arena_off
                    P.arena_off = save
                    if self.off > self.end:
                        raise RuntimeError(f"region overflow {name} {self.off}>{self.end}")
                    return t

            SZ_RW, SZ_LRU, SZ_GLA = 14900, 4900, 5780

            def rsqrt_act(out, in_, scale, bias):
                ACT(P, out, in_, AF.Ln, bias=bias, scale=scale)
                ACT(P, out, out, AF.Exp, scale=-0.5)

            def rw_thread():
                R = Region(base_thr, SZ_RW)
                A = R.alloc
                W2 = 2 * T
                NC2 = 2 * NCH
                ptmp = A(1 + T, F32, "ptmp")
                ltmp = A(T, F32, "ltmp")
                rW = A(W2, F32, "rW")
                kW = A(W2, F32, "kW")
                vW = A(W2, F32, "vW")
                waT = A(T, F32, "waT")
                gloT = A(T, F32, "gloT")
                dest = [rW[:, 0:T], rW[:, T:W2], kW[:, 0:T], kW[:, T:W2], vW[:, 0:T], vW[:, T:W2], waT, gloT]
                for gi in range(8):
                    def ev(ps, gi=gi):
                        CP(P, "act", ptmp[:, 1:1 + T], ps)
                        CP(P, "pool", ptmp[:, 0:1], rw_carry[:, gi:gi + 1])
                        TS(P, "dve", ltmp, ptmp[:, 0:T], col("mu", gi), ALU.mult)
                        STT(P, dest[gi], ptmp[:, 1:1 + T], omm[:, gi:gi + 1], ltmp, ALU.mult, ALU.add)
                        CP(P, "pool", rw_carry[:, gi:gi + 1], ptmp[:, T:T + 1])
                    proj(gi * 128, 128, ev)
                    yield
                wab = A(T, BF16, "wab")
                ACT(P, wab[0:64, :], waT[0:64, :], AF.Tanh)
                CP(P, "pool", wab[64:128, :], waT[64:128, :])
                sgl = A(T, BF16, "sgl")
                ACT(P, sgl, gloT, AF.Sigmoid)
                yield
                sgW = A(W2, F32, "sgW")
                aW = A(W2, F32, "aW")
                gSW = A(W2, F32, "gSW")
                for hp in range(2):
                    hsl = slice(hp * T, (hp + 1) * T)
                    ps_w, ps_a, ps_g = P.psum(), P.psum(), P.psum()
                    MM(P, ps_w[:, 0:T], wa_up[0:64, hp * 128:(hp + 1) * 128], wab[0:64, :])
                    MM(P, ps_a[:, 0:T], wa_up[64:128, hp * 128:(hp + 1) * 128], wab[64:128, :])
                    MM(P, ps_g[:, 0:T], g_up[:, hp * 128:(hp + 1) * 128], sgl)
                    ACT(P, sgW[:, hsl], ps_w[:, 0:T], AF.Sigmoid, bias=col("w0", hp))
                    ACT(P, aW[:, hsl], ps_a[:, 0:T], AF.Sigmoid, bias=col("a0", hp))
                    CP(P, "dve", gSW[:, hsl], ps_g[:, 0:T])
                    yield
                yield "B"
                css = A(W2, F32, "css")
                SCAN(P, css, rmask, sgW, 0.0)
                cse = A(W2, F32, "cse")
                TT(P, "pool", cse, css, sgW, ALU.subtract)
                E1 = A(W2, F32, "E1")
                E0 = cse
                Einv = A(W2, F32, "Einv")
                Eend = sgW
                nb = A(NC2, F32, "nb")
                TS(P, "pool", nb, css.re("p (c t) -> p c t", c=NC2)[:, :, 127], -DEC, ALU.mult)
                ACT(P, E1, css, AF.Exp, scale=-DEC)
                ACT(P, Einv, css, AF.Exp, scale=DEC)
                yield
                for c in range(NC2):
                    ACT(P, Eend[:, c * 128:(c + 1) * 128], css[:, c * 128:(c + 1) * 128], AF.Exp,
                        bias=nb[:, c:c + 1], scale=DEC)
                ACT(P, E0, cse, AF.Exp, scale=-DEC)
                gC = A(NC2, F32, "gC")
                CP(P, "pool", gC, E1.re("p (c t) -> p c t", c=NC2)[:, :, 127])
                yield
                kk = A(W2, F32, "kk")
                sqk = A(W2, BF16, "sqk")
                for hp in range(2):
                    hsl = slice(hp * T, (hp + 1) * T)
                    ACT(P, sqk[:, hsl], kW[:, hsl], AF.Square, scale=col("k_k", hp))
                ps_n = P.psum()
                MM(P, ps_n, ones64b, sqk)
                rn = A(W2, F32, "rn")
                rsqrt_act(rn, ps_n, 1.0, 1e-24)
                yield
                for hp in range(2):
                    hsl = slice(hp * T, (hp + 1) * T)
                    STT(P, kk[:, hsl], kW[:, hsl], col("k_k", hp), rn[:, hsl], ALU.mult, ALU.mult)
                kmod = A(W2, F32, "kmod")
                for hp in range(2):
                    hsl = slice(hp * T, (hp + 1) * T)
                    TS(P, "dve", kmod[:, hsl], aW[:, hsl], col("k_a", hp), ALU.mult, omka[:, hp:hp + 1], ALU.add)
                TT(P, "dve", kmod, kmod, kW, ALU.mult)
                yield
                bvec = A(W2, F32, "bvec")
                TT(P, "dve", bvec, kk, aW, ALU.mult)
                rkb = sqk
                for hp in range(2):
                    hsl = slice(hp * T, (hp + 1) * T)
                    STT(P, rkb[:, hsl], rW[:, hsl], col("r_k", hp), kmod[:, hsl], ALU.mult, ALU.mult)
                ps_b = P.psum()
                MM(P, ps_b, ones64b, rkb)
                bonus = A(W2, F32, "bonus")
                TT(P, "dve", bonus, ps_b, vW, ALU.mult)
                for hp in range(2):
                    hsl = slice(hp * T, (hp + 1) * T)
                    TS(P, "pool", bonus[:, hsl], bonus[:, hsl], col("ln_b", hp), ALU.add)
                yield
                Rt = A(W2, BF16, "Rt")
                At = A(W2, BF16, "At")
                Bt = A(W2, BF16, "Bt")
                Kt = A(W2, BF16, "Kt")
                Kh = A(W2, BF16, "Kh")
                Bh = A(W2, BF16, "Bh")
                vb = A(W2, BF16, "vb")
                TT(P, "dve", Rt, rW, E1, ALU.mult)
                STT(P, At, kk, -1.0, E0, ALU.mult, ALU.mult)
                TT(P, "dve", Bt, bvec, Einv, ALU.mult)
                TT(P, "dve", Kt, kmod, Einv, ALU.mult)
                yield
                TT(P, "dve", Kh, kmod, Eend, ALU.mult)
                TT(P, "pool", Bh, bvec, Eend, ALU.mult)
                CP(P, "pool", vb, vW)
                yield
                TMs = []
                for hp in range(2):
                    pst = P.psum()
                    pstb = pst.bitcast(BF16)
                    for c in range(NCH):
                        for i, src_ in enumerate([vb, Kh, Bh, At]):
                            TR(P, pstb[:, (c * 4 + i) * 128:(c * 4 + i + 1) * 128],
                               src_[:, hp * T + c * 128: hp * T + (c + 1) * 128], identb)
                    tm = A(NCH * 512, BF16, f"TM{hp}")
                    CP(P, "act", tm, pstb[:, 0:NCH * 512])
                    TMs.append(tm)
                    yield

                def TMc(hp, c):
                    return TMs[hp][:, c * 512:(c + 1) * 512]
                NJ = 2 * NCH
                NJ2 = 2 * NJ
                al = [t_.bitcast(BF16) for t_ in (kk, kmod, bvec, rn, css, cse, E1)]
                P0, P0T = al[0], al[1]
                Aall = [None] * NJ2
                for hp in range(2):
                    for s in range(2):
                        rows = slice(64 * s, 64 * s + 64)
                        ps_p0 = P.psum()
                        for c in range(NCH):
                            j = hp * NJ + s * NCH + c
                            cs = slice(hp * T + c * 128, hp * T + (c + 1) * 128)
                            ps = P.psum()
                            MM(P, ps[:, 0:128], Bt[rows, cs], At[rows, cs])
                            MM(P, ps[:, 128:256], Kt[rows, cs], At[rows, cs])
                            MM(P, ps[:, 256:384], Bt[rows, cs], Rt[rows, cs])
                            MM(P, ps[:, 384:512], Kt[rows, cs], Rt[rows, cs])
                            aa = A(512, BF16, f"Aall{j}")
                            TT(P, "dve", aa, ps, cmask, ALU.mult)
                            Aall[j] = aa
                            CP(P, "act", P0T[:, j * 128:(j + 1) * 128], aa[:, 0:128])
                            MM(P, ps_p0[:, c * 128:(c + 1) * 128], At[rows, cs], Bt[rows, cs])
                        j0 = hp * NJ + s * NCH
                        TT(P, "dve", P0[:, j0 * 128:(j0 + NCH) * 128], ps_p0[:, 0:T], sl4[:, 0:T], ALU.mult)
                        yield
                G = al[2]
                for hp in range(2):
                    hw = slice(hp * NJ * 128, (hp + 1) * NJ * 128)
                    TT(P, "pool", G[:, hw], P0T[:, hw], ident4b, ALU.add)
                Pk, PkT = P0, P0T
                Pn = [al[3], al[4]]
                PnT = [al[5], al[6]]
                NLEV = 6
                for lev in range(NLEV):
                    nP, nPT = Pn[lev % 2], PnT[lev % 2]
                    for hp in range(2):
                        hw = slice(hp * NJ * 128, (hp + 1) * NJ * 128)
                        ps1 = P.psum()
                        for j in range(NJ):
                            js = slice((hp * NJ + j) * 128, (hp * NJ + j + 1) * 128)
                            MM(P, ps1[:, j * 128:(j + 1) * 128], PkT[:, js], Pk[:, js])
                        CP(P, "act", nP[:, hw], ps1[:, 0:NJ * 128])
                        if lev < NLEV - 1:
                            ps2 = P.psum()
                            for j in range(NJ):
                                js = slice((hp * NJ + j) * 128, (hp * NJ + j + 1) * 128)
                                MM(P, ps2[:, j * 128:(j + 1) * 128], Pk[:, js], PkT[:, js])
                            CP(P, "dve", nPT[:, hw], ps2[:, 0:NJ * 128])
                    yield
                    for hp in range(2):
                        hw = slice(hp * NJ * 128, (hp + 1) * NJ * 128)
                        ps3 = P.psum()
                        for j in range(NJ):
                            js = slice((hp * NJ + j) * 128, (hp * NJ + j + 1) * 128)
                            MM(P, ps3[:, j * 128:(j + 1) * 128], nP[:, js], G[:, js])
                        TT(P, "dve", G[:, hw], G[:, hw], ps3[:, 0:NJ * 128], ALU.add)
                    Pk, PkT = nP, nPT
                    yield
                XW = Einv.bitcast(BF16)
                for hp in range(2):
                    ps = P.psum()
                    for s in range(2):
                        for c in range(NCH):
                            jl = s * NCH + c
                            j = hp * NJ + jl
                            MM(P, ps[:, jl * 128:jl * 128 + 64], Aall[j][:, 128:256], TMc(hp, c)[:, 64 * s:64 * s + 64])
                            MM(P, ps[:, jl * 128 + 64:jl * 128 + 128], G[:, j * 128:(j + 1) * 128],
                               TMc(hp, c)[:, 384 + 64 * s:384 + 64 * s + 64])
                    CP(P, "act", XW[:, hp * NJ * 128:(hp + 1) * NJ * 128], ps[:, 0:NJ * 128])
                yield
                U0 = rW.bitcast(BF16)[:, 0:NJ2 * 64]
                ps = P.psum()
                for j in range(NJ2):
                    MM(P, ps[:, j * 64:(j + 1) * 64], G[:, j * 128:(j + 1) * 128], XW[:, j * 128:j * 128 + 64])
                CP(P, "act", U0, ps[:, 0:NJ2 * 64])
                yield
                Nn = kW[:, 0:NC2 * 64]
                for hp in range(2):
                    ps = P.psum()
                    for s in range(2):
                        orow = slice(64 * s, 64 * s + 64)
                        for c in range(NCH):
                            j = hp * NJ + s * NCH + c
                            tmc = TMc(hp, c)
                            MM(P, ps[orow, c * 128:c * 128 + 64], XW[:, j * 128 + 64:j * 128 + 128],
                               tmc[:, 256 + 64 * s:256 + 64 * s + 64])
                            MM(P, ps[orow, c * 128 + 64:c * 128 + 128], tmc[:, 256 + 64 * s:256 + 64 * s + 64],
                               U0[:, j * 64:(j + 1) * 64], start=True, stop=False)
                            MM(P, ps[orow, c * 128 + 64:c * 128 + 128], tmc[:, 128 + 64 * s:128 + 64 * s + 64],
                               tmc[:, 64 * s:64 * s + 64], start=False, stop=True)
                    for c in range(NCH):
                        CP(P, "act", mbbd[hp][c][0:64, 0:64], ps[0:64, c * 128:c * 128 + 64])
                        CP(P, "act", mbbd[hp][c][64:128, 64:128], ps[64:128, c * 128:c * 128 + 64])
                        CP(P, "dve", Nn[:, (hp * NCH + c) * 64:(hp * NCH + c + 1) * 64], ps[:, c * 128 + 64:c * 128 + 128])
                yield
                RhT = kW[:, NC2 * 64:NC2 * 64 + W2 // 2].bitcast(BF16)
                Y0 = vW
                for hp in range(2):
                    hsl = slice(hp * T, (hp + 1) * T)
                    ps = P.psum()
                    for s in range(2):
                        orow = slice(64 * s, 64 * s + 64)
                        for c in range(NCH):
                            j = hp * NJ + s * NCH + c
                            MM(P, ps[orow, c * 128:(c + 1) * 128], XW[:, j * 128 + 64:j * 128 + 128],
                               Aall[j][:, 256:384])
                    TT(P, "dve", RhT[:, hsl], ps[:, 0:T], Rt[:, hsl], ALU.add)
                    ps = P.psum()
                    for s in range(2):
                        orow = slice(64 * s, 64 * s + 64)
                        for c in range(NCH):
                            j = hp * NJ + s * NCH + c
                            MM(P, ps[orow, c * 128:(c + 1) * 128], U0[:, j * 64:(j + 1) * 64], Aall[j][:, 256:384],
                               start=True, stop=False)
                            MM(P, ps[orow, c * 128:(c + 1) * 128], TMc(hp, c)[:, 64 * s:64 * s + 64],
                               Aall[j][:, 384:512], start=False, stop=True)
                    CP(P, "act", Y0[:, hsl], ps[:, 0:T])
                yield
                Hs = sgW[:, 0:2 * (NCH + 1) * 64]
                Hb = rW[:, NJ2 * 32:NJ2 * 32 + NC2 * 32].bitcast(BF16)

                def Hsl(hp, c):
                    o_ = (hp * (NCH + 1) + c) * 64
                    return Hs[:, o_:o_ + 64]

                def Hbl(hp, c):
                    o_ = (hp * NCH + c) * 64
                    return Hb[:, o_:o_ + 64]
                for hp in range(2):
                    CP(P, "pool", Hsl(hp, 0), rw_H[hp])
                for c in range(NCH):
                    for hp in range(2):
                        CP(P, "pool", Hbl(hp, c), Hsl(hp, c))
                        ps = P.psum()
                        MM(P, ps[:, 0:64], mbbd[hp][c], Hsl(hp, c), start=True, stop=False)
                        MM(P, ps[:, 0:64], identf, Nn[:, (hp * NCH + c) * 64:(hp * NCH + c + 1) * 64], start=False, stop=True)
                        STT(P, Hsl(hp, c + 1), Hsl(hp, c), gC[:, hp * NCH + c:hp * NCH + c + 1],
                            ps[:, 0:64], ALU.mult, ALU.add)
                    yield
                for hp in range(2):
                    CP(P, "pool", rw_H[hp], Hsl(hp, NCH))
                y = aW
                for hp in range(2):
                    hsl = slice(hp * T, (hp + 1) * T)
                    pse, pso = P.psum(), P.psum()
                    for c in range(NCH):
                        cs = slice(c * 128, (c + 1) * 128)
                        gcs = slice(hp * T + c * 128, hp * T + (c + 1) * 128)
                        MM(P, pse[0:64, cs], Hbl(hp, c)[0:64, :], RhT[0:64, gcs])
                        MM(P, pso[64:128, cs], Hbl(hp, c)[64:128, :], RhT[64:128, gcs])
                    TT(P, "dve", y[0:64, hsl], pse[0:64, 0:T], Y0[0:64, hsl], ALU.add)
                    TT(P, "dve", y[64:128, hsl], pso[64:128, 0:T], Y0[64:128, hsl], ALU.add)
                    dbg(f"rw_y{hp}", y[:, hsl], 128, T, tok0, ntok)
                yield
                yb = A(W2, BF16, "yb")
                CP(P, "pool", yb, y)
                ps_m = P.psum()
                MM(P, ps_m, ones64b, yb)
                yc = A(W2, F32, "yc")
                STT(P, yc, ps_m, -1.0 / 64, y, ALU.mult, ALU.add)
                yield
                TT(P, "pool", yb, yc, yc, ALU.mult)
                ps_v = P.psum()
                MM(P, ps_v, ones64b, yb)
                rs = y
                rsqrt_act(rs, ps_v, 1.0 / 64, epsc[:, 1:2])
                yield
                TT(P, "dve", yc, yc, rs, ALU.mult)
                for hp in range(2):
                    hsl = slice(hp * T, (hp + 1) * T)
                    STT(P, yc[:, hsl], yc[:, hsl], col("ln_g", hp), bonus[:, hsl], ALU.mult, ALU.add)
                TT(P, "dve", ycat[:, 0:W2], yc, gSW, ALU.mult)
                yield

            def lru_thread():
                R = Region(base_thr + SZ_RW, SZ_LRU)
                A = R.alloc
                lsets = [(A(3 + T, F32, f"xbuf{i}"), A(T, F32, f"gt{i}"), A(T, F32, f"xc{i}"), A(T, BF16, f"xcb{i}"))
                         for i in range(2)]
                st = []
                for j in range(4):
                    xbuf, gt, xc, xcb = lsets[j % 2]
                    CP(P, "pool", xbuf[:, 0:3], lru_xc[j])
                    proj(1024 + j * 128, 128, lambda ps: CP(P, "act", xbuf[:, 3:3 + T], ps))
                    yield
                    proj(1536 + j * 128, 128, lambda ps: CP(P, "act", gt, ps))
                    CP(P, "pool", lru_xc[j], xbuf[:, T:T + 3])
                    yield
                    TS(P, "pool", xc, xbuf[:, 0:T], col("cw0", j), ALU.mult, col("cb", j), ALU.add)
                    for tap in range(1, 4):
                        STT(P, xc, xbuf[:, tap:tap + T], col(f"cw{tap}", j), xc, ALU.mult, ALU.add)
                    CP(P, "pool", xcb, xc)
                    yield
                    ps_r, ps_i = P.psum(), P.psum()
                    MM(P, ps_r[:, 0:T], wabd[:, j * 128:(j + 1) * 128], xcb)
                    MM(P, ps_i[:, 0:T], wxbd[:, j * 128:(j + 1) * 128], xcb)
                    gr = A(T, F32, f"gr{j}")
                    uu = A(T, F32, f"uu{j}")
                    ge = A(T, F32, f"ge{j}")
                    ACT(P, gr, ps_r[:, 0:T], AF.Sigmoid, bias=col("ba", j))
                    ACT(P, uu, ps_i[:, 0:T], AF.Sigmoid, bias=col("bx", j))
                    yield
                    TT(P, "dve", uu, uu, xc, ALU.mult)
                    TT(P, "pool", ge, gt, gt, ALU.mult)
                    TS(P, "pool", ge, ge, 0.044715, ALU.mult, 1.0, ALU.add)
                    TT(P, "dve", ge, ge, gt, ALU.mult)
                    ACT(P, ge, ge, AF.Sigmoid, scale=1.5957691216057308)
                    TT(P, "dve", ge, ge, gt, ALU.mult)
                    st.append((gr, uu, ge))
                    yield
                yield "B"
                for j in range(4):
                    gr, uu, ge = st[j]
                    hh, av, a2, sqb = lsets[j % 2][0][:, 0:T], lsets[j % 2][1], lsets[j % 2][2], lsets[j % 2][3]
                    ACT(P, av, gr, AF.Exp, scale=c1[:, j:j + 1])
                    ACT(P, a2, gr, AF.Exp, scale=c2[:, j:j + 1])
                    ACT(P, a2, a2, AF.Ln, bias=1.0, scale=-1.0)
                    ACT(P, a2, a2, AF.Exp, scale=0.5)
                    yield
                    TT(P, "dve", uu, uu, a2, ALU.mult)
                    SCAN(P, hh, av, uu, lru_h[:, j:j + 1])
                    CP(P, "pool", lru_h[:, j:j + 1], hh[:, T - 1:T])
                    yl = av
                    TT(P, "dve", yl, hh, ge, ALU.mult)
                    dbg(f"lru_y{j}", yl, 128, T, tok0, ntok)
                    yield
                    TT(P, "pool", sqb, yl, yl, ALU.mult)
                    ps_m = P.psum()
                    MM(P, ps_m[:, 0:T], ones64b, sqb)
                    rs = a2
                    rsqrt_act(rs, ps_m[:, 0:T], 1.0 / 64, epsc[:, 0:1])
                    STT(P, ycat3[:, 2 + j, :], yl, col("lng", j), rs, ALU.mult, ALU.mult)
                    yield

            def gla_thread():
                R = Region(base_thr + SZ_RW + SZ_LRU, SZ_GLA)
                A = R.alloc
                q = A(T, F32, "q")
                kg = A(T, F32, "kg")
                vg = [A(T, BF16, f"vg{i}") for i in range(2)]
                gg = [A(T, F32, f"gg{i}") for i in range(2)]
                gklo = A(T, BF16, "gklo")
                sgm = A(T, F32, "sgm")
                proj(2048, 128, lambda ps: CP(P, "act", q, ps))
                yield
                proj(2176, 128, lambda ps: CP(P, "act", kg, ps))
                yield
                proj(2304, 128, lambda ps: CP(P, "act", vg[0], ps))
                yield
                proj(2432, 128, lambda ps: CP(P, "act", vg[1], ps))
                yield
                proj(2560, 16, lambda ps: CP(P, "act", gklo[0:16, :], ps))
                yield
                for i in range(2):
                    def evg(ps, i=i):
                        ACT(P, sgm, ps, AF.Sigmoid)
                        TT(P, "dve", gg[i], ps, sgm, ALU.mult)
                    proj(2576 + 128 * i, 128, evg)
                    yield
                ps_gk = P.psum()
                MM(P, ps_gk[:, 0:T], gkup[0:16, :], gklo[0:16, :])
                la = A(T, F32, "la")
                ACT(P, la, ps_gk[:, 0:T], AF.Sigmoid, bias=col("gkb"))
                yield "B"
                ACT(P, la, la, AF.Ln)
                bc = A(T, F32, "bc")
                SCAN(P, bc, rmask[:, 0:T], la, 0.0)
                Eq = A(T, F32, "Eq")
                Ek = A(T, F32, "Ek")
                Ee = la
                ACT(P, Eq, bc, AF.Exp, scale=1.0 / 16)
                ACT(P, Ek, bc, AF.Exp, scale=-1.0 / 16)
                yield
                nb = A(NCH, F32, "nbg")
                TS(P, "pool", nb, bc.re("p (c t) -> p c t", c=NCH)[:, :, 127], 1.0 / 16, ALU.mult)
                for c in range(NCH):
                    ACT(P, Ee[:, c * 128:(c + 1) * 128], bc[:, c * 128:(c + 1) * 128], AF.Exp,
                        bias=nb[:, c:c + 1], scale=-1.0 / 16)
                gCg = A(NCH, F32, "gCg")
                CP(P, "pool", gCg, Eq.re("p (c t) -> p c t", c=NCH)[:, :, 127])
                yield
                qin = A(T, BF16, "qin")
                STT(P, qin, q, 32.0 ** -0.5, Eq, ALU.mult, ALU.mult)
                kin = [A(T, BF16, f"kin{i}") for i in range(2)]
                for i in range(2):
                    STT(P, kin[i], kg, par[:, i:i + 1], Ek, ALU.mult, ALU.mult)
                kend = A(T, BF16, "kend")
                TT(P, "pool", kend, kg, Ee, ALU.mult)
                yield
                GT = []
                for c in range(NCH):
                    pst = P.psum()
                    pstb = pst.bitcast(BF16)
                    cs = slice(c * 128, (c + 1) * 128)
                    TR(P, pstb[:, 0:128], vg[0][:, cs], identb)
                    TR(P, pstb[:, 128:256], vg[1][:, cs], identb)
                    TR(P, pstb[:, 256:384], kend[:, cs], identb)
                    gt_ = A(384, BF16, f"GT{c}")
                    CP(P, "act", gt_, pstb[:, 0:384])
                    GT.append(gt_)
                    yield
                ST = []
                for h in range(4):
                    rows = slice(64 * (h // 2), 64 * (h // 2) + 64)
                    ps = P.psum()
                    for c in range(NCH):
                        cs = slice(c * 128, (c + 1) * 128)
                        MM(P, ps[:, cs], kin[h % 2][rows, cs], qin[rows, cs])
                    st_ = A(T, BF16, f"ST{h}")
                    TT(P, "dve", st_, ps[:, 0:T], ui4[:, 0:T], ALU.mult)
                    ST.append(st_)
                    yield
                Sb = []
                Scur = A((NCH + 1) * 256, F32, "Scur")
                CP(P, "pool", Scur[:, 0:256], gla_S)
                for c in range(NCH):
                    sb_ = A(256, BF16, f"Sb{c}")
                    TT(P, "pool", sb_, Scur[:, c * 256:(c + 1) * 256], bmask, ALU.mult)
                    Sb.append(sb_)
                    ps = P.psum()
                    MM(P, ps[:, 0:256], GT[c][:, 256:384], GT[c][:, 0:256])
                    STT(P, Scur[:, (c + 1) * 256:(c + 2) * 256], Scur[:, c * 256:(c + 1) * 256], gCg[:, c:c + 1],
                        ps[:, 0:256], ALU.mult, ALU.add)
                    yield
                CP(P, "pool", gla_S, Scur[:, NCH * 256:(NCH + 1) * 256])
                o = A(T, F32, "o")
                ob = A(T, BF16, "ob")
                rs = A(T, F32, "rsg")
                for vp in range(2):
                    ps_in, ps_it = P.psum(), P.psum()
                    for s in range(2):
                        h = 2 * vp + s
                        orow = slice(64 * s, 64 * s + 64)
                        krow = slice(64 * vp, 64 * vp + 64)
                        for c in range(NCH):
                            cs = slice(c * 128, (c + 1) * 128)
                            MM(P, ps_in[orow, cs], GT[c][:, 64 * h:64 * h + 64], ST[h][:, cs])
                            MM(P, ps_it[orow, cs], Sb[c][krow, 64 * h:64 * h + 64], qin[krow, cs])
                    CP(P, "act", o, ps_it[:, 0:T])
                    TT(P, "dve", o, o, ps_in[:, 0:T], ALU.add)
                    dbg(f"gla_o{vp}", o, 128, T, tok0, ntok)
                    yield
                    TT(P, "pool", ob, o, o, ALU.mult)
                    ps_m = P.psum()
                    MM(P, ps_m[:, 0:T], ones64b, ob)
                    rsqrt_act(rs, ps_m[:, 0:T], 1.0 / 64, epsc[:, 0:1])
                    STT(P, o, o, col("gng"), rs, ALU.mult, ALU.mult)
                    TT(P, "dve", ycat3[:, 6 + vp, :], o, gg[vp], ALU.mult)
                    yield

            _skip = os.environ.get("K_SKIP", "").split(",")
            threads = [g_ for n_, g_ in (("rw", rw_thread), ("lru", lru_thread), ("gla", gla_thread)) if n_ not in _skip]
            threads = [g_() for g_ in threads]
            def run_greedy(ths):
                ready = {id(t_): 0.0 for t_ in ths}
                out_ = []
                while ths:
                    th = min(ths, key=lambda t_: ready[id(t_)])
                    P.step_t = 0.0
                    try:
                        v_ = next(th)
                    except StopIteration:
                        ths.remove(th)
                        continue
                    if P.step_t > 0:
                        ready[id(th)] = P.step_t
                    if v_ == "B":
                        ths.remove(th)
                        out_.append(th)
                return out_

            if os.environ.get("K_GREEDY", "0") == "1":
                atB = run_greedy(threads)
                if ti + 1 < NT:
                    m_load(ti + 1)
                    atB.append(m_norm(ti + 1))
                run_greedy(atB)
            else:
                atB = []
                while threads:
                    for th in list(threads):
                        try:
                            v_ = next(th)
                        except StopIteration:
                            threads.remove(th)
                            continue
                        if v_ == "B":
                            threads.remove(th)
                            atB.append(th)
                threads = atB
                if ti > 0 and os.environ.get("K_DEFER", "1") == "1":
                    m_outproj(ti - 1)
                if ti + 1 < NT:
                    m_load(ti + 1)
                    threads.append(m_norm(ti + 1))
                rr_w = int(os.environ.get("K_RWW", "1"))
                while threads:
                    for ith, th in enumerate(list(threads)):
                        for _rep in range(rr_w if ith == 0 else 1):
                            try:
                                next(th)
                            except StopIteration:
                                threads.remove(th)
                                break
            P.arena_off = base_thr
            if os.environ.get("K_DEFER", "1") != "1" and ti < NT - 1:
                m_outproj(ti)

            for kc in range(8):
                dbg(f"ycat{kc}", ycat3[:, kc, :], 128, T, tok0, ntok)

        m_outproj(NT - 1)

        if os.environ.get("K_BARRIER", "1") == "1":
            P.barrier()
        P.arena_off = persist_mark
        if last:
            gBf = P.alloc(D, F32, "gBf")
            P.dma("sp", gBf.ap, final_g.partition_broadcast(128), writes=[gBf])
        gB2 = P.alloc(D, F32, "gB2")
        P.dma("sp", gB2.ap, norm2_g[l:l + 1, :].partition_broadcast(128), writes=[gB2])
        wg_sb = P.alloc(8 * DFF, BF16, "wg")
        wu_sb = P.alloc(8 * DFF, BF16, "wu")
        wd_sb = P.alloc(NFC * D, BF16, "wd")
        wg_v = wg_sb.ap.rearrange("p (k f) -> p k f", k=8)
        wu_v = wu_sb.ap.rearrange("p (k f) -> p k f", k=8)
        for kp in range(4):
            for hf in range(2):
                sl_ = slice(hf * 1408, (hf + 1) * 1408)
                P.dma("pool", wg_v[:, 2 * kp:2 * kp + 2, sl_],
                      w_gate[l, 2 * kp * 128:(2 * kp + 2) * 128, sl_].rearrange("(k p) f -> p k f", p=128),
                      writes=[wg_sb])
                P.dma("pool", wu_v[:, 2 * kp:2 * kp + 2, sl_],
                      w_up[l, 2 * kp * 128:(2 * kp + 2) * 128, sl_].rearrange("(k p) f -> p k f", p=128),
                      writes=[wu_sb])
        wd_v = wd_sb.ap.rearrange("p (i d) -> p i d", i=NFC)
        for fp in range(NFC // 2):
            P.dma("pool", wd_v[:, 2 * fp:2 * fp + 2, :],
                  w_down[l, 2 * fp * 128:(2 * fp + 2) * 128, :].rearrange("(i p) d -> p i d", p=128), writes=[wd_sb])
        xTs = [[P.alloc(D, F32, f"fxT{p_}{b}") for b in range(NCH)] for p_ in range(2)]
        hnFs = [P.alloc(8 * T, BF16, f"fhnF{p_}") for p_ in range(2)]
        hF = P.alloc(NFC * T, BF16, "hF")
        hF3 = hF.re("p (k t) -> p k t", k=NFC)
        fhn = P.alloc(D, BF16, "fhn")
        sgt = [P.alloc(T, F32, f"sgt{i}") for i in range(2)]
        scs = [P.alloc(2, F32, f"fsc{i}") for i in range(4)]
        yos = [P.alloc(D, F32, f"yo{i}") for i in range(2)] if last else []
        sci = [0]

        def nsc():
            sci[0] += 1
            return scs[sci[0] % 4]

        def f_pro(ti):
            tok0 = ti * T
            xT = xTs[ti % 2]
            hnF3 = hnFs[ti % 2].re("p (k t) -> p k t", k=8)
            for b in range(NCH):
                P.dma("sp", xT[b].ap, xa_d[tok0 + b * 128: tok0 + (b + 1) * 128, :], reads=[xa_buf], writes=[xT[b]])
            for b in range(NCH):
                rms_block(xT[b], gB2, fhn, nsc())
                pst = P.psum()
                pstb = pst.bitcast(BF16)
                for kc in range(8):
                    TR(P, pstb[:, kc * 128:(kc + 1) * 128], fhn[:, kc * 128:(kc + 1) * 128], identb)
                CP(P, "act", hnF3[:, :, b * 128:(b + 1) * 128], pstb.re("p (k t) -> p k t", k=8))

        def f_gateup(ti):
            hnF3 = hnFs[ti % 2].re("p (k t) -> p k t", k=8)
            for fc in range(NFC):
                ps = P.psum()
                for kc in range(8):
                    MM(P, ps[:, 0:T], wg_sb[:, kc * DFF + fc * 128: kc * DFF + (fc + 1) * 128], hnF3[:, kc, :],
                       start=(kc == 0), stop=(kc == 7))
                for kc in range(8):
                    MM(P, ps[:, T:2 * T], wu_sb[:, kc * DFF + fc * 128: kc * DFF + (fc + 1) * 128], hnF3[:, kc, :],
                       start=(kc == 0), stop=(kc == 7))
                s_ = sgt[fc % 2]
                ACT(P, s_, ps[:, 0:T], AF.Silu)
                TT(P, "dve", hF3[:, fc, :], s_, ps[:, T:2 * T], ALU.mult)

        def f_down(ti):
            tok0 = ti * T
            xT = xTs[ti % 2]
            for b in range(NCH):
                for hf in range(2):
                    ps = P.psum()
                    for fc in range(NFC):
                        MM(P, ps, hF3[:, fc, b * 128:(b + 1) * 128],
                           wd_sb[:, fc * D + hf * 512: fc * D + (hf + 1) * 512], start=(fc == 0), stop=(fc == NFC - 1))
                    TT(P, "dve", xT[b][:, hf * 512:(hf + 1) * 512], xT[b][:, hf * 512:(hf + 1) * 512], ps, ALU.add)
                if last:
                    yo = yos[b % 2]
                    rms_block(xT[b], gBf, yo, nsc())
                    fin.append(P.dma("sp", out_d[tok0 + b * 128: tok0 + (b + 1) * 128, :], yo.ap, reads=[yo]))
                else:
                    P.dma("sp", xb_d[tok0 + b * 128: tok0 + (b + 1) * 128, :], xT[b].ap, reads=[xT[b]],
                          writes=[xb_buf])

        f_pro(0)
        for ti in range(NT):
            f_gateup(ti)
            if ti + 1 < NT:
                f_pro(ti + 1)
            f_down(ti)
    P.finish(fin)
    P.build()
    return nc, list(dbg_out.keys())


def make_in_map(inputs, xs, nlayers):
    m = {"x": np.ascontiguousarray(xs, dtype=np.float32)}
    for k in ["w_in", "w_out", "ffn_w_gate", "ffn_w_up", "ffn_w_down", "rw_w_up", "rw_a_up", "rw_g_up",
              "lru_wa", "lru_wx", "gla_gk_up", "norm1_g", "norm2_g"]:
        m[k] = np.ascontiguousarray(np.asarray(inputs[k], np.float32)[:nlayers])
    m["cols"] = np.stack([pack_cols(inputs, l) for l in range(nlayers)])
    m["final_norm_g"] = np.asarray(inputs["final_norm_g"], np.float32).reshape(1, D)
    for k, v in make_consts().items():
        m["c_" + k] = v
    return m


_CACHE = {}


def kernel(**inputs):
    x = np.asarray(inputs["x"], np.float32)
    B, S, _ = x.shape
    L = np.asarray(inputs["w_in"]).shape[0]
    key = (S, L)
    if key not in _CACHE:
        _CACHE[key] = build_program(S, L)[0]
    nc = _CACHE[key]
    in_maps = [make_in_map(inputs, x[c % B], L) for c in range(8)]
    res = run_bass_kernel_spmd(nc, in_maps, core_ids=list(range(8)))
    return np.stack([res.results[b]["out"] for b in range(B)], axis=0)
```
